# Optimizing a Trainium2 kernel written in Bass

```python
import jax, jax.numpy as jnp
from jax import lax
import numpy as np

D_MODEL = 1024
BATCH = 4
SEQ = 8192
DEPTH = 1
DEC_BATCH = 16
DEC_SEQ = 64
PAST_LEN = 2048

CHUNK = 64
N_META = 16
N_HEADS_A = 8
HEAD_DIM = 128
ROPE_DIM_A = HEAD_DIM // 4
N_HEADS_I = 8
HEAD_DIM_I = 64
ROPE_DIM_I = HEAD_DIM_I // 4
TOPK_MAX = 256
ROPE_THETA = 500000.0
Q_BLOCK = 128
RW_HEAD = 64
RW_WIDTH = D_MODEL
RW_HEADS = RW_WIDTH // RW_HEAD
LORA_W = 64
LORA_A = 64
LORA_G = 128
RW_GN_EPS = 64e-5
D_FF = -(-8 * D_MODEL // (3 * 256)) * 256
LN_EPS = 1e-5
ALPHA = (2 * DEPTH) ** 0.25
BETA = (8 * DEPTH) ** -0.25
NEG = -1e30
ATTN_COLS = N_HEADS_A * HEAD_DIM
RW_SIZES = (RW_WIDTH, RW_WIDTH, RW_WIDTH, LORA_W, LORA_A, LORA_G)
SHIFT_COLS = sum(RW_SIZES)
IN_SIZES = (ATTN_COLS, HEAD_DIM, HEAD_DIM, N_HEADS_I * HEAD_DIM_I, HEAD_DIM_I, N_HEADS_I, 2 * D_MODEL, SHIFT_COLS)
IN_COLS = sum(IN_SIZES)

kernel_name = 'dsa_rwkv7_gated_hybrid_stream_step'


def split_cols(p, sizes):
    return jnp.split(p, np.cumsum(sizes)[:-1].tolist(), axis=-1)


def layernorm(x, g, b):
    xf = x.astype(jnp.float32)
    mu = xf.mean(-1, keepdims=True)
    var = jnp.square(xf - mu).mean(-1, keepdims=True)
    return ((xf - mu) * lax.rsqrt(var + LN_EPS) * g + b).astype(x.dtype)


def rope_tables(pos, rot_dim):
    inv = ROPE_THETA ** (-jnp.arange(0, rot_dim, 2, dtype=jnp.float32) / rot_dim)
    ang = pos.astype(jnp.float32)[:, None] * inv[None]
    return jnp.cos(ang)[:, None, :], jnp.sin(ang)[:, None, :]


def apply_rope(x, cos, sin):
    half = cos.shape[-1]
    x1 = x[..., :half].astype(jnp.float32)
    x2 = x[..., half:2 * half].astype(jnp.float32)
    rot = jnp.concatenate([x1 * cos - x2 * sin, x2 * cos + x1 * sin], axis=-1).astype(x.dtype)
    return jnp.concatenate([rot, x[..., 2 * half:]], axis=-1)


def gather_rows(rows, idx):
    return jax.vmap(lambda r, i: r[i])(rows, idx)


def dsa_block(q, qi, wi, qc, K, V, KI, kc, n_sel):
    idx_logits = jnp.einsum('bqhd,bsd->bqhs', qi, KI) * HEAD_DIM_I ** -0.5
    score = jnp.einsum('bqhs,bqh->bqs', jax.nn.relu(idx_logits), wi).astype(jnp.float32)
    admissible = kc[None, :] <= qc[:, None]
    score = jnp.where(admissible[None], score, NEG)
    top_val, top_idx = lax.top_k(score, n_sel)
    valid = top_val > 0.5 * NEG
    k_sel = gather_rows(K, top_idx)
    v_sel = gather_rows(V, top_idx)
    logits = jnp.einsum('bqhd,bqkd->bqhk', q, k_sel).astype(jnp.float32) * HEAD_DIM ** -0.5
    logits = jnp.where(valid[:, :, None, :], logits, NEG)
    prob = jax.nn.softmax(logits, axis=-1).astype(v_sel.dtype)
    return jnp.einsum('bqhk,bqkd->bqhd', prob, v_sel)


def dsa_attention(q, qi, wi, qc, K, V, KI, kc, n_sel):
    B, T = q.shape[:2]
    if T <= Q_BLOCK:
        return dsa_block(q, qi, wi, qc, K, V, KI, kc, n_sel)
    n_blk = -(-T // Q_BLOCK)
    pad = n_blk * Q_BLOCK - T

    def to_blocks(a):
        a = jnp.pad(a, [(0, 0), (0, pad)] + [(0, 0)] * (a.ndim - 2))
        return jnp.moveaxis(a.reshape((B, n_blk, Q_BLOCK) + a.shape[2:]), 1, 0)

    qc_b = jnp.pad(qc, (0, pad)).reshape(n_blk, Q_BLOCK)
    out = lax.map(lambda args: dsa_block(args[0], args[1], args[2], args[3], K, V, KI, kc, n_sel),
                  (to_blocks(q), to_blocks(qi), to_blocks(wi), qc_b))
    out = jnp.moveaxis(out, 0, 1).reshape(B, n_blk * Q_BLOCK, N_HEADS_A, HEAD_DIM)
    return out[:, :T]


def wkv7_scan(r, decay, k, v, a, b, S0):
    tm = lambda t: jnp.moveaxis(t.astype(jnp.float32), 1, 0)

    def step(S, inp):
        r_t, w_t, k_t, v_t, a_t, b_t = inp
        sa = jnp.einsum('bhij,bhj->bhi', S, a_t)
        S = S * w_t[:, :, None, :] + sa[..., None] * b_t[:, :, None, :] + v_t[..., None] * k_t[:, :, None, :]
        return S, jnp.einsum('bhij,bhj->bhi', S, r_t)

    S_fin, y = lax.scan(step, S0.astype(jnp.float32), (tm(r), tm(decay), tm(k), tm(v), tm(a), tm(b)))
    return jnp.moveaxis(y, 0, 1), S_fin.astype(S0.dtype)


def head_groupnorm(y, g, b):
    mu = y.mean(-1, keepdims=True)
    var = jnp.square(y - mu).mean(-1, keepdims=True)
    return (y - mu) * lax.rsqrt(var + RW_GN_EPS) * g.reshape(RW_HEADS, RW_HEAD) + b.reshape(RW_HEADS, RW_HEAD)


def rwkv7_branch(rw, shift0, wkv0, lp):
    B, T, _ = rw.shape
    prev = jnp.concatenate([shift0[:, None, :], rw[:, :-1]], axis=1)
    xs = rw + (prev - rw) * lp['rw_mu']
    r, k, v, lw, la, lg = split_cols(xs, RW_SIZES)
    w_log = -jax.nn.softplus(-(lp['rw_w0'] + jnp.tanh(lw) @ lp['rw_w2'])) - 0.5
    decay = jnp.exp(-jnp.exp(w_log.astype(jnp.float32)))
    a = jax.nn.sigmoid(lp['rw_a0'] + la @ lp['rw_a2'])
    g = jax.nn.sigmoid(lg) @ lp['rw_g2']
    heads = lambda t: t.reshape(B, T, RW_HEADS, RW_HEAD)
    kk = heads((k * lp['rw_k_k']).astype(jnp.float32))
    kk = kk / jnp.maximum(jnp.sqrt(jnp.sum(kk * kk, axis=-1, keepdims=True)), 1e-12)
    k = k * (1 + (a - 1) * lp['rw_k_a'])
    r_h, k_h, v_h = heads(r), heads(k), heads(v)
    y, wkv_new = wkv7_scan(r_h, heads(decay), k_h, v_h, -kk, kk * heads(a), wkv0)
    y = head_groupnorm(y, lp['rw_gn_g'], lp['rw_gn_b'])
    bonus = jnp.sum((r_h * k_h * lp['rw_r_k']).astype(jnp.float32), axis=-1, keepdims=True) * v_h
    out = ((y + bonus).reshape(B, T, RW_WIDTH) * g).astype(rw.dtype)
    return out @ lp['w_o_rwkv'], wkv_new


def hybrid_layer(h, pos, q_chunk, past_k, past_v, past_ik, past_chunk, wkv0, shift0, n_sel, lp):
    B, T, _ = h.shape
    q, k, v, qi, ki, wi, gate_logits, rw = split_cols(h @ lp['w_in'], IN_SIZES)
    cos_a, sin_a = rope_tables(pos, ROPE_DIM_A)
    cos_i, sin_i = rope_tables(pos, ROPE_DIM_I)
    q = apply_rope(q.reshape(B, T, N_HEADS_A, HEAD_DIM), cos_a, sin_a)
    k = apply_rope(k[:, :, None, :], cos_a, sin_a)[:, :, 0]
    qi = apply_rope(qi.reshape(B, T, N_HEADS_I, HEAD_DIM_I), cos_i, sin_i)
    ki = layernorm(ki, lp['idx_k_ln_g'], lp['idx_k_ln_b'])
    ki = apply_rope(ki[:, :, None, :], cos_i, sin_i)[:, :, 0]
    keys_k = jnp.concatenate([past_k, k], axis=1)
    keys_v = jnp.concatenate([past_v, v], axis=1)
    keys_ik = jnp.concatenate([past_ik, ki], axis=1)
    key_chunk = jnp.concatenate([past_chunk, q_chunk])
    o_attn = dsa_attention(q, qi, wi * N_HEADS_I ** -0.5, q_chunk, keys_k, keys_v, keys_ik, key_chunk, n_sel)
    o_attn = o_attn.reshape(B, T, ATTN_COLS) @ lp['w_o_attn']
    o_rw, wkv_new = rwkv7_branch(rw, shift0, wkv0, lp)
    g_attn, g_rw = jnp.split(jax.nn.sigmoid(gate_logits), 2, axis=-1)
    mixed = (g_attn * o_attn + g_rw * o_rw) @ lp['w_out']
    x1 = layernorm(ALPHA * h + mixed, lp['ln1_g'], lp['ln1_b'])
    ffn = (jax.nn.silu(x1 @ lp['ffn_w_gate']) * (x1 @ lp['ffn_w_up'])) @ lp['ffn_w_down']
    x2 = layernorm(ALPHA * x1 + ffn, lp['ln2_g'], lp['ln2_b'])
    return x2, k, v, ki, wkv_new, rw[:, -1]


def setup_inputs(seed: int = 0) -> dict:
    key = jax.random.key(seed)
    keys = iter(jax.random.split(key, 48))

    def nrm(shape, scale=1.0):
        return scale * jax.random.normal(next(keys), shape, jnp.float32)

    def unif(shape, lo, hi):
        return jax.random.uniform(next(keys), shape, jnp.float32, lo, hi)

    L = DEPTH
    P = N_META + PAST_LEN
    return {
        'x_prompt': nrm((BATCH, SEQ, D_MODEL)),
        'x_sample': nrm((DEC_BATCH, DEC_SEQ, D_MODEL)),
        'cache_k': nrm((L, DEC_BATCH, P, HEAD_DIM)),
        'cache_v': nrm((L, DEC_BATCH, P, HEAD_DIM)),
        'cache_idx_k': nrm((L, DEC_BATCH, P, HEAD_DIM_I)),
        'state_wkv': nrm((L, DEC_BATCH, RW_HEADS, RW_HEAD, RW_HEAD), 0.5),
        'state_shift': nrm((L, DEC_BATCH, SHIFT_COLS)),
        'meta_tokens': nrm((N_META, D_MODEL)),
        'ln0_g': 1.0 + nrm((D_MODEL,), 0.02),
        'ln0_b': nrm((D_MODEL,), 0.02),
        'w_in': nrm((L, D_MODEL, IN_COLS), D_MODEL ** -0.5),
        'idx_k_ln_g': 1.0 + nrm((L, HEAD_DIM_I), 0.02),
        'idx_k_ln_b': nrm((L, HEAD_DIM_I), 0.02),
        'rw_mu': unif((L, SHIFT_COLS), 0.0, 1.0),
        'rw_w0': unif((L, RW_WIDTH), -5.0, -1.0),
        'rw_w2': nrm((L, LORA_W, RW_WIDTH), LORA_W ** -0.5),
        'rw_a0': nrm((L, RW_WIDTH), 0.1),
        'rw_a2': nrm((L, LORA_A, RW_WIDTH), LORA_A ** -0.5),
        'rw_g2': nrm((L, LORA_G, RW_WIDTH), LORA_G ** -0.5),
        'rw_k_k': 0.85 + nrm((L, RW_WIDTH), 0.05),
        'rw_k_a': 1.0 + nrm((L, RW_WIDTH), 0.05),
        'rw_r_k': nrm((L, RW_HEADS, RW_HEAD), 0.1),
        'rw_gn_g': 1.0 + nrm((L, RW_WIDTH), 0.02),
        'rw_gn_b': nrm((L, RW_WIDTH), 0.02),
        'w_o_attn': nrm((L, ATTN_COLS, D_MODEL), BETA * ATTN_COLS ** -0.5),
        'w_o_rwkv': nrm((L, RW_WIDTH, D_MODEL), BETA * RW_WIDTH ** -0.5),
        'w_out': nrm((L, D_MODEL, D_MODEL), BETA * D_MODEL ** -0.5),
        'ln1_g': 1.0 + nrm((L, D_MODEL), 0.02),
        'ln1_b': nrm((L, D_MODEL), 0.02),
        'ffn_w_gate': nrm((L, D_MODEL, D_FF), D_MODEL ** -0.5),
        'ffn_w_up': nrm((L, D_MODEL, D_FF), D_MODEL ** -0.5),
        'ffn_w_down': nrm((L, D_FF, D_MODEL), BETA * D_FF ** -0.5),
        'ln2_g': 1.0 + nrm((L, D_MODEL), 0.02),
        'ln2_b': nrm((L, D_MODEL), 0.02),
    }


def reference(x_prompt, x_sample, cache_k, cache_v, cache_idx_k, state_wkv, state_shift, meta_tokens,
              ln0_g, ln0_b, w_in, idx_k_ln_g, idx_k_ln_b, rw_mu, rw_w0, rw_w2, rw_a0, rw_a2, rw_g2,
              rw_k_k, rw_k_a, rw_r_k, rw_gn_g, rw_gn_b, w_o_attn, w_o_rwkv, w_out, ln1_g, ln1_b,
              ffn_w_gate, ffn_w_up, ffn_w_down, ln2_g, ln2_b):
    B, S, _ = x_prompt.shape
    Ts = x_sample.shape[1]
    P = cache_k.shape[2] - N_META
    dt = x_prompt.dtype
    meta = jnp.broadcast_to(meta_tokens.astype(dt)[None], (B, N_META, D_MODEL))
    h_p = layernorm(jnp.concatenate([meta, x_prompt], axis=1), ln0_g, ln0_b)
    h_s = layernorm(x_sample, ln0_g, ln0_b)
    meta_chunk = jnp.zeros((N_META,), jnp.int32)
    pos_p = jnp.arange(N_META + S, dtype=jnp.int32)
    chunk_p = jnp.concatenate([meta_chunk, 1 + jnp.arange(S, dtype=jnp.int32) // CHUNK])
    pos_s = N_META + P + jnp.arange(Ts, dtype=jnp.int32)
    chunk_s = 1 + (P + jnp.arange(Ts, dtype=jnp.int32)) // CHUNK
    chunk_past = jnp.concatenate([meta_chunk, 1 + jnp.arange(P, dtype=jnp.int32) // CHUNK])
    n_sel_p = min(TOPK_MAX, S // 4)
    n_sel_s = min(TOPK_MAX, (P + Ts) // 4)
    no_rows = jnp.zeros((B, 0, HEAD_DIM), dt)
    no_rows_i = jnp.zeros((B, 0, HEAD_DIM_I), dt)
    no_chunk = jnp.zeros((0,), jnp.int32)
    wkv_zero = jnp.zeros((B, RW_HEADS, RW_HEAD, RW_HEAD), dt)
    shift_zero = jnp.zeros((B, SHIFT_COLS), dt)

    kp_l, vp_l, ikp_l, sp_l, shp_l = [], [], [], [], []
    ks_l, vs_l, iks_l, ss_l, shs_l = [], [], [], [], []
    for l in range(DEPTH):
        lp = {
            'w_in': w_in[l], 'idx_k_ln_g': idx_k_ln_g[l], 'idx_k_ln_b': idx_k_ln_b[l],
            'rw_mu': rw_mu[l], 'rw_w0': rw_w0[l], 'rw_w2': rw_w2[l], 'rw_a0': rw_a0[l], 'rw_a2': rw_a2[l],
            'rw_g2': rw_g2[l], 'rw_k_k': rw_k_k[l], 'rw_k_a': rw_k_a[l], 'rw_r_k': rw_r_k[l],
            'rw_gn_g': rw_gn_g[l], 'rw_gn_b': rw_gn_b[l], 'w_o_attn': w_o_attn[l], 'w_o_rwkv': w_o_rwkv[l],
            'w_out': w_out[l], 'ln1_g': ln1_g[l], 'ln1_b': ln1_b[l], 'ffn_w_gate': ffn_w_gate[l],
            'ffn_w_up': ffn_w_up[l], 'ffn_w_down': ffn_w_down[l], 'ln2_g': ln2_g[l], 'ln2_b': ln2_b[l],
        }
        h_p, kp, vp, ikp, sp, shp = hybrid_layer(h_p, pos_p, chunk_p, no_rows, no_rows, no_rows_i, no_chunk,
                                                 wkv_zero, shift_zero, n_sel_p, lp)
        h_s, ks, vs, iks, ss, shs = hybrid_layer(h_s, pos_s, chunk_s, cache_k[l], cache_v[l], cache_idx_k[l],
                                                 chunk_past, state_wkv[l], state_shift[l], n_sel_s, lp)
        kp_l.append(kp); vp_l.append(vp); ikp_l.append(ikp); sp_l.append(sp); shp_l.append(shp)
        ks_l.append(ks); vs_l.append(vs); iks_l.append(iks); ss_l.append(ss); shs_l.append(shs)

    y_prompt = h_p[:, N_META:]
    y_sample = h_s
    k_prompt, v_prompt, idx_k_prompt = jnp.stack(kp_l), jnp.stack(vp_l), jnp.stack(ikp_l)
    wkv_prompt, shift_prompt = jnp.stack(sp_l), jnp.stack(shp_l)
    k_sample, v_sample, idx_k_sample = jnp.stack(ks_l), jnp.stack(vs_l), jnp.stack(iks_l)
    wkv_sample, shift_sample = jnp.stack(ss_l), jnp.stack(shs_l)
    return (y_prompt, y_sample, k_prompt, v_prompt, idx_k_prompt, wkv_prompt, shift_prompt,
            k_sample, v_sample, idx_k_sample, wkv_sample, shift_sample)
```

```python
import bisect
import numpy as np
import ml_dtypes
from contextlib import ExitStack
import concourse.bass as bass
import concourse.mybir as mybir
from concourse.bass_utils import run_bass_kernel_spmd

F32 = mybir.dt.float32
BF16 = mybir.dt.bfloat16
U8 = mybir.dt.uint8
ALU = mybir.AluOpType
AF = mybir.ActivationFunctionType

SAME_ENGINE_SYNC = True
MAXOPS = None
ENGS = ('pe', 'act', 'dve', 'pool', 'sp')


class Prog:
    def __init__(self):
        self.nc = bass.Bass("TRN2", target_bir_lowering=False)
        self.es = ExitStack()
        self.ops = []
        self.state = {}
        self.names = set()
        self.psum_names = set()

    def _nm(self, name):
        assert name not in self.names, name
        self.names.add(name)
        return name

    def sb(self, name, shape, dt):
        return self.es.enter_context(self.nc.sbuf_tensor(self._nm(name), list(shape), dt))

    def ps(self, name, shape, dt=F32):
        self.psum_names.add(name)
        return self.es.enter_context(self.nc.psum_tensor(self._nm(name), list(shape), dt))

    def dram(self, name, shape, dt, kind):
        return self.nc.dram_tensor(self._nm(name), list(shape), dt, kind=kind)

    @staticmethod
    def _reg(x):
        if isinstance(x, tuple):
            base, sub = x[0], (x[1],)
        else:
            base, sub = x, ()
        if isinstance(base, Alias):
            return base.name, (base.key,) + sub
        return base.name, sub

    @staticmethod
    def _rel(a, b):
        n = min(len(a), len(b))
        return a[:n] == b[:n]

    def _deps(self, idx, reads, writes, eng=None):
        deps = set()
        for x in reads:
            n, k = self._reg(x)
            st = self.state.setdefault(n, {})
            for kk, ent in st.items():
                if not self._rel(kk, k):
                    continue
                if ent[0] is not None:
                    deps.add(ent[0])
                if n in self.psum_names:
                    for rr in ent[1]:
                        if self.ops[rr]['eng'] != eng:
                            deps.add(rr)
        for x in writes:
            n, k = self._reg(x)
            st = self.state.setdefault(n, {})
            for kk, ent in st.items():
                if not self._rel(kk, k):
                    continue
                if ent[0] is not None:
                    deps.add(ent[0])
                deps.update(ent[1])
        for x in reads:
            n, k = self._reg(x)
            self.state[n].setdefault(k, [None, []])[1].append(idx)
        for x in writes:
            n, k = self._reg(x)
            st = self.state[n]
            for kk in [kk for kk in st if len(kk) >= len(k) and kk[:len(k)] == k]:
                del st[kk]
            st[k] = [idx, []]
        deps.discard(idx)
        return sorted(deps)

    def op(self, eng, fn, r=(), w=()):
        if MAXOPS is not None and len(self.ops) >= MAXOPS:
            return None
        idx = len(self.ops)
        self.ops.append(dict(eng=eng, fn=fn, deps=self._deps(idx, r, w, eng), dma=False, semkey=None))
        return idx

    def dma(self, q, out, in_, r=(), w=(), semkey=None, **kw):
        if MAXOPS is not None and len(self.ops) >= MAXOPS:
            return None
        idx = len(self.ops)
        deps = self._deps(idx, r, w)
        if semkey is None:
            wn = self._reg(w[0])[0]
            semkey = wn if not (wn.startswith('D_') or wn.startswith('O_')) else self._reg(r[0])[0]
        fn = (lambda e, out=out, in_=in_, kw=kw: e.dma_start(out=out, in_=in_, **kw))
        self.ops.append(dict(eng=q, fn=fn, deps=deps, dma=True, semkey=semkey))
        return idx

    def build(self):
        nc, ops = self.nc, self.ops
        n = len(ops)
        need_sig = [False] * n
        for i, o in enumerate(ops):
            for j in o['deps']:
                pj = ops[j]
                if pj['dma']:
                    continue
                if pj['eng'] != o['eng'] or o['dma'] or (SAME_ENGINE_SYNC and o['eng'] != 'pe'):
                    need_sig[j] = True
        cnt = {e: 0 for e in ENGS}
        sigval = [0] * n
        dcnt, semkeys = {}, []
        for i, o in enumerate(ops):
            if o['dma']:
                k = o['semkey']
                if k not in dcnt:
                    dcnt[k] = 0
                    semkeys.append(k)
                dcnt[k] += 16
                sigval[i] = dcnt[k]
            elif need_sig[i]:
                cnt[o['eng']] += 1
                sigval[i] = cnt[o['eng']]
        esem = {e: self.es.enter_context(nc.semaphore("s_" + e)) for e in ENGS}
        dsem = {k: self.es.enter_context(nc.semaphore("d_" + k)) for k in semkeys}
        self.n_sems = len(esem) + len(dsem)
        dma_idx = {}
        for i, o in enumerate(ops):
            if o['dma']:
                dma_idx.setdefault(o['semkey'], []).append(i)
        waited = {e: {} for e in ENGS}
        plan = {e: [] for e in ENGS}
        for i, o in enumerate(ops):
            E = o['eng']
            waits = {}
            for j in o['deps']:
                pj = ops[j]
                if pj['dma']:
                    key = ('d', pj['semkey'])
                    lst = dma_idx[pj['semkey']]
                    val_d = 16 * bisect.bisect_left(lst, i)
                else:
                    if pj['eng'] == E and not o['dma'] and (E == 'pe' or not SAME_ENGINE_SYNC):
                        continue
                    key = ('e', pj['eng'])
                val = val_d if pj['dma'] else sigval[j]
                if waited[E].get(key, 0) >= val:
                    continue
                waits[key] = max(waits.get(key, 0), val)
            for key, val in waits.items():
                waited[E][key] = val
            plan[E].append((i, waits))
        blk = self.es.enter_context(nc.Block())
        engobj = {'pe': 'tensor', 'act': 'scalar', 'dve': 'vector', 'pool': 'gpsimd', 'sp': 'sync'}

        def emit_for(E):
            def body(eng):
                for i, waits in plan[E]:
                    o = ops[i]
                    for (kind, k), val in waits.items():
                        eng.wait_ge(dsem[k] if kind == 'd' else esem[k], val)
                    ins = o['fn'](eng)
                    if o['dma']:
                        ins.then_inc(dsem[o['semkey']], 16)
                    elif need_sig[i]:
                        ins.then_inc(esem[E], 1)
                if E == 'sp':
                    for k, v in dcnt.items():
                        eng.wait_ge(dsem[k], v)
                    for e2 in ENGS:
                        if e2 != 'sp' and cnt[e2] > 0:
                            eng.wait_ge(esem[e2], cnt[e2])
            return body

        for E in ENGS:
            getattr(blk, engobj[E])(emit_for(E))
        self.es.close()
        return nc


class Alias:
    def __init__(self, base, dtype, byte_off, shape, key):
        es = 2 if dtype == BF16 else 4
        self.h = base.bitcast(dtype)
        self.off = byte_off // es
        self.shape = list(shape)
        self.name = base.name
        self.key = key
        n = int(np.prod(shape[1:]))
        v = self.h[0:shape[0], self.off:self.off + n]
        if len(shape) == 3:
            v = v.rearrange("p (a b) -> p a b", a=shape[1])
        self.v = v

    def __getitem__(self, idx):
        return self.v[idx]


class Ring:
    def __init__(self, tiles):
        self.t, self.i = tiles, 0

    def next(self):
        t = self.t[self.i % len(self.t)]
        self.i += 1
        return t


D = 1024
DFF = 2816
GEOM = dict(NSO_B=32, NOWN_B=32, SAMPLE=True)
NB = 1
NT = 128 * NB
WIN_COLS = 7240
C_Q, C_K, C_V, C_QI, C_KI, C_WI, C_G, C_RW = 0, 1024, 1152, 1280, 1792, 1856, 1864, 3912
LN_EPS = 1e-5
GN_EPS = 64e-5
DEC = 0.6065306597126334
ALPHA = 2.0 ** 0.25
CACHE_ROWS = 2064
SSTRIDE = 2176
NBIS = 19
TOPK = 256
O_G0, O_B0, O_MU, O_W0, O_A0, O_KK, O_KA, O_RK, NCOLS = 0, 8, 16, 42, 50, 58, 66, 74, 82


def geom():
    nso = 128 * GEOM['NSO_B'] + 16
    nown = 128 * GEOM['NOWN_B']
    return nso, nown, nso + nown


def build_program():
    NSO, NOWN, NSLOT = geom()
    SAMPLE = GEOM['SAMPLE']
    NSLOT_ALL = NSLOT + 2 * SSTRIDE
    P = Prog()
    nc = P.nc
    V_ = lambda fn, r=(), w=(): P.op('dve', fn, r, w)
    A_ = lambda fn, r=(), w=(): P.op('act', fn, r, w)
    G_ = lambda fn, r=(), w=(): P.op('pool', fn, r, w)
    T_ = lambda fn, r=(), w=(): P.op('pe', fn, r, w)

    din = lambda n, s, dt=F32: P.dram(n, s, dt, "ExternalInput")
    dout = lambda n, s, dt=F32: P.dram(n, s, dt, "ExternalOutput")
    dint = lambda n, s, dt=BF16: P.dram(n, s, dt, "Internal")
    xso = din("xso", [NSO, D]); xown = din("xown", [NOWN, D]); xsm = din("xsm", [128, D])
    rope = din("rope", [NSO + NOWN + 128, 48])
    flag = din("flag", [128, 1])
    colsf = din("colsf", [128, NCOLS])
    cmask = din("cmask", [128, 640])
    identf = din("identf", [128, 128])
    blk1 = din("blk1", [128, 128])
    keyb = din("keyb", [1, NSLOT_ALL], BF16)
    w_in = din("w_in", [D, WIN_COLS])
    rw_w2 = din("rw_w2", [64, D]); rw_a2 = din("rw_a2", [64, D]); rw_g2 = din("rw_g2", [128, D])
    ikg = din("ikg", [64]); ikb = din("ikb", [64])
    vecs = {n: din(n, [D]) for n in ("ln0_g", "ln0_b", "ln1_g", "ln1_b", "ln2_g", "ln2_b", "gn_g", "gn_b")}
    w_oa = din("w_oa", [D, D]); w_or = din("w_or", [D, D]); w_out = din("w_out", [D, D])
    w_fg = din("w_fg", [D, DFF]); w_fu = din("w_fu", [D, DFF]); w_fd = din("w_fd", [DFF, D])
    ck = din("ck", [2, CACHE_ROWS, 128]); cv = din("cv", [2, CACHE_ROWS, 128]); cik = din("cik", [2, CACHE_ROWS, 64])
    swkv = din("swkv", [2, 128, 512]); sshift = din("sshift", [2, 128, 26])

    O_y = dout("O_y", [NOWN, D]); O_ys = dout("O_ys", [128, D])
    O_k = dout("O_k", [NSLOT, 128]); O_v = dout("O_v", [NSLOT, 128]); O_ki = dout("O_ki", [NSLOT, 64])
    O_wkv = dout("O_wkv", [128, 512]); O_shift = dout("O_shift", [128, 26])
    O_ks = dout("O_ks", [128, 128]); O_vs = dout("O_vs", [128, 128]); O_kis = dout("O_kis", [128, 64])
    O_wkvs = dout("O_wkvs", [2, 128, 512]); O_shifts = dout("O_shifts", [2, 128, 26])

    D_win = dint("D_win", [D, WIN_COLS])
    D_w2 = dint("D_w2", [64, D]); D_a2 = dint("D_a2", [64, D]); D_g2 = dint("D_g2", [128, D])
    D_woa = dint("D_woa", [D, D]); D_wor = dint("D_wor", [D, D]); D_wout = dint("D_wout", [D, D])
    D_wfg = dint("D_wfg", [D, DFF]); D_wfu = dint("D_wfu", [D, DFF]); D_wfd = dint("D_wfd", [DFF, D])
    D_kt = dint("D_kt", [128, NSLOT_ALL]); D_kit = dint("D_kit", [64, NSLOT_ALL]); D_v = dint("D_v", [NSLOT_ALL, 128])

    def cast_rows(dst, src, nrows, step):
        for i in range(0, nrows, step):
            n = min(step, nrows - i)
            P.dma('pool', dst.ap()[i:i + n, :], src.ap()[i:i + n, :], r=[src], w=[(dst, i)])
    cast_rows(D_win, w_in, D, 128)
    P.dma('pool', D_w2.ap(), rw_w2.ap(), r=[rw_w2], w=[D_w2])
    P.dma('pool', D_a2.ap(), rw_a2.ap(), r=[rw_a2], w=[D_a2])
    P.dma('pool', D_g2.ap(), rw_g2.ap(), r=[rw_g2], w=[D_g2])
    for (dd, ss, nr) in ((D_woa, w_oa, D), (D_wor, w_or, D), (D_wout, w_out, D), (D_wfg, w_fg, D), (D_wfu, w_fu, D), (D_wfd, w_fd, DFF)):
        cast_rows(dd, ss, nr, 256)
    if SAMPLE:
        for q in range(2):
            sb0 = NSLOT + SSTRIDE * q
            P.dma('pool', D_v.ap()[sb0:sb0 + CACHE_ROWS, :], cv.ap()[q], r=[cv], w=[(D_v, 'c%d' % q)])
    kview = lambda dt_: dt_.ap().rearrange("(k p) n -> p k n", p=128)
    win_v = kview(D_win)

    cols = P.sb("cols", [128, NCOLS], F32)
    colsd = P.sb("colsd", [128, 34], F32)
    identF = P.sb("identF", [128, 128], F32)
    identB = P.sb("identB", [128, 128], BF16)
    I4 = P.sb("I4", [128, 4, 128], BF16)
    blkones = P.sb("blkones", [128, 128], F32)
    blkonesB = P.sb("blkonesB", [128, 128], BF16)
    mask = P.sb("mask", [128, 640], F32)
    ones128 = P.sb("ones128", [128, 128], F32)
    onesrow = P.sb("onesrow", [1, 128], BF16)
    ikg_b = P.sb("ikg_b", [128, 64], F32); ikb_b = P.sb("ikb_b", [128, 64], F32)
    w2b = P.sb("w2b", [128, D], BF16); a2b = P.sb("a2b", [128, D], BF16); g2b = P.sb("g2b", [128, D], BF16)
    flg = P.sb("flg", [128, 1], F32)
    vbr = Ring([P.sb("vbr%d" % i, [128, D], F32) for i in range(4)])

    def getvec(n):
        t = vbr.next()
        P.dma('sp', t[:], vecs[n].ap().partition_broadcast(128), r=[vecs[n]], w=[t])
        return t
    P.dma('sp', cols[:], colsf.ap(), r=[colsf], w=[cols])
    P.dma('sp', identF[:], identf.ap(), r=[identf], w=[identF])
    P.dma('sp', blkones[:], blk1.ap(), r=[blk1], w=[blkones])
    P.dma('sp', mask[:], cmask.ap(), r=[cmask], w=[mask])
    P.dma('sp', flg[:], flag.ap(), r=[flag], w=[flg])
    P.dma('sp', ikg_b[:], ikg.ap().partition_broadcast(128), r=[ikg], w=[ikg_b])
    P.dma('sp', ikb_b[:], ikb.ap().partition_broadcast(128), r=[ikb], w=[ikb_b])
    P.dma('sp', w2b[0:64, :], D_w2.ap(), r=[D_w2], w=[w2b])
    P.dma('sp', a2b[64:128, :], D_a2.ap(), r=[D_a2], w=[a2b])
    P.dma('sp', g2b[:], D_g2.ap(), r=[D_g2], w=[g2b])
    V_(lambda e: e.tensor_copy(out=identB[:], in_=identF[:]), r=[identF], w=[identB])
    for i in range(4):
        V_(lambda e, i=i: e.tensor_copy(out=I4[:, i, :], in_=identF[:]), r=[identF], w=[I4])
    V_(lambda e: e.tensor_copy(out=blkonesB[:], in_=blkones[:]), r=[blkones], w=[blkonesB])
    V_(lambda e: e.memset(ones128[:], 1.0), w=[ones128])
    V_(lambda e: e.memset(onesrow[:], 1.0), w=[onesrow])
    V_(lambda e: e.tensor_scalar(out=colsd[:, 0:26], in0=cols[:, O_MU:O_MU + 26], scalar1=-1.0, scalar2=1.0, op0=ALU.mult, op1=ALU.add),
       r=[cols], w=[colsd])
    V_(lambda e: e.tensor_scalar(out=colsd[:, 26:34], in0=cols[:, O_KA:O_KA + 8], scalar1=-1.0, scalar2=1.0, op0=ALU.mult, op1=ALU.add),
       r=[cols], w=[colsd])

    pm = Ring([P.ps("pm0", [128, 512]), P.ps("pm1", [128, 512])])
    ptr = P.ps("ptr", [128, 8, 128], BF16)
    pLT = P.ps("pLT", [128, 1024])
    pO = P.ps("pO", [128, 1536])
    cbanks = [(pLT, 0), (pLT, 1), (pO, 0), (pO, 1)]
    cb_ap = lambda i: (cbanks[i][0])[:, 512 * cbanks[i][1]:512 * (cbanks[i][1] + 1)]
    xbanks = [(pLT, 0), (pLT, 1), (pO, 2)]
    xq = [0]

    xin = Ring([P.sb("xin%d" % i, [128, NB, D], F32) for i in range(2)])
    hn = P.sb("hn", [128, NB, D], BF16)
    hres = P.sb("hres", [128, NB, D], F32)
    hT = P.sb("hT", [128, 8, NT], BF16)
    small = Ring([P.sb("small%d" % i, [128, 24], F32) for i in range(4)])
    wt = Ring([P.sb("wt%d" % i, [128, 8, 512], BF16) for i in range(3)])
    kvs = [P.sb("kvs%d" % i, [128, 328], F32) for i in range(NB)]
    rtmp = Ring([P.sb("rtmp%d" % i, [128, 512], F32) for i in range(2)])
    rp = [P.sb("rp%d" % i, [128, 48], F32) for i in range(NB)]
    vbt = Ring([P.sb("vbt%d" % i, [128, 128], BF16) for i in range(2)])
    ktt = Ring([P.sb("ktt%d" % i, [128, NT], BF16) for i in range(2)])
    kitt = Ring([P.sb("kitt%d" % i, [64, NT], BF16) for i in range(2)])
    ki2 = Ring([P.sb("ki2_%d" % i, [128, 64], F32) for i in range(2)])
    qT = P.sb("qT", [128, NB, 8, 128], BF16)
    qiT = P.sb("qiT", [128, NB, 4, 128], BF16)
    gsig = P.sb("gsig", [128, NB, 2 * D], BF16)
    car = P.sb("car", [128, 26], F32)
    cars = P.sb("cars", [128, 2, 26], F32)
    TL = P.sb("TL", [128, NT], BF16)
    SLG = P.sb("SLG", [128, NT], BF16)
    f32t = Ring([P.sb("f32t%d" % i, [128, NT], F32) for i in range(8)])
    AT = P.sb("AT", [128, 8, NT], BF16); KTt = P.sb("KTt", [128, 8, NT], BF16); BTt = P.sb("BTt", [128, 8, NT], BF16)
    VTf = P.sb("VTf", [128, 8, NT], BF16); RTt = P.sb("RTt", [128, 8, NT], BF16); KMR = P.sb("KMR", [128, 8, NT], BF16)
    PRD = Ring([P.sb("PRD%d" % i, [128, NT], BF16) for i in range(2)])
    CS = P.sb("CS", [128, 8, NB, 129], F32)
    GC = P.sb("GC", [128, NB, 8], F32)
    Ktok = P.sb("Ktok", [128, NB, D], BF16); Btok = P.sb("Btok", [128, NB, D], BF16); Vtok = P.sb("Vtok", [128, NB, D], BF16)
    gtok = P.sb("gtok", [128, NB, D], BF16)
    bcoef = P.sb("bcoef", [128, NB, 16], F32)
    ST = P.sb("ST", [128, 8, 64], F32); STb = P.sb("STb", [128, 8, 64], BF16)
    F1 = P.sb("F1", [128, D], F32); F2 = P.sb("F2", [128, D], F32)
    Yt, Y2 = F1, F2
    gns = P.sb("gns", [128, 80], F32)
    orwT = P.sb("orwT", [128, NB, 8, 128], BF16)
    orw = P.sb("orw", [128, D], BF16)
    q32 = F1
    SMAX = max(NSLOT, 8208)
    SC = P.sb("SC", [128, SMAX], F32)
    aoff = [0]

    def alias(key, shape, dt_):
        nbytes = int(np.prod(shape[1:])) * (2 if dt_ == BF16 else 4)
        a = Alias(SC, dt_, aoff[0], shape, key)
        aoff[0] += nbytes
        assert aoff[0] <= 4 * SMAX, aoff[0]
        return a
    amat = Ring([alias("amat%d" % i, [128, 4, 128], BF16) for i in range(4)])
    pq = [Ring([alias("pq%d_%d" % (h, i), [128, 2, 128], BF16) for i in range(2)]) for h in range(4)]
    Mt = [Ring([alias("Mt%d_%d" % (h, i), [128, 128], BF16) for i in range(2)]) for h in range(4)]
    Mfin = alias("Mfin", [128, 16, 128], BF16)
    AKs = alias("AKs", [128, 16, 128], BF16)
    ARB = alias("ARB", [128, 16, 128], BF16); ARK = alias("ARK", [128, 16, 128], BF16)
    Wb = alias("Wb", [128, 16, 64], BF16); Ub = alias("Ub", [128, 16, 64], BF16)
    sttmp = alias("sttmp", [128, 8, 64], F32)
    junk = P.sb("junk", [128, 2], BF16)
    tau = P.sb("tau", [128, 8], F32)
    dg = P.sb("dg", [128, 8, 128], BF16)
    kitl = Ring([P.sb("kitl%d" % i, [128, 512], BF16) for i in range(2)])
    kbt = Ring([P.sb("kbt%d" % i, [1, 512], BF16) for i in range(2)])
    ktl = Ring([P.sb("ktl%d" % i, [128, 512], BF16) for i in range(2)])
    vtl = Ring([P.sb("vtl%d" % i, [128, 4, 130], BF16) for i in range(2)])
    mbt = Ring([P.sb("mbt%d" % i, [128, 512], BF16) for i in range(2)])
    Rt = Ring([P.sb("Rt%d" % i, [128, 512], BF16) for i in range(2)])
    PTt = Ring([P.sb("PTt%d" % i, [128, 1024], BF16) for i in range(2)])
    rden = P.sb("rden", [128, 8], F32)
    attn = P.sb("attn", [128, D], BF16)
    attnT = P.sb("attnT", [128, NB, 8, 128], BF16)
    mixs = [attn]
    mixT = P.sb("mixT", [128, 8, NT], BF16)
    x1 = F1.reshape([128, 1, D])
    x1b = hn.reshape([128, D])
    x1T = P.sb("x1T", [128, 8, NT], BF16)
    actg = Ring([P.sb("actg%d" % i, [128, NT], BF16) for i in range(2)])
    actT = Ring([P.sb("actT%d" % i, [128, NT], BF16) for i in range(3)])
    yo = Ring([F2])

    V_(lambda e: e.memset(ST[:], 0.0), w=[ST])
    V_(lambda e: e.memset(STb[:], 0.0), w=[STb])
    V_(lambda e: e.memset(car[:], 0.0), w=[car])
    V_(lambda e: e.memset(CS[:], 0.0), w=[CS])
    for t in vtl.t:
        V_(lambda e, t=t: e.memset(t[:], 1.0), w=[t])

    def load_w(view, c0, ncol):
        t = wt.next()
        P.dma('sp', t[:, :, 0:ncol], view[:, :, c0:c0 + ncol], r=[view.tensor], w=[t])
        return t

    def ln_rows(x_ap, L, n, out_ap, eps, g_b=None):
        xreg, oreg = x_ap.tensor, out_ap.tensor
        sm = small.next()
        nch = (n + 511) // 512
        for c in range(nch):
            V_(lambda e, c=c: e.bn_stats(out=sm[:L, 6 * c:6 * c + 6], in_=x_ap[:, 512 * c:min(n, 512 * (c + 1))]), r=[xreg], w=[sm])
        V_(lambda e: e.bn_aggr(out=sm[:L, 12:14], in_=sm[:L, 0:6 * nch]), r=[sm], w=[sm])
        A_(lambda e: e.activation(out=sm[:L, 14:15], in_=sm[:L, 13:14], func=AF.Sqrt, bias=float(eps), scale=1.0), r=[sm], w=[sm])
        V_(lambda e: e.reciprocal(out=sm[:L, 15:16], in_=sm[:L, 14:15]), r=[sm], w=[sm])
        V_(lambda e: e.tensor_scalar(out=sm[:L, 16:17], in0=sm[:L, 12:13], scalar1=sm[:L, 15:16], scalar2=-1.0, op0=ALU.mult, op1=ALU.mult),
           r=[sm], w=[sm])
        A_(lambda e: e.activation(out=out_ap, in_=x_ap, func=AF.Identity, scale=sm[:L, 15:16], bias=sm[:L, 16:17]),
           r=[sm, xreg], w=[oreg])
        if g_b is not None:
            g, b = g_b
            G_(lambda e: e.tensor_tensor(out=out_ap, in0=out_ap, in1=g[:L, 0:n], op=ALU.mult), r=[oreg, g], w=[oreg])
            G_(lambda e: e.tensor_tensor(out=out_ap, in0=out_ap, in1=b[:L, 0:n], op=ALU.add), r=[oreg, b], w=[oreg])

    def rope_rows(t, L, c0, half, cos_ap, sin_ap, nh=1, stride=0):
        tab = cos_ap.tensor
        tm = rtmp.next()
        if nh == 1:
            x1_ = t[:L, c0:c0 + half]; x2_ = t[:L, c0 + half:c0 + 2 * half]
            a = tm[:L, 0:half]; b = tm[:L, half:2 * half]; c = tm[:L, 2 * half:3 * half]; d = tm[:L, 3 * half:4 * half]
            cs, sn = cos_ap, sin_ap
        else:
            v = t[:L, c0:c0 + nh * stride].rearrange("p (h d) -> p h d", h=nh)
            x1_ = v[:, :, 0:half]; x2_ = v[:, :, half:2 * half]
            tv = tm[:L, 0:4 * nh * half].rearrange("p (q h d) -> p q h d", q=4, h=nh)
            a, b, c, d = tv[:, 0], tv[:, 1], tv[:, 2], tv[:, 3]
            cs = cos_ap.unsqueeze(1).to_broadcast([L, nh, half]); sn = sin_ap.unsqueeze(1).to_broadcast([L, nh, half])
        rr = [t, tm, tab]
        V_(lambda e: e.tensor_tensor(out=a, in0=cs, in1=x1_, op=ALU.mult), r=rr, w=[tm])
        V_(lambda e: e.tensor_tensor(out=b, in0=sn, in1=x2_, op=ALU.mult), r=rr, w=[tm])
        V_(lambda e: e.tensor_tensor(out=c, in0=cs, in1=x2_, op=ALU.mult), r=rr, w=[tm])
        V_(lambda e: e.tensor_tensor(out=d, in0=sn, in1=x1_, op=ALU.mult), r=rr, w=[tm])
        V_(lambda e: e.tensor_tensor(out=x1_, in0=a, in1=b, op=ALU.subtract), r=[tm], w=[t])
        V_(lambda e: e.tensor_tensor(out=x2_, in0=c, in1=d, op=ALU.add), r=[tm], w=[t])

    evq = [0]

    def evac_copy(out_ap, in_ap, r, w):
        evq[0] += 1
        if evq[0] % 2:
            A_(lambda e: e.activation(out=out_ap, in_=in_ap, func=AF.Copy), r=r, w=w)
        else:
            V_(lambda e: e.tensor_copy(out=out_ap, in_=in_ap), r=r, w=w)

    def transpose_to(dst_fn, src_fn, n, L, r, w, ident=None):
        for k in range(n):
            T_(lambda e, k=k: e.transpose(out=ptr[:, k, 0:L], in_=src_fn(k), identity=identB[:L, :L]), r=r + [identB], w=[ptr])

    def front(x_dram, rows, Ls, rope_rows0, slots, outs, own):
        O_k_, O_v_, O_ki_, orow = outs
        xt = xin.next()
        L = Ls[0]
        ntk = sum(Ls)
        for b in range(len(Ls)):
            P.dma('sp', xt[:L, b, :], x_dram.ap()[rows[b]:rows[b] + L, :], r=[x_dram], w=[(xt, b)])
            P.dma('sp', rp[b][:L, :], rope.ap()[rope_rows0[b]:rope_rows0[b] + L, :], r=[rope], w=[rp[b]])
        for b in range(len(Ls)):
            if own:
                ln_rows(xt[:L, b, :], L, D, hres[:L, b, :], LN_EPS)
                G_(lambda e, b=b: e.tensor_copy(out=hn[:L, b, :], in_=hres[:L, b, :]), r=[(hres, b)], w=[(hn, b)])
                vg, vbb = getvec("ln0_g"), getvec("ln0_b")
                G_(lambda e, b=b, vg=vg: e.tensor_tensor(out=hres[:L, b, :], in0=hres[:L, b, :], in1=vg[:L, :], op=ALU.mult),
                   r=[(hres, b), vg], w=[(hres, b)])
                G_(lambda e, b=b, vbb=vbb: e.tensor_tensor(out=hres[:L, b, :], in0=hres[:L, b, :], in1=vbb[:L, :], op=ALU.add),
                   r=[(hres, b), vbb], w=[(hres, b)])
            else:
                ln_rows(xt[:L, b, :], L, D, hn[:L, b, :], LN_EPS)
            for k in range(8):
                T_(lambda e, b=b, k=k: e.transpose(out=ptr[:, k, 0:L], in_=hn[:L, b, 128 * k:128 * (k + 1)], identity=identB[:L, :L]),
                   r=[hn, identB], w=[ptr])
            for k in range(8):
                if k % 2:
                    A_(lambda e, b=b, k=k: e.activation(out=hT[:, k, L * b:L * b + L], in_=ptr[:, k, 0:L], func=AF.Identity,
                                                        scale=cols[:, O_G0 + k:O_G0 + k + 1], bias=cols[:, O_B0 + k:O_B0 + k + 1]),
                       r=[ptr, cols], w=[(hT, (b, k))])
                else:
                    V_(lambda e, b=b, k=k: e.tensor_scalar(out=hT[:, k, L * b:L * b + L], in0=ptr[:, k, 0:L],
                                                           scalar1=cols[:, O_G0 + k:O_G0 + k + 1], scalar2=cols[:, O_B0 + k:O_B0 + k + 1],
                                                           op0=ALU.mult, op1=ALU.add), r=[ptr, cols], w=[(hT, (b, k))])
        wk = wt.next()
        P.dma('sp', wk[:, :, 0:256], win_v[:, :, C_K:C_K + 256], r=[D_win], w=[wk])
        P.dma('sp', wk[:, :, 256:328], win_v[:, :, C_KI:C_KI + 72], r=[D_win], w=[wk])
        kt_t = ktt.next(); kit_t = kitt.next()
        for b in range(len(Ls)):
            pk = pm.next()
            for k in range(8):
                T_(lambda e, b=b, k=k, pk=pk: e.matmul(pk[:L, 0:328], lhsT=hT[:, k, L * b:L * b + L], rhs=wk[:, k, 0:328],
                                                    start=(k == 0), stop=(k == 7)), r=[hT, wk], w=[pk])
            kv = kvs[b]
            A_(lambda e, pk=pk, kv=kv: e.activation(out=kv[:L, :], in_=pk[:L, 0:328], func=AF.Copy), r=[pk], w=[kv])
            rt = rp[b]
            rope_rows(kv, L, 0, 16, rt[:L, 0:16], rt[:L, 16:32])
            k2 = ki2.next()
            ln_rows(kv[:L, 256:320], L, 64, k2[:L, :], LN_EPS, g_b=(ikg_b, ikb_b))
            rope_rows(k2, L, 0, 8, rt[:L, 32:40], rt[:L, 40:48])
            s0 = slots[b]; o0 = orow[b]
            P.dma('pool', O_k_.ap()[o0:o0 + L, :], kv[:L, 0:128], r=[kv], w=[(O_k_, o0)])
            P.dma('pool', O_v_.ap()[o0:o0 + L, :], kv[:L, 128:256], r=[kv], w=[(O_v_, o0)])
            P.dma('pool', O_ki_.ap()[o0:o0 + L, :], k2[:L, :], r=[k2], w=[(O_ki_, o0)])
            vb = vbt.next()
            G_(lambda e, kv=kv, vb=vb: e.tensor_copy(out=vb[:L, :], in_=kv[:L, 128:256]), r=[kv], w=[vb])
            P.dma('pool', D_v.ap()[s0:s0 + L, :], vb[:L, :], r=[vb], w=[(D_v, s0)])
            pt_ = pm.next()
            T_(lambda e, kv=kv, pt_=pt_: e.transpose(out=pt_[:, 0:L], in_=kv[:L, 0:128], identity=identF[:L, :L]), r=[kv, identF], w=[pt_])
            T_(lambda e, k2=k2, pt_=pt_: e.transpose(out=pt_[0:64, 128:128 + L], in_=k2[:L, 0:64], identity=identF[:L, :L]), r=[k2, identF], w=[pt_])
            V_(lambda e, b=b, kt_t=kt_t, pt_=pt_: e.tensor_copy(out=kt_t[:, L * b:L * b + L], in_=pt_[:, 0:L]), r=[pt_], w=[kt_t])
            V_(lambda e, b=b, kit_t=kit_t, pt_=pt_: e.tensor_copy(out=kit_t[:, L * b:L * b + L], in_=pt_[0:64, 128:128 + L]), r=[pt_], w=[kit_t])
            P.dma('pool', D_kt.ap()[:, s0:s0 + L], kt_t[:, L * b:L * b + L], r=[kt_t], w=[(D_kt, s0)])
            P.dma('pool', D_kit.ap()[:, s0:s0 + L], kit_t[:, L * b:L * b + L], r=[kit_t], w=[(D_kit, s0)])

    def own_proj(Ls, rope_rows0):
        L = Ls[0]
        nbk = len(Ls)
        wq = [load_w(win_v, C_Q + 512 * g, 512) for g in range(2)]
        for b in range(nbk):
            for g in range(2):
                pq_ = pm.next()
                for k in range(8):
                    T_(lambda e, b=b, g=g, k=k, pq_=pq_: e.matmul(pq_[:L, :], lhsT=hT[:, k, L * b:L * b + L], rhs=wq[g][:, k, :], start=(k == 0), stop=(k == 7)),
                       r=[hT, wq[g]], w=[pq_])
                evac_copy(q32[:L, 512 * g:512 * (g + 1)], pq_[:L, :], r=[pq_], w=[(q32, g)])
            rope_rows(q32, L, 0, 16, rp[b][:L, 0:16], rp[b][:L, 16:32], nh=8, stride=128)
            for h in range(8):
                T_(lambda e, h=h: e.transpose(out=pLT[:, 128 * h:128 * h + L], in_=q32[:L, 128 * h:128 * (h + 1)], identity=identF[:L, :L]),
                   r=[q32, identF], w=[(pLT, h // 4)])
            evac_copy(qT[:, b, 0:4, 0:L], pLT[:, 0:512].rearrange("p (h t) -> p h t", h=4)[:, :, 0:L], r=[(pLT, 0)], w=[(qT, b)])
            evac_copy(qT[:, b, 4:8, 0:L], pLT[:, 512:1024].rearrange("p (h t) -> p h t", h=4)[:, :, 0:L], r=[(pLT, 1)], w=[(qT, b)])
        wqi = load_w(win_v, C_QI, 512)
        for b in range(nbk):
            pq_ = pm.next()
            for k in range(8):
                T_(lambda e, b=b, k=k, pq_=pq_: e.matmul(pq_[:L, :], lhsT=hT[:, k, L * b:L * b + L], rhs=wqi[:, k, :], start=(k == 0), stop=(k == 7)),
                   r=[hT, wqi], w=[pq_])
            evac_copy(q32[:L, 0:512], pq_[:L, :], r=[pq_], w=[(q32, 0)])
            rope_rows(q32, L, 0, 8, rp[b][:L, 32:40], rp[b][:L, 40:48], nh=8, stride=64)
            pt_ = pm.next()
            for pp in range(4):
                T_(lambda e, pp=pp, pt_=pt_: e.transpose(out=pt_[:, 128 * pp:128 * pp + L], in_=q32[:L, 128 * pp:128 * (pp + 1)], identity=identF[:L, :L]),
                   r=[q32, identF], w=[pt_])
            evac_copy(qiT[:, b, :, 0:L], pt_[:, :].rearrange("p (h t) -> p h t", h=4)[:, :, 0:L], r=[pt_], w=[(qiT, b)])
        for g in range(4):
            wg_ = load_w(win_v, C_G + 512 * g, 512)
            for b in range(nbk):
                pq_ = pm.next()
                for k in range(8):
                    T_(lambda e, b=b, k=k, pq_=pq_, wg_=wg_: e.matmul(pq_[:L, :], lhsT=hT[:, k, L * b:L * b + L], rhs=wg_[:, k, :], start=(k == 0), stop=(k == 7)),
                       r=[hT, wg_], w=[pq_])
                A_(lambda e, b=b, g=g, pq_=pq_: e.activation(out=gsig[:L, b, 512 * g:512 * (g + 1)], in_=pq_[:L, :], func=AF.Sigmoid), r=[pq_], w=[(gsig, (b, g))])

    def rw_tile(c, wtile, wcol, Ls, out_fn, carry_aps, save_last):
        L = Ls[0]
        ntk = sum(Ls)
        ps_ = pm.next()
        for k in range(8):
            T_(lambda e, k=k: e.matmul(ps_[:, 0:ntk], lhsT=wtile[:, k, wcol:wcol + 128], rhs=hT[:, k, 0:ntk], start=(k == 0), stop=(k == 7)),
               r=[hT, wtile], w=[ps_])
        tmp = f32t.next(); xs = f32t.next()
        A_(lambda e: e.activation(out=tmp[:, 0:ntk], in_=ps_[:, 0:ntk], func=AF.Identity, scale=colsd[:, c:c + 1]), r=[ps_, colsd], w=[tmp])
        if ntk > 1:
            V_(lambda e: e.scalar_tensor_tensor(out=xs[:, 1:ntk], in0=ps_[:, 0:ntk - 1], scalar=cols[:, O_MU + c:O_MU + c + 1],
                                                in1=tmp[:, 1:ntk], op0=ALU.mult, op1=ALU.add), r=[ps_, cols, tmp], w=[xs])
        for b in range(len(Ls)):
            ca = carry_aps[b]
            if ca is None:
                continue
            cap, creg = ca
            V_(lambda e, b=b, cap=cap: e.scalar_tensor_tensor(out=xs[:, L * b:L * b + 1], in0=cap[:, c:c + 1], scalar=cols[:, O_MU + c:O_MU + c + 1],
                                                             in1=tmp[:, L * b:L * b + 1], op0=ALU.mult, op1=ALU.add), r=[creg, cols, tmp], w=[xs])
        for b in range(len(Ls)):
            sl = save_last[b]
            if sl is None:
                continue
            sap, sreg = sl
            V_(lambda e, b=b, sap=sap: e.tensor_copy(out=sap[:, c:c + 1], in_=ps_[:, L * b + L - 1:L * b + L]), r=[ps_], w=[sreg])
        out_fn(xs)

    def rwkv_prep(Ls, own, carry_aps, save_last):
        ntk = sum(Ls)
        nb = len(Ls)
        L = Ls[0]
        rwt = lambda c, wtile, wcol, fn: rw_tile(c, wtile, wcol, Ls, fn, carry_aps, save_last)
        wl = load_w(win_v, C_RW + 3072, 256)

        def f24(xs):
            A_(lambda e: e.activation(out=TL[0:64, 0:ntk], in_=xs[0:64, 0:ntk], func=AF.Tanh), r=[xs], w=[TL])
            V_(lambda e: e.tensor_copy(out=TL[64:128, 0:ntk], in_=xs[64:128, 0:ntk]), r=[xs], w=[TL])
        rwt(24, wl, 0, f24)
        if own:
            def f25(xs):
                A_(lambda e: e.activation(out=SLG[:, 0:ntk], in_=xs[:, 0:ntk], func=AF.Sigmoid), r=[xs], w=[SLG])
            rwt(25, wl, 128, f25)
            for b in range(nb):
                for g in range(2):
                    pg = pm.next()
                    T_(lambda e, b=b, g=g, pg=pg: e.matmul(pg[:L, :], lhsT=SLG[:, L * b:L * b + L], rhs=g2b[:, 512 * g:512 * (g + 1)], start=True, stop=True),
                       r=[SLG, g2b], w=[pg])
                    evac_copy(gtok[:L, b, 512 * g:512 * (g + 1)], pg[:L, :], r=[pg], w=[(gtok, (b, g))])

        def blkview(t):
            return t[:, 0:ntk].rearrange("p (b l) -> p b l", b=nb)

        for half in range(2):
            wk_ = load_w(win_v, C_RW + 1024 + 512 * half, 512)
            for pp in range(4):
                p = 4 * half + pp

                def fk(kraw, p=p):
                    ps1 = pm.next()
                    T_(lambda e: e.matmul(ps1[:, 0:ntk], lhsT=w2b[0:64, 128 * p:128 * (p + 1)], rhs=TL[0:64, 0:ntk], start=True, stop=True),
                       r=[w2b, TL], w=[ps1])
                    sg = f32t.next()
                    A_(lambda e: e.activation(out=sg[:, 0:ntk], in_=ps1[:, 0:ntk], func=AF.Sigmoid, bias=cols[:, O_W0 + p:O_W0 + p + 1], scale=1.0),
                       r=[ps1, cols], w=[sg])
                    for b in range(nb):
                        V_(lambda e, b=b: e.tensor_tensor_scan(out=CS[:, p, b, 1:1 + L], data0=ones128[:, 0:L], data1=sg[:, L * b:L * (b + 1)],
                                                               initial=0.0, op0=ALU.mult, op1=ALU.add), r=[ones128, sg], w=[(CS, p)])
                    ps2 = pm.next()
                    T_(lambda e: e.matmul(ps2[:, 0:ntk], lhsT=a2b[64:128, 128 * p:128 * (p + 1)], rhs=TL[64:128, 0:ntk], start=True, stop=True),
                       r=[a2b, TL], w=[ps2])
                    asig = f32t.next()
                    A_(lambda e: e.activation(out=asig[:, 0:ntk], in_=ps2[:, 0:ntk], func=AF.Sigmoid, bias=cols[:, O_A0 + p:O_A0 + p + 1], scale=1.0),
                       r=[ps2, cols], w=[asig])
                    eex = f32t.next(); em = f32t.next()
                    A_(lambda e: e.activation(out=blkview(eex), in_=CS[:, p, 0:nb, 0:L], func=AF.Exp, scale=-DEC), r=[(CS, p)], w=[eex])
                    A_(lambda e: e.activation(out=blkview(em), in_=CS[:, p, 0:nb, 1:1 + L], func=AF.Exp, scale=DEC), r=[(CS, p)], w=[em])
                    A_(lambda e: e.activation(out=GC[:, 0:nb, p], in_=CS[:, p, 0:nb, L], func=AF.Exp, scale=-DEC), r=[(CS, p)], w=[(GC, p)])
                    kkr = f32t.next(); sq = f32t.next()
                    V_(lambda e: e.tensor_scalar(out=kkr[:, 0:ntk], in0=kraw[:, 0:ntk], scalar1=cols[:, O_KK + p:O_KK + p + 1], scalar2=None, op0=ALU.mult),
                       r=[kraw, cols], w=[kkr])
                    G_(lambda e: e.tensor_tensor(out=sq[:, 0:ntk], in0=kkr[:, 0:ntk], in1=kkr[:, 0:ntk], op=ALU.mult), r=[kkr], w=[sq])
                    ps3 = pm.next()
                    T_(lambda e: e.matmul(ps3[:, 0:ntk], lhsT=blkones[:], rhs=sq[:, 0:ntk], start=True, stop=True), r=[blkones, sq], w=[ps3])
                    A_(lambda e: e.activation(out=sq[:, 0:ntk], in_=ps3[:, 0:ntk], func=AF.Sqrt), r=[ps3], w=[sq])
                    V_(lambda e: e.tensor_scalar(out=sq[:, 0:ntk], in0=sq[:, 0:ntk], scalar1=1e-12, scalar2=None, op0=ALU.max), r=[sq], w=[sq])
                    V_(lambda e: e.reciprocal(out=sq[:, 0:ntk], in_=sq[:, 0:ntk]), r=[sq], w=[sq])
                    V_(lambda e: e.tensor_tensor(out=kkr[:, 0:ntk], in0=kkr[:, 0:ntk], in1=sq[:, 0:ntk], op=ALU.mult), r=[kkr, sq], w=[kkr])
                    V_(lambda e: e.tensor_scalar(out=sq[:, 0:ntk], in0=asig[:, 0:ntk], scalar1=cols[:, O_KA + p:O_KA + p + 1],
                                                 scalar2=colsd[:, 26 + p:27 + p], op0=ALU.mult, op1=ALU.add), r=[asig, cols, colsd], w=[sq])
                    G_(lambda e: e.tensor_tensor(out=sq[:, 0:ntk], in0=sq[:, 0:ntk], in1=kraw[:, 0:ntk], op=ALU.mult), r=[sq, kraw], w=[sq])
                    G_(lambda e: e.tensor_tensor(out=asig[:, 0:ntk], in0=asig[:, 0:ntk], in1=kkr[:, 0:ntk], op=ALU.mult), r=[asig, kkr], w=[asig])
                    V_(lambda e: e.scalar_tensor_tensor(out=AT[:, p, 0:ntk], in0=kkr[:, 0:ntk], scalar=-1.0, in1=eex[:, 0:ntk], op0=ALU.mult, op1=ALU.mult),
                       r=[kkr, eex], w=[(AT, p)])
                    V_(lambda e: e.tensor_tensor(out=KTt[:, p, 0:ntk], in0=sq[:, 0:ntk], in1=em[:, 0:ntk], op=ALU.mult), r=[sq, em], w=[(KTt, p)])
                    G_(lambda e: e.tensor_tensor(out=BTt[:, p, 0:ntk], in0=asig[:, 0:ntk], in1=em[:, 0:ntk], op=ALU.mult), r=[asig, em], w=[(BTt, p)])
                    if own:
                        G_(lambda e: e.tensor_scalar(out=KMR[:, p, 0:ntk], in0=sq[:, 0:ntk], scalar1=cols[:, O_RK + p:O_RK + p + 1], scalar2=None, op0=ALU.mult),
                           r=[sq, cols], w=[(KMR, p)])
                rwt(8 + p, wk_, 128 * pp, fk)
        for half in range(2):
            wv_ = load_w(win_v, C_RW + 2048 + 512 * half, 512)
            for pp in range(4):
                p = 4 * half + pp

                def fv(xs, p=p):
                    A_(lambda e: e.activation(out=VTf[:, p, 0:ntk], in_=xs[:, 0:ntk], func=AF.Copy), r=[xs], w=[(VTf, p)])
                rwt(16 + p, wv_, 128 * pp, fv)
        if own:
            pbc = pO[:, 1024:1536]
            pbreg = (pO, 2)
            first = [True]
            for half in range(2):
                wr_ = load_w(win_v, C_RW + 512 * half, 512)
                for pp in range(4):
                    p = 4 * half + pp

                    def fr(xs, p=p):
                        ep = f32t.next()
                        A_(lambda e: e.activation(out=blkview(ep), in_=CS[:, p, 0:nb, 1:1 + L], func=AF.Exp, scale=-DEC), r=[(CS, p)], w=[ep])
                        V_(lambda e: e.tensor_tensor(out=RTt[:, p, 0:ntk], in0=xs[:, 0:ntk], in1=ep[:, 0:ntk], op=ALU.mult), r=[xs, ep], w=[(RTt, p)])
                        prd = PRD.next()
                        G_(lambda e: e.tensor_tensor(out=prd[:, 0:ntk], in0=xs[:, 0:ntk], in1=KMR[:, p, 0:ntk], op=ALU.mult), r=[xs, (KMR, p)], w=[prd])
                        for b in range(nb):
                            T_(lambda e, b=b: e.matmul(pbc[:L, 16 * b + 2 * p:16 * b + 2 * p + 2], lhsT=prd[:, L * b:L * b + L], rhs=blkonesB[:, 0:128:64],
                                                       start=first[0] and b == 0, stop=True, skip_group_check=True), r=[prd, blkonesB], w=[pbreg])
                        first[0] = False
                    rwt(p, wr_, 128 * pp, fr)
            first[0] = True
            V_(lambda e: e.tensor_copy(out=bcoef[:L, 0:nb, :], in_=pbc[:L, 0:16 * nb].rearrange("p (b h) -> p b h", b=nb)), r=[pbreg], w=[bcoef])
        for b in range(nb):
            for (src, dst) in ((KTt, Ktok), (BTt, Btok), (VTf, Vtok)):
                for p in range(8):
                    T_(lambda e, p=p, b=b, src=src: e.transpose(out=ptr[0:L, p, :], in_=src[:, p, L * b:L * (b + 1)], identity=identB[:]),
                       r=[(src, p), identB], w=[ptr])
                evac_copy(dst[0:L, b, :].rearrange("p (k j) -> p k j", k=8), ptr[0:L, :, :], r=[ptr], w=[(dst, b)])

    def chunk_scan(b, L, own):
        nl = max(int(np.ceil(np.log2(L))) - 1, 0)
        c0 = L * b
        Lk = 128 if L == 64 else L
        if L == 64:
            for t_ in (AKs, ARB, ARK):
                G_(lambda e, t_=t_: e.memset(t_[64:128, :, :], 0.0), w=[t_])
            G_(lambda e: e.memset(Vtok[64:128, b, :], 0.0), w=[(Vtok, b)])
            G_(lambda e: e.memset(Ub[64:128, :, :], 0.0), w=[Ub])
        for hg in range(4):
            heads = [4 * hg + i for i in range(4)]
            cur = {}
            for i, h in enumerate(heads):
                p, bp = h // 2, 64 * (h % 2)
                bank = cb_ap(i); breg = cbanks[i]
                at = AT[bp:bp + 64, p, c0:c0 + L]; bt = BTt[bp:bp + 64, p, c0:c0 + L]; kt = KTt[bp:bp + 64, p, c0:c0 + L]
                T_(lambda e, bank=bank, bt=bt, at=at: e.matmul(bank[0:L, 0:L], lhsT=bt, rhs=at, start=True, stop=True), r=[(AT, p), (BTt, p)], w=[breg])
                T_(lambda e, bank=bank, bt=bt, at=at: e.matmul(bank[0:L, 128:128 + L], lhsT=at, rhs=bt, start=False, stop=True, skip_group_check=True),
                   r=[(AT, p), (BTt, p)], w=[breg])
                T_(lambda e, bank=bank, kt=kt, at=at: e.matmul(bank[0:L, 256:256 + L], lhsT=kt, rhs=at, start=False, stop=True, skip_group_check=True),
                   r=[(AT, p), (KTt, p)], w=[breg])
                nm_ = 3
                if own:
                    rt_ = RTt[bp:bp + 64, p, c0:c0 + L]
                    T_(lambda e, bank=bank, bt=bt, rt_=rt_: e.matmul(bank[0:L, 384:384 + L], lhsT=bt, rhs=rt_, start=False, stop=True, skip_group_check=True),
                       r=[(RTt, p), (BTt, p)], w=[breg])
                    nm_ = 4
                am = amat.next()
                V_(lambda e, bank=bank, am=am, nm_=nm_: e.tensor_tensor(out=am[0:L, 0:nm_, 0:L], in0=bank[0:L, 0:128 * nm_].rearrange("p (m t) -> p m t", m=nm_)[:, :, 0:L],
                                                                       in1=mask[0:L, 0:128 * nm_].rearrange("p (m t) -> p m t", m=nm_)[:, :, 0:L], op=ALU.mult),
                   r=[breg, mask], w=[am])
                G_(lambda e, am=am, h=h: e.tensor_copy(out=AKs[0:L, h, 0:L], in_=am[0:L, 2, 0:L]), r=[am], w=[(AKs, h)])
                if own:
                    G_(lambda e, am=am, h=h: e.tensor_copy(out=ARB[0:L, h, 0:L], in_=am[0:L, 3, 0:L]), r=[am], w=[(ARB, h)])
                m0 = Mt[i].next()
                G_(lambda e, am=am, m0=m0: e.tensor_tensor(out=m0[0:L, 0:L], in0=am[0:L, 0, 0:L], in1=identB[0:L, 0:L], op=ALU.add), r=[am, identB], w=[m0])
                cur[h] = dict(P=am[0:L, 0, 0:L], Q=am[0:L, 1, 0:L], Pt=am, Qt=am, M=m0)
            for lev in range(nl):
                last = (lev == nl - 1)
                for i, h in enumerate(heads):
                    bank = cb_ap(i); breg = cbanks[i]
                    c = cur[h]
                    if not last:
                        T_(lambda e, bank=bank, cq=c['Q'], cp=c['P']: e.matmul(bank[0:L, 0:L], lhsT=cq, rhs=cp, start=True, stop=True), r=[c['Pt'], c['Qt']], w=[breg])
                        T_(lambda e, bank=bank, cq=c['Q'], cp=c['P']: e.matmul(bank[0:L, 128:128 + L], lhsT=cp, rhs=cq, start=False, stop=True, skip_group_check=True),
                           r=[c['Pt'], c['Qt']], w=[breg])
                    else:
                        T_(lambda e, bank=bank, cq=c['Q'], cp=c['P']: e.matmul(bank[0:L, 128:128 + L], lhsT=cp, rhs=cq, start=True, stop=True),
                           r=[c['Pt'], c['Qt']], w=[breg])
                    nq = pq[i].next()
                    lo = 1 if last else 0
                    A_(lambda e, bank=bank, nq=nq, lo=lo: e.activation(out=nq[0:L, lo:2, 0:L],
                                                                      in_=bank[0:L, 128 * lo:256].rearrange("p (m t) -> p m t", m=2 - lo)[:, :, 0:L], func=AF.Copy),
                       r=[breg], w=[nq])
                    c['P'] = nq[0:L, 0, 0:L]; c['Q'] = nq[0:L, 1, 0:L]; c['Pt'] = nq; c['Qt'] = nq
                for i, h in enumerate(heads):
                    bank = cb_ap(i); breg = cbanks[i]
                    c = cur[h]
                    pmb = pm.next()
                    T_(lambda e, pmb=pmb, cq=c['Q'], cm=c['M']: e.matmul(pmb[0:L, 0:L], lhsT=cq, rhs=cm[0:L, 0:L], start=True, stop=True),
                       r=[c['Qt'], c['M']], w=[pmb])
                    mn = Mt[i].next()
                    V_(lambda e, pmb=pmb, cm=c['M'], mn=mn: e.tensor_tensor(out=mn[0:L, 0:L], in0=pmb[0:L, 0:L], in1=cm[0:L, 0:L], op=ALU.add),
                       r=[pmb, c['M']], w=[mn])
                    c['M'] = mn
            for i, h in enumerate(heads):
                G_(lambda e, h=h, m=cur[h]['M']: e.tensor_copy(out=Mfin[0:L, h, 0:L], in_=m[0:L, 0:L]), r=[cur[h]['M']], w=[(Mfin, h)])
        for g in range(2):
            bank = cb_ap(g); breg = cbanks[g]
            for hh in range(8):
                h = 8 * g + hh
                p, bp = h // 2, 64 * (h % 2)
                T_(lambda e, bank=bank, hh=hh, p=p, bp=bp: e.matmul(bank[0:L, 64 * hh:64 * hh + 64], lhsT=AT[bp:bp + 64, p, c0:c0 + L], rhs=STb[bp:bp + 64, p, :],
                                                                   start=(hh == 0), stop=False, skip_group_check=True), r=[(AT, p), STb], w=[breg])
                T_(lambda e, bank=bank, hh=hh, h=h: e.matmul(bank[0:L, 64 * hh:64 * hh + 64], lhsT=AKs[0:Lk, h, 0:L], rhs=Vtok[0:Lk, b, 64 * h:64 * h + 64],
                                                            start=False, stop=True, skip_group_check=True), r=[(AKs, h), (Vtok, b)], w=[breg])
            evac_copy(Wb[0:L, 8 * g:8 * g + 8, :], bank[0:L, 0:512].rearrange("p (h i) -> p h i", h=8), r=[breg], w=[(Wb, g)])
        for g in range(2):
            bank = cb_ap(2 + g); breg = cbanks[2 + g]
            for hh in range(8):
                h = 8 * g + hh
                T_(lambda e, bank=bank, hh=hh, h=h: e.matmul(bank[0:L, 64 * hh:64 * hh + 64], lhsT=Mfin[0:L, h, 0:L], rhs=Wb[0:L, h, :],
                                                            start=(hh == 0), stop=True, skip_group_check=True), r=[(Mfin, h), (Wb, g)], w=[breg])
            evac_copy(Ub[0:L, 8 * g:8 * g + 8, :], bank[0:L, 0:512].rearrange("p (h i) -> p h i", h=8), r=[breg], w=[(Ub, g)])
        if own:
            for rnd in range(2):
                for par in range(2):
                    bank = cb_ap(par); breg = cbanks[par]
                    for j in range(4):
                        h = 8 * rnd + 2 * j + par
                        p, bp = h // 2, 64 * par
                        T_(lambda e, bank=bank, j=j, p=p, bp=bp: e.matmul(bank[0:L, 128 * j:128 * j + L], lhsT=KTt[bp:bp + 64, p, c0:c0 + L], rhs=RTt[bp:bp + 64, p, c0:c0 + L],
                                                                         start=(j == 0), stop=True, skip_group_check=True), r=[(KTt, p), (RTt, p)], w=[breg])
                for par in range(2):
                    bank = cb_ap(par); breg = cbanks[par]
                    for j in range(4):
                        h = 8 * rnd + 2 * j + par
                        V_(lambda e, bank=bank, j=j, h=h: e.tensor_tensor(out=ARK[0:L, h, 0:L], in0=bank[0:L, 128 * j:128 * j + L], in1=mask[0:L, 384:384 + L], op=ALU.mult),
                           r=[breg, mask], w=[(ARK, h)])
            for g in range(2):
                bank = cb_ap(2 + g); breg = cbanks[2 + g]
                for hh in range(8):
                    h = 8 * g + hh
                    p, bp = h // 2, 64 * (h % 2)
                    oc = bank[0:L, 64 * hh:64 * hh + 64]
                    T_(lambda e, oc=oc, hh=hh, p=p, bp=bp: e.matmul(oc, lhsT=RTt[bp:bp + 64, p, c0:c0 + L], rhs=STb[bp:bp + 64, p, :],
                                                                   start=(hh == 0), stop=False, skip_group_check=True), r=[(RTt, p), STb], w=[breg])
                    T_(lambda e, oc=oc, h=h: e.matmul(oc, lhsT=ARK[0:Lk, h, 0:L], rhs=Vtok[0:Lk, b, 64 * h:64 * h + 64], start=False, stop=False, skip_group_check=True),
                       r=[(ARK, h), (Vtok, b)], w=[breg])
                    T_(lambda e, oc=oc, h=h: e.matmul(oc, lhsT=ARB[0:Lk, h, 0:L], rhs=Ub[0:Lk, h, :], start=False, stop=True, skip_group_check=True),
                       r=[(ARB, h), (Ub, h // 8)], w=[breg])
                evac_copy(Yt[0:L, 512 * g:512 * (g + 1)], bank[0:L, 0:512], r=[breg], w=[(Yt, g)])
        bank = cb_ap(0); breg = cbanks[0]
        for h in range(16):
            p, bp = h // 2, 64 * (h % 2)
            T_(lambda e, h=h, p=p, bp=bp: e.matmul(bank[bp:bp + 64, 64 * p:64 * p + 64], lhsT=Ktok[0:L, b, 64 * h:64 * h + 64], rhs=Vtok[0:L, b, 64 * h:64 * h + 64],
                                                  start=(h < 2), stop=False, skip_group_check=True), r=[(Ktok, b), (Vtok, b)], w=[breg])
            T_(lambda e, h=h, p=p, bp=bp: e.matmul(bank[bp:bp + 64, 64 * p:64 * p + 64], lhsT=Btok[0:L, b, 64 * h:64 * h + 64], rhs=Ub[0:L, h, :],
                                                  start=False, stop=True, skip_group_check=True), r=[(Btok, b), (Ub, h // 8)], w=[breg])
        V_(lambda e: e.tensor_tensor(out=sttmp[:].rearrange("p k i -> p (k i)"), in0=bank[:, 0:512], in1=ST[:].rearrange("p k i -> p (k i)"), op=ALU.add),
           r=[breg, ST], w=[sttmp])
        V_(lambda e: e.tensor_tensor(out=ST[:], in0=GC[:, b, :].unsqueeze(2).to_broadcast([128, 8, 64]), in1=sttmp[:], op=ALU.mult),
           r=[sttmp, GC], w=[ST])
        A_(lambda e: e.activation(out=STb[:], in_=ST[:], func=AF.Copy), r=[ST], w=[STb])
        if own:
            rwkv_out(b, L)

    def rwkv_out(b, L):
        y3 = Yt[0:L, :].rearrange("p (h i) -> p h i", h=16)
        y23 = Y2[0:L, :].rearrange("p (h i) -> p h i", h=16)
        V_(lambda e: e.tensor_reduce(out=gns[0:L, 0:16], in_=y3, axis=mybir.AxisListType.X, op=ALU.add), r=[Yt], w=[gns])
        A_(lambda e: e.activation(out=Y2[0:L, :], in_=Yt[0:L, :], func=AF.Square), r=[Yt], w=[Y2])
        V_(lambda e: e.tensor_reduce(out=gns[0:L, 16:32], in_=y23, axis=mybir.AxisListType.X, op=ALU.add), r=[Y2], w=[gns])
        V_(lambda e: e.tensor_scalar(out=gns[0:L, 0:16], in0=gns[0:L, 0:16], scalar1=1.0 / 64, scalar2=None, op0=ALU.mult), r=[gns], w=[gns])
        V_(lambda e: e.tensor_tensor(out=gns[0:L, 32:48], in0=gns[0:L, 0:16], in1=gns[0:L, 0:16], op=ALU.mult), r=[gns], w=[gns])
        V_(lambda e: e.scalar_tensor_tensor(out=gns[0:L, 48:64], in0=gns[0:L, 16:32], scalar=1.0 / 64, in1=gns[0:L, 32:48], op0=ALU.mult, op1=ALU.subtract),
           r=[gns], w=[gns])
        A_(lambda e: e.activation(out=gns[0:L, 48:64], in_=gns[0:L, 48:64], func=AF.Sqrt, bias=float(GN_EPS), scale=1.0), r=[gns], w=[gns])
        V_(lambda e: e.reciprocal(out=gns[0:L, 48:64], in_=gns[0:L, 48:64]), r=[gns], w=[gns])
        V_(lambda e: e.tensor_scalar(out=gns[0:L, 64:80], in0=gns[0:L, 48:64], scalar1=-1.0, scalar2=None, op0=ALU.mult), r=[gns], w=[gns])
        V_(lambda e: e.tensor_tensor(out=y23, in0=gns[0:L, 0:16].unsqueeze(2).to_broadcast([L, 16, 64]), in1=y3, op=ALU.subtract), r=[gns, Yt], w=[Y2])
        V_(lambda e: e.tensor_tensor(out=y23, in0=gns[0:L, 64:80].unsqueeze(2).to_broadcast([L, 16, 64]), in1=y23, op=ALU.mult), r=[gns, Y2], w=[Y2])
        vg, vbb = getvec("gn_g"), getvec("gn_b")
        G_(lambda e: e.tensor_tensor(out=Y2[0:L, :], in0=Y2[0:L, :], in1=vg[0:L, :], op=ALU.mult), r=[Y2, vg], w=[Y2])
        G_(lambda e: e.tensor_tensor(out=Y2[0:L, :], in0=Y2[0:L, :], in1=vbb[0:L, :], op=ALU.add), r=[Y2, vbb], w=[Y2])
        V_(lambda e: e.tensor_tensor(out=y3, in0=bcoef[0:L, b, :].unsqueeze(2).to_broadcast([L, 16, 64]), in1=Vtok[0:L, b, :].rearrange("p (h i) -> p h i", h=16), op=ALU.mult),
           r=[bcoef, (Vtok, b)], w=[Yt])
        G_(lambda e: e.tensor_tensor(out=Y2[0:L, :], in0=Y2[0:L, :], in1=Yt[0:L, :], op=ALU.add), r=[Y2, Yt], w=[Y2])
        G_(lambda e: e.tensor_tensor(out=orw[0:L, :], in0=Y2[0:L, :], in1=gtok[0:L, b, :], op=ALU.mult), r=[Y2, gtok], w=[orw])
        for k in range(8):
            T_(lambda e, k=k: e.transpose(out=ptr[:, k, 0:L], in_=orw[0:L, 128 * k:128 * (k + 1)], identity=identB[:L, :L]), r=[orw, identB], w=[ptr])
        evac_copy(orwT[:, b, :, 0:L], ptr[:, :, 0:L], r=[ptr], w=[(orwT, b)])

    def attention(b, Lq, slot_lo, slot_hi, corner):
        S = slot_hi - slot_lo
        wi = kvs[b]
        for h in range(8):
            V_(lambda e, h=h: e.tensor_scalar(out=dg[0:Lq, h, 0:Lq], in0=identB[0:Lq, 0:Lq], scalar1=wi[0:Lq, 320 + h:321 + h], scalar2=None, op0=ALU.mult),
               r=[identB, wi], w=[dg])
        cidx = 0.125 * (8.0 ** -0.5)
        ntile = (S + 511) // 512
        for ti in range(ntile):
            s0 = slot_lo + 512 * ti
            n = min(512, slot_hi - s0)
            kit = kitl.next(); kb_ = kbt.next()
            P.dma('sp', kit[0:64, 0:n], D_kit.ap()[:, s0:s0 + n], r=[D_kit], w=[kit])
            P.dma('sp', kit[64:128, 0:n], D_kit.ap()[:, s0:s0 + n], r=[D_kit], w=[kit])
            P.dma('sp', kb_[:, 0:n], keyb.ap()[:, s0:s0 + n], r=[keyb], w=[kb_])
            psc = pm.next()
            T_(lambda e, psc=psc, kb_=kb_, n=n: e.matmul(psc[0:Lq, 0:n], lhsT=onesrow[0:1, 0:Lq], rhs=kb_[0:1, 0:n], start=True, stop=False), r=[onesrow, kb_], w=[psc])
            for h in range(8):
                pp, bp = h // 2, 64 * (h % 2)
                xb_ = xbanks[xq[0] % 3]; xq[0] += 1
                px = xb_[0][:, 512 * xb_[1]:512 * (xb_[1] + 1)]
                T_(lambda e, px=px, pp=pp, bp=bp, kit=kit, n=n: e.matmul(px[0:Lq, 0:n], lhsT=qiT[bp:bp + 64, b, pp, 0:Lq], rhs=kit[bp:bp + 64, 0:n], start=True, stop=True),
                   r=[(qiT, b), kit], w=[xb_])
                r_ = Rt.next()
                A_(lambda e, px=px, r_=r_, n=n: e.activation(out=r_[0:Lq, 0:n], in_=px[0:Lq, 0:n], func=AF.Relu, scale=cidx), r=[xb_], w=[r_])
                T_(lambda e, psc=psc, h=h, r_=r_, n=n: e.matmul(psc[0:Lq, 0:n], lhsT=dg[0:Lq, h, 0:Lq], rhs=r_[0:Lq, 0:n], start=False, stop=(h == 7)), r=[dg, r_], w=[psc])
            V_(lambda e, psc=psc, ti=ti, n=n: e.tensor_copy(out=SC[0:Lq, 512 * ti:512 * ti + n], in_=psc[0:Lq, 0:n]), r=[psc], w=[(SC, ti)])
        if corner:
            V_(lambda e: e.memset(SC[0:64, S - 64:S], -1e30), r=[], w=[SC])
        V_(lambda e: e.memset(tau[0:Lq, 0:1], 0.0), w=[tau])
        for it in range(NBIS):
            s_ = 16.0 * (0.5 ** (it + 1))
            V_(lambda e: e.tensor_scalar(out=junk[0:Lq, 0:1].to_broadcast([Lq, S]), in0=SC[0:Lq, 0:S], scalar1=tau[0:Lq, 0:1], scalar2=None,
                                         op0=ALU.is_ge, op1=ALU.add, accum_out=tau[0:Lq, 1:2]), r=[SC, tau], w=[junk, tau])
            V_(lambda e, s_=s_: e.tensor_scalar(out=tau[0:Lq, 2:3], in0=tau[0:Lq, 1:2], scalar1=TOPK - 0.5, scalar2=2.0 * s_, op0=ALU.is_ge, op1=ALU.mult), r=[tau], w=[tau])
            V_(lambda e, s_=s_: e.scalar_tensor_tensor(out=tau[0:Lq, 0:1], in0=tau[0:Lq, 2:3], scalar=-s_, in1=tau[0:Lq, 0:1], op0=ALU.add, op1=ALU.add), r=[tau], w=[tau])
        s_last = 16.0 * (0.5 ** NBIS)
        V_(lambda e: e.tensor_scalar(out=tau[0:Lq, 3:4], in0=tau[0:Lq, 0:1], scalar1=-s_last, scalar2=None, op0=ALU.add), r=[tau], w=[tau])
        first = True
        nsb_total = (S + 127) // 128
        sbi = 0
        for ti in range(ntile):
            s0 = slot_lo + 512 * ti
            n = min(512, slot_hi - s0)
            kt_ = ktl.next(); vt_ = vtl.next(); mb = mbt.next()
            P.dma('sp', kt_[:, 0:n], D_kt.ap()[:, s0:s0 + n], r=[D_kt], w=[kt_])
            nfull, rem = n // 128, n % 128
            if nfull:
                P.dma('sp', vt_[:, 0:nfull, 0:128], D_v.ap()[s0:s0 + 128 * nfull, :].rearrange("(a p) d -> p a d", p=128), r=[D_v], w=[vt_])
            if rem:
                P.dma('sp', vt_[0:rem, nfull, 0:128], D_v.ap()[s0 + 128 * nfull:s0 + n, :], r=[D_v], w=[vt_])
            V_(lambda e, mb=mb, ti=ti, n=n: e.tensor_scalar(out=mb[0:Lq, 0:n], in0=SC[0:Lq, 512 * ti:512 * ti + n], scalar1=tau[0:Lq, 3:4], scalar2=-30000.0,
                                                           op0=ALU.is_lt, op1=ALU.mult), r=[(SC, ti), tau], w=[mb])
            for a in range((n + 127) // 128):
                ns = min(128, n - 128 * a)
                for hh in range(2):
                    T_(lambda e, a=a, ns=ns, hh=hh, kt_=kt_: e.matmul(pLT[0:ns, 512 * hh:512 * hh + 4 * Lq],
                                                                    lhsT=kt_[:, 128 * a:128 * a + ns], rhs=qT[:, b, 4 * hh:4 * hh + 4, 0:Lq], start=True, stop=False),
                       r=[kt_, (qT, b)], w=[(pLT, hh)])
                    T_(lambda e, a=a, ns=ns, hh=hh, mb=mb: e.matmul(pLT[0:ns, 512 * hh:512 * hh + 4 * Lq],
                                                                  lhsT=mb[0:Lq, 128 * a:128 * a + ns], rhs=I4[0:Lq, :, 0:Lq], start=False, stop=True),
                       r=[mb, I4], w=[(pLT, hh)])
                pt = PTt.next()
                A_(lambda e, ns=ns, pt=pt: e.activation(out=pt[0:ns, 0:8 * Lq].rearrange("p (g x) -> p g x", g=2),
                                                        in_=pLT[0:ns, :].rearrange("p (g x) -> p g x", g=2)[:, :, 0:4 * Lq], func=AF.Exp, scale=128.0 ** -0.5),
                   r=[pLT], w=[pt])
                last = (sbi == nsb_total - 1)
                for h in range(8):
                    off = 512 * (h // 3) + 129 * (h % 3)
                    T_(lambda e, h=h, off=off, ns=ns, a=a, pt=pt, vt_=vt_, st_=(first and h % 3 == 0), last=last: e.matmul(
                        pO[0:Lq, off:off + 129], lhsT=pt[0:ns, Lq * h:Lq * h + Lq], rhs=vt_[0:ns, a, 0:129], start=st_, stop=last, skip_group_check=True),
                       r=[pt, vt_], w=[(pO, h // 3)])
                first = False
                sbi += 1
        for h in range(8):
            off = 512 * (h // 3) + 129 * (h % 3)
            V_(lambda e, h=h, off=off: e.reciprocal(out=rden[0:Lq, h:h + 1], in_=pO[0:Lq, off + 128:off + 129]), r=[(pO, h // 3)], w=[rden])
            V_(lambda e, h=h, off=off: e.tensor_scalar(out=attn[0:Lq, 128 * h:128 * h + 128], in0=pO[0:Lq, off:off + 128], scalar1=rden[0:Lq, h:h + 1], scalar2=None, op0=ALU.mult),
               r=[(pO, h // 3), rden], w=[attn])
        for k in range(8):
            T_(lambda e, k=k: e.transpose(out=ptr[:, k, 0:Lq], in_=attn[0:Lq, 128 * k:128 * (k + 1)], identity=identB[:Lq, :Lq]), r=[attn, identB], w=[ptr])
        evac_copy(attnT[:, b, :, 0:Lq], ptr[:, :, 0:Lq], r=[ptr], w=[(attnT, b)])


    def post(Ls, y_dram, yrows):
        L = Ls[0]
        nbk = len(Ls)
        ntk = sum(Ls)
        woa = [load_w(kview(D_woa), 512 * g, 512) for g in range(2)]
        for b in range(nbk):
            for g in range(2):
                po_ = pm.next()
                for k in range(8):
                    T_(lambda e, b=b, g=g, k=k, po_=po_: e.matmul(po_[:L, :], lhsT=attnT[:, b, k, 0:L], rhs=woa[g][:, k, :], start=(k == 0), stop=(k == 7)),
                       r=[(attnT, b), woa[g]], w=[po_])
                mx = mixs[b]
                V_(lambda e, b=b, g=g, po_=po_, mx=mx: e.tensor_tensor(out=mx[:L, 512 * g:512 * (g + 1)], in0=po_[:L, :],
                                                                    in1=gsig[:L, b, 512 * g:512 * (g + 1)], op=ALU.mult), r=[po_, gsig], w=[(mx, g)])
        wor = [load_w(kview(D_wor), 512 * g, 512) for g in range(2)]
        for b in range(nbk):
            mx = mixs[b]
            for g in range(2):
                po_ = pm.next()
                for k in range(8):
                    T_(lambda e, b=b, g=g, k=k, po_=po_: e.matmul(po_[:L, :], lhsT=orwT[:, b, k, 0:L], rhs=wor[g][:, k, :], start=(k == 0), stop=(k == 7)),
                       r=[(orwT, b), wor[g]], w=[po_])
                tq = rtmp.next()
                V_(lambda e, b=b, g=g, po_=po_, tq=tq: e.tensor_tensor(out=tq[:L, :], in0=po_[:L, :], in1=gsig[:L, b, D + 512 * g:D + 512 * (g + 1)], op=ALU.mult),
                   r=[po_, gsig], w=[tq])
                G_(lambda e, g=g, mx=mx, tq=tq: e.tensor_tensor(out=mx[:L, 512 * g:512 * (g + 1)], in0=mx[:L, 512 * g:512 * (g + 1)], in1=tq[:L, :], op=ALU.add),
                   r=[tq, (mx, g)], w=[(mx, g)])
            for k in range(8):
                T_(lambda e, k=k, mx=mx: e.transpose(out=ptr[:, k, 0:L], in_=mx[:L, 128 * k:128 * (k + 1)], identity=identB[:L, :L]), r=[mx, identB], w=[ptr])
            evac_copy(mixT[:, :, L * b:L * b + L], ptr[:, :, 0:L], r=[ptr], w=[(mixT, b)])
        wo = [load_w(kview(D_wout), 512 * g, 512) for g in range(2)]
        for b in range(nbk):
            for g in range(2):
                po_ = pm.next()
                for k in range(8):
                    T_(lambda e, b=b, g=g, k=k, po_=po_: e.matmul(po_[:L, :], lhsT=mixT[:, k, L * b:L * b + L], rhs=wo[g][:, k, :], start=(k == 0), stop=(k == 7)),
                       r=[(mixT, b), wo[g]], w=[po_])
                V_(lambda e, b=b, g=g, po_=po_: e.scalar_tensor_tensor(out=x1[:L, b, 512 * g:512 * (g + 1)], in0=hres[:L, b, 512 * g:512 * (g + 1)], scalar=float(ALPHA),
                                                                    in1=po_[:L, :], op0=ALU.mult, op1=ALU.add), r=[po_, (hres, b)], w=[(x1, b)])
            ln_rows(x1[:L, b, :], L, D, x1[:L, b, :], LN_EPS, g_b=(getvec("ln1_g"), getvec("ln1_b")))
            G_(lambda e, b=b: e.tensor_copy(out=x1b[:L, :], in_=x1[:L, b, :]), r=[x1], w=[x1b])
            for k in range(8):
                T_(lambda e, k=k: e.transpose(out=ptr[:, k, 0:L], in_=x1b[:L, 128 * k:128 * (k + 1)], identity=identB[:L, :L]), r=[x1b, identB], w=[ptr])
            evac_copy(x1T[:, :, L * b:L * b + L], ptr[:, :, 0:L], r=[ptr], w=[(x1T, b)])
        accs = [[(pLT, 0), (pLT, 1)], [(pO, 0), (pO, 1)]]
        acc_ap = lambda b, g: (accs[b][g][0])[:, 512 * accs[b][g][1]:512 * (accs[b][g][1] + 1)]
        nfc = DFF // 128
        for fq in range(0, nfc, 4):
            nq = min(4, nfc - fq)
            wg_ = load_w(kview(D_wfg), 128 * fq, 128 * nq)
            wu_ = load_w(kview(D_wfu), 128 * fq, 128 * nq)
            wd_ = wt.next()
            wdv = wd_[:].rearrange("p (a c) n -> p a (c n)", c=2)
            P.dma('sp', wdv[:, 0:nq, :], D_wfd.ap()[128 * fq:128 * (fq + nq), :].rearrange("(a p) n -> p a n", p=128), r=[D_wfd], w=[wd_])
            for j in range(nq):
                fc = fq + j
                pg_ = pm.next()
                for k in range(8):
                    T_(lambda e, k=k, j=j, pg_=pg_, wg_=wg_: e.matmul(pg_[:, 0:ntk], lhsT=wg_[:, k, 128 * j:128 * (j + 1)], rhs=x1T[:, k, 0:ntk], start=(k == 0), stop=(k == 7)),
                       r=[x1T, wg_], w=[pg_])
                ag = actg.next()
                A_(lambda e, pg_=pg_, ag=ag: e.activation(out=ag[:, 0:ntk], in_=pg_[:, 0:ntk], func=AF.Silu), r=[pg_], w=[ag])
                pu_ = pm.next()
                for k in range(8):
                    T_(lambda e, k=k, j=j, pu_=pu_, wu_=wu_: e.matmul(pu_[:, 0:ntk], lhsT=wu_[:, k, 128 * j:128 * (j + 1)], rhs=x1T[:, k, 0:ntk], start=(k == 0), stop=(k == 7)),
                       r=[x1T, wu_], w=[pu_])
                at_ = actT.next()
                V_(lambda e, pu_=pu_, ag=ag, at_=at_: e.tensor_tensor(out=at_[:, 0:ntk], in0=pu_[:, 0:ntk], in1=ag[:, 0:ntk], op=ALU.mult), r=[pu_, ag], w=[at_])
                for b in range(nbk):
                    for g in range(2):
                        T_(lambda e, b=b, g=g, j=j, fc=fc, at_=at_, wdv=wdv: e.matmul(acc_ap(b, g)[:L, :], lhsT=at_[:, L * b:L * b + L], rhs=wdv[:, j, 512 * g:512 * (g + 1)],
                                                                                   start=(fc == 0), stop=(fc == nfc - 1)), r=[at_, wd_], w=[accs[b][g]])
        for b in range(nbk):
            yt = yo.next()
            for g in range(2):
                V_(lambda e, b=b, g=g, yt=yt: e.scalar_tensor_tensor(out=yt[:L, 512 * g:512 * (g + 1)], in0=x1[:L, b, 512 * g:512 * (g + 1)], scalar=float(ALPHA),
                                                                  in1=acc_ap(b, g)[:L, :], op0=ALU.mult, op1=ALU.add), r=[accs[b][g], x1], w=[(yt, g)])
            ln_rows(yt[:L, :], L, D, yt[:L, :], LN_EPS, g_b=(getvec("ln2_g"), getvec("ln2_b")))
            P.dma('pool', y_dram.ap()[yrows[b]:yrows[b] + L, :], yt[:L, :], r=[yt], w=[(y_dram, yrows[b])])

    nso_sb = GEOM['NSO_B'] // NB
    so_blocks = [(NT * i, [128] * NB) for i in range(nso_sb)] + [(128 * GEOM['NSO_B'], [16])]
    for (row0, Ls) in so_blocks:
        L = Ls[0]
        if L == 16:
            V_(lambda e: e.tensor_scalar(out=ST[:].rearrange("p k i -> p (k i)"), in0=ST[:].rearrange("p k i -> p (k i)"), scalar1=flg[:, 0:1], scalar2=None, op0=ALU.mult),
               r=[ST, flg], w=[ST])
            A_(lambda e: e.activation(out=STb[:], in_=ST[:], func=AF.Copy), r=[ST], w=[STb])
            V_(lambda e: e.tensor_scalar(out=car[:], in0=car[:], scalar1=flg[:, 0:1], scalar2=None, op0=ALU.mult), r=[car, flg], w=[car])
        rows = [row0 + L * b for b in range(len(Ls))]
        front(xso, rows, Ls, rows, rows, (O_k, O_v, O_ki, rows), own=False)
        carry = [(car, car)] + [None] * (len(Ls) - 1)
        save = [None] * (len(Ls) - 1) + [(car, car)]
        rwkv_prep(Ls, False, carry, save)
        if L == 16:
            noop = lambda xs: None
            for half in range(2):
                wr_ = load_w(win_v, C_RW + 512 * half, 512)
                for pp in range(4):
                    rw_tile(4 * half + pp, wr_, 128 * pp, Ls, noop, carry, save)
            wl_ = load_w(win_v, C_RW + 3072, 256)
            rw_tile(25, wl_, 128, Ls, noop, carry, save)
        for b in range(len(Ls)):
            chunk_scan(b, L, own=False)

    for sbi in range(GEOM['NOWN_B'] // NB):
        Ls = [128] * NB
        rows = [NT * sbi + 128 * b for b in range(NB)]
        slots = [NSO + r_ for r_ in rows]
        front(xown, rows, Ls, slots, slots, (O_k, O_v, O_ki, slots), own=True)
        own_proj(Ls, slots)
        carry = [(car, car)] + [None] * (NB - 1)
        save = [None] * (NB - 1) + [(car, car)]
        rwkv_prep(Ls, True, carry, save)
        for b in range(NB):
            chunk_scan(b, 128, own=True)
        for b in range(NB):
            attention(b, 128, 0, slots[b] + 128, corner=True)
        post(Ls, O_y, rows)
    P.dma('sp', O_wkv.ap(), ST[:].rearrange("p k i -> p (k i)"), r=[ST], w=[O_wkv])
    P.dma('sp', O_shift.ap(), car[:], r=[car], w=[O_shift])

    if SAMPLE:
        for q in range(2):
            sb0 = NSLOT + SSTRIDE * q
            for i0 in range(0, CACHE_ROWS, 128):
                L = min(128, CACHE_ROWS - i0)
                ct = xin.next()
                P.dma('sp', ct[:L, 0, 0:128], ck.ap()[q, i0:i0 + L, :], r=[ck], w=[ct])
                P.dma('sp', ct[:L, 0, 128:192], cik.ap()[q, i0:i0 + L, :], r=[cik], w=[ct])
                pt_ = pm.next()
                T_(lambda e, ct=ct, pt_=pt_, L=L: e.transpose(out=pt_[:, 0:L], in_=ct[:L, 0, 0:128], identity=identF[:L, :L]), r=[ct, identF], w=[pt_])
                T_(lambda e, ct=ct, pt_=pt_, L=L: e.transpose(out=pt_[0:64, 128:128 + L], in_=ct[:L, 0, 128:192], identity=identF[:L, :L]), r=[ct, identF], w=[pt_])
                kt_t = ktt.next(); kit_t = kitt.next()
                V_(lambda e, kt_t=kt_t, pt_=pt_, L=L: e.tensor_copy(out=kt_t[:, 0:L], in_=pt_[:, 0:L]), r=[pt_], w=[kt_t])
                V_(lambda e, kit_t=kit_t, pt_=pt_, L=L: e.tensor_copy(out=kit_t[:, 0:L], in_=pt_[0:64, 128:128 + L]), r=[pt_], w=[kit_t])
                P.dma('sp', D_kt.ap()[:, sb0 + i0:sb0 + i0 + L], kt_t[:, 0:L], r=[kt_t], w=[(D_kt, sb0 + i0)])
                P.dma('sp', D_kit.ap()[:, sb0 + i0:sb0 + i0 + L], kit_t[:, 0:L], r=[kit_t], w=[(D_kit, sb0 + i0)])
        for q in range(2):
            P.dma('sp', cars[:, q, :], sshift.ap()[q], r=[sshift], w=[(cars, q)])
        for q in range(2):
            sb0 = NSLOT + SSTRIDE * q
            Ls = [64]
            rows = [64 * q]
            rrows = [NSO + NOWN + 64 * q]
            slots = [sb0 + CACHE_ROWS]
            front(xsm, rows, Ls, rrows, slots, (O_ks, O_vs, O_kis, rows), own=True)
            own_proj(Ls, rrows)
            rwkv_prep(Ls, True, [(cars[:, q, :], (cars, q))], [(cars[:, q, :], (cars, q))])
            P.dma('sp', ST[:].rearrange("p k i -> p (k i)"), swkv.ap()[q], r=[swkv], w=[ST])
            A_(lambda e: e.activation(out=STb[:], in_=ST[:], func=AF.Copy), r=[ST], w=[STb])
            chunk_scan(0, 64, own=True)
            P.dma('sp', O_wkvs.ap()[q], ST[:].rearrange("p k i -> p (k i)"), r=[ST], w=[(O_wkvs, q)])
            P.dma('sp', O_shifts.ap()[q], cars[:, q, :], r=[(cars, q)], w=[(O_shifts, q)])
            attention(0, 64, sb0, sb0 + CACHE_ROWS + 64, corner=False)
            post(Ls, O_ys, rows)
    nc = P.build()
    return nc, P


def make_consts():
    c = {}
    c["identf"] = np.eye(128, dtype=np.float32)
    b = np.zeros((128, 128), np.float32); b[:64, :64] = 1; b[64:, 64:] = 1
    c["blk1"] = b
    us = np.triu(np.ones((128, 128), np.float32), 1)
    ui = np.triu(np.ones((128, 128), np.float32), 0)
    c["cmask"] = np.concatenate([us, us.T, us, ui, ui], axis=1).astype(np.float32)
    return c


def rope_table(pos):
    pos = np.asarray(pos, np.float32)
    out = np.zeros((len(pos), 48), np.float32)
    for (rot, o) in ((32, 0), (16, 32)):
        inv = (np.float32(500000.0) ** (-np.arange(0, rot, 2, dtype=np.float32) / np.float32(rot))).astype(np.float32)
        ang = (pos[:, None] * inv[None]).astype(np.float32)
        h = rot // 2
        out[:, o:o + h] = np.cos(ang); out[:, o + h:o + 2 * h] = np.sin(ang)
    return out


def colpack(v, n):
    return np.ascontiguousarray(np.asarray(v, np.float32).reshape(n, 128).T)


def st_layout(s):
    s = np.asarray(s, np.float32).reshape(8, 2, 64, 64)
    return np.ascontiguousarray(s.transpose(1, 3, 0, 2).reshape(128, 512))


def st_unlayout(a):
    a = np.asarray(a, np.float32).reshape(2, 64, 8, 64)
    return np.ascontiguousarray(a.transpose(2, 0, 3, 1).reshape(16, 64, 64))


def prep_inputs(inp):
    NSO, NOWN, NSLOT = geom()
    f32 = lambda a: np.ascontiguousarray(np.asarray(a, np.float32))
    consts = make_consts()
    maps = []
    colsf = np.concatenate([
        colpack(inp["ln0_g"], 8), colpack(inp["ln0_b"], 8), colpack(inp["rw_mu"][0], 26), colpack(inp["rw_w0"][0], 8),
        colpack(inp["rw_a0"][0], 8), colpack(inp["rw_k_k"][0], 8), colpack(inp["rw_k_a"][0], 8), colpack(np.asarray(inp["rw_r_k"][0]).reshape(-1), 8)], axis=1)
    shared = dict(consts)
    shared.update(colsf=colsf, w_in=f32(inp["w_in"][0]), rw_w2=f32(inp["rw_w2"][0]), rw_a2=f32(inp["rw_a2"][0]), rw_g2=f32(inp["rw_g2"][0]),
                  ikg=f32(inp["idx_k_ln_g"][0]), ikb=f32(inp["idx_k_ln_b"][0]),
                  ln0_g=f32(inp["ln0_g"]), ln0_b=f32(inp["ln0_b"]), ln1_g=f32(inp["ln1_g"][0]), ln1_b=f32(inp["ln1_b"][0]),
                  ln2_g=f32(inp["ln2_g"][0]), ln2_b=f32(inp["ln2_b"][0]), gn_g=f32(inp["rw_gn_g"][0]), gn_b=f32(inp["rw_gn_b"][0]),
                  w_oa=f32(inp["w_o_attn"][0]), w_or=f32(inp["w_o_rwkv"][0]), w_out=f32(inp["w_out"][0]),
                  w_fg=f32(inp["ffn_w_gate"][0]), w_fu=f32(inp["ffn_w_up"][0]), w_fd=f32(inp["ffn_w_down"][0]))
    meta = f32(inp["meta_tokens"])
    past = int(np.asarray(inp["cache_k"]).shape[2]) - 16
    for c in range(8):
        b, hf = c // 2, c % 2
        xp = f32(inp["x_prompt"][b])
        nfr = NSO - 16
        if hf == 1:
            xso = np.concatenate([meta, xp[:nfr]], 0)
            pos_so = np.arange(NSO)
            xown = xp[nfr:nfr + NOWN]; pos_own = NSO + np.arange(NOWN)
        else:
            xso = np.concatenate([xp[nfr:2 * nfr], meta], 0)
            pos_so = np.concatenate([np.zeros(nfr), np.arange(16)])
            xown = xp[:NOWN]; pos_own = 16 + np.arange(NOWN)
        keyb = np.zeros((1, NSLOT + 2 * SSTRIDE), np.float32)
        if hf == 0:
            keyb[0, :nfr] = -1e30
        xs = f32(inp["x_sample"][2 * c:2 * c + 2]).reshape(128, D)
        pos_sm = np.concatenate([16 + past + np.arange(64)] * 2)
        m = dict(shared)
        m.update(xso=np.ascontiguousarray(xso), xown=np.ascontiguousarray(xown), xsm=xs,
                 rope=rope_table(np.concatenate([pos_so, pos_own, pos_sm])),
                 flag=np.full((128, 1), float(hf), np.float32), keyb=keyb.astype(ml_dtypes.bfloat16),
                 ck=f32(inp["cache_k"][0, 2 * c:2 * c + 2]), cv=f32(inp["cache_v"][0, 2 * c:2 * c + 2]), cik=f32(inp["cache_idx_k"][0, 2 * c:2 * c + 2]),
                 swkv=np.stack([st_layout(inp["state_wkv"][0, 2 * c + q]) for q in range(2)]),
                 sshift=np.stack([colpack(inp["state_shift"][0, 2 * c + q], 26) for q in range(2)]))
        maps.append(m)
    return maps


_CACHE = {}


def run_device(inp):
    key = (GEOM['NSO_B'], GEOM['NOWN_B'], GEOM['SAMPLE'], MAXOPS)
    if key not in _CACHE:
        _CACHE[key] = build_program()
    nc, P = _CACHE[key]
    maps = prep_inputs(inp)
    used = set(P.names)
    maps = [{k: v for k, v in m.items() if k in used} for m in maps]
    res = run_bass_kernel_spmd(nc, maps, core_ids=list(range(8)))
    return res.results


def uncol(a, n):
    return np.ascontiguousarray(np.asarray(a, np.float32).T.reshape(-1))


def kernel(**inputs):
    NSO, NOWN, NSLOT = geom()
    res = run_device(inputs)
    B = 4
    y_p = np.zeros((B, 2 * NOWN, D), np.float32)
    k_p = np.zeros((1, B, NSLOT, 128), np.float32); v_p = np.zeros((1, B, NSLOT, 128), np.float32); ki_p = np.zeros((1, B, NSLOT, 64), np.float32)
    wkv_p = np.zeros((1, B, 16, 64, 64), np.float32); sh_p = np.zeros((1, B, 3328), np.float32)
    y_s = np.zeros((16, 64, D), np.float32)
    k_s = np.zeros((1, 16, 64, 128), np.float32); v_s = np.zeros((1, 16, 64, 128), np.float32); ki_s = np.zeros((1, 16, 64, 64), np.float32)
    wkv_s = np.zeros((1, 16, 16, 64, 64), np.float32); sh_s = np.zeros((1, 16, 3328), np.float32)
    for c in range(8):
        b, hf = c // 2, c % 2
        r = res[c]
        y_p[b, hf * NOWN:(hf + 1) * NOWN] = r["O_y"]
        if hf == 1:
            k_p[0, b] = r["O_k"]; v_p[0, b] = r["O_v"]; ki_p[0, b] = r["O_ki"]
            wkv_p[0, b] = st_unlayout(r["O_wkv"]); sh_p[0, b] = uncol(r["O_shift"], 26)
        y_s[2 * c:2 * c + 2] = r["O_ys"].reshape(2, 64, D)
        k_s[0, 2 * c:2 * c + 2] = r["O_ks"].reshape(2, 64, 128); v_s[0, 2 * c:2 * c + 2] = r["O_vs"].reshape(2, 64, 128)
        ki_s[0, 2 * c:2 * c + 2] = r["O_kis"].reshape(2, 64, 64)
        for q in range(2):
            wkv_s[0, 2 * c + q] = st_unlayout(r["O_wkvs"][q]); sh_s[0, 2 * c + q] = uncol(r["O_shifts"][q], 26)
    return (y_p, y_s, k_p, v_p, ki_p, wkv_p, sh_p, k_s, v_s, ki_s, wkv_s, sh_s)
```

```python
import bisect
import numpy as np
import ml_dtypes
from contextlib import ExitStack
import concourse.bass as bass
import concourse.mybir as mybir
from concourse.bass_utils import run_bass_kernel_spmd

F32 = mybir.dt.float32
BF16 = mybir.dt.bfloat16
U8 = mybir.dt.uint8
ALU = mybir.AluOpType
AF = mybir.ActivationFunctionType

SAME_ENGINE_SYNC = True
MAXOPS = None
ENGS = ('pe', 'act', 'dve', 'pool', 'sp')


class Prog:
    def __init__(self):
        self.nc = bass.Bass("TRN2", target_bir_lowering=False)
        self.es = ExitStack()
        self.ops = []
        self.state = {}
        self.names = set()
        self.psum_names = set()

    def _nm(self, name):
        assert name not in self.names, name
        self.names.add(name)
        return name

    def sb(self, name, shape, dt):
        return self.es.enter_context(self.nc.sbuf_tensor(self._nm(name), list(shape), dt))

    def ps(self, name, shape, dt=F32):
        self.psum_names.add(name)
        return self.es.enter_context(self.nc.psum_tensor(self._nm(name), list(shape), dt))

    def dram(self, name, shape, dt, kind):
        return self.nc.dram_tensor(self._nm(name), list(shape), dt, kind=kind)

    @staticmethod
    def _reg(x):
        if isinstance(x, tuple):
            base, sub = x[0], (x[1],)
        else:
            base, sub = x, ()
        if isinstance(base, Alias):
            return base.name, (base.key,) + sub
        return base.name, sub

    @staticmethod
    def _rel(a, b):
        n = min(len(a), len(b))
        return a[:n] == b[:n]

    def _deps(self, idx, reads, writes, eng=None):
        deps = set()
        for x in reads:
            n, k = self._reg(x)
            st = self.state.setdefault(n, {})
            for kk, ent in st.items():
                if not self._rel(kk, k):
                    continue
                if ent[0] is not None:
                    deps.add(ent[0])
                if n in self.psum_names:
                    for rr in ent[1]:
                        if self.ops[rr]['eng'] != eng:
                            deps.add(rr)
        for x in writes:
            n, k = self._reg(x)
            st = self.state.setdefault(n, {})
            for kk, ent in st.items():
                if not self._rel(kk, k):
                    continue
                if ent[0] is not None:
                    deps.add(ent[0])
                deps.update(ent[1])
        for x in reads:
            n, k = self._reg(x)
            self.state[n].setdefault(k, [None, []])[1].append(idx)
        for x in writes:
            n, k = self._reg(x)
            st = self.state[n]
            for kk in [kk for kk in st if len(kk) >= len(k) and kk[:len(k)] == k]:
                del st[kk]
            st[k] = [idx, []]
        deps.discard(idx)
        return sorted(deps)

    def op(self, eng, fn, r=(), w=()):
        if MAXOPS is not None and len(self.ops) >= MAXOPS:
            return None
        idx = len(self.ops)
        self.ops.append(dict(eng=eng, fn=fn, deps=self._deps(idx, r, w, eng), dma=False, semkey=None))
        return idx

    def dma(self, q, out, in_, r=(), w=(), semkey=None, **kw):
        if MAXOPS is not None and len(self.ops) >= MAXOPS:
            return None
        idx = len(self.ops)
        deps = self._deps(idx, r, w)
        if semkey is None:
            wn = self._reg(w[0])[0]
            semkey = wn if not (wn.startswith('D_') or wn.startswith('O_')) else self._reg(r[0])[0]
        fn = (lambda e, out=out, in_=in_, kw=kw: e.dma_start(out=out, in_=in_, **kw))
        self.ops.append(dict(eng=q, fn=fn, deps=deps, dma=True, semkey=semkey))
        return idx

    def build(self):
        nc, ops = self.nc, self.ops
        n = len(ops)
        need_sig = [False] * n
        for i, o in enumerate(ops):
            for j in o['deps']:
                pj = ops[j]
                if pj['dma']:
                    continue
                if pj['eng'] != o['eng'] or o['dma'] or (SAME_ENGINE_SYNC and o['eng'] != 'pe'):
                    need_sig[j] = True
        cnt = {e: 0 for e in ENGS}
        sigval = [0] * n
        dcnt, semkeys = {}, []
        for i, o in enumerate(ops):
            if o['dma']:
                k = o['semkey']
                if k not in dcnt:
                    dcnt[k] = 0
                    semkeys.append(k)
                dcnt[k] += 16
                sigval[i] = dcnt[k]
            elif need_sig[i]:
                cnt[o['eng']] += 1
                sigval[i] = cnt[o['eng']]
        esem = {e: self.es.enter_context(nc.semaphore("s_" + e)) for e in ENGS}
        dsem = {k: self.es.enter_context(nc.semaphore("d_" + k)) for k in semkeys}
        self.n_sems = len(esem) + len(dsem)
        dma_idx = {}
        for i, o in enumerate(ops):
            if o['dma']:
                dma_idx.setdefault(o['semkey'], []).append(i)
        waited = {e: {} for e in ENGS}
        plan = {e: [] for e in ENGS}
        for i, o in enumerate(ops):
            E = o['eng']
            waits = {}
            for j in o['deps']:
                pj = ops[j]
                if pj['dma']:
                    key = ('d', pj['semkey'])
                    lst = dma_idx[pj['semkey']]
                    val_d = 16 * bisect.bisect_left(lst, i)
                else:
                    if pj['eng'] == E and not o['dma'] and (E == 'pe' or not SAME_ENGINE_SYNC):
                        continue
                    key = ('e', pj['eng'])
                val = val_d if pj['dma'] else sigval[j]
                if waited[E].get(key, 0) >= val:
                    continue
                waits[key] = max(waits.get(key, 0), val)
            for key, val in waits.items():
                waited[E][key] = val
            plan[E].append((i, waits))
        blk = self.es.enter_context(nc.Block())
        engobj = {'pe': 'tensor', 'act': 'scalar', 'dve': 'vector', 'pool': 'gpsimd', 'sp': 'sync'}

        def emit_for(E):
            def body(eng):
                for i, waits in plan[E]:
                    o = ops[i]
                    for (kind, k), val in waits.items():
                        eng.wait_ge(dsem[k] if kind == 'd' else esem[k], val)
                    ins = o['fn'](eng)
                    if o['dma']:
                        ins.then_inc(dsem[o['semkey']], 16)
                    elif need_sig[i]:
                        ins.then_inc(esem[E], 1)
                if E == 'sp':
                    for k, v in dcnt.items():
                        eng.wait_ge(dsem[k], v)
                    for e2 in ENGS:
                        if e2 != 'sp' and cnt[e2] > 0:
                            eng.wait_ge(esem[e2], cnt[e2])
            return body

        for E in ENGS:
            getattr(blk, engobj[E])(emit_for(E))
        self.es.close()
        return nc


class Alias:
    def __init__(self, base, dtype, byte_off, shape, key):
        es = 2 if dtype == BF16 else 4
        self.h = base.bitcast(dtype)
        self.off = byte_off // es
        self.shape = list(shape)
        self.name = base.name
        self.key = key
        n = int(np.prod(shape[1:]))
        v = self.h[0:shape[0], self.off:self.off + n]
        if len(shape) == 3:
            v = v.rearrange("p (a b) -> p a b", a=shape[1])
        self.v = v

    def __getitem__(self, idx):
        return self.v[idx]


class Ring:
    def __init__(self, tiles):
        self.t, self.i = tiles, 0

    def next(self):
        t = self.t[self.i % len(self.t)]
        self.i += 1
        return t


D = 1024
DFF = 2816
GEOM = dict(NSO_B=32, NOWN_B=32, SAMPLE=True)
NB = 1
NT = 128 * NB
WIN_COLS = 7240
C_Q, C_K, C_V, C_QI, C_KI, C_WI, C_G, C_RW = 0, 1024, 1152, 1280, 1792, 1856, 1864, 3912
LN_EPS = 1e-5
GN_EPS = 64e-5
DEC = 0.6065306597126334
ALPHA = 2.0 ** 0.25
CACHE_ROWS = 2064
SSTRIDE = 2176
NBIS = 19
TOPK = 256
O_G0, O_B0, O_MU, O_W0, O_A0, O_KK, O_KA, O_RK, NCOLS = 0, 8, 16, 42, 50, 58, 66, 74, 82


def geom():
    nso = 128 * GEOM['NSO_B'] + 16
    nown = 128 * GEOM['NOWN_B']
    return nso, nown, nso + nown


def build_program():
    NSO, NOWN, NSLOT = geom()
    SAMPLE = GEOM['SAMPLE']
    NSLOT_ALL = NSLOT + 2 * SSTRIDE
    P = Prog()
    nc = P.nc
    V_ = lambda fn, r=(), w=(): P.op('dve', fn, r, w)
    A_ = lambda fn, r=(), w=(): P.op('act', fn, r, w)
    G_ = lambda fn, r=(), w=(): P.op('pool', fn, r, w)
    T_ = lambda fn, r=(), w=(): P.op('pe', fn, r, w)

    din = lambda n, s, dt=F32: P.dram(n, s, dt, "ExternalInput")
    dout = lambda n, s, dt=F32: P.dram(n, s, dt, "ExternalOutput")
    dint = lambda n, s, dt=BF16: P.dram(n, s, dt, "Internal")
    xso = din("xso", [NSO, D]); xown = din("xown", [NOWN, D]); xsm = din("xsm", [128, D])
    rope = din("rope", [NSO + NOWN + 128, 48])
    flag = din("flag", [128, 1])
    colsf = din("colsf", [128, NCOLS])
    cmask = din("cmask", [128, 640])
    identf = din("identf", [128, 128])
    blk1 = din("blk1", [128, 128])
    keyb = din("keyb", [1, NSLOT_ALL], BF16)
    w_in = din("w_in", [D, WIN_COLS])
    rw_w2 = din("rw_w2", [64, D]); rw_a2 = din("rw_a2", [64, D]); rw_g2 = din("rw_g2", [128, D])
    ikg = din("ikg", [64]); ikb = din("ikb", [64])
    vecs = {n: din(n, [D]) for n in ("ln0_g", "ln0_b", "ln1_g", "ln1_b", "ln2_g", "ln2_b", "gn_g", "gn_b")}
    w_oa = din("w_oa", [D, D]); w_or = din("w_or", [D, D]); w_out = din("w_out", [D, D])
    w_fg = din("w_fg", [D, DFF]); w_fu = din("w_fu", [D, DFF]); w_fd = din("w_fd", [DFF, D])
    ck = din("ck", [2, CACHE_ROWS, 128]); cv = din("cv", [2, CACHE_ROWS, 128]); cik = din("cik", [2, CACHE_ROWS, 64])
    swkv = din("swkv", [2, 128, 512]); sshift = din("sshift", [2, 128, 26])

    O_y = dout("O_y", [NOWN, D]); O_ys = dout("O_ys", [128, D])
    O_k = dout("O_k", [NSLOT, 128]); O_v = dout("O_v", [NSLOT, 128]); O_ki = dout("O_ki", [NSLOT, 64])
    O_wkv = dout("O_wkv", [128, 512]); O_shift = dout("O_shift", [128, 26])
    O_ks = dout("O_ks", [128, 128]); O_vs = dout("O_vs", [128, 128]); O_kis = dout("O_kis", [128, 64])
    O_wkvs = dout("O_wkvs", [2, 128, 512]); O_shifts = dout("O_shifts", [2, 128, 26])

    D_win = dint("D_win", [D, WIN_COLS])
    D_w2 = dint("D_w2", [64, D]); D_a2 = dint("D_a2", [64, D]); D_g2 = dint("D_g2", [128, D])
    D_woa = dint("D_woa", [D, D]); D_wor = dint("D_wor", [D, D]); D_wout = dint("D_wout", [D, D])
    D_wfg = dint("D_wfg", [D, DFF]); D_wfu = dint("D_wfu", [D, DFF]); D_wfd = dint("D_wfd", [DFF, D])
    D_kt = dint("D_kt", [128, NSLOT_ALL]); D_kit = dint("D_kit", [64, NSLOT_ALL]); D_v = dint("D_v", [NSLOT_ALL, 128])

    def cast_rows(dst, src, nrows, step):
        for i in range(0, nrows, step):
            n = min(step, nrows - i)
            P.dma('pool', dst.ap()[i:i + n, :], src.ap()[i:i + n, :], r=[src], w=[(dst, i)])
    cast_rows(D_win, w_in, D, 128)
    P.dma('pool', D_w2.ap(), rw_w2.ap(), r=[rw_w2], w=[D_w2])
    P.dma('pool', D_a2.ap(), rw_a2.ap(), r=[rw_a2], w=[D_a2])
    P.dma('pool', D_g2.ap(), rw_g2.ap(), r=[rw_g2], w=[D_g2])
    for (dd, ss, nr) in ((D_woa, w_oa, D), (D_wor, w_or, D), (D_wout, w_out, D), (D_wfg, w_fg, D), (D_wfu, w_fu, D), (D_wfd, w_fd, DFF)):
        cast_rows(dd, ss, nr, 256)
    if SAMPLE:
        for q in range(2):
            sb0 = NSLOT + SSTRIDE * q
            P.dma('pool', D_v.ap()[sb0:sb0 + CACHE_ROWS, :], cv.ap()[q], r=[cv], w=[(D_v, 'c%d' % q)])
    kview = lambda dt_: dt_.ap().rearrange("(k p) n -> p k n", p=128)
    win_v = kview(D_win)

    cols = P.sb("cols", [128, NCOLS], F32)
    colsd = P.sb("colsd", [128, 34], F32)
    identF = P.sb("identF", [128, 128], F32)
    identB = P.sb("identB", [128, 128], BF16)
    I4 = P.sb("I4", [128, 4, 128], BF16)
    blkones = P.sb("blkones", [128, 128], F32)
    blkonesB = P.sb("blkonesB", [128, 128], BF16)
    mask = P.sb("mask", [128, 640], F32)
    ones128 = P.sb("ones128", [128, 128], F32)
    onesrow = P.sb("onesrow", [1, 128], BF16)
    ikg_b = P.sb("ikg_b", [128, 64], F32); ikb_b = P.sb("ikb_b", [128, 64], F32)
    w2b = P.sb("w2b", [128, D], BF16); a2b = P.sb("a2b", [128, D], BF16); g2b = P.sb("g2b", [128, D], BF16)
    flg = P.sb("flg", [128, 1], F32)
    vbr = Ring([P.sb("vbr%d" % i, [128, D], F32) for i in range(4)])

    def getvec(n):
        t = vbr.next()
        P.dma('sp', t[:], vecs[n].ap().partition_broadcast(128), r=[vecs[n]], w=[t])
        return t
    P.dma('sp', cols[:], colsf.ap(), r=[colsf], w=[cols])
    P.dma('sp', identF[:], identf.ap(), r=[identf], w=[identF])
    P.dma('sp', blkones[:], blk1.ap(), r=[blk1], w=[blkones])
    P.dma('sp', mask[:], cmask.ap(), r=[cmask], w=[mask])
    P.dma('sp', flg[:], flag.ap(), r=[flag], w=[flg])
    P.dma('sp', ikg_b[:], ikg.ap().partition_broadcast(128), r=[ikg], w=[ikg_b])
    P.dma('sp', ikb_b[:], ikb.ap().partition_broadcast(128), r=[ikb], w=[ikb_b])
    P.dma('sp', w2b[0:64, :], D_w2.ap(), r=[D_w2], w=[w2b])
    P.dma('sp', a2b[64:128, :], D_a2.ap(), r=[D_a2], w=[a2b])
    P.dma('sp', g2b[:], D_g2.ap(), r=[D_g2], w=[g2b])
    V_(lambda e: e.tensor_copy(out=identB[:], in_=identF[:]), r=[identF], w=[identB])
    for i in range(4):
        V_(lambda e, i=i: e.tensor_copy(out=I4[:, i, :], in_=identF[:]), r=[identF], w=[I4])
    V_(lambda e: e.tensor_copy(out=blkonesB[:], in_=blkones[:]), r=[blkones], w=[blkonesB])
    V_(lambda e: e.memset(ones128[:], 1.0), w=[ones128])
    V_(lambda e: e.memset(onesrow[:], 1.0), w=[onesrow])
    V_(lambda e: e.tensor_scalar(out=colsd[:, 0:26], in0=cols[:, O_MU:O_MU + 26], scalar1=-1.0, scalar2=1.0, op0=ALU.mult, op1=ALU.add),
       r=[cols], w=[colsd])
    V_(lambda e: e.tensor_scalar(out=colsd[:, 26:34], in0=cols[:, O_KA:O_KA + 8], scalar1=-1.0, scalar2=1.0, op0=ALU.mult, op1=ALU.add),
       r=[cols], w=[colsd])

    pm = Ring([P.ps("pm0", [128, 512]), P.ps("pm1", [128, 512])])
    ptr = P.ps("ptr", [128, 8, 128], BF16)
    pLT = P.ps("pLT", [128, 1024])
    pO = P.ps("pO", [128, 1536])
    cbanks = [(pLT, 0), (pLT, 1), (pO, 0), (pO, 1)]
    cb_ap = lambda i: (cbanks[i][0])[:, 512 * cbanks[i][1]:512 * (cbanks[i][1] + 1)]
    xbanks = [(pLT, 0), (pLT, 1), (pO, 2)]
    xq = [0]

    xin = Ring([P.sb("xin%d" % i, [128, NB, D], F32) for i in range(2)])
    hn = P.sb("hn", [128, NB, D], BF16)
    hres = P.sb("hres", [128, NB, D], F32)
    hT = P.sb("hT", [128, 8, NT], BF16)
    small = Ring([P.sb("small%d" % i, [128, 24], F32) for i in range(4)])
    wt = Ring([P.sb("wt%d" % i, [128, 8, 512], BF16) for i in range(3)])
    kvs = [P.sb("kvs%d" % i, [128, 328], F32) for i in range(NB)]
    rtmp = Ring([P.sb("rtmp%d" % i, [128, 512], F32) for i in range(2)])
    rp = [P.sb("rp%d" % i, [128, 48], F32) for i in range(NB)]
    vbt = Ring([P.sb("vbt%d" % i, [128, 128], BF16) for i in range(2)])
    ktt = Ring([P.sb("ktt%d" % i, [128, NT], BF16) for i in range(2)])
    kitt = Ring([P.sb("kitt%d" % i, [64, NT], BF16) for i in range(2)])
    ki2 = Ring([P.sb("ki2_%d" % i, [128, 64], F32) for i in range(2)])
    qT = P.sb("qT", [128, NB, 8, 128], BF16)
    qiT = P.sb("qiT", [128, NB, 4, 128], BF16)
    gsig = P.sb("gsig", [128, NB, 2 * D], BF16)
    car = P.sb("car", [128, 26], F32)
    cars = P.sb("cars", [128, 2, 26], F32)
    TL = P.sb("TL", [128, NT], BF16)
    SLG = P.sb("SLG", [128, NT], BF16)
    f32t = Ring([P.sb("f32t%d" % i, [128, NT], F32) for i in range(8)])
    AT = P.sb("AT", [128, 8, NT], BF16); KTt = P.sb("KTt", [128, 8, NT], BF16); BTt = P.sb("BTt", [128, 8, NT], BF16)
    VTf = P.sb("VTf", [128, 8, NT], BF16); RTt = P.sb("RTt", [128, 8, NT], BF16); KMR = P.sb("KMR", [128, 8, NT], BF16)
    PRD = Ring([P.sb("PRD%d" % i, [128, NT], BF16) for i in range(2)])
    CS = P.sb("CS", [128, 8, NB, 129], F32)
    GC = P.sb("GC", [128, NB, 8], F32)
    Ktok = P.sb("Ktok", [128, NB, D], BF16); Btok = P.sb("Btok", [128, NB, D], BF16); Vtok = P.sb("Vtok", [128, NB, D], BF16)
    gtok = P.sb("gtok", [128, NB, D], BF16)
    bcoef = P.sb("bcoef", [128, NB, 16], F32)
    ST = P.sb("ST", [128, 8, 64], F32); STb = P.sb("STb", [128, 8, 64], BF16)
    F1 = P.sb("F1", [128, D], F32); F2 = P.sb("F2", [128, D], F32)
    Yt, Y2 = F1, F2
    gns = P.sb("gns", [128, 80], F32)
    orwT = P.sb("orwT", [128, NB, 8, 128], BF16)
    orw = P.sb("orw", [128, D], BF16)
    q32 = F1
    SMAX = max(NSLOT, 8208)
    SC = P.sb("SC", [128, SMAX], F32)
    aoff = [0]

    def alias(key, shape, dt_):
        nbytes = int(np.prod(shape[1:])) * (2 if dt_ == BF16 else 4)
        a = Alias(SC, dt_, aoff[0], shape, key)
        aoff[0] += nbytes
        assert aoff[0] <= 4 * SMAX, aoff[0]
        return a
    amat = Ring([alias("amat%d" % i, [128, 4, 128], BF16) for i in range(4)])
    pq = [Ring([alias("pq%d_%d" % (h, i), [128, 2, 128], BF16) for i in range(2)]) for h in range(4)]
    Mt = [Ring([alias("Mt%d_%d" % (h, i), [128, 128], BF16) for i in range(2)]) for h in range(4)]
    Mfin = alias("Mfin", [128, 16, 128], BF16)
    AKs = alias("AKs", [128, 16, 128], BF16)
    ARB = alias("ARB", [128, 16, 128], BF16); ARK = alias("ARK", [128, 16, 128], BF16)
    Wb = alias("Wb", [128, 16, 64], BF16); Ub = alias("Ub", [128, 16, 64], BF16)
    sttmp = alias("sttmp", [128, 8, 64], F32)
    junk = P.sb("junk", [128, 2], BF16)
    tau = P.sb("tau", [128, 8], F32)
    dg = P.sb("dg", [128, 8, 128], BF16)
    kitl = Ring([P.sb("kitl%d" % i, [128, 512], BF16) for i in range(2)])
    kbt = Ring([P.sb("kbt%d" % i, [1, 512], BF16) for i in range(2)])
    ktl = Ring([P.sb("ktl%d" % i, [128, 512], BF16) for i in range(2)])
    vtl = Ring([P.sb("vtl%d" % i, [128, 4, 130], BF16) for i in range(2)])
    mbt = Ring([P.sb("mbt%d" % i, [128, 512], BF16) for i in range(2)])
    Rt = Ring([P.sb("Rt%d" % i, [128, 512], BF16) for i in range(2)])
    PTt = Ring([P.sb("PTt%d" % i, [128, 1024], BF16) for i in range(2)])
    rden = P.sb("rden", [128, 8], F32)
    attn = P.sb("attn", [128, D], BF16)
    attnT = P.sb("attnT", [128, NB, 8, 128], BF16)
    mixs = [attn]
    mixT = P.sb("mixT", [128, 8, NT], BF16)
    x1 = F1.reshape([128, 1, D])
    x1b = hn.reshape([128, D])
    x1T = P.sb("x1T", [128, 8, NT], BF16)
    actg = Ring([P.sb("actg%d" % i, [128, NT], BF16) for i in range(2)])
    actT = Ring([P.sb("actT%d" % i, [128, NT], BF16) for i in range(3)])
    yo = Ring([F2])

    V_(lambda e: e.memset(ST[:], 0.0), w=[ST])
    V_(lambda e: e.memset(STb[:], 0.0), w=[STb])
    V_(lambda e: e.memset(car[:], 0.0), w=[car])
    V_(lambda e: e.memset(CS[:], 0.0), w=[CS])
    for t in vtl.t:
        V_(lambda e, t=t: e.memset(t[:], 1.0), w=[t])

    def load_w(view, c0, ncol):
        t = wt.next()
        P.dma('sp', t[:, :, 0:ncol], view[:, :, c0:c0 + ncol], r=[view.tensor], w=[t])
        return t

    def ln_rows(x_ap, L, n, out_ap, eps, g_b=None):
        xreg, oreg = x_ap.tensor, out_ap.tensor
        sm = small.next()
        nch = (n + 511) // 512
        for c in range(nch):
            V_(lambda e, c=c: e.bn_stats(out=sm[:L, 6 * c:6 * c + 6], in_=x_ap[:, 512 * c:min(n, 512 * (c + 1))]), r=[xreg], w=[sm])
        V_(lambda e: e.bn_aggr(out=sm[:L, 12:14], in_=sm[:L, 0:6 * nch]), r=[sm], w=[sm])
        A_(lambda e: e.activation(out=sm[:L, 14:15], in_=sm[:L, 13:14], func=AF.Sqrt, bias=float(eps), scale=1.0), r=[sm], w=[sm])
        V_(lambda e: e.reciprocal(out=sm[:L, 15:16], in_=sm[:L, 14:15]), r=[sm], w=[sm])
        V_(lambda e: e.tensor_scalar(out=sm[:L, 16:17], in0=sm[:L, 12:13], scalar1=sm[:L, 15:16], scalar2=-1.0, op0=ALU.mult, op1=ALU.mult),
           r=[sm], w=[sm])
        A_(lambda e: e.activation(out=out_ap, in_=x_ap, func=AF.Identity, scale=sm[:L, 15:16], bias=sm[:L, 16:17]),
           r=[sm, xreg], w=[oreg])
        if g_b is not None:
            g, b = g_b
            G_(lambda e: e.tensor_tensor(out=out_ap, in0=out_ap, in1=g[:L, 0:n], op=ALU.mult), r=[oreg, g], w=[oreg])
            G_(lambda e: e.tensor_tensor(out=out_ap, in0=out_ap, in1=b[:L, 0:n], op=ALU.add), r=[oreg, b], w=[oreg])

    def rope_rows(t, L, c0, half, cos_ap, sin_ap, nh=1, stride=0):
        tab = cos_ap.tensor
        tm = rtmp.next()
        if nh == 1:
            x1_ = t[:L, c0:c0 + half]; x2_ = t[:L, c0 + half:c0 + 2 * half]
            a = tm[:L, 0:half]; b = tm[:L, half:2 * half]; c = tm[:L, 2 * half:3 * half]; d = tm[:L, 3 * half:4 * half]
            cs, sn = cos_ap, sin_ap
        else:
            v = t[:L, c0:c0 + nh * stride].rearrange("p (h d) -> p h d", h=nh)
            x1_ = v[:, :, 0:half]; x2_ = v[:, :, half:2 * half]
            tv = tm[:L, 0:4 * nh * half].rearrange("p (q h d) -> p q h d", q=4, h=nh)
            a, b, c, d = tv[:, 0], tv[:, 1], tv[:, 2], tv[:, 3]
            cs = cos_ap.unsqueeze(1).to_broadcast([L, nh, half]); sn = sin_ap.unsqueeze(1).to_broadcast([L, nh, half])
        rr = [t, tm, tab]
        V_(lambda e: e.tensor_tensor(out=a, in0=cs, in1=x1_, op=ALU.mult), r=rr, w=[tm])
        V_(lambda e: e.tensor_tensor(out=b, in0=sn, in1=x2_, op=ALU.mult), r=rr, w=[tm])
        V_(lambda e: e.tensor_tensor(out=c, in0=cs, in1=x2_, op=ALU.mult), r=rr, w=[tm])
        V_(lambda e: e.tensor_tensor(out=d, in0=sn, in1=x1_, op=ALU.mult), r=rr, w=[tm])
        V_(lambda e: e.tensor_tensor(out=x1_, in0=a, in1=b, op=ALU.subtract), r=[tm], w=[t])
        V_(lambda e: e.tensor_tensor(out=x2_, in0=c, in1=d, op=ALU.add), r=[tm], w=[t])

    evq = [0]

    def evac_copy(out_ap, in_ap, r, w):
        evq[0] += 1
        if evq[0] % 2:
            A_(lambda e: e.activation(out=out_ap, in_=in_ap, func=AF.Copy), r=r, w=w)
        else:
            V_(lambda e: e.tensor_copy(out=out_ap, in_=in_ap), r=r, w=w)

    def transpose_to(dst_fn, src_fn, n, L, r, w, ident=None):
        for k in range(n):
            T_(lambda e, k=k: e.transpose(out=ptr[:, k, 0:L], in_=src_fn(k), identity=identB[:L, :L]), r=r + [identB], w=[ptr])

    def front(x_dram, rows, Ls, rope_rows0, slots, outs, own):
        O_k_, O_v_, O_ki_, orow = outs
        xt = xin.next()
        L = Ls[0]
        ntk = sum(Ls)
        for b in range(len(Ls)):
            P.dma('sp', xt[:L, b, :], x_dram.ap()[rows[b]:rows[b] + L, :], r=[x_dram], w=[(xt, b)])
            P.dma('sp', rp[b][:L, :], rope.ap()[rope_rows0[b]:rope_rows0[b] + L, :], r=[rope], w=[rp[b]])
        for b in range(len(Ls)):
            if own:
                ln_rows(xt[:L, b, :], L, D, hres[:L, b, :], LN_EPS)
                G_(lambda e, b=b: e.tensor_copy(out=hn[:L, b, :], in_=hres[:L, b, :]), r=[(hres, b)], w=[(hn, b)])
                vg, vbb = getvec("ln0_g"), getvec("ln0_b")
                G_(lambda e, b=b, vg=vg: e.tensor_tensor(out=hres[:L, b, :], in0=hres[:L, b, :], in1=vg[:L, :], op=ALU.mult),
                   r=[(hres, b), vg], w=[(hres, b)])
                G_(lambda e, b=b, vbb=vbb: e.tensor_tensor(out=hres[:L, b, :], in0=hres[:L, b, :], in1=vbb[:L, :], op=ALU.add),
                   r=[(hres, b), vbb], w=[(hres, b)])
            else:
                ln_rows(xt[:L, b, :], L, D, hn[:L, b, :], LN_EPS)
            for k in range(8):
                T_(lambda e, b=b, k=k: e.transpose(out=ptr[:, k, 0:L], in_=hn[:L, b, 128 * k:128 * (k + 1)], identity=identB[:L, :L]),
                   r=[hn, identB], w=[ptr])
            for k in range(8):
                if k % 2:
                    A_(lambda e, b=b, k=k: e.activation(out=hT[:, k, L * b:L * b + L], in_=ptr[:, k, 0:L], func=AF.Identity,
                                                        scale=cols[:, O_G0 + k:O_G0 + k + 1], bias=cols[:, O_B0 + k:O_B0 + k + 1]),
                       r=[ptr, cols], w=[(hT, (b, k))])
                else:
                    V_(lambda e, b=b, k=k: e.tensor_scalar(out=hT[:, k, L * b:L * b + L], in0=ptr[:, k, 0:L],
                                                           scalar1=cols[:, O_G0 + k:O_G0 + k + 1], scalar2=cols[:, O_B0 + k:O_B0 + k + 1],
                                                           op0=ALU.mult, op1=ALU.add), r=[ptr, cols], w=[(hT, (b, k))])
        wk = wt.next()
        P.dma('sp', wk[:, :, 0:256], win_v[:, :, C_K:C_K + 256], r=[D_win], w=[wk])
        P.dma('sp', wk[:, :, 256:328], win_v[:, :, C_KI:C_KI + 72], r=[D_win], w=[wk])
        kt_t = ktt.next(); kit_t = kitt.next()
        for b in range(len(Ls)):
            pk = pm.next()
            for k in range(8):
                T_(lambda e, b=b, k=k, pk=pk: e.matmul(pk[:L, 0:328], lhsT=hT[:, k, L * b:L * b + L], rhs=wk[:, k, 0:328],
                                                    start=(k == 0), stop=(k == 7)), r=[hT, wk], w=[pk])
            kv = kvs[b]
            A_(lambda e, pk=pk, kv=kv: e.activation(out=kv[:L, :], in_=pk[:L, 0:328], func=AF.Copy), r=[pk], w=[kv])
            rt = rp[b]
            rope_rows(kv, L, 0, 16, rt[:L, 0:16], rt[:L, 16:32])
            k2 = ki2.next()
            ln_rows(kv[:L, 256:320], L, 64, k2[:L, :], LN_EPS, g_b=(ikg_b, ikb_b))
            rope_rows(k2, L, 0, 8, rt[:L, 32:40], rt[:L, 40:48])
            s0 = slots[b]; o0 = orow[b]
            P.dma('pool', O_k_.ap()[o0:o0 + L, :], kv[:L, 0:128], r=[kv], w=[(O_k_, o0)])
            P.dma('pool', O_v_.ap()[o0:o0 + L, :], kv[:L, 128:256], r=[kv], w=[(O_v_, o0)])
            P.dma('pool', O_ki_.ap()[o0:o0 + L, :], k2[:L, :], r=[k2], w=[(O_ki_, o0)])
            vb = vbt.next()
            G_(lambda e, kv=kv, vb=vb: e.tensor_copy(out=vb[:L, :], in_=kv[:L, 128:256]), r=[kv], w=[vb])
            P.dma('pool', D_v.ap()[s0:s0 + L, :], vb[:L, :], r=[vb], w=[(D_v, s0)])
            pt_ = pm.next()
            T_(lambda e, kv=kv, pt_=pt_: e.transpose(out=pt_[:, 0:L], in_=kv[:L, 0:128], identity=identF[:L, :L]), r=[kv, identF], w=[pt_])
            T_(lambda e, k2=k2, pt_=pt_: e.transpose(out=pt_[0:64, 128:128 + L], in_=k2[:L, 0:64], identity=identF[:L, :L]), r=[k2, identF], w=[pt_])
            V_(lambda e, b=b, kt_t=kt_t, pt_=pt_: e.tensor_copy(out=kt_t[:, L * b:L * b + L], in_=pt_[:, 0:L]), r=[pt_], w=[kt_t])
            V_(lambda e, b=b, kit_t=kit_t, pt_=pt_: e.tensor_copy(out=kit_t[:, L * b:L * b + L], in_=pt_[0:64, 128:128 + L]), r=[pt_], w=[kit_t])
            P.dma('pool', D_kt.ap()[:, s0:s0 + L], kt_t[:, L * b:L * b + L], r=[kt_t], w=[(D_kt, s0)])
            P.dma('pool', D_kit.ap()[:, s0:s0 + L], kit_t[:, L * b:L * b + L], r=[kit_t], w=[(D_kit, s0)])

    def own_proj(Ls, rope_rows0):
        L = Ls[0]
        nbk = len(Ls)
        wq = [load_w(win_v, C_Q + 512 * g, 512) for g in range(2)]
        for b in range(nbk):
            for g in range(2):
                pq_ = pm.next()
                for k in range(8):
                    T_(lambda e, b=b, g=g, k=k, pq_=pq_: e.matmul(pq_[:L, :], lhsT=hT[:, k, L * b:L * b + L], rhs=wq[g][:, k, :], start=(k == 0), stop=(k == 7)),
                       r=[hT, wq[g]], w=[pq_])
                evac_copy(q32[:L, 512 * g:512 * (g + 1)], pq_[:L, :], r=[pq_], w=[(q32, g)])
            rope_rows(q32, L, 0, 16, rp[b][:L, 0:16], rp[b][:L, 16:32], nh=8, stride=128)
            for h in range(8):
                T_(lambda e, h=h: e.transpose(out=pLT[:, 128 * h:128 * h + L], in_=q32[:L, 128 * h:128 * (h + 1)], identity=identF[:L, :L]),
                   r=[q32, identF], w=[(pLT, h // 4)])
            evac_copy(qT[:, b, 0:4, 0:L], pLT[:, 0:512].rearrange("p (h t) -> p h t", h=4)[:, :, 0:L], r=[(pLT, 0)], w=[(qT, b)])
            evac_copy(qT[:, b, 4:8, 0:L], pLT[:, 512:1024].rearrange("p (h t) -> p h t", h=4)[:, :, 0:L], r=[(pLT, 1)], w=[(qT, b)])
        wqi = load_w(win_v, C_QI, 512)
        for b in range(nbk):
            pq_ = pm.next()
            for k in range(8):
                T_(lambda e, b=b, k=k, pq_=pq_: e.matmul(pq_[:L, :], lhsT=hT[:, k, L * b:L * b + L], rhs=wqi[:, k, :], start=(k == 0), stop=(k == 7)),
                   r=[hT, wqi], w=[pq_])
            evac_copy(q32[:L, 0:512], pq_[:L, :], r=[pq_], w=[(q32, 0)])
            rope_rows(q32, L, 0, 8, rp[b][:L, 32:40], rp[b][:L, 40:48], nh=8, stride=64)
            pt_ = pm.next()
            for pp in range(4):
                T_(lambda e, pp=pp, pt_=pt_: e.transpose(out=pt_[:, 128 * pp:128 * pp + L], in_=q32[:L, 128 * pp:128 * (pp + 1)], identity=identF[:L, :L]),
                   r=[q32, identF], w=[pt_])
            evac_copy(qiT[:, b, :, 0:L], pt_[:, :].rearrange("p (h t) -> p h t", h=4)[:, :, 0:L], r=[pt_], w=[(qiT, b)])
        for g in range(4):
            wg_ = load_w(win_v, C_G + 512 * g, 512)
            for b in range(nbk):
                pq_ = pm.next()
                for k in range(8):
                    T_(lambda e, b=b, k=k, pq_=pq_, wg_=wg_: e.matmul(pq_[:L, :], lhsT=hT[:, k, L * b:L * b + L], rhs=wg_[:, k, :], start=(k == 0), stop=(k == 7)),
                       r=[hT, wg_], w=[pq_])
                A_(lambda e, b=b, g=g, pq_=pq_: e.activation(out=gsig[:L, b, 512 * g:512 * (g + 1)], in_=pq_[:L, :], func=AF.Sigmoid), r=[pq_], w=[(gsig, (b, g))])

    def rw_tile(c, wtile, wcol, Ls, out_fn, carry_aps, save_last):
        L = Ls[0]
        ntk = sum(Ls)
        ps_ = pm.next()
        for k in range(8):
            T_(lambda e, k=k: e.matmul(ps_[:, 0:ntk], lhsT=wtile[:, k, wcol:wcol + 128], rhs=hT[:, k, 0:ntk], start=(k == 0), stop=(k == 7)),
               r=[hT, wtile], w=[ps_])
        tmp = f32t.next(); xs = f32t.next()
        A_(lambda e: e.activation(out=tmp[:, 0:ntk], in_=ps_[:, 0:ntk], func=AF.Identity, scale=colsd[:, c:c + 1]), r=[ps_, colsd], w=[tmp])
        if ntk > 1:
            V_(lambda e: e.scalar_tensor_tensor(out=xs[:, 1:ntk], in0=ps_[:, 0:ntk - 1], scalar=cols[:, O_MU + c:O_MU + c + 1],
                                                in1=tmp[:, 1:ntk], op0=ALU.mult, op1=ALU.add), r=[ps_, cols, tmp], w=[xs])
        for b in range(len(Ls)):
            ca = carry_aps[b]
            if ca is None:
                continue
            cap, creg = ca
            V_(lambda e, b=b, cap=cap: e.scalar_tensor_tensor(out=xs[:, L * b:L * b + 1], in0=cap[:, c:c + 1], scalar=cols[:, O_MU + c:O_MU + c + 1],
                                                             in1=tmp[:, L * b:L * b + 1], op0=ALU.mult, op1=ALU.add), r=[creg, cols, tmp], w=[xs])
        for b in range(len(Ls)):
            sl = save_last[b]
            if sl is None:
                continue
            sap, sreg = sl
            V_(lambda e, b=b, sap=sap: e.tensor_copy(out=sap[:, c:c + 1], in_=ps_[:, L * b + L - 1:L * b + L]), r=[ps_], w=[sreg])
        out_fn(xs)

    def rwkv_prep(Ls, own, carry_aps, save_last):
        ntk = sum(Ls)
        nb = len(Ls)
        L = Ls[0]
        rwt = lambda c, wtile, wcol, fn: rw_tile(c, wtile, wcol, Ls, fn, carry_aps, save_last)
        wl = load_w(win_v, C_RW + 3072, 256)

        def f24(xs):
            A_(lambda e: e.activation(out=TL[0:64, 0:ntk], in_=xs[0:64, 0:ntk], func=AF.Tanh), r=[xs], w=[TL])
            V_(lambda e: e.tensor_copy(out=TL[64:128, 0:ntk], in_=xs[64:128, 0:ntk]), r=[xs], w=[TL])
        rwt(24, wl, 0, f24)
        if own:
            def f25(xs):
                A_(lambda e: e.activation(out=SLG[:, 0:ntk], in_=xs[:, 0:ntk], func=AF.Sigmoid), r=[xs], w=[SLG])
            rwt(25, wl, 128, f25)
            for b in range(nb):
                for g in range(2):
                    pg = pm.next()
                    T_(lambda e, b=b, g=g, pg=pg: e.matmul(pg[:L, :], lhsT=SLG[:, L * b:L * b + L], rhs=g2b[:, 512 * g:512 * (g + 1)], start=True, stop=True),
                       r=[SLG, g2b], w=[pg])
                    evac_copy(gtok[:L, b, 512 * g:512 * (g + 1)], pg[:L, :], r=[pg], w=[(gtok, (b, g))])

        def blkview(t):
            return t[:, 0:ntk].rearrange("p (b l) -> p b l", b=nb)

        for half in range(2):
            wk_ = load_w(win_v, C_RW + 1024 + 512 * half, 512)
            for pp in range(4):
                p = 4 * half + pp

                def fk(kraw, p=p):
                    ps1 = pm.next()
                    T_(lambda e: e.matmul(ps1[:, 0:ntk], lhsT=w2b[0:64, 128 * p:128 * (p + 1)], rhs=TL[0:64, 0:ntk], start=True, stop=True),
                       r=[w2b, TL], w=[ps1])
                    sg = f32t.next()
                    A_(lambda e: e.activation(out=sg[:, 0:ntk], in_=ps1[:, 0:ntk], func=AF.Sigmoid, bias=cols[:, O_W0 + p:O_W0 + p + 1], scale=1.0),
                       r=[ps1, cols], w=[sg])
                    for b in range(nb):
                        V_(lambda e, b=b: e.tensor_tensor_scan(out=CS[:, p, b, 1:1 + L], data0=ones128[:, 0:L], data1=sg[:, L * b:L * (b + 1)],
                                                               initial=0.0, op0=ALU.mult, op1=ALU.add), r=[ones128, sg], w=[(CS, p)])
                    ps2 = pm.next()
                    T_(lambda e: e.matmul(ps2[:, 0:ntk], lhsT=a2b[64:128, 128 * p:128 * (p + 1)], rhs=TL[64:128, 0:ntk], start=True, stop=True),
                       r=[a2b, TL], w=[ps2])
                    asig = f32t.next()
                    A_(lambda e: e.activation(out=asig[:, 0:ntk], in_=ps2[:, 0:ntk], func=AF.Sigmoid, bias=cols[:, O_A0 + p:O_A0 + p + 1], scale=1.0),
                       r=[ps2, cols], w=[asig])
                    eex = f32t.next(); em = f32t.next()
                    A_(lambda e: e.activation(out=blkview(eex), in_=CS[:, p, 0:nb, 0:L], func=AF.Exp, scale=-DEC), r=[(CS, p)], w=[eex])
                    A_(lambda e: e.activation(out=blkview(em), in_=CS[:, p, 0:nb, 1:1 + L], func=AF.Exp, scale=DEC), r=[(CS, p)], w=[em])
                    A_(lambda e: e.activation(out=GC[:, 0:nb, p], in_=CS[:, p, 0:nb, L], func=AF.Exp, scale=-DEC), r=[(CS, p)], w=[(GC, p)])
                    kkr = f32t.next(); sq = f32t.next()
                    V_(lambda e: e.tensor_scalar(out=kkr[:, 0:ntk], in0=kraw[:, 0:ntk], scalar1=cols[:, O_KK + p:O_KK + p + 1], scalar2=None, op0=ALU.mult),
                       r=[kraw, cols], w=[kkr])
                    G_(lambda e: e.tensor_tensor(out=sq[:, 0:ntk], in0=kkr[:, 0:ntk], in1=kkr[:, 0:ntk], op=ALU.mult), r=[kkr], w=[sq])
                    ps3 = pm.next()
                    T_(lambda e: e.matmul(ps3[:, 0:ntk], lhsT=blkones[:], rhs=sq[:, 0:ntk], start=True, stop=True), r=[blkones, sq], w=[ps3])
                    A_(lambda e: e.activation(out=sq[:, 0:ntk], in_=ps3[:, 0:ntk], func=AF.Sqrt), r=[ps3], w=[sq])
                    V_(lambda e: e.tensor_scalar(out=sq[:, 0:ntk], in0=sq[:, 0:ntk], scalar1=1e-12, scalar2=None, op0=ALU.max), r=[sq], w=[sq])
                    V_(lambda e: e.reciprocal(out=sq[:, 0:ntk], in_=sq[:, 0:ntk]), r=[sq], w=[sq])
                    V_(lambda e: e.tensor_tensor(out=kkr[:, 0:ntk], in0=kkr[:, 0:ntk], in1=sq[:, 0:ntk], op=ALU.mult), r=[kkr, sq], w=[kkr])
                    V_(lambda e: e.tensor_scalar(out=sq[:, 0:ntk], in0=asig[:, 0:ntk], scalar1=cols[:, O_KA + p:O_KA + p + 1],
                                                 scalar2=colsd[:, 26 + p:27 + p], op0=ALU.mult, op1=ALU.add), r=[asig, cols, colsd], w=[sq])
                    G_(lambda e: e.tensor_tensor(out=sq[:, 0:ntk], in0=sq[:, 0:ntk], in1=kraw[:, 0:ntk], op=ALU.mult), r=[sq, kraw], w=[sq])
                    G_(lambda e: e.tensor_tensor(out=asig[:, 0:ntk], in0=asig[:, 0:ntk], in1=kkr[:, 0:ntk], op=ALU.mult), r=[asig, kkr], w=[asig])
                    V_(lambda e: e.scalar_tensor_tensor(out=AT[:, p, 0:ntk], in0=kkr[:, 0:ntk], scalar=-1.0, in1=eex[:, 0:ntk], op0=ALU.mult, op1=ALU.mult),
                       r=[kkr, eex], w=[(AT, p)])
                    V_(lambda e: e.tensor_tensor(out=KTt[:, p, 0:ntk], in0=sq[:, 0:ntk], in1=em[:, 0:ntk], op=ALU.mult), r=[sq, em], w=[(KTt, p)])
                    G_(lambda e: e.tensor_tensor(out=BTt[:, p, 0:ntk], in0=asig[:, 0:ntk], in1=em[:, 0:ntk], op=ALU.mult), r=[asig, em], w=[(BTt, p)])
                    if own:
                        G_(lambda e: e.tensor_scalar(out=KMR[:, p, 0:ntk], in0=sq[:, 0:ntk], scalar1=cols[:, O_RK + p:O_RK + p + 1], scalar2=None, op0=ALU.mult),
                           r=[sq, cols], w=[(KMR, p)])
                rwt(8 + p, wk_, 128 * pp, fk)
        for half in range(2):
            wv_ = load_w(win_v, C_RW + 2048 + 512 * half, 512)
            for pp in range(4):
                p = 4 * half + pp

                def fv(xs, p=p):
                    A_(lambda e: e.activation(out=VTf[:, p, 0:ntk], in_=xs[:, 0:ntk], func=AF.Copy), r=[xs], w=[(VTf, p)])
                rwt(16 + p, wv_, 128 * pp, fv)
        if own:
            pbc = pO[:, 1024:1536]
            pbreg = (pO, 2)
            first = [True]
            for half in range(2):
                wr_ = load_w(win_v, C_RW + 512 * half, 512)
                for pp in range(4):
                    p = 4 * half + pp

                    def fr(xs, p=p):
                        ep = f32t.next()
                        A_(lambda e: e.activation(out=blkview(ep), in_=CS[:, p, 0:nb, 1:1 + L], func=AF.Exp, scale=-DEC), r=[(CS, p)], w=[ep])
                        V_(lambda e: e.tensor_tensor(out=RTt[:, p, 0:ntk], in0=xs[:, 0:ntk], in1=ep[:, 0:ntk], op=ALU.mult), r=[xs, ep], w=[(RTt, p)])
                        prd = PRD.next()
                        G_(lambda e: e.tensor_tensor(out=prd[:, 0:ntk], in0=xs[:, 0:ntk], in1=KMR[:, p, 0:ntk], op=ALU.mult), r=[xs, (KMR, p)], w=[prd])
                        for b in range(nb):
                            T_(lambda e, b=b: e.matmul(pbc[:L, 16 * b + 2 * p:16 * b + 2 * p + 2], lhsT=prd[:, L * b:L * b + L], rhs=blkonesB[:, 0:128:64],
                                                       start=first[0] and b == 0, stop=True, skip_group_check=True), r=[prd, blkonesB], w=[pbreg])
                        first[0] = False
                    rwt(p, wr_, 128 * pp, fr)
            first[0] = True
            V_(lambda e: e.tensor_copy(out=bcoef[:L, 0:nb, :], in_=pbc[:L, 0:16 * nb].rearrange("p (b h) -> p b h", b=nb)), r=[pbreg], w=[bcoef])
        for b in range(nb):
            for (src, dst) in ((KTt, Ktok), (BTt, Btok), (VTf, Vtok)):
                for p in range(8):
                    T_(lambda e, p=p, b=b, src=src: e.transpose(out=ptr[0:L, p, :], in_=src[:, p, L * b:L * (b + 1)], identity=identB[:]),
                       r=[(src, p), identB], w=[ptr])
                evac_copy(dst[0:L, b, :].rearrange("p (k j) -> p k j", k=8), ptr[0:L, :, :], r=[ptr], w=[(dst, b)])

    def chunk_scan(b, L, own):
        nl = max(int(np.ceil(np.log2(L))) - 1, 0)
        c0 = L * b
        Lk = 128 if L == 64 else L
        if L == 64:
            for t_ in (AKs, ARB, ARK):
                G_(lambda e, t_=t_: e.memset(t_[64:128, :, :], 0.0), w=[t_])
            G_(lambda e: e.memset(Vtok[64:128, b, :], 0.0), w=[(Vtok, b)])
            G_(lambda e: e.memset(Ub[64:128, :, :], 0.0), w=[Ub])
        for hg in range(4):
            heads = [4 * hg + i for i in range(4)]
            cur = {}
            for i, h in enumerate(heads):
                p, bp = h // 2, 64 * (h % 2)
                bank = cb_ap(i); breg = cbanks[i]
                at = AT[bp:bp + 64, p, c0:c0 + L]; bt = BTt[bp:bp + 64, p, c0:c0 + L]; kt = KTt[bp:bp + 64, p, c0:c0 + L]
                T_(lambda e, bank=bank, bt=bt, at=at: e.matmul(bank[0:L, 0:L], lhsT=bt, rhs=at, start=True, stop=True), r=[(AT, p), (BTt, p)], w=[breg])
                T_(lambda e, bank=bank, bt=bt, at=at: e.matmul(bank[0:L, 128:128 + L], lhsT=at, rhs=bt, start=False, stop=True, skip_group_check=True),
                   r=[(AT, p), (BTt, p)], w=[breg])
                T_(lambda e, bank=bank, kt=kt, at=at: e.matmul(bank[0:L, 256:256 + L], lhsT=kt, rhs=at, start=False, stop=True, skip_group_check=True),
                   r=[(AT, p), (KTt, p)], w=[breg])
                nm_ = 3
                if own:
                    rt_ = RTt[bp:bp + 64, p, c0:c0 + L]
                    T_(lambda e, bank=bank, bt=bt, rt_=rt_: e.matmul(bank[0:L, 384:384 + L], lhsT=bt, rhs=rt_, start=False, stop=True, skip_group_check=True),
                       r=[(RTt, p), (BTt, p)], w=[breg])
                    nm_ = 4
                am = amat.next()
                V_(lambda e, bank=bank, am=am, nm_=nm_: e.tensor_tensor(out=am[0:L, 0:nm_, 0:L], in0=bank[0:L, 0:128 * nm_].rearrange("p (m t) -> p m t", m=nm_)[:, :, 0:L],
                                                                       in1=mask[0:L, 0:128 * nm_].rearrange("p (m t) -> p m t", m=nm_)[:, :, 0:L], op=ALU.mult),
                   r=[breg, mask], w=[am])
                G_(lambda e, am=am, h=h: e.tensor_copy(out=AKs[0:L, h, 0:L], in_=am[0:L, 2, 0:L]), r=[am], w=[(AKs, h)])
                if own:
                    G_(lambda e, am=am, h=h: e.tensor_copy(out=ARB[0:L, h, 0:L], in_=am[0:L, 3, 0:L]), r=[am], w=[(ARB, h)])
                m0 = Mt[i].next()
                G_(lambda e, am=am, m0=m0: e.tensor_tensor(out=m0[0:L, 0:L], in0=am[0:L, 0, 0:L], in1=identB[0:L, 0:L], op=ALU.add), r=[am, identB], w=[m0])
                cur[h] = dict(P=am[0:L, 0, 0:L], Q=am[0:L, 1, 0:L], Pt=am, Qt=am, M=m0)
            for lev in range(nl):
                last = (lev == nl - 1)
                for i, h in enumerate(heads):
                    bank = cb_ap(i); breg = cbanks[i]
                    c = cur[h]
                    if not last:
                        T_(lambda e, bank=bank, cq=c['Q'], cp=c['P']: e.matmul(bank[0:L, 0:L], lhsT=cq, rhs=cp, start=True, stop=True), r=[c['Pt'], c['Qt']], w=[breg])
                        T_(lambda e, bank=bank, cq=c['Q'], cp=c['P']: e.matmul(bank[0:L, 128:128 + L], lhsT=cp, rhs=cq, start=False, stop=True, skip_group_check=True),
                           r=[c['Pt'], c['Qt']], w=[breg])
                    else:
                        T_(lambda e, bank=bank, cq=c['Q'], cp=c['P']: e.matmul(bank[0:L, 128:128 + L], lhsT=cp, rhs=cq, start=True, stop=True),
                           r=[c['Pt'], c['Qt']], w=[breg])
                    nq = pq[i].next()
                    lo = 1 if last else 0
                    A_(lambda e, bank=bank, nq=nq, lo=lo: e.activation(out=nq[0:L, lo:2, 0:L],
                                                                      in_=bank[0:L, 128 * lo:256].rearrange("p (m t) -> p m t", m=2 - lo)[:, :, 0:L], func=AF.Copy),
                       r=[breg], w=[nq])
                    c['P'] = nq[0:L, 0, 0:L]; c['Q'] = nq[0:L, 1, 0:L]; c['Pt'] = nq; c['Qt'] = nq
                for i, h in enumerate(heads):
                    bank = cb_ap(i); breg = cbanks[i]
                    c = cur[h]
                    pmb = pm.next()
                    T_(lambda e, pmb=pmb, cq=c['Q'], cm=c['M']: e.matmul(pmb[0:L, 0:L], lhsT=cq, rhs=cm[0:L, 0:L], start=True, stop=True),
                       r=[c['Qt'], c['M']], w=[pmb])
                    mn = Mt[i].next()
                    V_(lambda e, pmb=pmb, cm=c['M'], mn=mn: e.tensor_tensor(out=mn[0:L, 0:L], in0=pmb[0:L, 0:L], in1=cm[0:L, 0:L], op=ALU.add),
                       r=[pmb, c['M']], w=[mn])
                    c['M'] = mn
            for i, h in enumerate(heads):
                G_(lambda e, h=h, m=cur[h]['M']: e.tensor_copy(out=Mfin[0:L, h, 0:L], in_=m[0:L, 0:L]), r=[cur[h]['M']], w=[(Mfin, h)])
        for g in range(2):
            bank = cb_ap(g); breg = cbanks[g]
            for hh in range(8):
                h = 8 * g + hh
                p, bp = h // 2, 64 * (h % 2)
                T_(lambda e, bank=bank, hh=hh, p=p, bp=bp: e.matmul(bank[0:L, 64 * hh:64 * hh + 64], lhsT=AT[bp:bp + 64, p, c0:c0 + L], rhs=STb[bp:bp + 64, p, :],
                                                                   start=(hh == 0), stop=False, skip_group_check=True), r=[(AT, p), STb], w=[breg])
                T_(lambda e, bank=bank, hh=hh, h=h: e.matmul(bank[0:L, 64 * hh:64 * hh + 64], lhsT=AKs[0:Lk, h, 0:L], rhs=Vtok[0:Lk, b, 64 * h:64 * h + 64],
                                                            start=False, stop=True, skip_group_check=True), r=[(AKs, h), (Vtok, b)], w=[breg])
            evac_copy(Wb[0:L, 8 * g:8 * g + 8, :], bank[0:L, 0:512].rearrange("p (h i) -> p h i", h=8), r=[breg], w=[(Wb, g)])
        for g in range(2):
            bank = cb_ap(2 + g); breg = cbanks[2 + g]
            for hh in range(8):
                h = 8 * g + hh
                T_(lambda e, bank=bank, hh=hh, h=h: e.matmul(bank[0:L, 64 * hh:64 * hh + 64], lhsT=Mfin[0:L, h, 0:L], rhs=Wb[0:L, h, :],
                                                            start=(hh == 0), stop=True, skip_group_check=True), r=[(Mfin, h), (Wb, g)], w=[breg])
            evac_copy(Ub[0:L, 8 * g:8 * g + 8, :], bank[0:L, 0:512].rearrange("p (h i) -> p h i", h=8), r=[breg], w=[(Ub, g)])
        if own:
            for rnd in range(2):
                for par in range(2):
                    bank = cb_ap(par); breg = cbanks[par]
                    for j in range(4):
                        h = 8 * rnd + 2 * j + par
                        p, bp = h // 2, 64 * par
                        T_(lambda e, bank=bank, j=j, p=p, bp=bp: e.matmul(bank[0:L, 128 * j:128 * j + L], lhsT=KTt[bp:bp + 64, p, c0:c0 + L], rhs=RTt[bp:bp + 64, p, c0:c0 + L],
                                                                         start=(j == 0), stop=True, skip_group_check=True), r=[(KTt, p), (RTt, p)], w=[breg])
                for par in range(2):
                    bank = cb_ap(par); breg = cbanks[par]
                    for j in range(4):
                        h = 8 * rnd + 2 * j + par
                        V_(lambda e, bank=bank, j=j, h=h: e.tensor_tensor(out=ARK[0:L, h, 0:L], in0=bank[0:L, 128 * j:128 * j + L], in1=mask[0:L, 384:384 + L], op=ALU.mult),
                           r=[breg, mask], w=[(ARK, h)])
            for g in range(2):
                bank = cb_ap(2 + g); breg = cbanks[2 + g]
                for hh in range(8):
                    h = 8 * g + hh
                    p, bp = h // 2, 64 * (h % 2)
                    oc = bank[0:L, 64 * hh:64 * hh + 64]
                    T_(lambda e, oc=oc, hh=hh, p=p, bp=bp: e.matmul(oc, lhsT=RTt[bp:bp + 64, p, c0:c0 + L], rhs=STb[bp:bp + 64, p, :],
                                                                   start=(hh == 0), stop=False, skip_group_check=True), r=[(RTt, p), STb], w=[breg])
                    T_(lambda e, oc=oc, h=h: e.matmul(oc, lhsT=ARK[0:Lk, h, 0:L], rhs=Vtok[0:Lk, b, 64 * h:64 * h + 64], start=False, stop=False, skip_group_check=True),
                       r=[(ARK, h), (Vtok, b)], w=[breg])
                    T_(lambda e, oc=oc, h=h: e.matmul(oc, lhsT=ARB[0:Lk, h, 0:L], rhs=Ub[0:Lk, h, :], start=False, stop=True, skip_group_check=True),
                       r=[(ARB, h), (Ub, h // 8)], w=[breg])
                evac_copy(Yt[0:L, 512 * g:512 * (g + 1)], bank[0:L, 0:512], r=[breg], w=[(Yt, g)])
        bank = cb_ap(0); breg = cbanks[0]
        for h in range(16):
            p, bp = h // 2, 64 * (h % 2)
            T_(lambda e, h=h, p=p, bp=bp: e.matmul(bank[bp:bp + 64, 64 * p:64 * p + 64], lhsT=Ktok[0:L, b, 64 * h:64 * h + 64], rhs=Vtok[0:L, b, 64 * h:64 * h + 64],
                                                  start=(h < 2), stop=False, skip_group_check=True), r=[(Ktok, b), (Vtok, b)], w=[breg])
            T_(lambda e, h=h, p=p, bp=bp: e.matmul(bank[bp:bp + 64, 64 * p:64 * p + 64], lhsT=Btok[0:L, b, 64 * h:64 * h + 64], rhs=Ub[0:L, h, :],
                                                  start=False, stop=True, skip_group_check=True), r=[(Btok, b), (Ub, h // 8)], w=[breg])
        V_(lambda e: e.tensor_tensor(out=sttmp[:].rearrange("p k i -> p (k i)"), in0=bank[:, 0:512], in1=ST[:].rearrange("p k i -> p (k i)"), op=ALU.add),
           r=[breg, ST], w=[sttmp])
        V_(lambda e: e.tensor_tensor(out=ST[:], in0=GC[:, b, :].unsqueeze(2).to_broadcast([128, 8, 64]), in1=sttmp[:], op=ALU.mult),
           r=[sttmp, GC], w=[ST])
        A_(lambda e: e.activation(out=STb[:], in_=ST[:], func=AF.Copy), r=[ST], w=[STb])
        if own:
            rwkv_out(b, L)

    def rwkv_out(b, L):
        y3 = Yt[0:L, :].rearrange("p (h i) -> p h i", h=16)
        y23 = Y2[0:L, :].rearrange("p (h i) -> p h i", h=16)
        V_(lambda e: e.tensor_reduce(out=gns[0:L, 0:16], in_=y3, axis=mybir.AxisListType.X, op=ALU.add), r=[Yt], w=[gns])
        A_(lambda e: e.activation(out=Y2[0:L, :], in_=Yt[0:L, :], func=AF.Square), r=[Yt], w=[Y2])
        V_(lambda e: e.tensor_reduce(out=gns[0:L, 16:32], in_=y23, axis=mybir.AxisListType.X, op=ALU.add), r=[Y2], w=[gns])
        V_(lambda e: e.tensor_scalar(out=gns[0:L, 0:16], in0=gns[0:L, 0:16], scalar1=1.0 / 64, scalar2=None, op0=ALU.mult), r=[gns], w=[gns])
        V_(lambda e: e.tensor_tensor(out=gns[0:L, 32:48], in0=gns[0:L, 0:16], in1=gns[0:L, 0:16], op=ALU.mult), r=[gns], w=[gns])
        V_(lambda e: e.scalar_tensor_tensor(out=gns[0:L, 48:64], in0=gns[0:L, 16:32], scalar=1.0 / 64, in1=gns[0:L, 32:48], op0=ALU.mult, op1=ALU.subtract),
           r=[gns], w=[gns])
        A_(lambda e: e.activation(out=gns[0:L, 48:64], in_=gns[0:L, 48:64], func=AF.Sqrt, bias=float(GN_EPS), scale=1.0), r=[gns], w=[gns])
        V_(lambda e: e.reciprocal(out=gns[0:L, 48:64], in_=gns[0:L, 48:64]), r=[gns], w=[gns])
        V_(lambda e: e.tensor_scalar(out=gns[0:L, 64:80], in0=gns[0:L, 48:64], scalar1=-1.0, scalar2=None, op0=ALU.mult), r=[gns], w=[gns])
        V_(lambda e: e.tensor_tensor(out=y23, in0=gns[0:L, 0:16].unsqueeze(2).to_broadcast([L, 16, 64]), in1=y3, op=ALU.subtract), r=[gns, Yt], w=[Y2])
        V_(lambda e: e.tensor_tensor(out=y23, in0=gns[0:L, 64:80].unsqueeze(2).to_broadcast([L, 16, 64]), in1=y23, op=ALU.mult), r=[gns, Y2], w=[Y2])
        vg, vbb = getvec("gn_g"), getvec("gn_b")
        G_(lambda e: e.tensor_tensor(out=Y2[0:L, :], in0=Y2[0:L, :], in1=vg[0:L, :], op=ALU.mult), r=[Y2, vg], w=[Y2])
        G_(lambda e: e.tensor_tensor(out=Y2[0:L, :], in0=Y2[0:L, :], in1=vbb[0:L, :], op=ALU.add), r=[Y2, vbb], w=[Y2])
        V_(lambda e: e.tensor_tensor(out=y3, in0=bcoef[0:L, b, :].unsqueeze(2).to_broadcast([L, 16, 64]), in1=Vtok[0:L, b, :].rearrange("p (h i) -> p h i", h=16), op=ALU.mult),
           r=[bcoef, (Vtok, b)], w=[Yt])
        G_(lambda e: e.tensor_tensor(out=Y2[0:L, :], in0=Y2[0:L, :], in1=Yt[0:L, :], op=ALU.add), r=[Y2, Yt], w=[Y2])
        G_(lambda e: e.tensor_tensor(out=orw[0:L, :], in0=Y2[0:L, :], in1=gtok[0:L, b, :], op=ALU.mult), r=[Y2, gtok], w=[orw])
        for k in range(8):
            T_(lambda e, k=k: e.transpose(out=ptr[:, k, 0:L], in_=orw[0:L, 128 * k:128 * (k + 1)], identity=identB[:L, :L]), r=[orw, identB], w=[ptr])
        evac_copy(orwT[:, b, :, 0:L], ptr[:, :, 0:L], r=[ptr], w=[(orwT, b)])

    def attention(b, Lq, slot_lo, slot_hi, corner):
        S = slot_hi - slot_lo
        wi = kvs[b]
        for h in range(8):
            V_(lambda e, h=h: e.tensor_scalar(out=dg[0:Lq, h, 0:Lq], in0=identB[0:Lq, 0:Lq], scalar1=wi[0:Lq, 320 + h:321 + h], scalar2=None, op0=ALU.mult),
               r=[identB, wi], w=[dg])
        cidx = 0.125 * (8.0 ** -0.5)
        ntile = (S + 511) // 512
        for ti in range(ntile):
            s0 = slot_lo + 512 * ti
            n = min(512, slot_hi - s0)
            kit = kitl.next(); kb_ = kbt.next()
            P.dma('sp', kit[0:64, 0:n], D_kit.ap()[:, s0:s0 + n], r=[D_kit], w=[kit])
            P.dma('sp', kit[64:128, 0:n], D_kit.ap()[:, s0:s0 + n], r=[D_kit], w=[kit])
            P.dma('sp', kb_[:, 0:n], keyb.ap()[:, s0:s0 + n], r=[keyb], w=[kb_])
            psc = pm.next()
            T_(lambda e, psc=psc, kb_=kb_, n=n: e.matmul(psc[0:Lq, 0:n], lhsT=onesrow[0:1, 0:Lq], rhs=kb_[0:1, 0:n], start=True, stop=False), r=[onesrow, kb_], w=[psc])
            def emit_x(h, kit=kit, n=n):
                pp, bp = h // 2, 64 * (h % 2)
                xb_ = xbanks[xq[0] % 3]; xq[0] += 1
                px = xb_[0][:, 512 * xb_[1]:512 * (xb_[1] + 1)]
                T_(lambda e, px=px, pp=pp, bp=bp: e.matmul(px[0:Lq, 0:n], lhsT=qiT[bp:bp + 64, b, pp, 0:Lq], rhs=kit[bp:bp + 64, 0:n], start=True, stop=True),
                   r=[(qiT, b), kit], w=[xb_])
                return px, xb_
            nxt = emit_x(0)
            for h in range(8):
                px, xb_ = nxt
                if h < 7:
                    nxt = emit_x(h + 1)
                r_ = Rt.next()
                A_(lambda e, px=px, r_=r_, n=n: e.activation(out=r_[0:Lq, 0:n], in_=px[0:Lq, 0:n], func=AF.Relu, scale=cidx), r=[xb_], w=[r_])
                T_(lambda e, psc=psc, h=h, r_=r_, n=n: e.matmul(psc[0:Lq, 0:n], lhsT=dg[0:Lq, h, 0:Lq], rhs=r_[0:Lq, 0:n], start=False, stop=(h == 7)), r=[dg, r_], w=[psc])
            V_(lambda e, psc=psc, ti=ti, n=n: e.tensor_copy(out=SC[0:Lq, 512 * ti:512 * ti + n], in_=psc[0:Lq, 0:n]), r=[psc], w=[(SC, ti)])
        if corner:
            V_(lambda e: e.memset(SC[0:64, S - 64:S], -1e30), r=[], w=[SC])
        V_(lambda e: e.memset(tau[0:Lq, 0:1], 0.0), w=[tau])
        for it in range(NBIS):
            s_ = 16.0 * (0.5 ** (it + 1))
            V_(lambda e: e.tensor_scalar(out=junk[0:Lq, 0:1].to_broadcast([Lq, S]), in0=SC[0:Lq, 0:S], scalar1=tau[0:Lq, 0:1], scalar2=None,
                                         op0=ALU.is_ge, op1=ALU.add, accum_out=tau[0:Lq, 1:2]), r=[SC, tau], w=[junk, tau])
            V_(lambda e, s_=s_: e.tensor_scalar(out=tau[0:Lq, 2:3], in0=tau[0:Lq, 1:2], scalar1=TOPK - 0.5, scalar2=2.0 * s_, op0=ALU.is_ge, op1=ALU.mult), r=[tau], w=[tau])
            V_(lambda e, s_=s_: e.scalar_tensor_tensor(out=tau[0:Lq, 0:1], in0=tau[0:Lq, 2:3], scalar=-s_, in1=tau[0:Lq, 0:1], op0=ALU.add, op1=ALU.add), r=[tau], w=[tau])
        s_last = 16.0 * (0.5 ** NBIS)
        V_(lambda e: e.tensor_scalar(out=tau[0:Lq, 3:4], in0=tau[0:Lq, 0:1], scalar1=-s_last, scalar2=None, op0=ALU.add), r=[tau], w=[tau])
        blocks = []
        for ti in range(ntile):
            s0 = slot_lo + 512 * ti
            n = min(512, slot_hi - s0)
            for a in range((n + 127) // 128):
                blocks.append((ti, s0, n, a, min(128, n - 128 * a)))
        tiles = {}

        def tile_res(ti, s0, n):
            if ti in tiles:
                return tiles[ti]
            kt_ = ktl.next(); vt_ = vtl.next(); mb = mbt.next()
            P.dma('sp', kt_[:, 0:n], D_kt.ap()[:, s0:s0 + n], r=[D_kt], w=[kt_])
            nfull, rem = n // 128, n % 128
            if nfull:
                P.dma('sp', vt_[:, 0:nfull, 0:128], D_v.ap()[s0:s0 + 128 * nfull, :].rearrange("(a p) d -> p a d", p=128), r=[D_v], w=[vt_])
            if rem:
                P.dma('sp', vt_[0:rem, nfull, 0:128], D_v.ap()[s0 + 128 * nfull:s0 + n, :], r=[D_v], w=[vt_])
            V_(lambda e: e.tensor_scalar(out=mb[0:Lq, 0:n], in0=SC[0:Lq, 512 * ti:512 * ti + n], scalar1=tau[0:Lq, 3:4], scalar2=-30000.0,
                                         op0=ALU.is_lt, op1=ALU.mult), r=[(SC, ti), tau], w=[mb])
            tiles[ti] = (kt_, vt_, mb)
            return tiles[ti]

        def buf(j, hh):
            if j % 2 == 0:
                return pLT[:, 512 * hh:512 * hh + 512], (pLT, hh)
            return pm.t[hh][:, :], pm.t[hh]

        def emit_LT(j):
            ti, s0, n, a, ns = blocks[j]
            kt_, vt_, mb = tile_res(ti, s0, n)
            for hh in range(2):
                ap_, reg = buf(j, hh)
                T_(lambda e, ap_=ap_, hh=hh: e.matmul(ap_[0:ns, 0:4 * Lq], lhsT=kt_[:, 128 * a:128 * a + ns], rhs=qT[:, b, 4 * hh:4 * hh + 4, 0:Lq], start=True, stop=False),
                   r=[kt_, (qT, b)], w=[reg])
                T_(lambda e, ap_=ap_: e.matmul(ap_[0:ns, 0:4 * Lq], lhsT=mb[0:Lq, 128 * a:128 * a + ns], rhs=I4[0:Lq, :, 0:Lq], start=False, stop=True),
                   r=[mb, I4], w=[reg])
        emit_LT(0)
        for j in range(len(blocks)):
            if j + 1 < len(blocks):
                emit_LT(j + 1)
            ti, s0, n, a, ns = blocks[j]
            kt_, vt_, mb = tiles[ti]
            pt = PTt.next()
            for hh in range(2):
                ap_, reg = buf(j, hh)
                A_(lambda e, ap_=ap_, hh=hh, pt=pt, ns=ns: e.activation(out=pt[0:ns, 4 * Lq * hh:4 * Lq * (hh + 1)], in_=ap_[0:ns, 0:4 * Lq], func=AF.Exp, scale=128.0 ** -0.5),
                   r=[reg], w=[(pt, hh)])
            last = (j == len(blocks) - 1)
            for h in range(8):
                off = 512 * (h // 3) + 129 * (h % 3)
                T_(lambda e, h=h, off=off, ns=ns, a=a, pt=pt, vt_=vt_, st_=(j == 0 and h % 3 == 0), last=last: e.matmul(
                    pO[0:Lq, off:off + 129], lhsT=pt[0:ns, Lq * h:Lq * h + Lq], rhs=vt_[0:ns, a, 0:129], start=st_, stop=last, skip_group_check=True),
                   r=[pt, vt_], w=[(pO, h // 3)])
        for h in range(8):
            off = 512 * (h // 3) + 129 * (h % 3)
            V_(lambda e, h=h, off=off: e.reciprocal(out=rden[0:Lq, h:h + 1], in_=pO[0:Lq, off + 128:off + 129]), r=[(pO, h // 3)], w=[rden])
            V_(lambda e, h=h, off=off: e.tensor_scalar(out=attn[0:Lq, 128 * h:128 * h + 128], in0=pO[0:Lq, off:off + 128], scalar1=rden[0:Lq, h:h + 1], scalar2=None, op0=ALU.mult),
               r=[(pO, h // 3), rden], w=[attn])
        for k in range(8):
            T_(lambda e, k=k: e.transpose(out=ptr[:, k, 0:Lq], in_=attn[0:Lq, 128 * k:128 * (k + 1)], identity=identB[:Lq, :Lq]), r=[attn, identB], w=[ptr])
        evac_copy(attnT[:, b, :, 0:Lq], ptr[:, :, 0:Lq], r=[ptr], w=[(attnT, b)])


    def post(Ls, y_dram, yrows):
        L = Ls[0]
        nbk = len(Ls)
        ntk = sum(Ls)
        woa = [load_w(kview(D_woa), 512 * g, 512) for g in range(2)]
        for b in range(nbk):
            for g in range(2):
                po_ = pm.next()
                for k in range(8):
                    T_(lambda e, b=b, g=g, k=k, po_=po_: e.matmul(po_[:L, :], lhsT=attnT[:, b, k, 0:L], rhs=woa[g][:, k, :], start=(k == 0), stop=(k == 7)),
                       r=[(attnT, b), woa[g]], w=[po_])
                mx = mixs[b]
                V_(lambda e, b=b, g=g, po_=po_, mx=mx: e.tensor_tensor(out=mx[:L, 512 * g:512 * (g + 1)], in0=po_[:L, :],
                                                                    in1=gsig[:L, b, 512 * g:512 * (g + 1)], op=ALU.mult), r=[po_, gsig], w=[(mx, g)])
        wor = [load_w(kview(D_wor), 512 * g, 512) for g in range(2)]
        for b in range(nbk):
            mx = mixs[b]
            for g in range(2):
                po_ = pm.next()
                for k in range(8):
                    T_(lambda e, b=b, g=g, k=k, po_=po_: e.matmul(po_[:L, :], lhsT=orwT[:, b, k, 0:L], rhs=wor[g][:, k, :], start=(k == 0), stop=(k == 7)),
                       r=[(orwT, b), wor[g]], w=[po_])
                tq = rtmp.next()
                V_(lambda e, b=b, g=g, po_=po_, tq=tq: e.tensor_tensor(out=tq[:L, :], in0=po_[:L, :], in1=gsig[:L, b, D + 512 * g:D + 512 * (g + 1)], op=ALU.mult),
                   r=[po_, gsig], w=[tq])
                G_(lambda e, g=g, mx=mx, tq=tq: e.tensor_tensor(out=mx[:L, 512 * g:512 * (g + 1)], in0=mx[:L, 512 * g:512 * (g + 1)], in1=tq[:L, :], op=ALU.add),
                   r=[tq, (mx, g)], w=[(mx, g)])
            for k in range(8):
                T_(lambda e, k=k, mx=mx: e.transpose(out=ptr[:, k, 0:L], in_=mx[:L, 128 * k:128 * (k + 1)], identity=identB[:L, :L]), r=[mx, identB], w=[ptr])
            evac_copy(mixT[:, :, L * b:L * b + L], ptr[:, :, 0:L], r=[ptr], w=[(mixT, b)])
        wo = [load_w(kview(D_wout), 512 * g, 512) for g in range(2)]
        for b in range(nbk):
            for g in range(2):
                po_ = pm.next()
                for k in range(8):
                    T_(lambda e, b=b, g=g, k=k, po_=po_: e.matmul(po_[:L, :], lhsT=mixT[:, k, L * b:L * b + L], rhs=wo[g][:, k, :], start=(k == 0), stop=(k == 7)),
                       r=[(mixT, b), wo[g]], w=[po_])
                V_(lambda e, b=b, g=g, po_=po_: e.scalar_tensor_tensor(out=x1[:L, b, 512 * g:512 * (g + 1)], in0=hres[:L, b, 512 * g:512 * (g + 1)], scalar=float(ALPHA),
                                                                    in1=po_[:L, :], op0=ALU.mult, op1=ALU.add), r=[po_, (hres, b)], w=[(x1, b)])
            ln_rows(x1[:L, b, :], L, D, x1[:L, b, :], LN_EPS, g_b=(getvec("ln1_g"), getvec("ln1_b")))
            G_(lambda e, b=b: e.tensor_copy(out=x1b[:L, :], in_=x1[:L, b, :]), r=[x1], w=[x1b])
            for k in range(8):
                T_(lambda e, k=k: e.transpose(out=ptr[:, k, 0:L], in_=x1b[:L, 128 * k:128 * (k + 1)], identity=identB[:L, :L]), r=[x1b, identB], w=[ptr])
            evac_copy(x1T[:, :, L * b:L * b + L], ptr[:, :, 0:L], r=[ptr], w=[(x1T, b)])
        accs = [[(pLT, 0), (pLT, 1)], [(pO, 0), (pO, 1)]]
        acc_ap = lambda b, g: (accs[b][g][0])[:, 512 * accs[b][g][1]:512 * (accs[b][g][1] + 1)]
        nfc = DFF // 128
        for fq in range(0, nfc, 4):
            nq = min(4, nfc - fq)
            wg_ = load_w(kview(D_wfg), 128 * fq, 128 * nq)
            wu_ = load_w(kview(D_wfu), 128 * fq, 128 * nq)
            wd_ = wt.next()
            wdv = wd_[:].rearrange("p (a c) n -> p a (c n)", c=2)
            P.dma('sp', wdv[:, 0:nq, :], D_wfd.ap()[128 * fq:128 * (fq + nq), :].rearrange("(a p) n -> p a n", p=128), r=[D_wfd], w=[wd_])
            for j in range(nq):
                fc = fq + j
                pg_ = pm.next()
                for k in range(8):
                    T_(lambda e, k=k, j=j, pg_=pg_, wg_=wg_: e.matmul(pg_[:, 0:ntk], lhsT=wg_[:, k, 128 * j:128 * (j + 1)], rhs=x1T[:, k, 0:ntk], start=(k == 0), stop=(k == 7)),
                       r=[x1T, wg_], w=[pg_])
                ag = actg.next()
                A_(lambda e, pg_=pg_, ag=ag: e.activation(out=ag[:, 0:ntk], in_=pg_[:, 0:ntk], func=AF.Silu), r=[pg_], w=[ag])
                pu_ = pm.next()
                for k in range(8):
                    T_(lambda e, k=k, j=j, pu_=pu_, wu_=wu_: e.matmul(pu_[:, 0:ntk], lhsT=wu_[:, k, 128 * j:128 * (j + 1)], rhs=x1T[:, k, 0:ntk], start=(k == 0), stop=(k == 7)),
                       r=[x1T, wu_], w=[pu_])
                at_ = actT.next()
                V_(lambda e, pu_=pu_, ag=ag, at_=at_: e.tensor_tensor(out=at_[:, 0:ntk], in0=pu_[:, 0:ntk], in1=ag[:, 0:ntk], op=ALU.mult), r=[pu_, ag], w=[at_])
                for b in range(nbk):
                    for g in range(2):
                        T_(lambda e, b=b, g=g, j=j, fc=fc, at_=at_, wdv=wdv: e.matmul(acc_ap(b, g)[:L, :], lhsT=at_[:, L * b:L * b + L], rhs=wdv[:, j, 512 * g:512 * (g + 1)],
                                                                                   start=(fc == 0), stop=(fc == nfc - 1)), r=[at_, wd_], w=[accs[b][g]])
        for b in range(nbk):
            yt = yo.next()
            for g in range(2):
                V_(lambda e, b=b, g=g, yt=yt: e.scalar_tensor_tensor(out=yt[:L, 512 * g:512 * (g + 1)], in0=x1[:L, b, 512 * g:512 * (g + 1)], scalar=float(ALPHA),
                                                                  in1=acc_ap(b, g)[:L, :], op0=ALU.mult, op1=ALU.add), r=[accs[b][g], x1], w=[(yt, g)])
            ln_rows(yt[:L, :], L, D, yt[:L, :], LN_EPS, g_b=(getvec("ln2_g"), getvec("ln2_b")))
            P.dma('pool', y_dram.ap()[yrows[b]:yrows[b] + L, :], yt[:L, :], r=[yt], w=[(y_dram, yrows[b])])

    nso_sb = GEOM['NSO_B'] // NB
    so_blocks = [(NT * i, [128] * NB) for i in range(nso_sb)] + [(128 * GEOM['NSO_B'], [16])]
    for (row0, Ls) in so_blocks:
        L = Ls[0]
        if L == 16:
            V_(lambda e: e.tensor_scalar(out=ST[:].rearrange("p k i -> p (k i)"), in0=ST[:].rearrange("p k i -> p (k i)"), scalar1=flg[:, 0:1], scalar2=None, op0=ALU.mult),
               r=[ST, flg], w=[ST])
            A_(lambda e: e.activation(out=STb[:], in_=ST[:], func=AF.Copy), r=[ST], w=[STb])
            V_(lambda e: e.tensor_scalar(out=car[:], in0=car[:], scalar1=flg[:, 0:1], scalar2=None, op0=ALU.mult), r=[car, flg], w=[car])
        rows = [row0 + L * b for b in range(len(Ls))]
        front(xso, rows, Ls, rows, rows, (O_k, O_v, O_ki, rows), own=False)
        carry = [(car, car)] + [None] * (len(Ls) - 1)
        save = [None] * (len(Ls) - 1) + [(car, car)]
        rwkv_prep(Ls, False, carry, save)
        if L == 16:
            noop = lambda xs: None
            for half in range(2):
                wr_ = load_w(win_v, C_RW + 512 * half, 512)
                for pp in range(4):
                    rw_tile(4 * half + pp, wr_, 128 * pp, Ls, noop, carry, save)
            wl_ = load_w(win_v, C_RW + 3072, 256)
            rw_tile(25, wl_, 128, Ls, noop, carry, save)
        for b in range(len(Ls)):
            chunk_scan(b, L, own=False)

    for sbi in range(GEOM['NOWN_B'] // NB):
        Ls = [128] * NB
        rows = [NT * sbi + 128 * b for b in range(NB)]
        slots = [NSO + r_ for r_ in rows]
        front(xown, rows, Ls, slots, slots, (O_k, O_v, O_ki, slots), own=True)
        own_proj(Ls, slots)
        carry = [(car, car)] + [None] * (NB - 1)
        save = [None] * (NB - 1) + [(car, car)]
        rwkv_prep(Ls, True, carry, save)
        for b in range(NB):
            chunk_scan(b, 128, own=True)
        for b in range(NB):
            attention(b, 128, 0, slots[b] + 128, corner=True)
        post(Ls, O_y, rows)
    P.dma('sp', O_wkv.ap(), ST[:].rearrange("p k i -> p (k i)"), r=[ST], w=[O_wkv])
    P.dma('sp', O_shift.ap(), car[:], r=[car], w=[O_shift])

    if SAMPLE:
        for q in range(2):
            sb0 = NSLOT + SSTRIDE * q
            for i0 in range(0, CACHE_ROWS, 128):
                L = min(128, CACHE_ROWS - i0)
                ct = xin.next()
                P.dma('sp', ct[:L, 0, 0:128], ck.ap()[q, i0:i0 + L, :], r=[ck], w=[ct])
                P.dma('sp', ct[:L, 0, 128:192], cik.ap()[q, i0:i0 + L, :], r=[cik], w=[ct])
                pt_ = pm.next()
                T_(lambda e, ct=ct, pt_=pt_, L=L: e.transpose(out=pt_[:, 0:L], in_=ct[:L, 0, 0:128], identity=identF[:L, :L]), r=[ct, identF], w=[pt_])
                T_(lambda e, ct=ct, pt_=pt_, L=L: e.transpose(out=pt_[0:64, 128:128 + L], in_=ct[:L, 0, 128:192], identity=identF[:L, :L]), r=[ct, identF], w=[pt_])
                kt_t = ktt.next(); kit_t = kitt.next()
                V_(lambda e, kt_t=kt_t, pt_=pt_, L=L: e.tensor_copy(out=kt_t[:, 0:L], in_=pt_[:, 0:L]), r=[pt_], w=[kt_t])
                V_(lambda e, kit_t=kit_t, pt_=pt_, L=L: e.tensor_copy(out=kit_t[:, 0:L], in_=pt_[0:64, 128:128 + L]), r=[pt_], w=[kit_t])
                P.dma('pool', D_kt.ap()[:, sb0 + i0:sb0 + i0 + L], kt_t[:, 0:L], r=[kt_t], w=[(D_kt, sb0 + i0)])
                P.dma('pool', D_kit.ap()[:, sb0 + i0:sb0 + i0 + L], kit_t[:, 0:L], r=[kit_t], w=[(D_kit, sb0 + i0)])
        for q in range(2):
            P.dma('sp', cars[:, q, :], sshift.ap()[q], r=[sshift], w=[(cars, q)])
        for q in range(2):
            sb0 = NSLOT + SSTRIDE * q
            Ls = [64]
            rows = [64 * q]
            rrows = [NSO + NOWN + 64 * q]
            slots = [sb0 + CACHE_ROWS]
            front(xsm, rows, Ls, rrows, slots, (O_ks, O_vs, O_kis, rows), own=True)
            own_proj(Ls, rrows)
            rwkv_prep(Ls, True, [(cars[:, q, :], (cars, q))], [(cars[:, q, :], (cars, q))])
            P.dma('sp', ST[:].rearrange("p k i -> p (k i)"), swkv.ap()[q], r=[swkv], w=[ST])
            A_(lambda e: e.activation(out=STb[:], in_=ST[:], func=AF.Copy), r=[ST], w=[STb])
            chunk_scan(0, 64, own=True)
            P.dma('sp', O_wkvs.ap()[q], ST[:].rearrange("p k i -> p (k i)"), r=[ST], w=[(O_wkvs, q)])
            P.dma('sp', O_shifts.ap()[q], cars[:, q, :], r=[(cars, q)], w=[(O_shifts, q)])
            attention(0, 64, sb0, sb0 + CACHE_ROWS + 64, corner=False)
            post(Ls, O_ys, rows)
    nc = P.build()
    return nc, P


def make_consts():
    c = {}
    c["identf"] = np.eye(128, dtype=np.float32)
    b = np.zeros((128, 128), np.float32); b[:64, :64] = 1; b[64:, 64:] = 1
    c["blk1"] = b
    us = np.triu(np.ones((128, 128), np.float32), 1)
    ui = np.triu(np.ones((128, 128), np.float32), 0)
    c["cmask"] = np.concatenate([us, us.T, us, ui, ui], axis=1).astype(np.float32)
    return c


def rope_table(pos):
    pos = np.asarray(pos, np.float32)
    out = np.zeros((len(pos), 48), np.float32)
    for (rot, o) in ((32, 0), (16, 32)):
        inv = (np.float32(500000.0) ** (-np.arange(0, rot, 2, dtype=np.float32) / np.float32(rot))).astype(np.float32)
        ang = (pos[:, None] * inv[None]).astype(np.float32)
        h = rot // 2
        out[:, o:o + h] = np.cos(ang); out[:, o + h:o + 2 * h] = np.sin(ang)
    return out


def colpack(v, n):
    return np.ascontiguousarray(np.asarray(v, np.float32).reshape(n, 128).T)


def st_layout(s):
    s = np.asarray(s, np.float32).reshape(8, 2, 64, 64)
    return np.ascontiguousarray(s.transpose(1, 3, 0, 2).reshape(128, 512))


def st_unlayout(a):
    a = np.asarray(a, np.float32).reshape(2, 64, 8, 64)
    return np.ascontiguousarray(a.transpose(2, 0, 3, 1).reshape(16, 64, 64))


def prep_inputs(inp):
    NSO, NOWN, NSLOT = geom()
    f32 = lambda a: np.ascontiguousarray(np.asarray(a, np.float32))
    consts = make_consts()
    maps = []
    colsf = np.concatenate([
        colpack(inp["ln0_g"], 8), colpack(inp["ln0_b"], 8), colpack(inp["rw_mu"][0], 26), colpack(inp["rw_w0"][0], 8),
        colpack(inp["rw_a0"][0], 8), colpack(inp["rw_k_k"][0], 8), colpack(inp["rw_k_a"][0], 8), colpack(np.asarray(inp["rw_r_k"][0]).reshape(-1), 8)], axis=1)
    shared = dict(consts)
    shared.update(colsf=colsf, w_in=f32(inp["w_in"][0]), rw_w2=f32(inp["rw_w2"][0]), rw_a2=f32(inp["rw_a2"][0]), rw_g2=f32(inp["rw_g2"][0]),
                  ikg=f32(inp["idx_k_ln_g"][0]), ikb=f32(inp["idx_k_ln_b"][0]),
                  ln0_g=f32(inp["ln0_g"]), ln0_b=f32(inp["ln0_b"]), ln1_g=f32(inp["ln1_g"][0]), ln1_b=f32(inp["ln1_b"][0]),
                  ln2_g=f32(inp["ln2_g"][0]), ln2_b=f32(inp["ln2_b"][0]), gn_g=f32(inp["rw_gn_g"][0]), gn_b=f32(inp["rw_gn_b"][0]),
                  w_oa=f32(inp["w_o_attn"][0]), w_or=f32(inp["w_o_rwkv"][0]), w_out=f32(inp["w_out"][0]),
                  w_fg=f32(inp["ffn_w_gate"][0]), w_fu=f32(inp["ffn_w_up"][0]), w_fd=f32(inp["ffn_w_down"][0]))
    meta = f32(inp["meta_tokens"])
    past = int(np.asarray(inp["cache_k"]).shape[2]) - 16
    for c in range(8):
        b, hf = c // 2, c % 2
        xp = f32(inp["x_prompt"][b])
        nfr = NSO - 16
        if hf == 1:
            xso = np.concatenate([meta, xp[:nfr]], 0)
            pos_so = np.arange(NSO)
            xown = xp[nfr:nfr + NOWN]; pos_own = NSO + np.arange(NOWN)
        else:
            xso = np.concatenate([xp[nfr:2 * nfr], meta], 0)
            pos_so = np.concatenate([np.zeros(nfr), np.arange(16)])
            xown = xp[:NOWN]; pos_own = 16 + np.arange(NOWN)
        keyb = np.zeros((1, NSLOT + 2 * SSTRIDE), np.float32)
        if hf == 0:
            keyb[0, :nfr] = -1e30
        xs = f32(inp["x_sample"][2 * c:2 * c + 2]).reshape(128, D)
        pos_sm = np.concatenate([16 + past + np.arange(64)] * 2)
        m = dict(shared)
        m.update(xso=np.ascontiguousarray(xso), xown=np.ascontiguousarray(xown), xsm=xs,
                 rope=rope_table(np.concatenate([pos_so, pos_own, pos_sm])),
                 flag=np.full((128, 1), float(hf), np.float32), keyb=keyb.astype(ml_dtypes.bfloat16),
                 ck=f32(inp["cache_k"][0, 2 * c:2 * c + 2]), cv=f32(inp["cache_v"][0, 2 * c:2 * c + 2]), cik=f32(inp["cache_idx_k"][0, 2 * c:2 * c + 2]),
                 swkv=np.stack([st_layout(inp["state_wkv"][0, 2 * c + q]) for q in range(2)]),
                 sshift=np.stack([colpack(inp["state_shift"][0, 2 * c + q], 26) for q in range(2)]))
        maps.append(m)
    return maps


_CACHE = {}


def run_device(inp):
    key = (GEOM['NSO_B'], GEOM['NOWN_B'], GEOM['SAMPLE'], MAXOPS)
    if key not in _CACHE:
        _CACHE[key] = build_program()
    nc, P = _CACHE[key]
    maps = prep_inputs(inp)
    used = set(P.names)
    maps = [{k: v for k, v in m.items() if k in used} for m in maps]
    res = run_bass_kernel_spmd(nc, maps, core_ids=list(range(8)))
    return res.results


def uncol(a, n):
    return np.ascontiguousarray(np.asarray(a, np.float32).T.reshape(-1))


def kernel(**inputs):
    NSO, NOWN, NSLOT = geom()
    res = run_device(inputs)
    B = 4
    y_p = np.zeros((B, 2 * NOWN, D), np.float32)
    k_p = np.zeros((1, B, NSLOT, 128), np.float32); v_p = np.zeros((1, B, NSLOT, 128), np.float32); ki_p = np.zeros((1, B, NSLOT, 64), np.float32)
    wkv_p = np.zeros((1, B, 16, 64, 64), np.float32); sh_p = np.zeros((1, B, 3328), np.float32)
    y_s = np.zeros((16, 64, D), np.float32)
    k_s = np.zeros((1, 16, 64, 128), np.float32); v_s = np.zeros((1, 16, 64, 128), np.float32); ki_s = np.zeros((1, 16, 64, 64), np.float32)
    wkv_s = np.zeros((1, 16, 16, 64, 64), np.float32); sh_s = np.zeros((1, 16, 3328), np.float32)
    for c in range(8):
        b, hf = c // 2, c % 2
        r = res[c]
        y_p[b, hf * NOWN:(hf + 1) * NOWN] = r["O_y"]
        if hf == 1:
            k_p[0, b] = r["O_k"]; v_p[0, b] = r["O_v"]; ki_p[0, b] = r["O_ki"]
            wkv_p[0, b] = st_unlayout(r["O_wkv"]); sh_p[0, b] = uncol(r["O_shift"], 26)
        y_s[2 * c:2 * c + 2] = r["O_ys"].reshape(2, 64, D)
        k_s[0, 2 * c:2 * c + 2] = r["O_ks"].reshape(2, 64, 128); v_s[0, 2 * c:2 * c + 2] = r["O_vs"].reshape(2, 64, 128)
        ki_s[0, 2 * c:2 * c + 2] = r["O_kis"].reshape(2, 64, 64)
        for q in range(2):
            wkv_s[0, 2 * c + q] = st_unlayout(r["O_wkvs"][q]); sh_s[0, 2 * c + q] = uncol(r["O_shifts"][q], 26)
    return (y_p, y_s, k_p, v_p, ki_p, wkv_p, sh_p, k_s, v_s, ki_s, wkv_s, sh_s)
```

```python
import bisect
import numpy as np
import ml_dtypes
from contextlib import ExitStack
import concourse.bass as bass
import concourse.mybir as mybir
from concourse.bass_utils import run_bass_kernel_spmd

F32 = mybir.dt.float32
BF16 = mybir.dt.bfloat16
U8 = mybir.dt.uint8
ALU = mybir.AluOpType
AF = mybir.ActivationFunctionType

SAME_ENGINE_SYNC = True
MAXOPS = None
ENGS = ('pe', 'act', 'dve', 'pool', 'sp')


class Prog:
    def __init__(self):
        self.nc = bass.Bass("TRN2", target_bir_lowering=False)
        self.es = ExitStack()
        self.ops = []
        self.state = {}
        self.names = set()
        self.psum_names = set()

    def _nm(self, name):
        assert name not in self.names, name
        self.names.add(name)
        return name

    def sb(self, name, shape, dt):
        return self.es.enter_context(self.nc.sbuf_tensor(self._nm(name), list(shape), dt))

    def ps(self, name, shape, dt=F32):
        self.psum_names.add(name)
        return self.es.enter_context(self.nc.psum_tensor(self._nm(name), list(shape), dt))

    def dram(self, name, shape, dt, kind):
        return self.nc.dram_tensor(self._nm(name), list(shape), dt, kind=kind)

    @staticmethod
    def _reg(x):
        if isinstance(x, tuple):
            base, sub = x[0], (x[1],)
        else:
            base, sub = x, ()
        if isinstance(base, Alias):
            return base.name, (base.key,) + sub
        return base.name, sub

    @staticmethod
    def _rel(a, b):
        n = min(len(a), len(b))
        return a[:n] == b[:n]

    def _deps(self, idx, reads, writes, eng=None):
        deps = set()
        for x in reads:
            n, k = self._reg(x)
            st = self.state.setdefault(n, {})
            for kk, ent in st.items():
                if not self._rel(kk, k):
                    continue
                if ent[0] is not None:
                    deps.add(ent[0])
                if n in self.psum_names:
                    for rr in ent[1]:
                        if self.ops[rr]['eng'] != eng:
                            deps.add(rr)
        for x in writes:
            n, k = self._reg(x)
            st = self.state.setdefault(n, {})
            for kk, ent in st.items():
                if not self._rel(kk, k):
                    continue
                if ent[0] is not None:
                    deps.add(ent[0])
                deps.update(ent[1])
        for x in reads:
            n, k = self._reg(x)
            self.state[n].setdefault(k, [None, []])[1].append(idx)
        for x in writes:
            n, k = self._reg(x)
            st = self.state[n]
            for kk in [kk for kk in st if len(kk) >= len(k) and kk[:len(k)] == k]:
                del st[kk]
            st[k] = [idx, []]
        deps.discard(idx)
        return sorted(deps)

    def op(self, eng, fn, r=(), w=()):
        if MAXOPS is not None and len(self.ops) >= MAXOPS:
            return None
        idx = len(self.ops)
        self.ops.append(dict(eng=eng, fn=fn, deps=self._deps(idx, r, w, eng), dma=False, semkey=None))
        return idx

    def dma(self, q, out, in_, r=(), w=(), semkey=None, **kw):
        if MAXOPS is not None and len(self.ops) >= MAXOPS:
            return None
        idx = len(self.ops)
        deps = self._deps(idx, r, w)
        if semkey is None:
            wn = self._reg(w[0])[0]
            semkey = wn if not (wn.startswith('D_') or wn.startswith('O_')) else self._reg(r[0])[0]
        fn = (lambda e, out=out, in_=in_, kw=kw: e.dma_start(out=out, in_=in_, **kw))
        self.ops.append(dict(eng=q, fn=fn, deps=deps, dma=True, semkey=semkey))
        return idx

    def build(self):
        nc, ops = self.nc, self.ops
        n = len(ops)
        need_sig = [False] * n
        for i, o in enumerate(ops):
            for j in o['deps']:
                pj = ops[j]
                if pj['dma']:
                    continue
                if pj['eng'] != o['eng'] or o['dma'] or (SAME_ENGINE_SYNC and o['eng'] != 'pe'):
                    need_sig[j] = True
        cnt = {e: 0 for e in ENGS}
        sigval = [0] * n
        dcnt, semkeys = {}, []
        for i, o in enumerate(ops):
            if o['dma']:
                k = o['semkey']
                if k not in dcnt:
                    dcnt[k] = 0
                    semkeys.append(k)
                dcnt[k] += 16
                sigval[i] = dcnt[k]
            elif need_sig[i]:
                cnt[o['eng']] += 1
                sigval[i] = cnt[o['eng']]
        esem = {e: self.es.enter_context(nc.semaphore("s_" + e)) for e in ENGS}
        dsem = {k: self.es.enter_context(nc.semaphore("d_" + k)) for k in semkeys}
        self.n_sems = len(esem) + len(dsem)
        dma_idx = {}
        for i, o in enumerate(ops):
            if o['dma']:
                dma_idx.setdefault(o['semkey'], []).append(i)
        waited = {e: {} for e in ENGS}
        plan = {e: [] for e in ENGS}
        for i, o in enumerate(ops):
            E = o['eng']
            waits = {}
            for j in o['deps']:
                pj = ops[j]
                if pj['dma']:
                    key = ('d', pj['semkey'])
                    lst = dma_idx[pj['semkey']]
                    val_d = 16 * bisect.bisect_left(lst, i)
                else:
                    if pj['eng'] == E and not o['dma'] and (E == 'pe' or not SAME_ENGINE_SYNC):
                        continue
                    key = ('e', pj['eng'])
                val = val_d if pj['dma'] else sigval[j]
                if waited[E].get(key, 0) >= val:
                    continue
                waits[key] = max(waits.get(key, 0), val)
            for key, val in waits.items():
                waited[E][key] = val
            plan[E].append((i, waits))
        blk = self.es.enter_context(nc.Block())
        engobj = {'pe': 'tensor', 'act': 'scalar', 'dve': 'vector', 'pool': 'gpsimd', 'sp': 'sync'}

        def emit_for(E):
            def body(eng):
                for i, waits in plan[E]:
                    o = ops[i]
                    for (kind, k), val in waits.items():
                        eng.wait_ge(dsem[k] if kind == 'd' else esem[k], val)
                    ins = o['fn'](eng)
                    if o['dma']:
                        ins.then_inc(dsem[o['semkey']], 16)
                    elif need_sig[i]:
                        ins.then_inc(esem[E], 1)
                if E == 'sp':
                    for k, v in dcnt.items():
                        eng.wait_ge(dsem[k], v)
                    for e2 in ENGS:
                        if e2 != 'sp' and cnt[e2] > 0:
                            eng.wait_ge(esem[e2], cnt[e2])
            return body

        for E in ENGS:
            getattr(blk, engobj[E])(emit_for(E))
        self.es.close()
        return nc


class Alias:
    def __init__(self, base, dtype, byte_off, shape, key):
        es = 2 if dtype == BF16 else 4
        self.h = base.bitcast(dtype)
        self.off = byte_off // es
        self.shape = list(shape)
        self.name = base.name
        self.key = key
        n = int(np.prod(shape[1:]))
        v = self.h[0:shape[0], self.off:self.off + n]
        if len(shape) == 3:
            v = v.rearrange("p (a b) -> p a b", a=shape[1])
        self.v = v

    def __getitem__(self, idx):
        return self.v[idx]


class Ring:
    def __init__(self, tiles):
        self.t, self.i = tiles, 0

    def next(self):
        t = self.t[self.i % len(self.t)]
        self.i += 1
        return t


D = 1024
DFF = 2816
GEOM = dict(NSO_B=32, NOWN_B=32, SAMPLE=True)
NB = 1
NT = 128 * NB
WIN_COLS = 7240
C_Q, C_K, C_V, C_QI, C_KI, C_WI, C_G, C_RW = 0, 1024, 1152, 1280, 1792, 1856, 1864, 3912
LN_EPS = 1e-5
GN_EPS = 64e-5
DEC = 0.6065306597126334
ALPHA = 2.0 ** 0.25
CACHE_ROWS = 2064
SSTRIDE = 2176
NBIS = 19
TOPK = 256
O_G0, O_B0, O_MU, O_W0, O_A0, O_KK, O_KA, O_RK, NCOLS = 0, 8, 16, 42, 50, 58, 66, 74, 82


def geom():
    nso = 128 * GEOM['NSO_B'] + 16
    nown = 128 * GEOM['NOWN_B']
    return nso, nown, nso + nown


def build_program():
    NSO, NOWN, NSLOT = geom()
    SAMPLE = GEOM['SAMPLE']
    NSLOT_ALL = NSLOT + 2 * SSTRIDE
    P = Prog()
    nc = P.nc
    V_ = lambda fn, r=(), w=(): P.op('dve', fn, r, w)
    A_ = lambda fn, r=(), w=(): P.op('act', fn, r, w)
    G_ = lambda fn, r=(), w=(): P.op('pool', fn, r, w)
    T_ = lambda fn, r=(), w=(): P.op('pe', fn, r, w)

    din = lambda n, s, dt=F32: P.dram(n, s, dt, "ExternalInput")
    dout = lambda n, s, dt=F32: P.dram(n, s, dt, "ExternalOutput")
    dint = lambda n, s, dt=BF16: P.dram(n, s, dt, "Internal")
    xso = din("xso", [NSO, D]); xown = din("xown", [NOWN, D]); xsm = din("xsm", [128, D])
    rope = din("rope", [NSO + NOWN + 128, 48])
    flag = din("flag", [128, 1])
    colsf = din("colsf", [128, NCOLS])
    cmask = din("cmask", [128, 640])
    identf = din("identf", [128, 128])
    blk1 = din("blk1", [128, 128])
    keyb = din("keyb", [1, NSLOT_ALL], BF16)
    w_in = din("w_in", [D, WIN_COLS])
    rw_w2 = din("rw_w2", [64, D]); rw_a2 = din("rw_a2", [64, D]); rw_g2 = din("rw_g2", [128, D])
    ikg = din("ikg", [64]); ikb = din("ikb", [64])
    vecs = {n: din(n, [D]) for n in ("ln0_g", "ln0_b", "ln1_g", "ln1_b", "ln2_g", "ln2_b", "gn_g", "gn_b")}
    w_oa = din("w_oa", [D, D]); w_or = din("w_or", [D, D]); w_out = din("w_out", [D, D])
    w_fg = din("w_fg", [D, DFF]); w_fu = din("w_fu", [D, DFF]); w_fd = din("w_fd", [DFF, D])
    ck = din("ck", [2, CACHE_ROWS, 128]); cv = din("cv", [2, CACHE_ROWS, 128]); cik = din("cik", [2, CACHE_ROWS, 64])
    swkv = din("swkv", [2, 128, 512]); sshift = din("sshift", [2, 128, 26])

    O_y = dout("O_y", [NOWN, D]); O_ys = dout("O_ys", [128, D])
    O_k = dout("O_k", [NSLOT, 128]); O_v = dout("O_v", [NSLOT, 128]); O_ki = dout("O_ki", [NSLOT, 64])
    O_wkv = dout("O_wkv", [128, 512]); O_shift = dout("O_shift", [128, 26])
    O_ks = dout("O_ks", [128, 128]); O_vs = dout("O_vs", [128, 128]); O_kis = dout("O_kis", [128, 64])
    O_wkvs = dout("O_wkvs", [2, 128, 512]); O_shifts = dout("O_shifts", [2, 128, 26])

    D_win = dint("D_win", [D, WIN_COLS])
    D_w2 = dint("D_w2", [64, D]); D_a2 = dint("D_a2", [64, D]); D_g2 = dint("D_g2", [128, D])
    D_woa = dint("D_woa", [D, D]); D_wor = dint("D_wor", [D, D]); D_wout = dint("D_wout", [D, D])
    D_wfg = dint("D_wfg", [D, DFF]); D_wfu = dint("D_wfu", [D, DFF]); D_wfd = dint("D_wfd", [DFF, D])
    D_kt = dint("D_kt", [128, NSLOT_ALL]); D_kit = dint("D_kit", [64, NSLOT_ALL]); D_v = dint("D_v", [NSLOT_ALL, 128])

    def cast_rows(dst, src, nrows, step):
        for i in range(0, nrows, step):
            n = min(step, nrows - i)
            P.dma('pool', dst.ap()[i:i + n, :], src.ap()[i:i + n, :], r=[src], w=[(dst, i)])
    cast_rows(D_win, w_in, D, 128)
    P.dma('pool', D_w2.ap(), rw_w2.ap(), r=[rw_w2], w=[D_w2])
    P.dma('pool', D_a2.ap(), rw_a2.ap(), r=[rw_a2], w=[D_a2])
    P.dma('pool', D_g2.ap(), rw_g2.ap(), r=[rw_g2], w=[D_g2])
    for (dd, ss, nr) in ((D_woa, w_oa, D), (D_wor, w_or, D), (D_wout, w_out, D), (D_wfg, w_fg, D), (D_wfu, w_fu, D), (D_wfd, w_fd, DFF)):
        cast_rows(dd, ss, nr, 256)
    if SAMPLE:
        for q in range(2):
            sb0 = NSLOT + SSTRIDE * q
            P.dma('pool', D_v.ap()[sb0:sb0 + CACHE_ROWS, :], cv.ap()[q], r=[cv], w=[(D_v, 'c%d' % q)])
    kview = lambda dt_: dt_.ap().rearrange("(k p) n -> p k n", p=128)
    win_v = kview(D_win)

    cols = P.sb("cols", [128, NCOLS], F32)
    colsd = P.sb("colsd", [128, 34], F32)
    identF = P.sb("identF", [128, 128], F32)
    identB = P.sb("identB", [128, 128], BF16)
    I4 = P.sb("I4", [128, 4, 128], BF16)
    blkones = P.sb("blkones", [128, 128], F32)
    blkonesB = P.sb("blkonesB", [128, 128], BF16)
    mask = P.sb("mask", [128, 640], F32)
    ones128 = P.sb("ones128", [128, 128], F32)
    onesrow = P.sb("onesrow", [1, 128], BF16)
    ikg_b = P.sb("ikg_b", [128, 64], F32); ikb_b = P.sb("ikb_b", [128, 64], F32)
    w2b = P.sb("w2b", [128, D], BF16); a2b = P.sb("a2b", [128, D], BF16); g2b = P.sb("g2b", [128, D], BF16)
    flg = P.sb("flg", [128, 1], F32)
    vbr = Ring([P.sb("vbr%d" % i, [128, D], F32) for i in range(4)])

    def getvec(n):
        t = vbr.next()
        P.dma('sp', t[:], vecs[n].ap().partition_broadcast(128), r=[vecs[n]], w=[t])
        return t
    P.dma('sp', cols[:], colsf.ap(), r=[colsf], w=[cols])
    P.dma('sp', identF[:], identf.ap(), r=[identf], w=[identF])
    P.dma('sp', blkones[:], blk1.ap(), r=[blk1], w=[blkones])
    P.dma('sp', mask[:], cmask.ap(), r=[cmask], w=[mask])
    P.dma('sp', flg[:], flag.ap(), r=[flag], w=[flg])
    P.dma('sp', ikg_b[:], ikg.ap().partition_broadcast(128), r=[ikg], w=[ikg_b])
    P.dma('sp', ikb_b[:], ikb.ap().partition_broadcast(128), r=[ikb], w=[ikb_b])
    P.dma('sp', w2b[0:64, :], D_w2.ap(), r=[D_w2], w=[w2b])
    P.dma('sp', a2b[64:128, :], D_a2.ap(), r=[D_a2], w=[a2b])
    P.dma('sp', g2b[:], D_g2.ap(), r=[D_g2], w=[g2b])
    V_(lambda e: e.tensor_copy(out=identB[:], in_=identF[:]), r=[identF], w=[identB])
    for i in range(4):
        V_(lambda e, i=i: e.tensor_copy(out=I4[:, i, :], in_=identF[:]), r=[identF], w=[I4])
    V_(lambda e: e.tensor_copy(out=blkonesB[:], in_=blkones[:]), r=[blkones], w=[blkonesB])
    V_(lambda e: e.memset(ones128[:], 1.0), w=[ones128])
    V_(lambda e: e.memset(onesrow[:], 1.0), w=[onesrow])
    V_(lambda e: e.tensor_scalar(out=colsd[:, 0:26], in0=cols[:, O_MU:O_MU + 26], scalar1=-1.0, scalar2=1.0, op0=ALU.mult, op1=ALU.add),
       r=[cols], w=[colsd])
    V_(lambda e: e.tensor_scalar(out=colsd[:, 26:34], in0=cols[:, O_KA:O_KA + 8], scalar1=-1.0, scalar2=1.0, op0=ALU.mult, op1=ALU.add),
       r=[cols], w=[colsd])

    pm = Ring([P.ps("pm0", [128, 512]), P.ps("pm1", [128, 512])])
    ptr = P.ps("ptr", [128, 8, 128], BF16)
    pLT = P.ps("pLT", [128, 1024])
    pO = P.ps("pO", [128, 1536])
    cbanks = [(pLT, 0), (pLT, 1), (pO, 0), (pO, 1)]
    cb_ap = lambda i: (cbanks[i][0])[:, 512 * cbanks[i][1]:512 * (cbanks[i][1] + 1)]
    xbanks = [(pLT, 0), (pLT, 1), (pO, 2)]
    xq = [0]

    xin = Ring([P.sb("xin%d" % i, [128, NB, D], F32) for i in range(2)])
    hn = P.sb("hn", [128, NB, D], BF16)
    hres = P.sb("hres", [128, NB, D], F32)
    hT = P.sb("hT", [128, 8, NT], BF16)
    small = Ring([P.sb("small%d" % i, [128, 24], F32) for i in range(4)])
    wt = Ring([P.sb("wt%d" % i, [128, 8, 512], BF16) for i in range(3)])
    kvs = [P.sb("kvs%d" % i, [128, 328], F32) for i in range(NB)]
    rtmp = Ring([P.sb("rtmp%d" % i, [128, 512], F32) for i in range(2)])
    rp = [P.sb("rp%d" % i, [128, 48], F32) for i in range(NB)]
    vbt = Ring([P.sb("vbt%d" % i, [128, 128], BF16) for i in range(2)])
    ktt = Ring([P.sb("ktt%d" % i, [128, NT], BF16) for i in range(2)])
    kitt = Ring([P.sb("kitt%d" % i, [64, NT], BF16) for i in range(2)])
    ki2 = Ring([P.sb("ki2_%d" % i, [128, 64], F32) for i in range(2)])
    qT = P.sb("qT", [128, NB, 8, 128], BF16)
    qiT = P.sb("qiT", [128, NB, 4, 128], BF16)
    gsig = P.sb("gsig", [128, NB, 2 * D], BF16)
    car = P.sb("car", [128, 26], F32)
    cars = P.sb("cars", [128, 2, 26], F32)
    TL = P.sb("TL", [128, NT], BF16)
    SLG = P.sb("SLG", [128, NT], BF16)
    f32t = Ring([P.sb("f32t%d" % i, [128, NT], F32) for i in range(8)])
    AT = P.sb("AT", [128, 8, NT], BF16); KTt = P.sb("KTt", [128, 8, NT], BF16); BTt = P.sb("BTt", [128, 8, NT], BF16)
    VTf = P.sb("VTf", [128, 8, NT], BF16); RTt = P.sb("RTt", [128, 8, NT], BF16); KMR = P.sb("KMR", [128, 8, NT], BF16)
    PRD = Ring([P.sb("PRD%d" % i, [128, NT], BF16) for i in range(2)])
    CS = P.sb("CS", [128, 8, NB, 129], F32)
    GC = P.sb("GC", [128, NB, 8], F32)
    Ktok = P.sb("Ktok", [128, NB, D], BF16); Btok = P.sb("Btok", [128, NB, D], BF16); Vtok = P.sb("Vtok", [128, NB, D], BF16)
    gtok = P.sb("gtok", [128, NB, D], BF16)
    bcoef = P.sb("bcoef", [128, NB, 16], F32)
    ST = P.sb("ST", [128, 8, 64], F32); STb = P.sb("STb", [128, 8, 64], BF16)
    F1 = P.sb("F1", [128, D], F32); F2 = P.sb("F2", [128, D], F32)
    Yt, Y2 = F1, F2
    gns = P.sb("gns", [128, 80], F32)
    orwT = P.sb("orwT", [128, NB, 8, 128], BF16)
    orw = P.sb("orw", [128, D], BF16)
    q32 = F1
    SMAX = max(NSLOT, 8208)
    SC = P.sb("SC", [128, SMAX], F32)
    aoff = [0]

    def alias(key, shape, dt_):
        nbytes = int(np.prod(shape[1:])) * (2 if dt_ == BF16 else 4)
        a = Alias(SC, dt_, aoff[0], shape, key)
        aoff[0] += nbytes
        assert aoff[0] <= 4 * SMAX, aoff[0]
        return a
    amat = Ring([alias("amat%d" % i, [128, 4, 128], BF16) for i in range(4)])
    pq = [Ring([alias("pq%d_%d" % (h, i), [128, 2, 128], BF16) for i in range(2)]) for h in range(4)]
    Mt = [Ring([alias("Mt%d_%d" % (h, i), [128, 128], BF16) for i in range(2)]) for h in range(4)]
    Mfin = alias("Mfin", [128, 16, 128], BF16)
    AKs = alias("AKs", [128, 16, 128], BF16)
    ARB = alias("ARB", [128, 16, 128], BF16); ARK = alias("ARK", [128, 16, 128], BF16)
    Wb = alias("Wb", [128, 16, 64], BF16); Ub = alias("Ub", [128, 16, 64], BF16)
    sttmp = alias("sttmp", [128, 8, 64], F32)
    junk = P.sb("junk", [128, 2], BF16)
    tau = P.sb("tau", [128, 8], F32)
    dg = P.sb("dg", [128, 8, 128], BF16)
    kitl = Ring([P.sb("kitl%d" % i, [128, 512], BF16) for i in range(2)])
    kbt = Ring([P.sb("kbt%d" % i, [1, 512], BF16) for i in range(2)])
    ktl = Ring([P.sb("ktl%d" % i, [128, 512], BF16) for i in range(2)])
    vtl = Ring([P.sb("vtl%d" % i, [128, 4, 130], BF16) for i in range(2)])
    mbt = Ring([P.sb("mbt%d" % i, [128, 512], BF16) for i in range(2)])
    Rt = Ring([P.sb("Rt%d" % i, [128, 512], BF16) for i in range(2)])
    PTt = Ring([P.sb("PTt%d" % i, [128, 1024], BF16) for i in range(2)])
    rden = P.sb("rden", [128, 8], F32)
    attn = P.sb("attn", [128, D], BF16)
    attnT = P.sb("attnT", [128, NB, 8, 128], BF16)
    mixs = [attn]
    mixT = P.sb("mixT", [128, 8, NT], BF16)
    x1 = F1.reshape([128, 1, D])
    x1b = hn.reshape([128, D])
    x1T = P.sb("x1T", [128, 8, NT], BF16)
    actg = Ring([P.sb("actg%d" % i, [128, NT], BF16) for i in range(2)])
    actT = Ring([P.sb("actT%d" % i, [128, NT], BF16) for i in range(3)])
    yo = Ring([F2])

    V_(lambda e: e.memset(ST[:], 0.0), w=[ST])
    V_(lambda e: e.memset(STb[:], 0.0), w=[STb])
    V_(lambda e: e.memset(car[:], 0.0), w=[car])
    V_(lambda e: e.memset(CS[:], 0.0), w=[CS])
    for t in vtl.t:
        V_(lambda e, t=t: e.memset(t[:], 1.0), w=[t])

    def load_w(view, c0, ncol):
        t = wt.next()
        P.dma('sp', t[:, :, 0:ncol], view[:, :, c0:c0 + ncol], r=[view.tensor], w=[t])
        return t

    def ln_rows(x_ap, L, n, out_ap, eps, g_b=None):
        xreg, oreg = x_ap.tensor, out_ap.tensor
        sm = small.next()
        nch = (n + 511) // 512
        for c in range(nch):
            V_(lambda e, c=c: e.bn_stats(out=sm[:L, 6 * c:6 * c + 6], in_=x_ap[:, 512 * c:min(n, 512 * (c + 1))]), r=[xreg], w=[sm])
        V_(lambda e: e.bn_aggr(out=sm[:L, 12:14], in_=sm[:L, 0:6 * nch]), r=[sm], w=[sm])
        A_(lambda e: e.activation(out=sm[:L, 14:15], in_=sm[:L, 13:14], func=AF.Sqrt, bias=float(eps), scale=1.0), r=[sm], w=[sm])
        V_(lambda e: e.reciprocal(out=sm[:L, 15:16], in_=sm[:L, 14:15]), r=[sm], w=[sm])
        V_(lambda e: e.tensor_scalar(out=sm[:L, 16:17], in0=sm[:L, 12:13], scalar1=sm[:L, 15:16], scalar2=-1.0, op0=ALU.mult, op1=ALU.mult),
           r=[sm], w=[sm])
        A_(lambda e: e.activation(out=out_ap, in_=x_ap, func=AF.Identity, scale=sm[:L, 15:16], bias=sm[:L, 16:17]),
           r=[sm, xreg], w=[oreg])
        if g_b is not None:
            g, b = g_b
            G_(lambda e: e.tensor_tensor(out=out_ap, in0=out_ap, in1=g[:L, 0:n], op=ALU.mult), r=[oreg, g], w=[oreg])
            G_(lambda e: e.tensor_tensor(out=out_ap, in0=out_ap, in1=b[:L, 0:n], op=ALU.add), r=[oreg, b], w=[oreg])

    def rope_rows(t, L, c0, half, cos_ap, sin_ap, nh=1, stride=0):
        tab = cos_ap.tensor
        tm = rtmp.next()
        if nh == 1:
            x1_ = t[:L, c0:c0 + half]; x2_ = t[:L, c0 + half:c0 + 2 * half]
            a = tm[:L, 0:half]; b = tm[:L, half:2 * half]; c = tm[:L, 2 * half:3 * half]; d = tm[:L, 3 * half:4 * half]
            cs, sn = cos_ap, sin_ap
        else:
            v = t[:L, c0:c0 + nh * stride].rearrange("p (h d) -> p h d", h=nh)
            x1_ = v[:, :, 0:half]; x2_ = v[:, :, half:2 * half]
            tv = tm[:L, 0:4 * nh * half].rearrange("p (q h d) -> p q h d", q=4, h=nh)
            a, b, c, d = tv[:, 0], tv[:, 1], tv[:, 2], tv[:, 3]
            cs = cos_ap.unsqueeze(1).to_broadcast([L, nh, half]); sn = sin_ap.unsqueeze(1).to_broadcast([L, nh, half])
        rr = [t, tm, tab]
        V_(lambda e: e.tensor_tensor(out=a, in0=cs, in1=x1_, op=ALU.mult), r=rr, w=[tm])
        V_(lambda e: e.tensor_tensor(out=b, in0=sn, in1=x2_, op=ALU.mult), r=rr, w=[tm])
        V_(lambda e: e.tensor_tensor(out=c, in0=cs, in1=x2_, op=ALU.mult), r=rr, w=[tm])
        V_(lambda e: e.tensor_tensor(out=d, in0=sn, in1=x1_, op=ALU.mult), r=rr, w=[tm])
        V_(lambda e: e.tensor_tensor(out=x1_, in0=a, in1=b, op=ALU.subtract), r=[tm], w=[t])
        V_(lambda e: e.tensor_tensor(out=x2_, in0=c, in1=d, op=ALU.add), r=[tm], w=[t])

    evq = [0]

    def evac_copy(out_ap, in_ap, r, w):
        evq[0] += 1
        if evq[0] % 2:
            A_(lambda e: e.activation(out=out_ap, in_=in_ap, func=AF.Copy), r=r, w=w)
        else:
            V_(lambda e: e.tensor_copy(out=out_ap, in_=in_ap), r=r, w=w)

    def transpose_to(dst_fn, src_fn, n, L, r, w, ident=None):
        for k in range(n):
            T_(lambda e, k=k: e.transpose(out=ptr[:, k, 0:L], in_=src_fn(k), identity=identB[:L, :L]), r=r + [identB], w=[ptr])

    def front(x_dram, rows, Ls, rope_rows0, slots, outs, own):
        O_k_, O_v_, O_ki_, orow = outs
        xt = xin.next()
        L = Ls[0]
        ntk = sum(Ls)
        for b in range(len(Ls)):
            P.dma('sp', xt[:L, b, :], x_dram.ap()[rows[b]:rows[b] + L, :], r=[x_dram], w=[(xt, b)])
            P.dma('sp', rp[b][:L, :], rope.ap()[rope_rows0[b]:rope_rows0[b] + L, :], r=[rope], w=[rp[b]])
        for b in range(len(Ls)):
            if own:
                ln_rows(xt[:L, b, :], L, D, hres[:L, b, :], LN_EPS)
                G_(lambda e, b=b: e.tensor_copy(out=hn[:L, b, :], in_=hres[:L, b, :]), r=[(hres, b)], w=[(hn, b)])
                vg, vbb = getvec("ln0_g"), getvec("ln0_b")
                G_(lambda e, b=b, vg=vg: e.tensor_tensor(out=hres[:L, b, :], in0=hres[:L, b, :], in1=vg[:L, :], op=ALU.mult),
                   r=[(hres, b), vg], w=[(hres, b)])
                G_(lambda e, b=b, vbb=vbb: e.tensor_tensor(out=hres[:L, b, :], in0=hres[:L, b, :], in1=vbb[:L, :], op=ALU.add),
                   r=[(hres, b), vbb], w=[(hres, b)])
            else:
                ln_rows(xt[:L, b, :], L, D, hn[:L, b, :], LN_EPS)
            for k in range(8):
                T_(lambda e, b=b, k=k: e.transpose(out=ptr[:, k, 0:L], in_=hn[:L, b, 128 * k:128 * (k + 1)], identity=identB[:L, :L]),
                   r=[hn, identB], w=[ptr])
            for k in range(8):
                if k % 2:
                    A_(lambda e, b=b, k=k: e.activation(out=hT[:, k, L * b:L * b + L], in_=ptr[:, k, 0:L], func=AF.Identity,
                                                        scale=cols[:, O_G0 + k:O_G0 + k + 1], bias=cols[:, O_B0 + k:O_B0 + k + 1]),
                       r=[ptr, cols], w=[(hT, (b, k))])
                else:
                    V_(lambda e, b=b, k=k: e.tensor_scalar(out=hT[:, k, L * b:L * b + L], in0=ptr[:, k, 0:L],
                                                           scalar1=cols[:, O_G0 + k:O_G0 + k + 1], scalar2=cols[:, O_B0 + k:O_B0 + k + 1],
                                                           op0=ALU.mult, op1=ALU.add), r=[ptr, cols], w=[(hT, (b, k))])
        wk = wt.next()
        P.dma('sp', wk[:, :, 0:256], win_v[:, :, C_K:C_K + 256], r=[D_win], w=[wk])
        P.dma('sp', wk[:, :, 256:328], win_v[:, :, C_KI:C_KI + 72], r=[D_win], w=[wk])
        kt_t = ktt.next(); kit_t = kitt.next()
        for b in range(len(Ls)):
            pk = pm.next()
            for k in range(8):
                T_(lambda e, b=b, k=k, pk=pk: e.matmul(pk[:L, 0:328], lhsT=hT[:, k, L * b:L * b + L], rhs=wk[:, k, 0:328],
                                                    start=(k == 0), stop=(k == 7)), r=[hT, wk], w=[pk])
            kv = kvs[b]
            A_(lambda e, pk=pk, kv=kv: e.activation(out=kv[:L, :], in_=pk[:L, 0:328], func=AF.Copy), r=[pk], w=[kv])
            rt = rp[b]
            rope_rows(kv, L, 0, 16, rt[:L, 0:16], rt[:L, 16:32])
            k2 = ki2.next()
            ln_rows(kv[:L, 256:320], L, 64, k2[:L, :], LN_EPS, g_b=(ikg_b, ikb_b))
            rope_rows(k2, L, 0, 8, rt[:L, 32:40], rt[:L, 40:48])
            s0 = slots[b]; o0 = orow[b]
            P.dma('pool', O_k_.ap()[o0:o0 + L, :], kv[:L, 0:128], r=[kv], w=[(O_k_, o0)])
            P.dma('pool', O_v_.ap()[o0:o0 + L, :], kv[:L, 128:256], r=[kv], w=[(O_v_, o0)])
            P.dma('pool', O_ki_.ap()[o0:o0 + L, :], k2[:L, :], r=[k2], w=[(O_ki_, o0)])
            vb = vbt.next()
            G_(lambda e, kv=kv, vb=vb: e.tensor_copy(out=vb[:L, :], in_=kv[:L, 128:256]), r=[kv], w=[vb])
            P.dma('pool', D_v.ap()[s0:s0 + L, :], vb[:L, :], r=[vb], w=[(D_v, s0)])
            pt_ = pm.next()
            T_(lambda e, kv=kv, pt_=pt_: e.transpose(out=pt_[:, 0:L], in_=kv[:L, 0:128], identity=identF[:L, :L]), r=[kv, identF], w=[pt_])
            T_(lambda e, k2=k2, pt_=pt_: e.transpose(out=pt_[0:64, 128:128 + L], in_=k2[:L, 0:64], identity=identF[:L, :L]), r=[k2, identF], w=[pt_])
            V_(lambda e, b=b, kt_t=kt_t, pt_=pt_: e.tensor_copy(out=kt_t[:, L * b:L * b + L], in_=pt_[:, 0:L]), r=[pt_], w=[kt_t])
            V_(lambda e, b=b, kit_t=kit_t, pt_=pt_: e.tensor_copy(out=kit_t[:, L * b:L * b + L], in_=pt_[0:64, 128:128 + L]), r=[pt_], w=[kit_t])
            P.dma('pool', D_kt.ap()[:, s0:s0 + L], kt_t[:, L * b:L * b + L], r=[kt_t], w=[(D_kt, s0)])
            P.dma('pool', D_kit.ap()[:, s0:s0 + L], kit_t[:, L * b:L * b + L], r=[kit_t], w=[(D_kit, s0)])

    def own_proj(Ls, rope_rows0):
        L = Ls[0]
        nbk = len(Ls)
        wq = [load_w(win_v, C_Q + 512 * g, 512) for g in range(2)]
        for b in range(nbk):
            for g in range(2):
                pq_ = pm.next()
                for k in range(8):
                    T_(lambda e, b=b, g=g, k=k, pq_=pq_: e.matmul(pq_[:L, :], lhsT=hT[:, k, L * b:L * b + L], rhs=wq[g][:, k, :], start=(k == 0), stop=(k == 7)),
                       r=[hT, wq[g]], w=[pq_])
                evac_copy(q32[:L, 512 * g:512 * (g + 1)], pq_[:L, :], r=[pq_], w=[(q32, g)])
            rope_rows(q32, L, 0, 16, rp[b][:L, 0:16], rp[b][:L, 16:32], nh=8, stride=128)
            for h in range(8):
                T_(lambda e, h=h: e.transpose(out=pLT[:, 128 * h:128 * h + L], in_=q32[:L, 128 * h:128 * (h + 1)], identity=identF[:L, :L]),
                   r=[q32, identF], w=[(pLT, h // 4)])
            evac_copy(qT[:, b, 0:4, 0:L], pLT[:, 0:512].rearrange("p (h t) -> p h t", h=4)[:, :, 0:L], r=[(pLT, 0)], w=[(qT, b)])
            evac_copy(qT[:, b, 4:8, 0:L], pLT[:, 512:1024].rearrange("p (h t) -> p h t", h=4)[:, :, 0:L], r=[(pLT, 1)], w=[(qT, b)])
        wqi = load_w(win_v, C_QI, 512)
        for b in range(nbk):
            pq_ = pm.next()
            for k in range(8):
                T_(lambda e, b=b, k=k, pq_=pq_: e.matmul(pq_[:L, :], lhsT=hT[:, k, L * b:L * b + L], rhs=wqi[:, k, :], start=(k == 0), stop=(k == 7)),
                   r=[hT, wqi], w=[pq_])
            evac_copy(q32[:L, 0:512], pq_[:L, :], r=[pq_], w=[(q32, 0)])
            rope_rows(q32, L, 0, 8, rp[b][:L, 32:40], rp[b][:L, 40:48], nh=8, stride=64)
            pt_ = pm.next()
            for pp in range(4):
                T_(lambda e, pp=pp, pt_=pt_: e.transpose(out=pt_[:, 128 * pp:128 * pp + L], in_=q32[:L, 128 * pp:128 * (pp + 1)], identity=identF[:L, :L]),
                   r=[q32, identF], w=[pt_])
            evac_copy(qiT[:, b, :, 0:L], pt_[:, :].rearrange("p (h t) -> p h t", h=4)[:, :, 0:L], r=[pt_], w=[(qiT, b)])
        for g in range(4):
            wg_ = load_w(win_v, C_G + 512 * g, 512)
            for b in range(nbk):
                pq_ = pm.next()
                for k in range(8):
                    T_(lambda e, b=b, k=k, pq_=pq_, wg_=wg_: e.matmul(pq_[:L, :], lhsT=hT[:, k, L * b:L * b + L], rhs=wg_[:, k, :], start=(k == 0), stop=(k == 7)),
                       r=[hT, wg_], w=[pq_])
                A_(lambda e, b=b, g=g, pq_=pq_: e.activation(out=gsig[:L, b, 512 * g:512 * (g + 1)], in_=pq_[:L, :], func=AF.Sigmoid), r=[pq_], w=[(gsig, (b, g))])

    def rw_tile(c, wtile, wcol, Ls, out_fn, carry_aps, save_last):
        L = Ls[0]
        ntk = sum(Ls)
        ps_ = pm.next()
        for k in range(8):
            T_(lambda e, k=k: e.matmul(ps_[:, 0:ntk], lhsT=wtile[:, k, wcol:wcol + 128], rhs=hT[:, k, 0:ntk], start=(k == 0), stop=(k == 7)),
               r=[hT, wtile], w=[ps_])
        tmp = f32t.next(); xs = f32t.next()
        A_(lambda e: e.activation(out=tmp[:, 0:ntk], in_=ps_[:, 0:ntk], func=AF.Identity, scale=colsd[:, c:c + 1]), r=[ps_, colsd], w=[tmp])
        if ntk > 1:
            V_(lambda e: e.scalar_tensor_tensor(out=xs[:, 1:ntk], in0=ps_[:, 0:ntk - 1], scalar=cols[:, O_MU + c:O_MU + c + 1],
                                                in1=tmp[:, 1:ntk], op0=ALU.mult, op1=ALU.add), r=[ps_, cols, tmp], w=[xs])
        for b in range(len(Ls)):
            ca = carry_aps[b]
            if ca is None:
                continue
            cap, creg = ca
            V_(lambda e, b=b, cap=cap: e.scalar_tensor_tensor(out=xs[:, L * b:L * b + 1], in0=cap[:, c:c + 1], scalar=cols[:, O_MU + c:O_MU + c + 1],
                                                             in1=tmp[:, L * b:L * b + 1], op0=ALU.mult, op1=ALU.add), r=[creg, cols, tmp], w=[xs])
        for b in range(len(Ls)):
            sl = save_last[b]
            if sl is None:
                continue
            sap, sreg = sl
            V_(lambda e, b=b, sap=sap: e.tensor_copy(out=sap[:, c:c + 1], in_=ps_[:, L * b + L - 1:L * b + L]), r=[ps_], w=[sreg])
        out_fn(xs)

    def rwkv_prep(Ls, own, carry_aps, save_last):
        ntk = sum(Ls)
        nb = len(Ls)
        L = Ls[0]
        rwt = lambda c, wtile, wcol, fn: rw_tile(c, wtile, wcol, Ls, fn, carry_aps, save_last)
        wl = load_w(win_v, C_RW + 3072, 256)

        def f24(xs):
            A_(lambda e: e.activation(out=TL[0:64, 0:ntk], in_=xs[0:64, 0:ntk], func=AF.Tanh), r=[xs], w=[TL])
            V_(lambda e: e.tensor_copy(out=TL[64:128, 0:ntk], in_=xs[64:128, 0:ntk]), r=[xs], w=[TL])
        rwt(24, wl, 0, f24)
        if own:
            def f25(xs):
                A_(lambda e: e.activation(out=SLG[:, 0:ntk], in_=xs[:, 0:ntk], func=AF.Sigmoid), r=[xs], w=[SLG])
            rwt(25, wl, 128, f25)
            for b in range(nb):
                for g in range(2):
                    pg = pm.next()
                    T_(lambda e, b=b, g=g, pg=pg: e.matmul(pg[:L, :], lhsT=SLG[:, L * b:L * b + L], rhs=g2b[:, 512 * g:512 * (g + 1)], start=True, stop=True),
                       r=[SLG, g2b], w=[pg])
                    evac_copy(gtok[:L, b, 512 * g:512 * (g + 1)], pg[:L, :], r=[pg], w=[(gtok, (b, g))])

        def blkview(t):
            return t[:, 0:ntk].rearrange("p (b l) -> p b l", b=nb)

        for half in range(2):
            wk_ = load_w(win_v, C_RW + 1024 + 512 * half, 512)
            for pp in range(4):
                p = 4 * half + pp

                def fk(kraw, p=p):
                    ps1 = pm.next()
                    T_(lambda e: e.matmul(ps1[:, 0:ntk], lhsT=w2b[0:64, 128 * p:128 * (p + 1)], rhs=TL[0:64, 0:ntk], start=True, stop=True),
                       r=[w2b, TL], w=[ps1])
                    sg = f32t.next()
                    A_(lambda e: e.activation(out=sg[:, 0:ntk], in_=ps1[:, 0:ntk], func=AF.Sigmoid, bias=cols[:, O_W0 + p:O_W0 + p + 1], scale=1.0),
                       r=[ps1, cols], w=[sg])
                    for b in range(nb):
                        V_(lambda e, b=b: e.tensor_tensor_scan(out=CS[:, p, b, 1:1 + L], data0=ones128[:, 0:L], data1=sg[:, L * b:L * (b + 1)],
                                                               initial=0.0, op0=ALU.mult, op1=ALU.add), r=[ones128, sg], w=[(CS, p)])
                    ps2 = pm.next()
                    T_(lambda e: e.matmul(ps2[:, 0:ntk], lhsT=a2b[64:128, 128 * p:128 * (p + 1)], rhs=TL[64:128, 0:ntk], start=True, stop=True),
                       r=[a2b, TL], w=[ps2])
                    asig = f32t.next()
                    A_(lambda e: e.activation(out=asig[:, 0:ntk], in_=ps2[:, 0:ntk], func=AF.Sigmoid, bias=cols[:, O_A0 + p:O_A0 + p + 1], scale=1.0),
                       r=[ps2, cols], w=[asig])
                    eex = f32t.next(); em = f32t.next()
                    A_(lambda e: e.activation(out=blkview(eex), in_=CS[:, p, 0:nb, 0:L], func=AF.Exp, scale=-DEC), r=[(CS, p)], w=[eex])
                    A_(lambda e: e.activation(out=blkview(em), in_=CS[:, p, 0:nb, 1:1 + L], func=AF.Exp, scale=DEC), r=[(CS, p)], w=[em])
                    A_(lambda e: e.activation(out=GC[:, 0:nb, p], in_=CS[:, p, 0:nb, L], func=AF.Exp, scale=-DEC), r=[(CS, p)], w=[(GC, p)])
                    kkr = f32t.next(); sq = f32t.next()
                    V_(lambda e: e.tensor_scalar(out=kkr[:, 0:ntk], in0=kraw[:, 0:ntk], scalar1=cols[:, O_KK + p:O_KK + p + 1], scalar2=None, op0=ALU.mult),
                       r=[kraw, cols], w=[kkr])
                    G_(lambda e: e.tensor_tensor(out=sq[:, 0:ntk], in0=kkr[:, 0:ntk], in1=kkr[:, 0:ntk], op=ALU.mult), r=[kkr], w=[sq])
                    ps3 = pm.next()
                    T_(lambda e: e.matmul(ps3[:, 0:ntk], lhsT=blkones[:], rhs=sq[:, 0:ntk], start=True, stop=True), r=[blkones, sq], w=[ps3])
                    A_(lambda e: e.activation(out=sq[:, 0:ntk], in_=ps3[:, 0:ntk], func=AF.Sqrt), r=[ps3], w=[sq])
                    V_(lambda e: e.tensor_scalar(out=sq[:, 0:ntk], in0=sq[:, 0:ntk], scalar1=1e-12, scalar2=None, op0=ALU.max), r=[sq], w=[sq])
                    V_(lambda e: e.reciprocal(out=sq[:, 0:ntk], in_=sq[:, 0:ntk]), r=[sq], w=[sq])
                    V_(lambda e: e.tensor_tensor(out=kkr[:, 0:ntk], in0=kkr[:, 0:ntk], in1=sq[:, 0:ntk], op=ALU.mult), r=[kkr, sq], w=[kkr])
                    V_(lambda e: e.tensor_scalar(out=sq[:, 0:ntk], in0=asig[:, 0:ntk], scalar1=cols[:, O_KA + p:O_KA + p + 1],
                                                 scalar2=colsd[:, 26 + p:27 + p], op0=ALU.mult, op1=ALU.add), r=[asig, cols, colsd], w=[sq])
                    G_(lambda e: e.tensor_tensor(out=sq[:, 0:ntk], in0=sq[:, 0:ntk], in1=kraw[:, 0:ntk], op=ALU.mult), r=[sq, kraw], w=[sq])
                    G_(lambda e: e.tensor_tensor(out=asig[:, 0:ntk], in0=asig[:, 0:ntk], in1=kkr[:, 0:ntk], op=ALU.mult), r=[asig, kkr], w=[asig])
                    V_(lambda e: e.scalar_tensor_tensor(out=AT[:, p, 0:ntk], in0=kkr[:, 0:ntk], scalar=-1.0, in1=eex[:, 0:ntk], op0=ALU.mult, op1=ALU.mult),
                       r=[kkr, eex], w=[(AT, p)])
                    V_(lambda e: e.tensor_tensor(out=KTt[:, p, 0:ntk], in0=sq[:, 0:ntk], in1=em[:, 0:ntk], op=ALU.mult), r=[sq, em], w=[(KTt, p)])
                    G_(lambda e: e.tensor_tensor(out=BTt[:, p, 0:ntk], in0=asig[:, 0:ntk], in1=em[:, 0:ntk], op=ALU.mult), r=[asig, em], w=[(BTt, p)])
                    if own:
                        G_(lambda e: e.tensor_scalar(out=KMR[:, p, 0:ntk], in0=sq[:, 0:ntk], scalar1=cols[:, O_RK + p:O_RK + p + 1], scalar2=None, op0=ALU.mult),
                           r=[sq, cols], w=[(KMR, p)])
                rwt(8 + p, wk_, 128 * pp, fk)
        for half in range(2):
            wv_ = load_w(win_v, C_RW + 2048 + 512 * half, 512)
            for pp in range(4):
                p = 4 * half + pp

                def fv(xs, p=p):
                    A_(lambda e: e.activation(out=VTf[:, p, 0:ntk], in_=xs[:, 0:ntk], func=AF.Copy), r=[xs], w=[(VTf, p)])
                rwt(16 + p, wv_, 128 * pp, fv)
        if own:
            pbc = pO[:, 1024:1536]
            pbreg = (pO, 2)
            first = [True]
            for half in range(2):
                wr_ = load_w(win_v, C_RW + 512 * half, 512)
                for pp in range(4):
                    p = 4 * half + pp

                    def fr(xs, p=p):
                        ep = f32t.next()
                        A_(lambda e: e.activation(out=blkview(ep), in_=CS[:, p, 0:nb, 1:1 + L], func=AF.Exp, scale=-DEC), r=[(CS, p)], w=[ep])
                        V_(lambda e: e.tensor_tensor(out=RTt[:, p, 0:ntk], in0=xs[:, 0:ntk], in1=ep[:, 0:ntk], op=ALU.mult), r=[xs, ep], w=[(RTt, p)])
                        prd = PRD.next()
                        G_(lambda e: e.tensor_tensor(out=prd[:, 0:ntk], in0=xs[:, 0:ntk], in1=KMR[:, p, 0:ntk], op=ALU.mult), r=[xs, (KMR, p)], w=[prd])
                        for b in range(nb):
                            T_(lambda e, b=b: e.matmul(pbc[:L, 16 * b + 2 * p:16 * b + 2 * p + 2], lhsT=prd[:, L * b:L * b + L], rhs=blkonesB[:, 0:128:64],
                                                       start=first[0] and b == 0, stop=True, skip_group_check=True), r=[prd, blkonesB], w=[pbreg])
                        first[0] = False
                    rwt(p, wr_, 128 * pp, fr)
            first[0] = True
            V_(lambda e: e.tensor_copy(out=bcoef[:L, 0:nb, :], in_=pbc[:L, 0:16 * nb].rearrange("p (b h) -> p b h", b=nb)), r=[pbreg], w=[bcoef])
        for b in range(nb):
            for (src, dst) in ((KTt, Ktok), (BTt, Btok), (VTf, Vtok)):
                for p in range(8):
                    T_(lambda e, p=p, b=b, src=src: e.transpose(out=ptr[0:L, p, :], in_=src[:, p, L * b:L * (b + 1)], identity=identB[:]),
                       r=[(src, p), identB], w=[ptr])
                evac_copy(dst[0:L, b, :].rearrange("p (k j) -> p k j", k=8), ptr[0:L, :, :], r=[ptr], w=[(dst, b)])

    def chunk_scan(b, L, own):
        nl = max(int(np.ceil(np.log2(L))) - 1, 0)
        c0 = L * b
        Lk = 128 if L == 64 else L
        if L == 64:
            for t_ in (AKs, ARB, ARK):
                G_(lambda e, t_=t_: e.memset(t_[64:128, :, :], 0.0), w=[t_])
            G_(lambda e: e.memset(Vtok[64:128, b, :], 0.0), w=[(Vtok, b)])
            G_(lambda e: e.memset(Ub[64:128, :, :], 0.0), w=[Ub])
        for hg in range(4):
            heads = [4 * hg + i for i in range(4)]
            cur = {}
            for i, h in enumerate(heads):
                p, bp = h // 2, 64 * (h % 2)
                bank = cb_ap(i); breg = cbanks[i]
                at = AT[bp:bp + 64, p, c0:c0 + L]; bt = BTt[bp:bp + 64, p, c0:c0 + L]; kt = KTt[bp:bp + 64, p, c0:c0 + L]
                T_(lambda e, bank=bank, bt=bt, at=at: e.matmul(bank[0:L, 0:L], lhsT=bt, rhs=at, start=True, stop=True), r=[(AT, p), (BTt, p)], w=[breg])
                T_(lambda e, bank=bank, bt=bt, at=at: e.matmul(bank[0:L, 128:128 + L], lhsT=at, rhs=bt, start=False, stop=True, skip_group_check=True),
                   r=[(AT, p), (BTt, p)], w=[breg])
                T_(lambda e, bank=bank, kt=kt, at=at: e.matmul(bank[0:L, 256:256 + L], lhsT=kt, rhs=at, start=False, stop=True, skip_group_check=True),
                   r=[(AT, p), (KTt, p)], w=[breg])
                nm_ = 3
                if own:
                    rt_ = RTt[bp:bp + 64, p, c0:c0 + L]
                    T_(lambda e, bank=bank, bt=bt, rt_=rt_: e.matmul(bank[0:L, 384:384 + L], lhsT=bt, rhs=rt_, start=False, stop=True, skip_group_check=True),
                       r=[(RTt, p), (BTt, p)], w=[breg])
                    nm_ = 4
                am = amat.next()
                V_(lambda e, bank=bank, am=am, nm_=nm_: e.tensor_tensor(out=am[0:L, 0:nm_, 0:L], in0=bank[0:L, 0:128 * nm_].rearrange("p (m t) -> p m t", m=nm_)[:, :, 0:L],
                                                                       in1=mask[0:L, 0:128 * nm_].rearrange("p (m t) -> p m t", m=nm_)[:, :, 0:L], op=ALU.mult),
                   r=[breg, mask], w=[am])
                G_(lambda e, am=am, h=h: e.tensor_copy(out=AKs[0:L, h, 0:L], in_=am[0:L, 2, 0:L]), r=[am], w=[(AKs, h)])
                if own:
                    G_(lambda e, am=am, h=h: e.tensor_copy(out=ARB[0:L, h, 0:L], in_=am[0:L, 3, 0:L]), r=[am], w=[(ARB, h)])
                m0 = Mt[i].next()
                G_(lambda e, am=am, m0=m0: e.tensor_tensor(out=m0[0:L, 0:L], in0=am[0:L, 0, 0:L], in1=identB[0:L, 0:L], op=ALU.add), r=[am, identB], w=[m0])
                cur[h] = dict(P=am[0:L, 0, 0:L], Q=am[0:L, 1, 0:L], Pt=am, Qt=am, M=m0)
            for lev in range(nl):
                last = (lev == nl - 1)
                for i, h in enumerate(heads):
                    bank = cb_ap(i); breg = cbanks[i]
                    c = cur[h]
                    if not last:
                        T_(lambda e, bank=bank, cq=c['Q'], cp=c['P']: e.matmul(bank[0:L, 0:L], lhsT=cq, rhs=cp, start=True, stop=True), r=[c['Pt'], c['Qt']], w=[breg])
                        T_(lambda e, bank=bank, cq=c['Q'], cp=c['P']: e.matmul(bank[0:L, 128:128 + L], lhsT=cp, rhs=cq, start=False, stop=True, skip_group_check=True),
                           r=[c['Pt'], c['Qt']], w=[breg])
                    else:
                        T_(lambda e, bank=bank, cq=c['Q'], cp=c['P']: e.matmul(bank[0:L, 128:128 + L], lhsT=cp, rhs=cq, start=True, stop=True),
                           r=[c['Pt'], c['Qt']], w=[breg])
                    nq = pq[i].next()
                    lo = 1 if last else 0
                    A_(lambda e, bank=bank, nq=nq, lo=lo: e.activation(out=nq[0:L, lo:2, 0:L],
                                                                      in_=bank[0:L, 128 * lo:256].rearrange("p (m t) -> p m t", m=2 - lo)[:, :, 0:L], func=AF.Copy),
                       r=[breg], w=[nq])
                    c['P'] = nq[0:L, 0, 0:L]; c['Q'] = nq[0:L, 1, 0:L]; c['Pt'] = nq; c['Qt'] = nq
                for i, h in enumerate(heads):
                    bank = cb_ap(i); breg = cbanks[i]
                    c = cur[h]
                    pmb = pm.next()
                    T_(lambda e, pmb=pmb, cq=c['Q'], cm=c['M']: e.matmul(pmb[0:L, 0:L], lhsT=cq, rhs=cm[0:L, 0:L], start=True, stop=True),
                       r=[c['Qt'], c['M']], w=[pmb])
                    mn = Mt[i].next()
                    V_(lambda e, pmb=pmb, cm=c['M'], mn=mn: e.tensor_tensor(out=mn[0:L, 0:L], in0=pmb[0:L, 0:L], in1=cm[0:L, 0:L], op=ALU.add),
                       r=[pmb, c['M']], w=[mn])
                    c['M'] = mn
            for i, h in enumerate(heads):
                G_(lambda e, h=h, m=cur[h]['M']: e.tensor_copy(out=Mfin[0:L, h, 0:L], in_=m[0:L, 0:L]), r=[cur[h]['M']], w=[(Mfin, h)])
        for g in range(2):
            bank = cb_ap(g); breg = cbanks[g]
            for hh in range(8):
                h = 8 * g + hh
                p, bp = h // 2, 64 * (h % 2)
                T_(lambda e, bank=bank, hh=hh, p=p, bp=bp: e.matmul(bank[0:L, 64 * hh:64 * hh + 64], lhsT=AT[bp:bp + 64, p, c0:c0 + L], rhs=STb[bp:bp + 64, p, :],
                                                                   start=(hh == 0), stop=False, skip_group_check=True), r=[(AT, p), STb], w=[breg])
                T_(lambda e, bank=bank, hh=hh, h=h: e.matmul(bank[0:L, 64 * hh:64 * hh + 64], lhsT=AKs[0:Lk, h, 0:L], rhs=Vtok[0:Lk, b, 64 * h:64 * h + 64],
                                                            start=False, stop=True, skip_group_check=True), r=[(AKs, h), (Vtok, b)], w=[breg])
            evac_copy(Wb[0:L, 8 * g:8 * g + 8, :], bank[0:L, 0:512].rearrange("p (h i) -> p h i", h=8), r=[breg], w=[(Wb, g)])
        for g in range(2):
            bank = cb_ap(2 + g); breg = cbanks[2 + g]
            for hh in range(8):
                h = 8 * g + hh
                T_(lambda e, bank=bank, hh=hh, h=h: e.matmul(bank[0:L, 64 * hh:64 * hh + 64], lhsT=Mfin[0:L, h, 0:L], rhs=Wb[0:L, h, :],
                                                            start=(hh == 0), stop=True, skip_group_check=True), r=[(Mfin, h), (Wb, g)], w=[breg])
            evac_copy(Ub[0:L, 8 * g:8 * g + 8, :], bank[0:L, 0:512].rearrange("p (h i) -> p h i", h=8), r=[breg], w=[(Ub, g)])
        if own:
            for rnd in range(2):
                for par in range(2):
                    bank = cb_ap(par); breg = cbanks[par]
                    for j in range(4):
                        h = 8 * rnd + 2 * j + par
                        p, bp = h // 2, 64 * par
                        T_(lambda e, bank=bank, j=j, p=p, bp=bp: e.matmul(bank[0:L, 128 * j:128 * j + L], lhsT=KTt[bp:bp + 64, p, c0:c0 + L], rhs=RTt[bp:bp + 64, p, c0:c0 + L],
                                                                         start=(j == 0), stop=True, skip_group_check=True), r=[(KTt, p), (RTt, p)], w=[breg])
                for par in range(2):
                    bank = cb_ap(par); breg = cbanks[par]
                    for j in range(4):
                        h = 8 * rnd + 2 * j + par
                        V_(lambda e, bank=bank, j=j, h=h: e.tensor_tensor(out=ARK[0:L, h, 0:L], in0=bank[0:L, 128 * j:128 * j + L], in1=mask[0:L, 384:384 + L], op=ALU.mult),
                           r=[breg, mask], w=[(ARK, h)])
            for g in range(2):
                bank = cb_ap(2 + g); breg = cbanks[2 + g]
                for hh in range(8):
                    h = 8 * g + hh
                    p, bp = h // 2, 64 * (h % 2)
                    oc = bank[0:L, 64 * hh:64 * hh + 64]
                    T_(lambda e, oc=oc, hh=hh, p=p, bp=bp: e.matmul(oc, lhsT=RTt[bp:bp + 64, p, c0:c0 + L], rhs=STb[bp:bp + 64, p, :],
                                                                   start=(hh == 0), stop=False, skip_group_check=True), r=[(RTt, p), STb], w=[breg])
                    T_(lambda e, oc=oc, h=h: e.matmul(oc, lhsT=ARK[0:Lk, h, 0:L], rhs=Vtok[0:Lk, b, 64 * h:64 * h + 64], start=False, stop=False, skip_group_check=True),
                       r=[(ARK, h), (Vtok, b)], w=[breg])
                    T_(lambda e, oc=oc, h=h: e.matmul(oc, lhsT=ARB[0:Lk, h, 0:L], rhs=Ub[0:Lk, h, :], start=False, stop=True, skip_group_check=True),
                       r=[(ARB, h), (Ub, h // 8)], w=[breg])
                evac_copy(Yt[0:L, 512 * g:512 * (g + 1)], bank[0:L, 0:512], r=[breg], w=[(Yt, g)])
        bank = cb_ap(0); breg = cbanks[0]
        for h in range(16):
            p, bp = h // 2, 64 * (h % 2)
            T_(lambda e, h=h, p=p, bp=bp: e.matmul(bank[bp:bp + 64, 64 * p:64 * p + 64], lhsT=Ktok[0:L, b, 64 * h:64 * h + 64], rhs=Vtok[0:L, b, 64 * h:64 * h + 64],
                                                  start=(h < 2), stop=False, skip_group_check=True), r=[(Ktok, b), (Vtok, b)], w=[breg])
            T_(lambda e, h=h, p=p, bp=bp: e.matmul(bank[bp:bp + 64, 64 * p:64 * p + 64], lhsT=Btok[0:L, b, 64 * h:64 * h + 64], rhs=Ub[0:L, h, :],
                                                  start=False, stop=True, skip_group_check=True), r=[(Btok, b), (Ub, h // 8)], w=[breg])
        V_(lambda e: e.tensor_tensor(out=sttmp[:].rearrange("p k i -> p (k i)"), in0=bank[:, 0:512], in1=ST[:].rearrange("p k i -> p (k i)"), op=ALU.add),
           r=[breg, ST], w=[sttmp])
        V_(lambda e: e.tensor_tensor(out=ST[:], in0=GC[:, b, :].unsqueeze(2).to_broadcast([128, 8, 64]), in1=sttmp[:], op=ALU.mult),
           r=[sttmp, GC], w=[ST])
        A_(lambda e: e.activation(out=STb[:], in_=ST[:], func=AF.Copy), r=[ST], w=[STb])
        if own:
            rwkv_out(b, L)

    def rwkv_out(b, L):
        y3 = Yt[0:L, :].rearrange("p (h i) -> p h i", h=16)
        y23 = Y2[0:L, :].rearrange("p (h i) -> p h i", h=16)
        V_(lambda e: e.tensor_reduce(out=gns[0:L, 0:16], in_=y3, axis=mybir.AxisListType.X, op=ALU.add), r=[Yt], w=[gns])
        A_(lambda e: e.activation(out=Y2[0:L, :], in_=Yt[0:L, :], func=AF.Square), r=[Yt], w=[Y2])
        V_(lambda e: e.tensor_reduce(out=gns[0:L, 16:32], in_=y23, axis=mybir.AxisListType.X, op=ALU.add), r=[Y2], w=[gns])
        V_(lambda e: e.tensor_scalar(out=gns[0:L, 0:16], in0=gns[0:L, 0:16], scalar1=1.0 / 64, scalar2=None, op0=ALU.mult), r=[gns], w=[gns])
        V_(lambda e: e.tensor_tensor(out=gns[0:L, 32:48], in0=gns[0:L, 0:16], in1=gns[0:L, 0:16], op=ALU.mult), r=[gns], w=[gns])
        V_(lambda e: e.scalar_tensor_tensor(out=gns[0:L, 48:64], in0=gns[0:L, 16:32], scalar=1.0 / 64, in1=gns[0:L, 32:48], op0=ALU.mult, op1=ALU.subtract),
           r=[gns], w=[gns])
        A_(lambda e: e.activation(out=gns[0:L, 48:64], in_=gns[0:L, 48:64], func=AF.Sqrt, bias=float(GN_EPS), scale=1.0), r=[gns], w=[gns])
        V_(lambda e: e.reciprocal(out=gns[0:L, 48:64], in_=gns[0:L, 48:64]), r=[gns], w=[gns])
        V_(lambda e: e.tensor_scalar(out=gns[0:L, 64:80], in0=gns[0:L, 48:64], scalar1=-1.0, scalar2=None, op0=ALU.mult), r=[gns], w=[gns])
        V_(lambda e: e.tensor_tensor(out=y23, in0=gns[0:L, 0:16].unsqueeze(2).to_broadcast([L, 16, 64]), in1=y3, op=ALU.subtract), r=[gns, Yt], w=[Y2])
        V_(lambda e: e.tensor_tensor(out=y23, in0=gns[0:L, 64:80].unsqueeze(2).to_broadcast([L, 16, 64]), in1=y23, op=ALU.mult), r=[gns, Y2], w=[Y2])
        vg, vbb = getvec("gn_g"), getvec("gn_b")
        G_(lambda e: e.tensor_tensor(out=Y2[0:L, :], in0=Y2[0:L, :], in1=vg[0:L, :], op=ALU.mult), r=[Y2, vg], w=[Y2])
        G_(lambda e: e.tensor_tensor(out=Y2[0:L, :], in0=Y2[0:L, :], in1=vbb[0:L, :], op=ALU.add), r=[Y2, vbb], w=[Y2])
        V_(lambda e: e.tensor_tensor(out=y3, in0=bcoef[0:L, b, :].unsqueeze(2).to_broadcast([L, 16, 64]), in1=Vtok[0:L, b, :].rearrange("p (h i) -> p h i", h=16), op=ALU.mult),
           r=[bcoef, (Vtok, b)], w=[Yt])
        G_(lambda e: e.tensor_tensor(out=Y2[0:L, :], in0=Y2[0:L, :], in1=Yt[0:L, :], op=ALU.add), r=[Y2, Yt], w=[Y2])
        G_(lambda e: e.tensor_tensor(out=orw[0:L, :], in0=Y2[0:L, :], in1=gtok[0:L, b, :], op=ALU.mult), r=[Y2, gtok], w=[orw])
        for k in range(8):
            T_(lambda e, k=k: e.transpose(out=ptr[:, k, 0:L], in_=orw[0:L, 128 * k:128 * (k + 1)], identity=identB[:L, :L]), r=[orw, identB], w=[ptr])
        evac_copy(orwT[:, b, :, 0:L], ptr[:, :, 0:L], r=[ptr], w=[(orwT, b)])

    def attention(b, Lq, slot_lo, slot_hi, corner):
        S = slot_hi - slot_lo
        wi = kvs[b]
        for h in range(8):
            V_(lambda e, h=h: e.tensor_scalar(out=dg[0:Lq, h, 0:Lq], in0=identB[0:Lq, 0:Lq], scalar1=wi[0:Lq, 320 + h:321 + h], scalar2=None, op0=ALU.mult),
               r=[identB, wi], w=[dg])
        cidx = 0.125 * (8.0 ** -0.5)
        ntile = (S + 511) // 512
        for ti in range(ntile):
            s0 = slot_lo + 512 * ti
            n = min(512, slot_hi - s0)
            kit = kitl.next(); kb_ = kbt.next()
            P.dma('sp', kit[0:64, 0:n], D_kit.ap()[:, s0:s0 + n], r=[D_kit], w=[kit])
            P.dma('sp', kit[64:128, 0:n], D_kit.ap()[:, s0:s0 + n], r=[D_kit], w=[kit])
            P.dma('sp', kb_[:, 0:n], keyb.ap()[:, s0:s0 + n], r=[keyb], w=[kb_])
            psc = pm.next()
            T_(lambda e, psc=psc, kb_=kb_, n=n: e.matmul(psc[0:Lq, 0:n], lhsT=onesrow[0:1, 0:Lq], rhs=kb_[0:1, 0:n], start=True, stop=False), r=[onesrow, kb_], w=[psc])
            def emit_x(h, kit=kit, n=n):
                pp, bp = h // 2, 64 * (h % 2)
                xb_ = xbanks[xq[0] % 3]; xq[0] += 1
                px = xb_[0][:, 512 * xb_[1]:512 * (xb_[1] + 1)]
                T_(lambda e, px=px, pp=pp, bp=bp: e.matmul(px[0:Lq, 0:n], lhsT=qiT[bp:bp + 64, b, pp, 0:Lq], rhs=kit[bp:bp + 64, 0:n], start=True, stop=True),
                   r=[(qiT, b), kit], w=[xb_])
                return px, xb_
            nxt = emit_x(0)
            for h in range(8):
                px, xb_ = nxt
                if h < 7:
                    nxt = emit_x(h + 1)
                r_ = Rt.next()
                A_(lambda e, px=px, r_=r_, n=n: e.activation(out=r_[0:Lq, 0:n], in_=px[0:Lq, 0:n], func=AF.Relu, scale=cidx), r=[xb_], w=[r_])
                T_(lambda e, psc=psc, h=h, r_=r_, n=n: e.matmul(psc[0:Lq, 0:n], lhsT=dg[0:Lq, h, 0:Lq], rhs=r_[0:Lq, 0:n], start=False, stop=(h == 7)), r=[dg, r_], w=[psc])
            V_(lambda e, psc=psc, ti=ti, n=n: e.tensor_copy(out=SC[0:Lq, 512 * ti:512 * ti + n], in_=psc[0:Lq, 0:n]), r=[psc], w=[(SC, ti)])
        if corner:
            V_(lambda e: e.memset(SC[0:64, S - 64:S], -1e30), r=[], w=[SC])
        V_(lambda e: e.memset(tau[0:Lq, 0:1], 0.0), w=[tau])
        for it in range(NBIS):
            s_ = 16.0 * (0.5 ** (it + 1))
            V_(lambda e: e.tensor_scalar(out=junk[0:Lq, 0:1].to_broadcast([Lq, S]), in0=SC[0:Lq, 0:S], scalar1=tau[0:Lq, 0:1], scalar2=None,
                                         op0=ALU.is_ge, op1=ALU.add, accum_out=tau[0:Lq, 1:2]), r=[SC, tau], w=[junk, tau])
            V_(lambda e, s_=s_: e.tensor_scalar(out=tau[0:Lq, 2:3], in0=tau[0:Lq, 1:2], scalar1=TOPK - 0.5, scalar2=2.0 * s_, op0=ALU.is_ge, op1=ALU.mult), r=[tau], w=[tau])
            V_(lambda e, s_=s_: e.scalar_tensor_tensor(out=tau[0:Lq, 0:1], in0=tau[0:Lq, 2:3], scalar=-s_, in1=tau[0:Lq, 0:1], op0=ALU.add, op1=ALU.add), r=[tau], w=[tau])
        s_last = 16.0 * (0.5 ** NBIS)
        V_(lambda e: e.tensor_scalar(out=tau[0:Lq, 3:4], in0=tau[0:Lq, 0:1], scalar1=-s_last, scalar2=None, op0=ALU.add), r=[tau], w=[tau])
        blocks = []
        for ti in range(ntile):
            s0 = slot_lo + 512 * ti
            n = min(512, slot_hi - s0)
            for a in range((n + 127) // 128):
                blocks.append((ti, s0, n, a, min(128, n - 128 * a)))
        tiles = {}

        def tile_res(ti, s0, n):
            if ti in tiles:
                return tiles[ti]
            kt_ = ktl.next(); vt_ = vtl.next(); mb = mbt.next()
            P.dma('sp', kt_[:, 0:n], D_kt.ap()[:, s0:s0 + n], r=[D_kt], w=[kt_])
            nfull, rem = n // 128, n % 128
            if nfull:
                P.dma('sp', vt_[:, 0:nfull, 0:128], D_v.ap()[s0:s0 + 128 * nfull, :].rearrange("(a p) d -> p a d", p=128), r=[D_v], w=[vt_])
            if rem:
                P.dma('sp', vt_[0:rem, nfull, 0:128], D_v.ap()[s0 + 128 * nfull:s0 + n, :], r=[D_v], w=[vt_])
            V_(lambda e: e.tensor_scalar(out=mb[0:Lq, 0:n], in0=SC[0:Lq, 512 * ti:512 * ti + n], scalar1=tau[0:Lq, 3:4], scalar2=-30000.0,
                                         op0=ALU.is_lt, op1=ALU.mult), r=[(SC, ti), tau], w=[mb])
            tiles[ti] = (kt_, vt_, mb)
            return tiles[ti]

        def buf(j, hh):
            if j % 2 == 0:
                return pLT[:, 512 * hh:512 * hh + 512], (pLT, hh)
            return pm.t[hh][:, :], pm.t[hh]

        def emit_LT(j):
            ti, s0, n, a, ns = blocks[j]
            kt_, vt_, mb = tile_res(ti, s0, n)
            for hh in range(2):
                ap_, reg = buf(j, hh)
                T_(lambda e, ap_=ap_, hh=hh: e.matmul(ap_[0:ns, 0:4 * Lq], lhsT=kt_[:, 128 * a:128 * a + ns], rhs=qT[:, b, 4 * hh:4 * hh + 4, 0:Lq], start=True, stop=False),
                   r=[kt_, (qT, b)], w=[reg])
                T_(lambda e, ap_=ap_: e.matmul(ap_[0:ns, 0:4 * Lq], lhsT=mb[0:Lq, 128 * a:128 * a + ns], rhs=I4[0:Lq, :, 0:Lq], start=False, stop=True),
                   r=[mb, I4], w=[reg])
        emit_LT(0)
        for j in range(len(blocks)):
            if j + 1 < len(blocks):
                emit_LT(j + 1)
            ti, s0, n, a, ns = blocks[j]
            kt_, vt_, mb = tiles[ti]
            pt = PTt.next()
            for hh in range(2):
                ap_, reg = buf(j, hh)
                A_(lambda e, ap_=ap_, hh=hh, pt=pt, ns=ns: e.activation(out=pt[0:ns, 4 * Lq * hh:4 * Lq * (hh + 1)], in_=ap_[0:ns, 0:4 * Lq], func=AF.Exp, scale=128.0 ** -0.5),
                   r=[reg], w=[(pt, hh)])
            last = (j == len(blocks) - 1)
            for h in range(8):
                off = 512 * (h // 3) + 129 * (h % 3)
                T_(lambda e, h=h, off=off, ns=ns, a=a, pt=pt, vt_=vt_, st_=(j == 0 and h % 3 == 0), last=last: e.matmul(
                    pO[0:Lq, off:off + 129], lhsT=pt[0:ns, Lq * h:Lq * h + Lq], rhs=vt_[0:ns, a, 0:129], start=st_, stop=last, skip_group_check=True),
                   r=[pt, vt_], w=[(pO, h // 3)])
        for h in range(8):
            off = 512 * (h // 3) + 129 * (h % 3)
            V_(lambda e, h=h, off=off: e.reciprocal(out=rden[0:Lq, h:h + 1], in_=pO[0:Lq, off + 128:off + 129]), r=[(pO, h // 3)], w=[rden])
            V_(lambda e, h=h, off=off: e.tensor_scalar(out=attn[0:Lq, 128 * h:128 * h + 128], in0=pO[0:Lq, off:off + 128], scalar1=rden[0:Lq, h:h + 1], scalar2=None, op0=ALU.mult),
               r=[(pO, h // 3), rden], w=[attn])
        for k in range(8):
            T_(lambda e, k=k: e.transpose(out=ptr[:, k, 0:Lq], in_=attn[0:Lq, 128 * k:128 * (k + 1)], identity=identB[:Lq, :Lq]), r=[attn, identB], w=[ptr])
        evac_copy(attnT[:, b, :, 0:Lq], ptr[:, :, 0:Lq], r=[ptr], w=[(attnT, b)])


    def post(Ls, y_dram, yrows):
        L = Ls[0]
        nbk = len(Ls)
        ntk = sum(Ls)
        woa = [load_w(kview(D_woa), 512 * g, 512) for g in range(2)]
        for b in range(nbk):
            for g in range(2):
                po_ = pm.next()
                for k in range(8):
                    T_(lambda e, b=b, g=g, k=k, po_=po_: e.matmul(po_[:L, :], lhsT=attnT[:, b, k, 0:L], rhs=woa[g][:, k, :], start=(k == 0), stop=(k == 7)),
                       r=[(attnT, b), woa[g]], w=[po_])
                mx = mixs[b]
                V_(lambda e, b=b, g=g, po_=po_, mx=mx: e.tensor_tensor(out=mx[:L, 512 * g:512 * (g + 1)], in0=po_[:L, :],
                                                                    in1=gsig[:L, b, 512 * g:512 * (g + 1)], op=ALU.mult), r=[po_, gsig], w=[(mx, g)])
        wor = [load_w(kview(D_wor), 512 * g, 512) for g in range(2)]
        for b in range(nbk):
            mx = mixs[b]
            for g in range(2):
                po_ = pm.next()
                for k in range(8):
                    T_(lambda e, b=b, g=g, k=k, po_=po_: e.matmul(po_[:L, :], lhsT=orwT[:, b, k, 0:L], rhs=wor[g][:, k, :], start=(k == 0), stop=(k == 7)),
                       r=[(orwT, b), wor[g]], w=[po_])
                tq = rtmp.next()
                V_(lambda e, b=b, g=g, po_=po_, tq=tq: e.tensor_tensor(out=tq[:L, :], in0=po_[:L, :], in1=gsig[:L, b, D + 512 * g:D + 512 * (g + 1)], op=ALU.mult),
                   r=[po_, gsig], w=[tq])
                G_(lambda e, g=g, mx=mx, tq=tq: e.tensor_tensor(out=mx[:L, 512 * g:512 * (g + 1)], in0=mx[:L, 512 * g:512 * (g + 1)], in1=tq[:L, :], op=ALU.add),
                   r=[tq, (mx, g)], w=[(mx, g)])
            for k in range(8):
                T_(lambda e, k=k, mx=mx: e.transpose(out=ptr[:, k, 0:L], in_=mx[:L, 128 * k:128 * (k + 1)], identity=identB[:L, :L]), r=[mx, identB], w=[ptr])
            evac_copy(mixT[:, :, L * b:L * b + L], ptr[:, :, 0:L], r=[ptr], w=[(mixT, b)])
        wo = [load_w(kview(D_wout), 512 * g, 512) for g in range(2)]
        for b in range(nbk):
            for g in range(2):
                po_ = pm.next()
                for k in range(8):
                    T_(lambda e, b=b, g=g, k=k, po_=po_: e.matmul(po_[:L, :], lhsT=mixT[:, k, L * b:L * b + L], rhs=wo[g][:, k, :], start=(k == 0), stop=(k == 7)),
                       r=[(mixT, b), wo[g]], w=[po_])
                V_(lambda e, b=b, g=g, po_=po_: e.scalar_tensor_tensor(out=x1[:L, b, 512 * g:512 * (g + 1)], in0=hres[:L, b, 512 * g:512 * (g + 1)], scalar=float(ALPHA),
                                                                    in1=po_[:L, :], op0=ALU.mult, op1=ALU.add), r=[po_, (hres, b)], w=[(x1, b)])
            ln_rows(x1[:L, b, :], L, D, x1[:L, b, :], LN_EPS, g_b=(getvec("ln1_g"), getvec("ln1_b")))
            G_(lambda e, b=b: e.tensor_copy(out=x1b[:L, :], in_=x1[:L, b, :]), r=[x1], w=[x1b])
            for k in range(8):
                T_(lambda e, k=k: e.transpose(out=ptr[:, k, 0:L], in_=x1b[:L, 128 * k:128 * (k + 1)], identity=identB[:L, :L]), r=[x1b, identB], w=[ptr])
            evac_copy(x1T[:, :, L * b:L * b + L], ptr[:, :, 0:L], r=[ptr], w=[(x1T, b)])
        accs = [[(pLT, 0), (pLT, 1)], [(pO, 0), (pO, 1)]]
        acc_ap = lambda b, g: (accs[b][g][0])[:, 512 * accs[b][g][1]:512 * (accs[b][g][1] + 1)]
        nfc = DFF // 128
        fbanks = [pm.t[0], pm.t[1], (pO, 2)]
        fbq = [0]
        for fq in range(0, nfc, 4):
            nq = min(4, nfc - fq)
            wg_ = load_w(kview(D_wfg), 128 * fq, 128 * nq)
            wu_ = load_w(kview(D_wfu), 128 * fq, 128 * nq)
            wd_ = wt.next()
            wdv = wd_[:].rearrange("p (a c) n -> p a (c n)", c=2)
            P.dma('sp', wdv[:, 0:nq, :], D_wfd.ap()[128 * fq:128 * (fq + nq), :].rearrange("(a p) n -> p a n", p=128), r=[D_wfd], w=[wd_])
            def emit_gu(j, wg_=wg_, wu_=wu_):
                bg = fbanks[fbq[0] % 3]; bu = fbanks[(fbq[0] + 1) % 3]; fbq[0] += 2
                pg_ = bg[0][:, 512 * bg[1]:512 * bg[1] + 512] if isinstance(bg, tuple) else bg[:, :]
                pu_ = bu[0][:, 512 * bu[1]:512 * bu[1] + 512] if isinstance(bu, tuple) else bu[:, :]
                for k in range(8):
                    T_(lambda e, k=k: e.matmul(pg_[:, 0:ntk], lhsT=wg_[:, k, 128 * j:128 * (j + 1)], rhs=x1T[:, k, 0:ntk], start=(k == 0), stop=(k == 7)),
                       r=[x1T, wg_], w=[bg])
                ag = actg.next()
                A_(lambda e: e.activation(out=ag[:, 0:ntk], in_=pg_[:, 0:ntk], func=AF.Silu), r=[bg], w=[ag])
                for k in range(8):
                    T_(lambda e, k=k: e.matmul(pu_[:, 0:ntk], lhsT=wu_[:, k, 128 * j:128 * (j + 1)], rhs=x1T[:, k, 0:ntk], start=(k == 0), stop=(k == 7)),
                       r=[x1T, wu_], w=[bu])
                at_ = actT.next()
                V_(lambda e: e.tensor_tensor(out=at_[:, 0:ntk], in0=pu_[:, 0:ntk], in1=ag[:, 0:ntk], op=ALU.mult), r=[bu, ag], w=[at_])
                return at_

            def emit_down(j, at_, wd_=wd_, wdv=wdv, fq=fq):
                fc = fq + j
                for b in range(nbk):
                    for g in range(2):
                        T_(lambda e, b=b, g=g: e.matmul(acc_ap(b, g)[:L, :], lhsT=at_[:, L * b:L * b + L], rhs=wdv[:, j, 512 * g:512 * (g + 1)],
                                                        start=(fc == 0), stop=(fc == nfc - 1)), r=[at_, wd_], w=[accs[b][g]])
            pend = emit_gu(0)
            for j in range(nq):
                nxt_ = emit_gu(j + 1) if j + 1 < nq else None
                emit_down(j, pend)
                pend = nxt_
        for b in range(nbk):
            yt = yo.next()
            for g in range(2):
                V_(lambda e, b=b, g=g, yt=yt: e.scalar_tensor_tensor(out=yt[:L, 512 * g:512 * (g + 1)], in0=x1[:L, b, 512 * g:512 * (g + 1)], scalar=float(ALPHA),
                                                                  in1=acc_ap(b, g)[:L, :], op0=ALU.mult, op1=ALU.add), r=[accs[b][g], x1], w=[(yt, g)])
            ln_rows(yt[:L, :], L, D, yt[:L, :], LN_EPS, g_b=(getvec("ln2_g"), getvec("ln2_b")))
            P.dma('pool', y_dram.ap()[yrows[b]:yrows[b] + L, :], yt[:L, :], r=[yt], w=[(y_dram, yrows[b])])

    nso_sb = GEOM['NSO_B'] // NB
    so_blocks = [(NT * i, [128] * NB) for i in range(nso_sb)] + [(128 * GEOM['NSO_B'], [16])]
    for (row0, Ls) in so_blocks:
        L = Ls[0]
        if L == 16:
            V_(lambda e: e.tensor_scalar(out=ST[:].rearrange("p k i -> p (k i)"), in0=ST[:].rearrange("p k i -> p (k i)"), scalar1=flg[:, 0:1], scalar2=None, op0=ALU.mult),
               r=[ST, flg], w=[ST])
            A_(lambda e: e.activation(out=STb[:], in_=ST[:], func=AF.Copy), r=[ST], w=[STb])
            V_(lambda e: e.tensor_scalar(out=car[:], in0=car[:], scalar1=flg[:, 0:1], scalar2=None, op0=ALU.mult), r=[car, flg], w=[car])
        rows = [row0 + L * b for b in range(len(Ls))]
        front(xso, rows, Ls, rows, rows, (O_k, O_v, O_ki, rows), own=False)
        carry = [(car, car)] + [None] * (len(Ls) - 1)
        save = [None] * (len(Ls) - 1) + [(car, car)]
        rwkv_prep(Ls, False, carry, save)
        if L == 16:
            noop = lambda xs: None
            for half in range(2):
                wr_ = load_w(win_v, C_RW + 512 * half, 512)
                for pp in range(4):
                    rw_tile(4 * half + pp, wr_, 128 * pp, Ls, noop, carry, save)
            wl_ = load_w(win_v, C_RW + 3072, 256)
            rw_tile(25, wl_, 128, Ls, noop, carry, save)
        for b in range(len(Ls)):
            chunk_scan(b, L, own=False)

    for sbi in range(GEOM['NOWN_B'] // NB):
        Ls = [128] * NB
        rows = [NT * sbi + 128 * b for b in range(NB)]
        slots = [NSO + r_ for r_ in rows]
        front(xown, rows, Ls, slots, slots, (O_k, O_v, O_ki, slots), own=True)
        own_proj(Ls, slots)
        carry = [(car, car)] + [None] * (NB - 1)
        save = [None] * (NB - 1) + [(car, car)]
        rwkv_prep(Ls, True, carry, save)
        for b in range(NB):
            chunk_scan(b, 128, own=True)
        for b in range(NB):
            attention(b, 128, 0, slots[b] + 128, corner=True)
        post(Ls, O_y, rows)
    P.dma('sp', O_wkv.ap(), ST[:].rearrange("p k i -> p (k i)"), r=[ST], w=[O_wkv])
    P.dma('sp', O_shift.ap(), car[:], r=[car], w=[O_shift])

    if SAMPLE:
        for q in range(2):
            sb0 = NSLOT + SSTRIDE * q
            for i0 in range(0, CACHE_ROWS, 128):
                L = min(128, CACHE_ROWS - i0)
                ct = xin.next()
                P.dma('sp', ct[:L, 0, 0:128], ck.ap()[q, i0:i0 + L, :], r=[ck], w=[ct])
                P.dma('sp', ct[:L, 0, 128:192], cik.ap()[q, i0:i0 + L, :], r=[cik], w=[ct])
                pt_ = pm.next()
                T_(lambda e, ct=ct, pt_=pt_, L=L: e.transpose(out=pt_[:, 0:L], in_=ct[:L, 0, 0:128], identity=identF[:L, :L]), r=[ct, identF], w=[pt_])
                T_(lambda e, ct=ct, pt_=pt_, L=L: e.transpose(out=pt_[0:64, 128:128 + L], in_=ct[:L, 0, 128:192], identity=identF[:L, :L]), r=[ct, identF], w=[pt_])
                kt_t = ktt.next(); kit_t = kitt.next()
                V_(lambda e, kt_t=kt_t, pt_=pt_, L=L: e.tensor_copy(out=kt_t[:, 0:L], in_=pt_[:, 0:L]), r=[pt_], w=[kt_t])
                V_(lambda e, kit_t=kit_t, pt_=pt_, L=L: e.tensor_copy(out=kit_t[:, 0:L], in_=pt_[0:64, 128:128 + L]), r=[pt_], w=[kit_t])
                P.dma('pool', D_kt.ap()[:, sb0 + i0:sb0 + i0 + L], kt_t[:, 0:L], r=[kt_t], w=[(D_kt, sb0 + i0)])
                P.dma('pool', D_kit.ap()[:, sb0 + i0:sb0 + i0 + L], kit_t[:, 0:L], r=[kit_t], w=[(D_kit, sb0 + i0)])
        for q in range(2):
            P.dma('sp', cars[:, q, :], sshift.ap()[q], r=[sshift], w=[(cars, q)])
        for q in range(2):
            sb0 = NSLOT + SSTRIDE * q
            Ls = [64]
            rows = [64 * q]
            rrows = [NSO + NOWN + 64 * q]
            slots = [sb0 + CACHE_ROWS]
            front(xsm, rows, Ls, rrows, slots, (O_ks, O_vs, O_kis, rows), own=True)
            own_proj(Ls, rrows)
            rwkv_prep(Ls, True, [(cars[:, q, :], (cars, q))], [(cars[:, q, :], (cars, q))])
            P.dma('sp', ST[:].rearrange("p k i -> p (k i)"), swkv.ap()[q], r=[swkv], w=[ST])
            A_(lambda e: e.activation(out=STb[:], in_=ST[:], func=AF.Copy), r=[ST], w=[STb])
            chunk_scan(0, 64, own=True)
            P.dma('sp', O_wkvs.ap()[q], ST[:].rearrange("p k i -> p (k i)"), r=[ST], w=[(O_wkvs, q)])
            P.dma('sp', O_shifts.ap()[q], cars[:, q, :], r=[(cars, q)], w=[(O_shifts, q)])
            attention(0, 64, sb0, sb0 + CACHE_ROWS + 64, corner=False)
            post(Ls, O_ys, rows)
    nc = P.build()
    return nc, P


def make_consts():
    c = {}
    c["identf"] = np.eye(128, dtype=np.float32)
    b = np.zeros((128, 128), np.float32); b[:64, :64] = 1; b[64:, 64:] = 1
    c["blk1"] = b
    us = np.triu(np.ones((128, 128), np.float32), 1)
    ui = np.triu(np.ones((128, 128), np.float32), 0)
    c["cmask"] = np.concatenate([us, us.T, us, ui, ui], axis=1).astype(np.float32)
    return c


def rope_table(pos):
    pos = np.asarray(pos, np.float32)
    out = np.zeros((len(pos), 48), np.float32)
    for (rot, o) in ((32, 0), (16, 32)):
        inv = (np.float32(500000.0) ** (-np.arange(0, rot, 2, dtype=np.float32) / np.float32(rot))).astype(np.float32)
        ang = (pos[:, None] * inv[None]).astype(np.float32)
        h = rot // 2
        out[:, o:o + h] = np.cos(ang); out[:, o + h:o + 2 * h] = np.sin(ang)
    return out


def colpack(v, n):
    return np.ascontiguousarray(np.asarray(v, np.float32).reshape(n, 128).T)


def st_layout(s):
    s = np.asarray(s, np.float32).reshape(8, 2, 64, 64)
    return np.ascontiguousarray(s.transpose(1, 3, 0, 2).reshape(128, 512))


def st_unlayout(a):
    a = np.asarray(a, np.float32).reshape(2, 64, 8, 64)
    return np.ascontiguousarray(a.transpose(2, 0, 3, 1).reshape(16, 64, 64))


def prep_inputs(inp):
    NSO, NOWN, NSLOT = geom()
    f32 = lambda a: np.ascontiguousarray(np.asarray(a, np.float32))
    consts = make_consts()
    maps = []
    colsf = np.concatenate([
        colpack(inp["ln0_g"], 8), colpack(inp["ln0_b"], 8), colpack(inp["rw_mu"][0], 26), colpack(inp["rw_w0"][0], 8),
        colpack(inp["rw_a0"][0], 8), colpack(inp["rw_k_k"][0], 8), colpack(inp["rw_k_a"][0], 8), colpack(np.asarray(inp["rw_r_k"][0]).reshape(-1), 8)], axis=1)
    shared = dict(consts)
    shared.update(colsf=colsf, w_in=f32(inp["w_in"][0]), rw_w2=f32(inp["rw_w2"][0]), rw_a2=f32(inp["rw_a2"][0]), rw_g2=f32(inp["rw_g2"][0]),
                  ikg=f32(inp["idx_k_ln_g"][0]), ikb=f32(inp["idx_k_ln_b"][0]),
                  ln0_g=f32(inp["ln0_g"]), ln0_b=f32(inp["ln0_b"]), ln1_g=f32(inp["ln1_g"][0]), ln1_b=f32(inp["ln1_b"][0]),
                  ln2_g=f32(inp["ln2_g"][0]), ln2_b=f32(inp["ln2_b"][0]), gn_g=f32(inp["rw_gn_g"][0]), gn_b=f32(inp["rw_gn_b"][0]),
                  w_oa=f32(inp["w_o_attn"][0]), w_or=f32(inp["w_o_rwkv"][0]), w_out=f32(inp["w_out"][0]),
                  w_fg=f32(inp["ffn_w_gate"][0]), w_fu=f32(inp["ffn_w_up"][0]), w_fd=f32(inp["ffn_w_down"][0]))
    meta = f32(inp["meta_tokens"])
    past = int(np.asarray(inp["cache_k"]).shape[2]) - 16
    for c in range(8):
        b, hf = c // 2, c % 2
        xp = f32(inp["x_prompt"][b])
        nfr = NSO - 16
        if hf == 1:
            xso = np.concatenate([meta, xp[:nfr]], 0)
            pos_so = np.arange(NSO)
            xown = xp[nfr:nfr + NOWN]; pos_own = NSO + np.arange(NOWN)
        else:
            xso = np.concatenate([xp[nfr:2 * nfr], meta], 0)
            pos_so = np.concatenate([np.zeros(nfr), np.arange(16)])
            xown = xp[:NOWN]; pos_own = 16 + np.arange(NOWN)
        keyb = np.zeros((1, NSLOT + 2 * SSTRIDE), np.float32)
        if hf == 0:
            keyb[0, :nfr] = -1e30
        xs = f32(inp["x_sample"][2 * c:2 * c + 2]).reshape(128, D)
        pos_sm = np.concatenate([16 + past + np.arange(64)] * 2)
        m = dict(shared)
        m.update(xso=np.ascontiguousarray(xso), xown=np.ascontiguousarray(xown), xsm=xs,
                 rope=rope_table(np.concatenate([pos_so, pos_own, pos_sm])),
                 flag=np.full((128, 1), float(hf), np.float32), keyb=keyb.astype(ml_dtypes.bfloat16),
                 ck=f32(inp["cache_k"][0, 2 * c:2 * c + 2]), cv=f32(inp["cache_v"][0, 2 * c:2 * c + 2]), cik=f32(inp["cache_idx_k"][0, 2 * c:2 * c + 2]),
                 swkv=np.stack([st_layout(inp["state_wkv"][0, 2 * c + q]) for q in range(2)]),
                 sshift=np.stack([colpack(inp["state_shift"][0, 2 * c + q], 26) for q in range(2)]))
        maps.append(m)
    return maps


_CACHE = {}


def run_device(inp):
    key = (GEOM['NSO_B'], GEOM['NOWN_B'], GEOM['SAMPLE'], MAXOPS)
    if key not in _CACHE:
        _CACHE[key] = build_program()
    nc, P = _CACHE[key]
    maps = prep_inputs(inp)
    used = set(P.names)
    maps = [{k: v for k, v in m.items() if k in used} for m in maps]
    res = run_bass_kernel_spmd(nc, maps, core_ids=list(range(8)))
    return res.results


def uncol(a, n):
    return np.ascontiguousarray(np.asarray(a, np.float32).T.reshape(-1))


def kernel(**inputs):
    NSO, NOWN, NSLOT = geom()
    res = run_device(inputs)
    B = 4
    y_p = np.zeros((B, 2 * NOWN, D), np.float32)
    k_p = np.zeros((1, B, NSLOT, 128), np.float32); v_p = np.zeros((1, B, NSLOT, 128), np.float32); ki_p = np.zeros((1, B, NSLOT, 64), np.float32)
    wkv_p = np.zeros((1, B, 16, 64, 64), np.float32); sh_p = np.zeros((1, B, 3328), np.float32)
    y_s = np.zeros((16, 64, D), np.float32)
    k_s = np.zeros((1, 16, 64, 128), np.float32); v_s = np.zeros((1, 16, 64, 128), np.float32); ki_s = np.zeros((1, 16, 64, 64), np.float32)
    wkv_s = np.zeros((1, 16, 16, 64, 64), np.float32); sh_s = np.zeros((1, 16, 3328), np.float32)
    for c in range(8):
        b, hf = c // 2, c % 2
        r = res[c]
        y_p[b, hf * NOWN:(hf + 1) * NOWN] = r["O_y"]
        if hf == 1:
            k_p[0, b] = r["O_k"]; v_p[0, b] = r["O_v"]; ki_p[0, b] = r["O_ki"]
            wkv_p[0, b] = st_unlayout(r["O_wkv"]); sh_p[0, b] = uncol(r["O_shift"], 26)
        y_s[2 * c:2 * c + 2] = r["O_ys"].reshape(2, 64, D)
        k_s[0, 2 * c:2 * c + 2] = r["O_ks"].reshape(2, 64, 128); v_s[0, 2 * c:2 * c + 2] = r["O_vs"].reshape(2, 64, 128)
        ki_s[0, 2 * c:2 * c + 2] = r["O_kis"].reshape(2, 64, 64)
        for q in range(2):
            wkv_s[0, 2 * c + q] = st_unlayout(r["O_wkvs"][q]); sh_s[0, 2 * c + q] = uncol(r["O_shifts"][q], 26)
    return (y_p, y_s, k_p, v_p, ki_p, wkv_p, sh_p, k_s, v_s, ki_s, wkv_s, sh_s)
```

```python
import bisect
import numpy as np
import ml_dtypes
from contextlib import ExitStack
import concourse.bass as bass
import concourse.mybir as mybir
from concourse.bass_utils import run_bass_kernel_spmd

F32 = mybir.dt.float32
BF16 = mybir.dt.bfloat16
U8 = mybir.dt.uint8
ALU = mybir.AluOpType
AF = mybir.ActivationFunctionType

SAME_ENGINE_SYNC = True
MAXOPS = None
ENGS = ('pe', 'act', 'dve', 'pool', 'sp')


class Prog:
    def __init__(self):
        self.nc = bass.Bass("TRN2", target_bir_lowering=False)
        self.es = ExitStack()
        self.ops = []
        self.state = {}
        self.names = set()
        self.psum_names = set()

    def _nm(self, name):
        assert name not in self.names, name
        self.names.add(name)
        return name

    def sb(self, name, shape, dt):
        return self.es.enter_context(self.nc.sbuf_tensor(self._nm(name), list(shape), dt))

    def ps(self, name, shape, dt=F32):
        self.psum_names.add(name)
        return self.es.enter_context(self.nc.psum_tensor(self._nm(name), list(shape), dt))

    def dram(self, name, shape, dt, kind):
        return self.nc.dram_tensor(self._nm(name), list(shape), dt, kind=kind)

    @staticmethod
    def _reg(x):
        if isinstance(x, tuple):
            base, sub = x[0], (x[1],)
        else:
            base, sub = x, ()
        if isinstance(base, Alias):
            return base.name, (base.key,) + sub
        return base.name, sub

    @staticmethod
    def _rel(a, b):
        n = min(len(a), len(b))
        return a[:n] == b[:n]

    def _deps(self, idx, reads, writes, eng=None):
        deps = set()
        for x in reads:
            n, k = self._reg(x)
            st = self.state.setdefault(n, {})
            for kk, ent in st.items():
                if not self._rel(kk, k):
                    continue
                if ent[0] is not None:
                    deps.add(ent[0])
                if n in self.psum_names:
                    for rr in ent[1]:
                        if self.ops[rr]['eng'] != eng:
                            deps.add(rr)
        for x in writes:
            n, k = self._reg(x)
            st = self.state.setdefault(n, {})
            for kk, ent in st.items():
                if not self._rel(kk, k):
                    continue
                if ent[0] is not None:
                    deps.add(ent[0])
                deps.update(ent[1])
        for x in reads:
            n, k = self._reg(x)
            self.state[n].setdefault(k, [None, []])[1].append(idx)
        for x in writes:
            n, k = self._reg(x)
            st = self.state[n]
            for kk in [kk for kk in st if len(kk) >= len(k) and kk[:len(k)] == k]:
                del st[kk]
            st[k] = [idx, []]
        deps.discard(idx)
        return sorted(deps)

    def op(self, eng, fn, r=(), w=()):
        if MAXOPS is not None and len(self.ops) >= MAXOPS:
            return None
        idx = len(self.ops)
        self.ops.append(dict(eng=eng, fn=fn, deps=self._deps(idx, r, w, eng), dma=False, semkey=None))
        return idx

    def dma(self, q, out, in_, r=(), w=(), semkey=None, **kw):
        if MAXOPS is not None and len(self.ops) >= MAXOPS:
            return None
        idx = len(self.ops)
        deps = self._deps(idx, r, w)
        if semkey is None:
            wn = self._reg(w[0])[0]
            semkey = wn if not (wn.startswith('D_') or wn.startswith('O_')) else self._reg(r[0])[0]
        fn = (lambda e, out=out, in_=in_, kw=kw: e.dma_start(out=out, in_=in_, **kw))
        self.ops.append(dict(eng=q, fn=fn, deps=deps, dma=True, semkey=semkey))
        return idx

    def build(self):
        nc, ops = self.nc, self.ops
        n = len(ops)
        need_sig = [False] * n
        for i, o in enumerate(ops):
            for j in o['deps']:
                pj = ops[j]
                if pj['dma']:
                    continue
                if pj['eng'] != o['eng'] or o['dma'] or (SAME_ENGINE_SYNC and o['eng'] != 'pe'):
                    need_sig[j] = True
        cnt = {e: 0 for e in ENGS}
        sigval = [0] * n
        dcnt, semkeys = {}, []
        for i, o in enumerate(ops):
            if o['dma']:
                k = o['semkey']
                if k not in dcnt:
                    dcnt[k] = 0
                    semkeys.append(k)
                dcnt[k] += 16
                sigval[i] = dcnt[k]
            elif need_sig[i]:
                cnt[o['eng']] += 1
                sigval[i] = cnt[o['eng']]
        esem = {e: self.es.enter_context(nc.semaphore("s_" + e)) for e in ENGS}
        dsem = {k: self.es.enter_context(nc.semaphore("d_" + k)) for k in semkeys}
        self.n_sems = len(esem) + len(dsem)
        dma_idx = {}
        for i, o in enumerate(ops):
            if o['dma']:
                dma_idx.setdefault(o['semkey'], []).append(i)
        waited = {e: {} for e in ENGS}
        plan = {e: [] for e in ENGS}
        for i, o in enumerate(ops):
            E = o['eng']
            waits = {}
            for j in o['deps']:
                pj = ops[j]
                if pj['dma']:
                    key = ('d', pj['semkey'])
                    lst = dma_idx[pj['semkey']]
                    val_d = 16 * bisect.bisect_left(lst, i)
                else:
                    if pj['eng'] == E and not o['dma'] and (E == 'pe' or not SAME_ENGINE_SYNC):
                        continue
                    key = ('e', pj['eng'])
                val = val_d if pj['dma'] else sigval[j]
                if waited[E].get(key, 0) >= val:
                    continue
                waits[key] = max(waits.get(key, 0), val)
            for key, val in waits.items():
                waited[E][key] = val
            plan[E].append((i, waits))
        blk = self.es.enter_context(nc.Block())
        engobj = {'pe': 'tensor', 'act': 'scalar', 'dve': 'vector', 'pool': 'gpsimd', 'sp': 'sync'}

        def emit_for(E):
            def body(eng):
                for i, waits in plan[E]:
                    o = ops[i]
                    for (kind, k), val in waits.items():
                        eng.wait_ge(dsem[k] if kind == 'd' else esem[k], val)
                    ins = o['fn'](eng)
                    if o['dma']:
                        ins.then_inc(dsem[o['semkey']], 16)
                    elif need_sig[i]:
                        ins.then_inc(esem[E], 1)
                if E == 'sp':
                    for k, v in dcnt.items():
                        eng.wait_ge(dsem[k], v)
                    for e2 in ENGS:
                        if e2 != 'sp' and cnt[e2] > 0:
                            eng.wait_ge(esem[e2], cnt[e2])
            return body

        for E in ENGS:
            getattr(blk, engobj[E])(emit_for(E))
        self.es.close()
        return nc


class Alias:
    def __init__(self, base, dtype, byte_off, shape, key):
        es = 2 if dtype == BF16 else 4
        self.h = base.bitcast(dtype)
        self.off = byte_off // es
        self.shape = list(shape)
        self.name = base.name
        self.key = key
        n = int(np.prod(shape[1:]))
        v = self.h[0:shape[0], self.off:self.off + n]
        if len(shape) == 3:
            v = v.rearrange("p (a b) -> p a b", a=shape[1])
        self.v = v

    def __getitem__(self, idx):
        return self.v[idx]


class Ring:
    def __init__(self, tiles):
        self.t, self.i = tiles, 0

    def next(self):
        t = self.t[self.i % len(self.t)]
        self.i += 1
        return t


D = 1024
DFF = 2816
GEOM = dict(NSO_B=32, NOWN_B=32, SAMPLE=True)
NB = 1
NT = 128 * NB
WIN_COLS = 7240
C_Q, C_K, C_V, C_QI, C_KI, C_WI, C_G, C_RW = 0, 1024, 1152, 1280, 1792, 1856, 1864, 3912
LN_EPS = 1e-5
GN_EPS = 64e-5
DEC = 0.6065306597126334
ALPHA = 2.0 ** 0.25
CACHE_ROWS = 2064
SSTRIDE = 2176
NBIS = 19
TOPK = 256
O_G0, O_B0, O_MU, O_W0, O_A0, O_KK, O_KA, O_RK, NCOLS = 0, 8, 16, 42, 50, 58, 66, 74, 82


def geom():
    nso = 128 * GEOM['NSO_B'] + 16
    nown = 128 * GEOM['NOWN_B']
    return nso, nown, nso + nown


def build_program():
    NSO, NOWN, NSLOT = geom()
    SAMPLE = GEOM['SAMPLE']
    NSLOT_ALL = NSLOT + 2 * SSTRIDE
    P = Prog()
    nc = P.nc
    V_ = lambda fn, r=(), w=(): P.op('dve', fn, r, w)
    A_ = lambda fn, r=(), w=(): P.op('act', fn, r, w)
    G_ = lambda fn, r=(), w=(): P.op('pool', fn, r, w)
    T_ = lambda fn, r=(), w=(): P.op('pe', fn, r, w)

    din = lambda n, s, dt=F32: P.dram(n, s, dt, "ExternalInput")
    dout = lambda n, s, dt=F32: P.dram(n, s, dt, "ExternalOutput")
    dint = lambda n, s, dt=BF16: P.dram(n, s, dt, "Internal")
    xso = din("xso", [NSO, D]); xown = din("xown", [NOWN, D]); xsm = din("xsm", [128, D])
    rope = din("rope", [NSO + NOWN + 128, 48])
    flag = din("flag", [128, 1])
    colsf = din("colsf", [128, NCOLS])
    cmask = din("cmask", [128, 640])
    identf = din("identf", [128, 128])
    blk1 = din("blk1", [128, 128])
    keyb = din("keyb", [1, NSLOT_ALL], BF16)
    w_in = din("w_in", [D, WIN_COLS])
    rw_w2 = din("rw_w2", [64, D]); rw_a2 = din("rw_a2", [64, D]); rw_g2 = din("rw_g2", [128, D])
    ikg = din("ikg", [64]); ikb = din("ikb", [64])
    vecs = {n: din(n, [D]) for n in ("ln0_g", "ln0_b", "ln1_g", "ln1_b", "ln2_g", "ln2_b", "gn_g", "gn_b")}
    w_oa = din("w_oa", [D, D]); w_or = din("w_or", [D, D]); w_out = din("w_out", [D, D])
    w_fg = din("w_fg", [D, DFF]); w_fu = din("w_fu", [D, DFF]); w_fd = din("w_fd", [DFF, D])
    ck = din("ck", [2, CACHE_ROWS, 128]); cv = din("cv", [2, CACHE_ROWS, 128]); cik = din("cik", [2, CACHE_ROWS, 64])
    swkv = din("swkv", [2, 128, 512]); sshift = din("sshift", [2, 128, 26])

    O_y = dout("O_y", [NOWN, D]); O_ys = dout("O_ys", [128, D])
    O_k = dout("O_k", [NSLOT, 128]); O_v = dout("O_v", [NSLOT, 128]); O_ki = dout("O_ki", [NSLOT, 64])
    O_wkv = dout("O_wkv", [128, 512]); O_shift = dout("O_shift", [128, 26])
    O_ks = dout("O_ks", [128, 128]); O_vs = dout("O_vs", [128, 128]); O_kis = dout("O_kis", [128, 64])
    O_wkvs = dout("O_wkvs", [2, 128, 512]); O_shifts = dout("O_shifts", [2, 128, 26])

    D_win = dint("D_win", [D, WIN_COLS])
    D_w2 = dint("D_w2", [64, D]); D_a2 = dint("D_a2", [64, D]); D_g2 = dint("D_g2", [128, D])
    D_woa = dint("D_woa", [D, D]); D_wor = dint("D_wor", [D, D]); D_wout = dint("D_wout", [D, D])
    D_wfg = dint("D_wfg", [D, DFF]); D_wfu = dint("D_wfu", [D, DFF]); D_wfd = dint("D_wfd", [DFF, D])
    D_kt = dint("D_kt", [128, NSLOT_ALL]); D_kit = dint("D_kit", [64, NSLOT_ALL]); D_v = dint("D_v", [NSLOT_ALL, 128])

    def cast_rows(dst, src, nrows, step):
        for i in range(0, nrows, step):
            n = min(step, nrows - i)
            P.dma('pool', dst.ap()[i:i + n, :], src.ap()[i:i + n, :], r=[src], w=[(dst, i)])
    cast_rows(D_win, w_in, D, 128)
    P.dma('pool', D_w2.ap(), rw_w2.ap(), r=[rw_w2], w=[D_w2])
    P.dma('pool', D_a2.ap(), rw_a2.ap(), r=[rw_a2], w=[D_a2])
    P.dma('pool', D_g2.ap(), rw_g2.ap(), r=[rw_g2], w=[D_g2])
    for (dd, ss, nr) in ((D_woa, w_oa, D), (D_wor, w_or, D), (D_wout, w_out, D), (D_wfg, w_fg, D), (D_wfu, w_fu, D), (D_wfd, w_fd, DFF)):
        cast_rows(dd, ss, nr, 256)
    if SAMPLE:
        for q in range(2):
            sb0 = NSLOT + SSTRIDE * q
            P.dma('pool', D_v.ap()[sb0:sb0 + CACHE_ROWS, :], cv.ap()[q], r=[cv], w=[(D_v, 'c%d' % q)])
    kview = lambda dt_: dt_.ap().rearrange("(k p) n -> p k n", p=128)
    win_v = kview(D_win)

    cols = P.sb("cols", [128, NCOLS], F32)
    colsd = P.sb("colsd", [128, 34], F32)
    identF = P.sb("identF", [128, 128], F32)
    identB = P.sb("identB", [128, 128], BF16)
    I4 = P.sb("I4", [128, 4, 128], BF16)
    blkones = P.sb("blkones", [128, 128], F32)
    blkonesB = P.sb("blkonesB", [128, 128], BF16)
    mask = P.sb("mask", [128, 640], F32)
    ones128 = P.sb("ones128", [128, 128], F32)
    onesrow = P.sb("onesrow", [1, 128], BF16)
    ikg_b = P.sb("ikg_b", [128, 64], F32); ikb_b = P.sb("ikb_b", [128, 64], F32)
    w2b = P.sb("w2b", [128, D], BF16); a2b = P.sb("a2b", [128, D], BF16); g2b = P.sb("g2b", [128, D], BF16)
    flg = P.sb("flg", [128, 1], F32)
    vbr = Ring([P.sb("vbr%d" % i, [128, D], F32) for i in range(4)])

    def getvec(n):
        t = vbr.next()
        P.dma('sp', t[:], vecs[n].ap().partition_broadcast(128), r=[vecs[n]], w=[t])
        return t
    P.dma('sp', cols[:], colsf.ap(), r=[colsf], w=[cols])
    P.dma('sp', identF[:], identf.ap(), r=[identf], w=[identF])
    P.dma('sp', blkones[:], blk1.ap(), r=[blk1], w=[blkones])
    P.dma('sp', mask[:], cmask.ap(), r=[cmask], w=[mask])
    P.dma('sp', flg[:], flag.ap(), r=[flag], w=[flg])
    P.dma('sp', ikg_b[:], ikg.ap().partition_broadcast(128), r=[ikg], w=[ikg_b])
    P.dma('sp', ikb_b[:], ikb.ap().partition_broadcast(128), r=[ikb], w=[ikb_b])
    P.dma('sp', w2b[0:64, :], D_w2.ap(), r=[D_w2], w=[w2b])
    P.dma('sp', a2b[64:128, :], D_a2.ap(), r=[D_a2], w=[a2b])
    P.dma('sp', g2b[:], D_g2.ap(), r=[D_g2], w=[g2b])
    V_(lambda e: e.tensor_copy(out=identB[:], in_=identF[:]), r=[identF], w=[identB])
    for i in range(4):
        V_(lambda e, i=i: e.tensor_copy(out=I4[:, i, :], in_=identF[:]), r=[identF], w=[I4])
    V_(lambda e: e.tensor_copy(out=blkonesB[:], in_=blkones[:]), r=[blkones], w=[blkonesB])
    V_(lambda e: e.memset(ones128[:], 1.0), w=[ones128])
    V_(lambda e: e.memset(onesrow[:], 1.0), w=[onesrow])
    V_(lambda e: e.tensor_scalar(out=colsd[:, 0:26], in0=cols[:, O_MU:O_MU + 26], scalar1=-1.0, scalar2=1.0, op0=ALU.mult, op1=ALU.add),
       r=[cols], w=[colsd])
    V_(lambda e: e.tensor_scalar(out=colsd[:, 26:34], in0=cols[:, O_KA:O_KA + 8], scalar1=-1.0, scalar2=1.0, op0=ALU.mult, op1=ALU.add),
       r=[cols], w=[colsd])

    pm = Ring([P.ps("pm0", [128, 512]), P.ps("pm1", [128, 512])])
    ptr = P.ps("ptr", [128, 8, 128], BF16)
    pLT = P.ps("pLT", [128, 1024])
    pO = P.ps("pO", [128, 1536])
    cbanks = [(pLT, 0), (pLT, 1), (pO, 0), (pO, 1)]
    cb_ap = lambda i: (cbanks[i][0])[:, 512 * cbanks[i][1]:512 * (cbanks[i][1] + 1)]
    xbanks = [(pLT, 0), (pLT, 1), (pO, 2)]
    xq = [0]

    xin = Ring([P.sb("xin%d" % i, [128, NB, D], F32) for i in range(2)])
    hn = P.sb("hn", [128, NB, D], BF16)
    hres = P.sb("hres", [128, NB, D], F32)
    hT = P.sb("hT", [128, 8, NT], BF16)
    small = Ring([P.sb("small%d" % i, [128, 24], F32) for i in range(4)])
    wt = Ring([P.sb("wt%d" % i, [128, 8, 512], BF16) for i in range(3)])
    kvs = [P.sb("kvs%d" % i, [128, 328], F32) for i in range(NB)]
    rtmp = Ring([P.sb("rtmp%d" % i, [128, 512], F32) for i in range(2)])
    rp = [P.sb("rp%d" % i, [128, 48], F32) for i in range(NB)]
    vbt = Ring([P.sb("vbt%d" % i, [128, 128], BF16) for i in range(2)])
    ktt = Ring([P.sb("ktt%d" % i, [128, NT], BF16) for i in range(2)])
    kitt = Ring([P.sb("kitt%d" % i, [64, NT], BF16) for i in range(2)])
    ki2 = Ring([P.sb("ki2_%d" % i, [128, 64], F32) for i in range(2)])
    qT = P.sb("qT", [128, NB, 8, 128], BF16)
    qiT = P.sb("qiT", [128, NB, 4, 128], BF16)
    gsig = P.sb("gsig", [128, NB, 2 * D], BF16)
    car = P.sb("car", [128, 26], F32)
    cars = P.sb("cars", [128, 2, 26], F32)
    TL = P.sb("TL", [128, NT], BF16)
    SLG = P.sb("SLG", [128, NT], BF16)
    f32t = Ring([P.sb("f32t%d" % i, [128, NT], F32) for i in range(8)])
    AT = P.sb("AT", [128, 8, NT], BF16); KTt = P.sb("KTt", [128, 8, NT], BF16); BTt = P.sb("BTt", [128, 8, NT], BF16)
    VTf = P.sb("VTf", [128, 8, NT], BF16); RTt = P.sb("RTt", [128, 8, NT], BF16); KMR = P.sb("KMR", [128, 8, NT], BF16)
    PRD = Ring([P.sb("PRD%d" % i, [128, NT], BF16) for i in range(2)])
    CS = P.sb("CS", [128, 8, NB, 129], F32)
    GC = P.sb("GC", [128, NB, 8], F32)
    Ktok = P.sb("Ktok", [128, NB, D], BF16); Btok = P.sb("Btok", [128, NB, D], BF16); Vtok = P.sb("Vtok", [128, NB, D], BF16)
    gtok = P.sb("gtok", [128, NB, D], BF16)
    bcoef = P.sb("bcoef", [128, NB, 16], F32)
    ST = P.sb("ST", [128, 8, 64], F32); STb = P.sb("STb", [128, 8, 64], BF16)
    F1 = P.sb("F1", [128, D], F32); F2 = P.sb("F2", [128, D], F32)
    Yt, Y2 = F1, F2
    gns = P.sb("gns", [128, 80], F32)
    orwT = P.sb("orwT", [128, NB, 8, 128], BF16)
    orw = P.sb("orw", [128, D], BF16)
    q32 = F1
    SMAX = max(NSLOT, 8208)
    SC = P.sb("SC", [128, SMAX], F32)
    aoff = [0]

    def alias(key, shape, dt_):
        nbytes = int(np.prod(shape[1:])) * (2 if dt_ == BF16 else 4)
        a = Alias(SC, dt_, aoff[0], shape, key)
        aoff[0] += nbytes
        assert aoff[0] <= 4 * SMAX, aoff[0]
        return a
    amat = Ring([alias("amat%d" % i, [128, 4, 128], BF16) for i in range(4)])
    pq = [Ring([alias("pq%d_%d" % (h, i), [128, 2, 128], BF16) for i in range(2)]) for h in range(4)]
    Mt = [Ring([alias("Mt%d_%d" % (h, i), [128, 128], BF16) for i in range(2)]) for h in range(4)]
    Mfin = alias("Mfin", [128, 16, 128], BF16)
    AKs = alias("AKs", [128, 16, 128], BF16)
    ARB = alias("ARB", [128, 16, 128], BF16); ARK = alias("ARK", [128, 16, 128], BF16)
    Wb = alias("Wb", [128, 16, 64], BF16); Ub = alias("Ub", [128, 16, 64], BF16)
    sttmp = alias("sttmp", [128, 8, 64], F32)
    junk = P.sb("junk", [128, 2], BF16)
    tau = P.sb("tau", [128, 8], F32)
    dg = P.sb("dg", [128, 8, 128], BF16)
    kitl = Ring([P.sb("kitl%d" % i, [128, 512], BF16) for i in range(2)])
    kbt = Ring([P.sb("kbt%d" % i, [1, 512], BF16) for i in range(2)])
    ktl = Ring([P.sb("ktl%d" % i, [128, 512], BF16) for i in range(2)])
    vtl = Ring([P.sb("vtl%d" % i, [128, 4, 130], BF16) for i in range(2)])
    mbt = Ring([P.sb("mbt%d" % i, [128, 512], BF16) for i in range(2)])
    Rt = Ring([P.sb("Rt%d" % i, [128, 512], BF16) for i in range(2)])
    PTt = Ring([P.sb("PTt%d" % i, [128, 1024], BF16) for i in range(2)])
    rden = P.sb("rden", [128, 8], F32)
    attn = P.sb("attn", [128, D], BF16)
    attnT = P.sb("attnT", [128, NB, 8, 128], BF16)
    mixs = [attn]
    mixT = P.sb("mixT", [128, 8, NT], BF16)
    x1 = F1.reshape([128, 1, D])
    x1b = hn.reshape([128, D])
    x1T = P.sb("x1T", [128, 8, NT], BF16)
    actg = Ring([P.sb("actg%d" % i, [128, NT], BF16) for i in range(2)])
    actT = Ring([P.sb("actT%d" % i, [128, NT], BF16) for i in range(3)])
    yo = Ring([F2])

    V_(lambda e: e.memset(ST[:], 0.0), w=[ST])
    V_(lambda e: e.memset(STb[:], 0.0), w=[STb])
    V_(lambda e: e.memset(car[:], 0.0), w=[car])
    V_(lambda e: e.memset(CS[:], 0.0), w=[CS])
    for t in vtl.t:
        V_(lambda e, t=t: e.memset(t[:], 1.0), w=[t])

    def load_w(view, c0, ncol):
        t = wt.next()
        P.dma('sp', t[:, :, 0:ncol], view[:, :, c0:c0 + ncol], r=[view.tensor], w=[t])
        return t

    def ln_rows(x_ap, L, n, out_ap, eps, g_b=None):
        xreg, oreg = x_ap.tensor, out_ap.tensor
        sm = small.next()
        nch = (n + 511) // 512
        for c in range(nch):
            V_(lambda e, c=c: e.bn_stats(out=sm[:L, 6 * c:6 * c + 6], in_=x_ap[:, 512 * c:min(n, 512 * (c + 1))]), r=[xreg], w=[sm])
        V_(lambda e: e.bn_aggr(out=sm[:L, 12:14], in_=sm[:L, 0:6 * nch]), r=[sm], w=[sm])
        A_(lambda e: e.activation(out=sm[:L, 14:15], in_=sm[:L, 13:14], func=AF.Sqrt, bias=float(eps), scale=1.0), r=[sm], w=[sm])
        V_(lambda e: e.reciprocal(out=sm[:L, 15:16], in_=sm[:L, 14:15]), r=[sm], w=[sm])
        V_(lambda e: e.tensor_scalar(out=sm[:L, 16:17], in0=sm[:L, 12:13], scalar1=sm[:L, 15:16], scalar2=-1.0, op0=ALU.mult, op1=ALU.mult),
           r=[sm], w=[sm])
        A_(lambda e: e.activation(out=out_ap, in_=x_ap, func=AF.Identity, scale=sm[:L, 15:16], bias=sm[:L, 16:17]),
           r=[sm, xreg], w=[oreg])
        if g_b is not None:
            g, b = g_b
            G_(lambda e: e.tensor_tensor(out=out_ap, in0=out_ap, in1=g[:L, 0:n], op=ALU.mult), r=[oreg, g], w=[oreg])
            G_(lambda e: e.tensor_tensor(out=out_ap, in0=out_ap, in1=b[:L, 0:n], op=ALU.add), r=[oreg, b], w=[oreg])

    def rope_rows(t, L, c0, half, cos_ap, sin_ap, nh=1, stride=0):
        tab = cos_ap.tensor
        tm = rtmp.next()
        if nh == 1:
            x1_ = t[:L, c0:c0 + half]; x2_ = t[:L, c0 + half:c0 + 2 * half]
            a = tm[:L, 0:half]; b = tm[:L, half:2 * half]; c = tm[:L, 2 * half:3 * half]; d = tm[:L, 3 * half:4 * half]
            cs, sn = cos_ap, sin_ap
        else:
            v = t[:L, c0:c0 + nh * stride].rearrange("p (h d) -> p h d", h=nh)
            x1_ = v[:, :, 0:half]; x2_ = v[:, :, half:2 * half]
            tv = tm[:L, 0:4 * nh * half].rearrange("p (q h d) -> p q h d", q=4, h=nh)
            a, b, c, d = tv[:, 0], tv[:, 1], tv[:, 2], tv[:, 3]
            cs = cos_ap.unsqueeze(1).to_broadcast([L, nh, half]); sn = sin_ap.unsqueeze(1).to_broadcast([L, nh, half])
        rr = [t, tm, tab]
        V_(lambda e: e.tensor_tensor(out=a, in0=cs, in1=x1_, op=ALU.mult), r=rr, w=[tm])
        V_(lambda e: e.tensor_tensor(out=b, in0=sn, in1=x2_, op=ALU.mult), r=rr, w=[tm])
        V_(lambda e: e.tensor_tensor(out=c, in0=cs, in1=x2_, op=ALU.mult), r=rr, w=[tm])
        V_(lambda e: e.tensor_tensor(out=d, in0=sn, in1=x1_, op=ALU.mult), r=rr, w=[tm])
        V_(lambda e: e.tensor_tensor(out=x1_, in0=a, in1=b, op=ALU.subtract), r=[tm], w=[t])
        V_(lambda e: e.tensor_tensor(out=x2_, in0=c, in1=d, op=ALU.add), r=[tm], w=[t])

    evq = [0]

    def evac_copy(out_ap, in_ap, r, w):
        evq[0] += 1
        if evq[0] % 2:
            A_(lambda e: e.activation(out=out_ap, in_=in_ap, func=AF.Copy), r=r, w=w)
        else:
            V_(lambda e: e.tensor_copy(out=out_ap, in_=in_ap), r=r, w=w)

    def transpose_to(dst_fn, src_fn, n, L, r, w, ident=None):
        for k in range(n):
            T_(lambda e, k=k: e.transpose(out=ptr[:, k, 0:L], in_=src_fn(k), identity=identB[:L, :L]), r=r + [identB], w=[ptr])

    def front(x_dram, rows, Ls, rope_rows0, slots, outs, own):
        O_k_, O_v_, O_ki_, orow = outs
        xt = xin.next()
        L = Ls[0]
        ntk = sum(Ls)
        for b in range(len(Ls)):
            P.dma('sp', xt[:L, b, :], x_dram.ap()[rows[b]:rows[b] + L, :], r=[x_dram], w=[(xt, b)])
            P.dma('sp', rp[b][:L, :], rope.ap()[rope_rows0[b]:rope_rows0[b] + L, :], r=[rope], w=[rp[b]])
        for b in range(len(Ls)):
            if own:
                ln_rows(xt[:L, b, :], L, D, hres[:L, b, :], LN_EPS)
                G_(lambda e, b=b: e.tensor_copy(out=hn[:L, b, :], in_=hres[:L, b, :]), r=[(hres, b)], w=[(hn, b)])
                vg, vbb = getvec("ln0_g"), getvec("ln0_b")
                G_(lambda e, b=b, vg=vg: e.tensor_tensor(out=hres[:L, b, :], in0=hres[:L, b, :], in1=vg[:L, :], op=ALU.mult),
                   r=[(hres, b), vg], w=[(hres, b)])
                G_(lambda e, b=b, vbb=vbb: e.tensor_tensor(out=hres[:L, b, :], in0=hres[:L, b, :], in1=vbb[:L, :], op=ALU.add),
                   r=[(hres, b), vbb], w=[(hres, b)])
            else:
                ln_rows(xt[:L, b, :], L, D, hn[:L, b, :], LN_EPS)
            for k in range(8):
                T_(lambda e, b=b, k=k: e.transpose(out=ptr[:, k, 0:L], in_=hn[:L, b, 128 * k:128 * (k + 1)], identity=identB[:L, :L]),
                   r=[hn, identB], w=[ptr])
            for k in range(8):
                if k % 2:
                    A_(lambda e, b=b, k=k: e.activation(out=hT[:, k, L * b:L * b + L], in_=ptr[:, k, 0:L], func=AF.Identity,
                                                        scale=cols[:, O_G0 + k:O_G0 + k + 1], bias=cols[:, O_B0 + k:O_B0 + k + 1]),
                       r=[ptr, cols], w=[(hT, (b, k))])
                else:
                    V_(lambda e, b=b, k=k: e.tensor_scalar(out=hT[:, k, L * b:L * b + L], in0=ptr[:, k, 0:L],
                                                           scalar1=cols[:, O_G0 + k:O_G0 + k + 1], scalar2=cols[:, O_B0 + k:O_B0 + k + 1],
                                                           op0=ALU.mult, op1=ALU.add), r=[ptr, cols], w=[(hT, (b, k))])
        wk = wt.next()
        P.dma('sp', wk[:, :, 0:256], win_v[:, :, C_K:C_K + 256], r=[D_win], w=[wk])
        P.dma('sp', wk[:, :, 256:328], win_v[:, :, C_KI:C_KI + 72], r=[D_win], w=[wk])
        kt_t = ktt.next(); kit_t = kitt.next()
        for b in range(len(Ls)):
            pk = pm.next()
            for k in range(8):
                T_(lambda e, b=b, k=k, pk=pk: e.matmul(pk[:L, 0:328], lhsT=hT[:, k, L * b:L * b + L], rhs=wk[:, k, 0:328],
                                                    start=(k == 0), stop=(k == 7)), r=[hT, wk], w=[pk])
            kv = kvs[b]
            A_(lambda e, pk=pk, kv=kv: e.activation(out=kv[:L, :], in_=pk[:L, 0:328], func=AF.Copy), r=[pk], w=[kv])
            rt = rp[b]
            rope_rows(kv, L, 0, 16, rt[:L, 0:16], rt[:L, 16:32])
            k2 = ki2.next()
            ln_rows(kv[:L, 256:320], L, 64, k2[:L, :], LN_EPS, g_b=(ikg_b, ikb_b))
            rope_rows(k2, L, 0, 8, rt[:L, 32:40], rt[:L, 40:48])
            s0 = slots[b]; o0 = orow[b]
            P.dma('pool', O_k_.ap()[o0:o0 + L, :], kv[:L, 0:128], r=[kv], w=[(O_k_, o0)])
            P.dma('pool', O_v_.ap()[o0:o0 + L, :], kv[:L, 128:256], r=[kv], w=[(O_v_, o0)])
            P.dma('pool', O_ki_.ap()[o0:o0 + L, :], k2[:L, :], r=[k2], w=[(O_ki_, o0)])
            vb = vbt.next()
            G_(lambda e, kv=kv, vb=vb: e.tensor_copy(out=vb[:L, :], in_=kv[:L, 128:256]), r=[kv], w=[vb])
            P.dma('pool', D_v.ap()[s0:s0 + L, :], vb[:L, :], r=[vb], w=[(D_v, s0)])
            pt_ = pm.next()
            T_(lambda e, kv=kv, pt_=pt_: e.transpose(out=pt_[:, 0:L], in_=kv[:L, 0:128], identity=identF[:L, :L]), r=[kv, identF], w=[pt_])
            T_(lambda e, k2=k2, pt_=pt_: e.transpose(out=pt_[0:64, 128:128 + L], in_=k2[:L, 0:64], identity=identF[:L, :L]), r=[k2, identF], w=[pt_])
            V_(lambda e, b=b, kt_t=kt_t, pt_=pt_: e.tensor_copy(out=kt_t[:, L * b:L * b + L], in_=pt_[:, 0:L]), r=[pt_], w=[kt_t])
            V_(lambda e, b=b, kit_t=kit_t, pt_=pt_: e.tensor_copy(out=kit_t[:, L * b:L * b + L], in_=pt_[0:64, 128:128 + L]), r=[pt_], w=[kit_t])
            P.dma('pool', D_kt.ap()[:, s0:s0 + L], kt_t[:, L * b:L * b + L], r=[kt_t], w=[(D_kt, s0)])
            P.dma('pool', D_kit.ap()[:, s0:s0 + L], kit_t[:, L * b:L * b + L], r=[kit_t], w=[(D_kit, s0)])

    def own_proj(Ls, rope_rows0):
        L = Ls[0]
        nbk = len(Ls)
        wq = [load_w(win_v, C_Q + 512 * g, 512) for g in range(2)]
        for b in range(nbk):
            for g in range(2):
                pq_ = pm.next()
                for k in range(8):
                    T_(lambda e, b=b, g=g, k=k, pq_=pq_: e.matmul(pq_[:L, :], lhsT=hT[:, k, L * b:L * b + L], rhs=wq[g][:, k, :], start=(k == 0), stop=(k == 7)),
                       r=[hT, wq[g]], w=[pq_])
                evac_copy(q32[:L, 512 * g:512 * (g + 1)], pq_[:L, :], r=[pq_], w=[(q32, g)])
            rope_rows(q32, L, 0, 16, rp[b][:L, 0:16], rp[b][:L, 16:32], nh=8, stride=128)
            for h in range(8):
                T_(lambda e, h=h: e.transpose(out=pLT[:, 128 * h:128 * h + L], in_=q32[:L, 128 * h:128 * (h + 1)], identity=identF[:L, :L]),
                   r=[q32, identF], w=[(pLT, h // 4)])
            evac_copy(qT[:, b, 0:4, 0:L], pLT[:, 0:512].rearrange("p (h t) -> p h t", h=4)[:, :, 0:L], r=[(pLT, 0)], w=[(qT, b)])
            evac_copy(qT[:, b, 4:8, 0:L], pLT[:, 512:1024].rearrange("p (h t) -> p h t", h=4)[:, :, 0:L], r=[(pLT, 1)], w=[(qT, b)])
        wqi = load_w(win_v, C_QI, 512)
        for b in range(nbk):
            pq_ = pm.next()
            for k in range(8):
                T_(lambda e, b=b, k=k, pq_=pq_: e.matmul(pq_[:L, :], lhsT=hT[:, k, L * b:L * b + L], rhs=wqi[:, k, :], start=(k == 0), stop=(k == 7)),
                   r=[hT, wqi], w=[pq_])
            evac_copy(q32[:L, 0:512], pq_[:L, :], r=[pq_], w=[(q32, 0)])
            rope_rows(q32, L, 0, 8, rp[b][:L, 32:40], rp[b][:L, 40:48], nh=8, stride=64)
            pt_ = pm.next()
            for pp in range(4):
                T_(lambda e, pp=pp, pt_=pt_: e.transpose(out=pt_[:, 128 * pp:128 * pp + L], in_=q32[:L, 128 * pp:128 * (pp + 1)], identity=identF[:L, :L]),
                   r=[q32, identF], w=[pt_])
            evac_copy(qiT[:, b, :, 0:L], pt_[:, :].rearrange("p (h t) -> p h t", h=4)[:, :, 0:L], r=[pt_], w=[(qiT, b)])
        for g in range(4):
            wg_ = load_w(win_v, C_G + 512 * g, 512)
            for b in range(nbk):
                pq_ = pm.next()
                for k in range(8):
                    T_(lambda e, b=b, k=k, pq_=pq_, wg_=wg_: e.matmul(pq_[:L, :], lhsT=hT[:, k, L * b:L * b + L], rhs=wg_[:, k, :], start=(k == 0), stop=(k == 7)),
                       r=[hT, wg_], w=[pq_])
                A_(lambda e, b=b, g=g, pq_=pq_: e.activation(out=gsig[:L, b, 512 * g:512 * (g + 1)], in_=pq_[:L, :], func=AF.Sigmoid), r=[pq_], w=[(gsig, (b, g))])

    def rw_tile(c, wtile, wcol, Ls, out_fn, carry_aps, save_last):
        L = Ls[0]
        ntk = sum(Ls)
        ps_ = pm.next()
        for k in range(8):
            T_(lambda e, k=k: e.matmul(ps_[:, 0:ntk], lhsT=wtile[:, k, wcol:wcol + 128], rhs=hT[:, k, 0:ntk], start=(k == 0), stop=(k == 7)),
               r=[hT, wtile], w=[ps_])
        tmp = f32t.next(); xs = f32t.next()
        A_(lambda e: e.activation(out=tmp[:, 0:ntk], in_=ps_[:, 0:ntk], func=AF.Identity, scale=colsd[:, c:c + 1]), r=[ps_, colsd], w=[tmp])
        if ntk > 1:
            V_(lambda e: e.scalar_tensor_tensor(out=xs[:, 1:ntk], in0=ps_[:, 0:ntk - 1], scalar=cols[:, O_MU + c:O_MU + c + 1],
                                                in1=tmp[:, 1:ntk], op0=ALU.mult, op1=ALU.add), r=[ps_, cols, tmp], w=[xs])
        for b in range(len(Ls)):
            ca = carry_aps[b]
            if ca is None:
                continue
            cap, creg = ca
            V_(lambda e, b=b, cap=cap: e.scalar_tensor_tensor(out=xs[:, L * b:L * b + 1], in0=cap[:, c:c + 1], scalar=cols[:, O_MU + c:O_MU + c + 1],
                                                             in1=tmp[:, L * b:L * b + 1], op0=ALU.mult, op1=ALU.add), r=[creg, cols, tmp], w=[xs])
        for b in range(len(Ls)):
            sl = save_last[b]
            if sl is None:
                continue
            sap, sreg = sl
            V_(lambda e, b=b, sap=sap: e.tensor_copy(out=sap[:, c:c + 1], in_=ps_[:, L * b + L - 1:L * b + L]), r=[ps_], w=[sreg])
        out_fn(xs)

    def rwkv_prep(Ls, own, carry_aps, save_last):
        ntk = sum(Ls)
        nb = len(Ls)
        L = Ls[0]
        rwt = lambda c, wtile, wcol, fn: rw_tile(c, wtile, wcol, Ls, fn, carry_aps, save_last)
        wl = load_w(win_v, C_RW + 3072, 256)

        def f24(xs):
            A_(lambda e: e.activation(out=TL[0:64, 0:ntk], in_=xs[0:64, 0:ntk], func=AF.Tanh), r=[xs], w=[TL])
            V_(lambda e: e.tensor_copy(out=TL[64:128, 0:ntk], in_=xs[64:128, 0:ntk]), r=[xs], w=[TL])
        rwt(24, wl, 0, f24)
        if own:
            def f25(xs):
                A_(lambda e: e.activation(out=SLG[:, 0:ntk], in_=xs[:, 0:ntk], func=AF.Sigmoid), r=[xs], w=[SLG])
            rwt(25, wl, 128, f25)
            for b in range(nb):
                for g in range(2):
                    pg = pm.next()
                    T_(lambda e, b=b, g=g, pg=pg: e.matmul(pg[:L, :], lhsT=SLG[:, L * b:L * b + L], rhs=g2b[:, 512 * g:512 * (g + 1)], start=True, stop=True),
                       r=[SLG, g2b], w=[pg])
                    evac_copy(gtok[:L, b, 512 * g:512 * (g + 1)], pg[:L, :], r=[pg], w=[(gtok, (b, g))])

        def blkview(t):
            return t[:, 0:ntk].rearrange("p (b l) -> p b l", b=nb)

        for half in range(2):
            wk_ = load_w(win_v, C_RW + 1024 + 512 * half, 512)
            for pp in range(4):
                p = 4 * half + pp

                def fk(kraw, p=p):
                    ps1 = pm.next()
                    T_(lambda e: e.matmul(ps1[:, 0:ntk], lhsT=w2b[0:64, 128 * p:128 * (p + 1)], rhs=TL[0:64, 0:ntk], start=True, stop=True),
                       r=[w2b, TL], w=[ps1])
                    sg = f32t.next()
                    A_(lambda e: e.activation(out=sg[:, 0:ntk], in_=ps1[:, 0:ntk], func=AF.Sigmoid, bias=cols[:, O_W0 + p:O_W0 + p + 1], scale=1.0),
                       r=[ps1, cols], w=[sg])
                    for b in range(nb):
                        V_(lambda e, b=b: e.tensor_tensor_scan(out=CS[:, p, b, 1:1 + L], data0=ones128[:, 0:L], data1=sg[:, L * b:L * (b + 1)],
                                                               initial=0.0, op0=ALU.mult, op1=ALU.add), r=[ones128, sg], w=[(CS, p)])
                    ps2 = pm.next()
                    T_(lambda e: e.matmul(ps2[:, 0:ntk], lhsT=a2b[64:128, 128 * p:128 * (p + 1)], rhs=TL[64:128, 0:ntk], start=True, stop=True),
                       r=[a2b, TL], w=[ps2])
                    asig = f32t.next()
                    A_(lambda e: e.activation(out=asig[:, 0:ntk], in_=ps2[:, 0:ntk], func=AF.Sigmoid, bias=cols[:, O_A0 + p:O_A0 + p + 1], scale=1.0),
                       r=[ps2, cols], w=[asig])
                    eex = f32t.next(); em = f32t.next()
                    A_(lambda e: e.activation(out=blkview(eex), in_=CS[:, p, 0:nb, 0:L], func=AF.Exp, scale=-DEC), r=[(CS, p)], w=[eex])
                    A_(lambda e: e.activation(out=blkview(em), in_=CS[:, p, 0:nb, 1:1 + L], func=AF.Exp, scale=DEC), r=[(CS, p)], w=[em])
                    A_(lambda e: e.activation(out=GC[:, 0:nb, p], in_=CS[:, p, 0:nb, L], func=AF.Exp, scale=-DEC), r=[(CS, p)], w=[(GC, p)])
                    kkr = f32t.next(); sq = f32t.next()
                    V_(lambda e: e.tensor_scalar(out=kkr[:, 0:ntk], in0=kraw[:, 0:ntk], scalar1=cols[:, O_KK + p:O_KK + p + 1], scalar2=None, op0=ALU.mult),
                       r=[kraw, cols], w=[kkr])
                    G_(lambda e: e.tensor_tensor(out=sq[:, 0:ntk], in0=kkr[:, 0:ntk], in1=kkr[:, 0:ntk], op=ALU.mult), r=[kkr], w=[sq])
                    ps3 = pm.next()
                    T_(lambda e: e.matmul(ps3[:, 0:ntk], lhsT=blkones[:], rhs=sq[:, 0:ntk], start=True, stop=True), r=[blkones, sq], w=[ps3])
                    A_(lambda e: e.activation(out=sq[:, 0:ntk], in_=ps3[:, 0:ntk], func=AF.Sqrt), r=[ps3], w=[sq])
                    V_(lambda e: e.tensor_scalar(out=sq[:, 0:ntk], in0=sq[:, 0:ntk], scalar1=1e-12, scalar2=None, op0=ALU.max), r=[sq], w=[sq])
                    V_(lambda e: e.reciprocal(out=sq[:, 0:ntk], in_=sq[:, 0:ntk]), r=[sq], w=[sq])
                    V_(lambda e: e.tensor_tensor(out=kkr[:, 0:ntk], in0=kkr[:, 0:ntk], in1=sq[:, 0:ntk], op=ALU.mult), r=[kkr, sq], w=[kkr])
                    V_(lambda e: e.tensor_scalar(out=sq[:, 0:ntk], in0=asig[:, 0:ntk], scalar1=cols[:, O_KA + p:O_KA + p + 1],
                                                 scalar2=colsd[:, 26 + p:27 + p], op0=ALU.mult, op1=ALU.add), r=[asig, cols, colsd], w=[sq])
                    G_(lambda e: e.tensor_tensor(out=sq[:, 0:ntk], in0=sq[:, 0:ntk], in1=kraw[:, 0:ntk], op=ALU.mult), r=[sq, kraw], w=[sq])
                    G_(lambda e: e.tensor_tensor(out=asig[:, 0:ntk], in0=asig[:, 0:ntk], in1=kkr[:, 0:ntk], op=ALU.mult), r=[asig, kkr], w=[asig])
                    V_(lambda e: e.scalar_tensor_tensor(out=AT[:, p, 0:ntk], in0=kkr[:, 0:ntk], scalar=-1.0, in1=eex[:, 0:ntk], op0=ALU.mult, op1=ALU.mult),
                       r=[kkr, eex], w=[(AT, p)])
                    V_(lambda e: e.tensor_tensor(out=KTt[:, p, 0:ntk], in0=sq[:, 0:ntk], in1=em[:, 0:ntk], op=ALU.mult), r=[sq, em], w=[(KTt, p)])
                    G_(lambda e: e.tensor_tensor(out=BTt[:, p, 0:ntk], in0=asig[:, 0:ntk], in1=em[:, 0:ntk], op=ALU.mult), r=[asig, em], w=[(BTt, p)])
                    if own:
                        G_(lambda e: e.tensor_scalar(out=KMR[:, p, 0:ntk], in0=sq[:, 0:ntk], scalar1=cols[:, O_RK + p:O_RK + p + 1], scalar2=None, op0=ALU.mult),
                           r=[sq, cols], w=[(KMR, p)])
                rwt(8 + p, wk_, 128 * pp, fk)
        for half in range(2):
            wv_ = load_w(win_v, C_RW + 2048 + 512 * half, 512)
            for pp in range(4):
                p = 4 * half + pp

                def fv(xs, p=p):
                    A_(lambda e: e.activation(out=VTf[:, p, 0:ntk], in_=xs[:, 0:ntk], func=AF.Copy), r=[xs], w=[(VTf, p)])
                rwt(16 + p, wv_, 128 * pp, fv)
        if own:
            pbc = pO[:, 1024:1536]
            pbreg = (pO, 2)
            first = [True]
            for half in range(2):
                wr_ = load_w(win_v, C_RW + 512 * half, 512)
                for pp in range(4):
                    p = 4 * half + pp

                    def fr(xs, p=p):
                        ep = f32t.next()
                        A_(lambda e: e.activation(out=blkview(ep), in_=CS[:, p, 0:nb, 1:1 + L], func=AF.Exp, scale=-DEC), r=[(CS, p)], w=[ep])
                        V_(lambda e: e.tensor_tensor(out=RTt[:, p, 0:ntk], in0=xs[:, 0:ntk], in1=ep[:, 0:ntk], op=ALU.mult), r=[xs, ep], w=[(RTt, p)])
                        prd = PRD.next()
                        G_(lambda e: e.tensor_tensor(out=prd[:, 0:ntk], in0=xs[:, 0:ntk], in1=KMR[:, p, 0:ntk], op=ALU.mult), r=[xs, (KMR, p)], w=[prd])
                        for b in range(nb):
                            T_(lambda e, b=b: e.matmul(pbc[:L, 16 * b + 2 * p:16 * b + 2 * p + 2], lhsT=prd[:, L * b:L * b + L], rhs=blkonesB[:, 0:128:64],
                                                       start=first[0] and b == 0, stop=True, skip_group_check=True), r=[prd, blkonesB], w=[pbreg])
                        first[0] = False
                    rwt(p, wr_, 128 * pp, fr)
            first[0] = True
            V_(lambda e: e.tensor_copy(out=bcoef[:L, 0:nb, :], in_=pbc[:L, 0:16 * nb].rearrange("p (b h) -> p b h", b=nb)), r=[pbreg], w=[bcoef])
        for b in range(nb):
            for (src, dst) in ((KTt, Ktok), (BTt, Btok), (VTf, Vtok)):
                for p in range(8):
                    T_(lambda e, p=p, b=b, src=src: e.transpose(out=ptr[0:L, p, :], in_=src[:, p, L * b:L * (b + 1)], identity=identB[:]),
                       r=[(src, p), identB], w=[ptr])
                evac_copy(dst[0:L, b, :].rearrange("p (k j) -> p k j", k=8), ptr[0:L, :, :], r=[ptr], w=[(dst, b)])

    def chunk_scan(b, L, own):
        nl = max(int(np.ceil(np.log2(L))) - 1, 0)
        c0 = L * b
        Lk = 128 if L == 64 else L
        if L == 64:
            for t_ in (AKs, ARB, ARK):
                G_(lambda e, t_=t_: e.memset(t_[64:128, :, :], 0.0), w=[t_])
            G_(lambda e: e.memset(Vtok[64:128, b, :], 0.0), w=[(Vtok, b)])
            G_(lambda e: e.memset(Ub[64:128, :, :], 0.0), w=[Ub])
        for hg in range(4):
            heads = [4 * hg + i for i in range(4)]
            cur = {}
            for i, h in enumerate(heads):
                p, bp = h // 2, 64 * (h % 2)
                bank = cb_ap(i); breg = cbanks[i]
                at = AT[bp:bp + 64, p, c0:c0 + L]; bt = BTt[bp:bp + 64, p, c0:c0 + L]; kt = KTt[bp:bp + 64, p, c0:c0 + L]
                T_(lambda e, bank=bank, bt=bt, at=at: e.matmul(bank[0:L, 0:L], lhsT=bt, rhs=at, start=True, stop=True), r=[(AT, p), (BTt, p)], w=[breg])
                T_(lambda e, bank=bank, bt=bt, at=at: e.matmul(bank[0:L, 128:128 + L], lhsT=at, rhs=bt, start=False, stop=True, skip_group_check=True),
                   r=[(AT, p), (BTt, p)], w=[breg])
                T_(lambda e, bank=bank, kt=kt, at=at: e.matmul(bank[0:L, 256:256 + L], lhsT=kt, rhs=at, start=False, stop=True, skip_group_check=True),
                   r=[(AT, p), (KTt, p)], w=[breg])
                nm_ = 3
                if own:
                    rt_ = RTt[bp:bp + 64, p, c0:c0 + L]
                    T_(lambda e, bank=bank, bt=bt, rt_=rt_: e.matmul(bank[0:L, 384:384 + L], lhsT=bt, rhs=rt_, start=False, stop=True, skip_group_check=True),
                       r=[(RTt, p), (BTt, p)], w=[breg])
                    nm_ = 4
                am = amat.next()
                V_(lambda e, bank=bank, am=am, nm_=nm_: e.tensor_tensor(out=am[0:L, 0:nm_, 0:L], in0=bank[0:L, 0:128 * nm_].rearrange("p (m t) -> p m t", m=nm_)[:, :, 0:L],
                                                                       in1=mask[0:L, 0:128 * nm_].rearrange("p (m t) -> p m t", m=nm_)[:, :, 0:L], op=ALU.mult),
                   r=[breg, mask], w=[am])
                G_(lambda e, am=am, h=h: e.tensor_copy(out=AKs[0:L, h, 0:L], in_=am[0:L, 2, 0:L]), r=[am], w=[(AKs, h)])
                if own:
                    G_(lambda e, am=am, h=h: e.tensor_copy(out=ARB[0:L, h, 0:L], in_=am[0:L, 3, 0:L]), r=[am], w=[(ARB, h)])
                m0 = Mt[i].next()
                G_(lambda e, am=am, m0=m0: e.tensor_tensor(out=m0[0:L, 0:L], in0=am[0:L, 0, 0:L], in1=identB[0:L, 0:L], op=ALU.add), r=[am, identB], w=[m0])
                cur[h] = dict(P=am[0:L, 0, 0:L], Q=am[0:L, 1, 0:L], Pt=am, Qt=am, M=m0)
            for lev in range(nl):
                last = (lev == nl - 1)
                for i, h in enumerate(heads):
                    bank = cb_ap(i); breg = cbanks[i]
                    c = cur[h]
                    if not last:
                        T_(lambda e, bank=bank, cq=c['Q'], cp=c['P']: e.matmul(bank[0:L, 0:L], lhsT=cq, rhs=cp, start=True, stop=True), r=[c['Pt'], c['Qt']], w=[breg])
                        T_(lambda e, bank=bank, cq=c['Q'], cp=c['P']: e.matmul(bank[0:L, 128:128 + L], lhsT=cp, rhs=cq, start=False, stop=True, skip_group_check=True),
                           r=[c['Pt'], c['Qt']], w=[breg])
                    else:
                        T_(lambda e, bank=bank, cq=c['Q'], cp=c['P']: e.matmul(bank[0:L, 128:128 + L], lhsT=cp, rhs=cq, start=True, stop=True),
                           r=[c['Pt'], c['Qt']], w=[breg])
                    nq = pq[i].next()
                    lo = 1 if last else 0
                    A_(lambda e, bank=bank, nq=nq, lo=lo: e.activation(out=nq[0:L, lo:2, 0:L],
                                                                      in_=bank[0:L, 128 * lo:256].rearrange("p (m t) -> p m t", m=2 - lo)[:, :, 0:L], func=AF.Copy),
                       r=[breg], w=[nq])
                    c['P'] = nq[0:L, 0, 0:L]; c['Q'] = nq[0:L, 1, 0:L]; c['Pt'] = nq; c['Qt'] = nq
                for i, h in enumerate(heads):
                    bank = cb_ap(i); breg = cbanks[i]
                    c = cur[h]
                    pmb = pm.next()
                    T_(lambda e, pmb=pmb, cq=c['Q'], cm=c['M']: e.matmul(pmb[0:L, 0:L], lhsT=cq, rhs=cm[0:L, 0:L], start=True, stop=True),
                       r=[c['Qt'], c['M']], w=[pmb])
                    mn = Mt[i].next()
                    V_(lambda e, pmb=pmb, cm=c['M'], mn=mn: e.tensor_tensor(out=mn[0:L, 0:L], in0=pmb[0:L, 0:L], in1=cm[0:L, 0:L], op=ALU.add),
                       r=[pmb, c['M']], w=[mn])
                    c['M'] = mn
            for i, h in enumerate(heads):
                G_(lambda e, h=h, m=cur[h]['M']: e.tensor_copy(out=Mfin[0:L, h, 0:L], in_=m[0:L, 0:L]), r=[cur[h]['M']], w=[(Mfin, h)])
        for g in range(2):
            bank = cb_ap(g); breg = cbanks[g]
            for hh in range(8):
                h = 8 * g + hh
                p, bp = h // 2, 64 * (h % 2)
                T_(lambda e, bank=bank, hh=hh, p=p, bp=bp: e.matmul(bank[0:L, 64 * hh:64 * hh + 64], lhsT=AT[bp:bp + 64, p, c0:c0 + L], rhs=STb[bp:bp + 64, p, :],
                                                                   start=(hh == 0), stop=False, skip_group_check=True), r=[(AT, p), STb], w=[breg])
                T_(lambda e, bank=bank, hh=hh, h=h: e.matmul(bank[0:L, 64 * hh:64 * hh + 64], lhsT=AKs[0:Lk, h, 0:L], rhs=Vtok[0:Lk, b, 64 * h:64 * h + 64],
                                                            start=False, stop=True, skip_group_check=True), r=[(AKs, h), (Vtok, b)], w=[breg])
            evac_copy(Wb[0:L, 8 * g:8 * g + 8, :], bank[0:L, 0:512].rearrange("p (h i) -> p h i", h=8), r=[breg], w=[(Wb, g)])
        for g in range(2):
            bank = cb_ap(2 + g); breg = cbanks[2 + g]
            for hh in range(8):
                h = 8 * g + hh
                T_(lambda e, bank=bank, hh=hh, h=h: e.matmul(bank[0:L, 64 * hh:64 * hh + 64], lhsT=Mfin[0:L, h, 0:L], rhs=Wb[0:L, h, :],
                                                            start=(hh == 0), stop=True, skip_group_check=True), r=[(Mfin, h), (Wb, g)], w=[breg])
            evac_copy(Ub[0:L, 8 * g:8 * g + 8, :], bank[0:L, 0:512].rearrange("p (h i) -> p h i", h=8), r=[breg], w=[(Ub, g)])
        if own:
            for rnd in range(2):
                for par in range(2):
                    bank = cb_ap(par); breg = cbanks[par]
                    for j in range(4):
                        h = 8 * rnd + 2 * j + par
                        p, bp = h // 2, 64 * par
                        T_(lambda e, bank=bank, j=j, p=p, bp=bp: e.matmul(bank[0:L, 128 * j:128 * j + L], lhsT=KTt[bp:bp + 64, p, c0:c0 + L], rhs=RTt[bp:bp + 64, p, c0:c0 + L],
                                                                         start=(j == 0), stop=True, skip_group_check=True), r=[(KTt, p), (RTt, p)], w=[breg])
                for par in range(2):
                    bank = cb_ap(par); breg = cbanks[par]
                    for j in range(4):
                        h = 8 * rnd + 2 * j + par
                        V_(lambda e, bank=bank, j=j, h=h: e.tensor_tensor(out=ARK[0:L, h, 0:L], in0=bank[0:L, 128 * j:128 * j + L], in1=mask[0:L, 384:384 + L], op=ALU.mult),
                           r=[breg, mask], w=[(ARK, h)])
            for g in range(2):
                bank = cb_ap(2 + g); breg = cbanks[2 + g]
                for hh in range(8):
                    h = 8 * g + hh
                    p, bp = h // 2, 64 * (h % 2)
                    oc = bank[0:L, 64 * hh:64 * hh + 64]
                    T_(lambda e, oc=oc, hh=hh, p=p, bp=bp: e.matmul(oc, lhsT=RTt[bp:bp + 64, p, c0:c0 + L], rhs=STb[bp:bp + 64, p, :],
                                                                   start=(hh == 0), stop=False, skip_group_check=True), r=[(RTt, p), STb], w=[breg])
                    T_(lambda e, oc=oc, h=h: e.matmul(oc, lhsT=ARK[0:Lk, h, 0:L], rhs=Vtok[0:Lk, b, 64 * h:64 * h + 64], start=False, stop=False, skip_group_check=True),
                       r=[(ARK, h), (Vtok, b)], w=[breg])
                    T_(lambda e, oc=oc, h=h: e.matmul(oc, lhsT=ARB[0:Lk, h, 0:L], rhs=Ub[0:Lk, h, :], start=False, stop=True, skip_group_check=True),
                       r=[(ARB, h), (Ub, h // 8)], w=[breg])
                evac_copy(Yt[0:L, 512 * g:512 * (g + 1)], bank[0:L, 0:512], r=[breg], w=[(Yt, g)])
        bank = cb_ap(0); breg = cbanks[0]
        for h in range(16):
            p, bp = h // 2, 64 * (h % 2)
            T_(lambda e, h=h, p=p, bp=bp: e.matmul(bank[bp:bp + 64, 64 * p:64 * p + 64], lhsT=Ktok[0:L, b, 64 * h:64 * h + 64], rhs=Vtok[0:L, b, 64 * h:64 * h + 64],
                                                  start=(h < 2), stop=False, skip_group_check=True), r=[(Ktok, b), (Vtok, b)], w=[breg])
            T_(lambda e, h=h, p=p, bp=bp: e.matmul(bank[bp:bp + 64, 64 * p:64 * p + 64], lhsT=Btok[0:L, b, 64 * h:64 * h + 64], rhs=Ub[0:L, h, :],
                                                  start=False, stop=True, skip_group_check=True), r=[(Btok, b), (Ub, h // 8)], w=[breg])
        V_(lambda e: e.tensor_tensor(out=sttmp[:].rearrange("p k i -> p (k i)"), in0=bank[:, 0:512], in1=ST[:].rearrange("p k i -> p (k i)"), op=ALU.add),
           r=[breg, ST], w=[sttmp])
        V_(lambda e: e.tensor_tensor(out=ST[:], in0=GC[:, b, :].unsqueeze(2).to_broadcast([128, 8, 64]), in1=sttmp[:], op=ALU.mult),
           r=[sttmp, GC], w=[ST])
        A_(lambda e: e.activation(out=STb[:], in_=ST[:], func=AF.Copy), r=[ST], w=[STb])
        if own:
            rwkv_out(b, L)

    def rwkv_out(b, L):
        y3 = Yt[0:L, :].rearrange("p (h i) -> p h i", h=16)
        y23 = Y2[0:L, :].rearrange("p (h i) -> p h i", h=16)
        V_(lambda e: e.tensor_reduce(out=gns[0:L, 0:16], in_=y3, axis=mybir.AxisListType.X, op=ALU.add), r=[Yt], w=[gns])
        A_(lambda e: e.activation(out=Y2[0:L, :], in_=Yt[0:L, :], func=AF.Square), r=[Yt], w=[Y2])
        V_(lambda e: e.tensor_reduce(out=gns[0:L, 16:32], in_=y23, axis=mybir.AxisListType.X, op=ALU.add), r=[Y2], w=[gns])
        V_(lambda e: e.tensor_scalar(out=gns[0:L, 0:16], in0=gns[0:L, 0:16], scalar1=1.0 / 64, scalar2=None, op0=ALU.mult), r=[gns], w=[gns])
        V_(lambda e: e.tensor_tensor(out=gns[0:L, 32:48], in0=gns[0:L, 0:16], in1=gns[0:L, 0:16], op=ALU.mult), r=[gns], w=[gns])
        V_(lambda e: e.scalar_tensor_tensor(out=gns[0:L, 48:64], in0=gns[0:L, 16:32], scalar=1.0 / 64, in1=gns[0:L, 32:48], op0=ALU.mult, op1=ALU.subtract),
           r=[gns], w=[gns])
        A_(lambda e: e.activation(out=gns[0:L, 48:64], in_=gns[0:L, 48:64], func=AF.Sqrt, bias=float(GN_EPS), scale=1.0), r=[gns], w=[gns])
        V_(lambda e: e.reciprocal(out=gns[0:L, 48:64], in_=gns[0:L, 48:64]), r=[gns], w=[gns])
        V_(lambda e: e.tensor_scalar(out=gns[0:L, 64:80], in0=gns[0:L, 48:64], scalar1=-1.0, scalar2=None, op0=ALU.mult), r=[gns], w=[gns])
        V_(lambda e: e.tensor_tensor(out=y23, in0=gns[0:L, 0:16].unsqueeze(2).to_broadcast([L, 16, 64]), in1=y3, op=ALU.subtract), r=[gns, Yt], w=[Y2])
        V_(lambda e: e.tensor_tensor(out=y23, in0=gns[0:L, 64:80].unsqueeze(2).to_broadcast([L, 16, 64]), in1=y23, op=ALU.mult), r=[gns, Y2], w=[Y2])
        vg, vbb = getvec("gn_g"), getvec("gn_b")
        G_(lambda e: e.tensor_tensor(out=Y2[0:L, :], in0=Y2[0:L, :], in1=vg[0:L, :], op=ALU.mult), r=[Y2, vg], w=[Y2])
        G_(lambda e: e.tensor_tensor(out=Y2[0:L, :], in0=Y2[0:L, :], in1=vbb[0:L, :], op=ALU.add), r=[Y2, vbb], w=[Y2])
        V_(lambda e: e.tensor_tensor(out=y3, in0=bcoef[0:L, b, :].unsqueeze(2).to_broadcast([L, 16, 64]), in1=Vtok[0:L, b, :].rearrange("p (h i) -> p h i", h=16), op=ALU.mult),
           r=[bcoef, (Vtok, b)], w=[Yt])
        G_(lambda e: e.tensor_tensor(out=Y2[0:L, :], in0=Y2[0:L, :], in1=Yt[0:L, :], op=ALU.add), r=[Y2, Yt], w=[Y2])
        G_(lambda e: e.tensor_tensor(out=orw[0:L, :], in0=Y2[0:L, :], in1=gtok[0:L, b, :], op=ALU.mult), r=[Y2, gtok], w=[orw])
        for k in range(8):
            T_(lambda e, k=k: e.transpose(out=ptr[:, k, 0:L], in_=orw[0:L, 128 * k:128 * (k + 1)], identity=identB[:L, :L]), r=[orw, identB], w=[ptr])
        evac_copy(orwT[:, b, :, 0:L], ptr[:, :, 0:L], r=[ptr], w=[(orwT, b)])

    def attention(b, Lq, slot_lo, slot_hi, corner):
        S = slot_hi - slot_lo
        wi = kvs[b]
        for h in range(8):
            V_(lambda e, h=h: e.tensor_scalar(out=dg[0:Lq, h, 0:Lq], in0=identB[0:Lq, 0:Lq], scalar1=wi[0:Lq, 320 + h:321 + h], scalar2=None, op0=ALU.mult),
               r=[identB, wi], w=[dg])
        cidx = 0.125 * (8.0 ** -0.5)
        ntile = (S + 511) // 512
        for ti in range(ntile):
            s0 = slot_lo + 512 * ti
            n = min(512, slot_hi - s0)
            kit = kitl.next(); kb_ = kbt.next()
            P.dma('sp', kit[0:64, 0:n], D_kit.ap()[:, s0:s0 + n], r=[D_kit], w=[kit])
            P.dma('sp', kit[64:128, 0:n], D_kit.ap()[:, s0:s0 + n], r=[D_kit], w=[kit])
            P.dma('sp', kb_[:, 0:n], keyb.ap()[:, s0:s0 + n], r=[keyb], w=[kb_])
            psc = pm.next()
            T_(lambda e, psc=psc, kb_=kb_, n=n: e.matmul(psc[0:Lq, 0:n], lhsT=onesrow[0:1, 0:Lq], rhs=kb_[0:1, 0:n], start=True, stop=False), r=[onesrow, kb_], w=[psc])
            def emit_x(h, kit=kit, n=n):
                pp, bp = h // 2, 64 * (h % 2)
                xb_ = xbanks[xq[0] % 3]; xq[0] += 1
                px = xb_[0][:, 512 * xb_[1]:512 * (xb_[1] + 1)]
                T_(lambda e, px=px, pp=pp, bp=bp: e.matmul(px[0:Lq, 0:n], lhsT=qiT[bp:bp + 64, b, pp, 0:Lq], rhs=kit[bp:bp + 64, 0:n], start=True, stop=True),
                   r=[(qiT, b), kit], w=[xb_])
                return px, xb_
            nxt = emit_x(0)
            for h in range(8):
                px, xb_ = nxt
                if h < 7:
                    nxt = emit_x(h + 1)
                r_ = Rt.next()
                A_(lambda e, px=px, r_=r_, n=n: e.activation(out=r_[0:Lq, 0:n], in_=px[0:Lq, 0:n], func=AF.Relu, scale=cidx), r=[xb_], w=[r_])
                T_(lambda e, psc=psc, h=h, r_=r_, n=n: e.matmul(psc[0:Lq, 0:n], lhsT=dg[0:Lq, h, 0:Lq], rhs=r_[0:Lq, 0:n], start=False, stop=(h == 7)), r=[dg, r_], w=[psc])
            V_(lambda e, psc=psc, ti=ti, n=n: e.tensor_copy(out=SC[0:Lq, 512 * ti:512 * ti + n], in_=psc[0:Lq, 0:n]), r=[psc], w=[(SC, ti)])
        if corner:
            V_(lambda e: e.memset(SC[0:64, S - 64:S], -1e30), r=[], w=[SC])
        V_(lambda e: e.memset(tau[0:Lq, 0:1], 0.0), w=[tau])
        V_(lambda e: e.memset(tau[0:Lq, 4:5], 0.0), w=[tau])
        Sa = (int(S * 0.42) // 16) * 16
        nbk_ = S - Sa
        for it in range(NBIS):
            s_ = 16.0 * (0.5 ** (it + 1))
            V_(lambda e: e.tensor_scalar(out=junk[0:Lq, 0:1].to_broadcast([Lq, Sa]), in0=SC[0:Lq, 0:Sa], scalar1=tau[0:Lq, 0:1], scalar2=None,
                                         op0=ALU.is_ge, op1=ALU.add, accum_out=tau[0:Lq, 1:2]), r=[SC, (tau, 't')], w=[(junk, 0), (tau, 'ca')])
            A_(lambda e: e.activation(out=junk[0:Lq, 1:2].to_broadcast([Lq, nbk_]), in_=SC[0:Lq, Sa:S], func=AF.Sign, bias=tau[0:Lq, 4:5], scale=1.0,
                                      accum_out=tau[0:Lq, 5:6]), r=[SC, (tau, 'nt')], w=[(junk, 1), (tau, 'cb')])
            V_(lambda e: e.scalar_tensor_tensor(out=tau[0:Lq, 2:3], in0=tau[0:Lq, 1:2], scalar=2.0, in1=tau[0:Lq, 5:6], op0=ALU.mult, op1=ALU.add),
               r=[(tau, 'ca'), (tau, 'cb')], w=[(tau, 'd')])
            V_(lambda e, s_=s_: e.tensor_scalar(out=tau[0:Lq, 2:3], in0=tau[0:Lq, 2:3], scalar1=float(2 * TOPK - 1 - nbk_), scalar2=2.0 * s_, op0=ALU.is_ge, op1=ALU.mult),
               r=[(tau, 'd')], w=[(tau, 'd')])
            V_(lambda e, s_=s_: e.scalar_tensor_tensor(out=tau[0:Lq, 0:1], in0=tau[0:Lq, 2:3], scalar=-s_, in1=tau[0:Lq, 0:1], op0=ALU.add, op1=ALU.add),
               r=[(tau, 'd'), (tau, 't')], w=[(tau, 't')])
            V_(lambda e: e.tensor_scalar(out=tau[0:Lq, 4:5], in0=tau[0:Lq, 0:1], scalar1=-1.0, scalar2=None, op0=ALU.mult), r=[(tau, 't')], w=[(tau, 'nt')])
        s_last = 16.0 * (0.5 ** NBIS)
        V_(lambda e: e.tensor_scalar(out=tau[0:Lq, 3:4], in0=tau[0:Lq, 0:1], scalar1=-s_last, scalar2=None, op0=ALU.add), r=[(tau, 't')], w=[(tau, 'u')])
        blocks = []
        for ti in range(ntile):
            s0 = slot_lo + 512 * ti
            n = min(512, slot_hi - s0)
            for a in range((n + 127) // 128):
                blocks.append((ti, s0, n, a, min(128, n - 128 * a)))
        tiles = {}

        def tile_res(ti, s0, n):
            if ti in tiles:
                return tiles[ti]
            kt_ = ktl.next(); vt_ = vtl.next(); mb = mbt.next()
            P.dma('sp', kt_[:, 0:n], D_kt.ap()[:, s0:s0 + n], r=[D_kt], w=[kt_])
            nfull, rem = n // 128, n % 128
            if nfull:
                P.dma('sp', vt_[:, 0:nfull, 0:128], D_v.ap()[s0:s0 + 128 * nfull, :].rearrange("(a p) d -> p a d", p=128), r=[D_v], w=[vt_])
            if rem:
                P.dma('sp', vt_[0:rem, nfull, 0:128], D_v.ap()[s0 + 128 * nfull:s0 + n, :], r=[D_v], w=[vt_])
            V_(lambda e: e.tensor_scalar(out=mb[0:Lq, 0:n], in0=SC[0:Lq, 512 * ti:512 * ti + n], scalar1=tau[0:Lq, 3:4], scalar2=-30000.0,
                                         op0=ALU.is_lt, op1=ALU.mult), r=[(SC, ti), (tau, 'u')], w=[mb])
            tiles[ti] = (kt_, vt_, mb)
            return tiles[ti]

        def buf(j, hh):
            if j % 2 == 0:
                return pLT[:, 512 * hh:512 * hh + 512], (pLT, hh)
            return pm.t[hh][:, :], pm.t[hh]

        def emit_LT(j):
            ti, s0, n, a, ns = blocks[j]
            kt_, vt_, mb = tile_res(ti, s0, n)
            for hh in range(2):
                ap_, reg = buf(j, hh)
                T_(lambda e, ap_=ap_, hh=hh: e.matmul(ap_[0:ns, 0:4 * Lq], lhsT=kt_[:, 128 * a:128 * a + ns], rhs=qT[:, b, 4 * hh:4 * hh + 4, 0:Lq], start=True, stop=False),
                   r=[kt_, (qT, b)], w=[reg])
                T_(lambda e, ap_=ap_: e.matmul(ap_[0:ns, 0:4 * Lq], lhsT=mb[0:Lq, 128 * a:128 * a + ns], rhs=I4[0:Lq, :, 0:Lq], start=False, stop=True),
                   r=[mb, I4], w=[reg])
        emit_LT(0)
        for j in range(len(blocks)):
            if j + 1 < len(blocks):
                emit_LT(j + 1)
            ti, s0, n, a, ns = blocks[j]
            kt_, vt_, mb = tiles[ti]
            pt = PTt.next()
            for hh in range(2):
                ap_, reg = buf(j, hh)
                A_(lambda e, ap_=ap_, hh=hh, pt=pt, ns=ns: e.activation(out=pt[0:ns, 4 * Lq * hh:4 * Lq * (hh + 1)], in_=ap_[0:ns, 0:4 * Lq], func=AF.Exp, scale=128.0 ** -0.5),
                   r=[reg], w=[(pt, hh)])
            last = (j == len(blocks) - 1)
            for h in range(8):
                off = 512 * (h // 3) + 129 * (h % 3)
                T_(lambda e, h=h, off=off, ns=ns, a=a, pt=pt, vt_=vt_, st_=(j == 0 and h % 3 == 0), last=last: e.matmul(
                    pO[0:Lq, off:off + 129], lhsT=pt[0:ns, Lq * h:Lq * h + Lq], rhs=vt_[0:ns, a, 0:129], start=st_, stop=last, skip_group_check=True),
                   r=[pt, vt_], w=[(pO, h // 3)])
        for h in range(8):
            off = 512 * (h // 3) + 129 * (h % 3)
            V_(lambda e, h=h, off=off: e.reciprocal(out=rden[0:Lq, h:h + 1], in_=pO[0:Lq, off + 128:off + 129]), r=[(pO, h // 3)], w=[rden])
            V_(lambda e, h=h, off=off: e.tensor_scalar(out=attn[0:Lq, 128 * h:128 * h + 128], in0=pO[0:Lq, off:off + 128], scalar1=rden[0:Lq, h:h + 1], scalar2=None, op0=ALU.mult),
               r=[(pO, h // 3), rden], w=[attn])
        for k in range(8):
            T_(lambda e, k=k: e.transpose(out=ptr[:, k, 0:Lq], in_=attn[0:Lq, 128 * k:128 * (k + 1)], identity=identB[:Lq, :Lq]), r=[attn, identB], w=[ptr])
        evac_copy(attnT[:, b, :, 0:Lq], ptr[:, :, 0:Lq], r=[ptr], w=[(attnT, b)])


    def post(Ls, y_dram, yrows):
        L = Ls[0]
        nbk = len(Ls)
        ntk = sum(Ls)
        woa = [load_w(kview(D_woa), 512 * g, 512) for g in range(2)]
        for b in range(nbk):
            for g in range(2):
                po_ = pm.next()
                for k in range(8):
                    T_(lambda e, b=b, g=g, k=k, po_=po_: e.matmul(po_[:L, :], lhsT=attnT[:, b, k, 0:L], rhs=woa[g][:, k, :], start=(k == 0), stop=(k == 7)),
                       r=[(attnT, b), woa[g]], w=[po_])
                mx = mixs[b]
                V_(lambda e, b=b, g=g, po_=po_, mx=mx: e.tensor_tensor(out=mx[:L, 512 * g:512 * (g + 1)], in0=po_[:L, :],
                                                                    in1=gsig[:L, b, 512 * g:512 * (g + 1)], op=ALU.mult), r=[po_, gsig], w=[(mx, g)])
        wor = [load_w(kview(D_wor), 512 * g, 512) for g in range(2)]
        for b in range(nbk):
            mx = mixs[b]
            for g in range(2):
                po_ = pm.next()
                for k in range(8):
                    T_(lambda e, b=b, g=g, k=k, po_=po_: e.matmul(po_[:L, :], lhsT=orwT[:, b, k, 0:L], rhs=wor[g][:, k, :], start=(k == 0), stop=(k == 7)),
                       r=[(orwT, b), wor[g]], w=[po_])
                tq = rtmp.next()
                V_(lambda e, b=b, g=g, po_=po_, tq=tq: e.tensor_tensor(out=tq[:L, :], in0=po_[:L, :], in1=gsig[:L, b, D + 512 * g:D + 512 * (g + 1)], op=ALU.mult),
                   r=[po_, gsig], w=[tq])
                G_(lambda e, g=g, mx=mx, tq=tq: e.tensor_tensor(out=mx[:L, 512 * g:512 * (g + 1)], in0=mx[:L, 512 * g:512 * (g + 1)], in1=tq[:L, :], op=ALU.add),
                   r=[tq, (mx, g)], w=[(mx, g)])
            for k in range(8):
                T_(lambda e, k=k, mx=mx: e.transpose(out=ptr[:, k, 0:L], in_=mx[:L, 128 * k:128 * (k + 1)], identity=identB[:L, :L]), r=[mx, identB], w=[ptr])
            evac_copy(mixT[:, :, L * b:L * b + L], ptr[:, :, 0:L], r=[ptr], w=[(mixT, b)])
        wo = [load_w(kview(D_wout), 512 * g, 512) for g in range(2)]
        for b in range(nbk):
            for g in range(2):
                po_ = pm.next()
                for k in range(8):
                    T_(lambda e, b=b, g=g, k=k, po_=po_: e.matmul(po_[:L, :], lhsT=mixT[:, k, L * b:L * b + L], rhs=wo[g][:, k, :], start=(k == 0), stop=(k == 7)),
                       r=[(mixT, b), wo[g]], w=[po_])
                V_(lambda e, b=b, g=g, po_=po_: e.scalar_tensor_tensor(out=x1[:L, b, 512 * g:512 * (g + 1)], in0=hres[:L, b, 512 * g:512 * (g + 1)], scalar=float(ALPHA),
                                                                    in1=po_[:L, :], op0=ALU.mult, op1=ALU.add), r=[po_, (hres, b)], w=[(x1, b)])
            ln_rows(x1[:L, b, :], L, D, x1[:L, b, :], LN_EPS, g_b=(getvec("ln1_g"), getvec("ln1_b")))
            G_(lambda e, b=b: e.tensor_copy(out=x1b[:L, :], in_=x1[:L, b, :]), r=[x1], w=[x1b])
            for k in range(8):
                T_(lambda e, k=k: e.transpose(out=ptr[:, k, 0:L], in_=x1b[:L, 128 * k:128 * (k + 1)], identity=identB[:L, :L]), r=[x1b, identB], w=[ptr])
            evac_copy(x1T[:, :, L * b:L * b + L], ptr[:, :, 0:L], r=[ptr], w=[(x1T, b)])
        accs = [[(pLT, 0), (pLT, 1)], [(pO, 0), (pO, 1)]]
        acc_ap = lambda b, g: (accs[b][g][0])[:, 512 * accs[b][g][1]:512 * (accs[b][g][1] + 1)]
        nfc = DFF // 128
        fbanks = [pm.t[0], pm.t[1], (pO, 2)]
        fbq = [0]
        for fq in range(0, nfc, 4):
            nq = min(4, nfc - fq)
            wg_ = load_w(kview(D_wfg), 128 * fq, 128 * nq)
            wu_ = load_w(kview(D_wfu), 128 * fq, 128 * nq)
            wd_ = wt.next()
            wdv = wd_[:].rearrange("p (a c) n -> p a (c n)", c=2)
            P.dma('sp', wdv[:, 0:nq, :], D_wfd.ap()[128 * fq:128 * (fq + nq), :].rearrange("(a p) n -> p a n", p=128), r=[D_wfd], w=[wd_])
            def emit_gu(j, wg_=wg_, wu_=wu_):
                bg = fbanks[fbq[0] % 3]; bu = fbanks[(fbq[0] + 1) % 3]; fbq[0] += 2
                pg_ = bg[0][:, 512 * bg[1]:512 * bg[1] + 512] if isinstance(bg, tuple) else bg[:, :]
                pu_ = bu[0][:, 512 * bu[1]:512 * bu[1] + 512] if isinstance(bu, tuple) else bu[:, :]
                for k in range(8):
                    T_(lambda e, k=k: e.matmul(pg_[:, 0:ntk], lhsT=wg_[:, k, 128 * j:128 * (j + 1)], rhs=x1T[:, k, 0:ntk], start=(k == 0), stop=(k == 7)),
                       r=[x1T, wg_], w=[bg])
                ag = actg.next()
                A_(lambda e: e.activation(out=ag[:, 0:ntk], in_=pg_[:, 0:ntk], func=AF.Silu), r=[bg], w=[ag])
                for k in range(8):
                    T_(lambda e, k=k: e.matmul(pu_[:, 0:ntk], lhsT=wu_[:, k, 128 * j:128 * (j + 1)], rhs=x1T[:, k, 0:ntk], start=(k == 0), stop=(k == 7)),
                       r=[x1T, wu_], w=[bu])
                at_ = actT.next()
                V_(lambda e: e.tensor_tensor(out=at_[:, 0:ntk], in0=pu_[:, 0:ntk], in1=ag[:, 0:ntk], op=ALU.mult), r=[bu, ag], w=[at_])
                return at_

            def emit_down(j, at_, wd_=wd_, wdv=wdv, fq=fq):
                fc = fq + j
                for b in range(nbk):
                    for g in range(2):
                        T_(lambda e, b=b, g=g: e.matmul(acc_ap(b, g)[:L, :], lhsT=at_[:, L * b:L * b + L], rhs=wdv[:, j, 512 * g:512 * (g + 1)],
                                                        start=(fc == 0), stop=(fc == nfc - 1)), r=[at_, wd_], w=[accs[b][g]])
            pend = emit_gu(0)
            for j in range(nq):
                nxt_ = emit_gu(j + 1) if j + 1 < nq else None
                emit_down(j, pend)
                pend = nxt_
        for b in range(nbk):
            yt = yo.next()
            for g in range(2):
                V_(lambda e, b=b, g=g, yt=yt: e.scalar_tensor_tensor(out=yt[:L, 512 * g:512 * (g + 1)], in0=x1[:L, b, 512 * g:512 * (g + 1)], scalar=float(ALPHA),
                                                                  in1=acc_ap(b, g)[:L, :], op0=ALU.mult, op1=ALU.add), r=[accs[b][g], x1], w=[(yt, g)])
            ln_rows(yt[:L, :], L, D, yt[:L, :], LN_EPS, g_b=(getvec("ln2_g"), getvec("ln2_b")))
            P.dma('pool', y_dram.ap()[yrows[b]:yrows[b] + L, :], yt[:L, :], r=[yt], w=[(y_dram, yrows[b])])

    nso_sb = GEOM['NSO_B'] // NB
    so_blocks = [(NT * i, [128] * NB) for i in range(nso_sb)] + [(128 * GEOM['NSO_B'], [16])]
    for (row0, Ls) in so_blocks:
        L = Ls[0]
        if L == 16:
            V_(lambda e: e.tensor_scalar(out=ST[:].rearrange("p k i -> p (k i)"), in0=ST[:].rearrange("p k i -> p (k i)"), scalar1=flg[:, 0:1], scalar2=None, op0=ALU.mult),
               r=[ST, flg], w=[ST])
            A_(lambda e: e.activation(out=STb[:], in_=ST[:], func=AF.Copy), r=[ST], w=[STb])
            V_(lambda e: e.tensor_scalar(out=car[:], in0=car[:], scalar1=flg[:, 0:1], scalar2=None, op0=ALU.mult), r=[car, flg], w=[car])
        rows = [row0 + L * b for b in range(len(Ls))]
        front(xso, rows, Ls, rows, rows, (O_k, O_v, O_ki, rows), own=False)
        carry = [(car, car)] + [None] * (len(Ls) - 1)
        save = [None] * (len(Ls) - 1) + [(car, car)]
        rwkv_prep(Ls, False, carry, save)
        if L == 16:
            noop = lambda xs: None
            for half in range(2):
                wr_ = load_w(win_v, C_RW + 512 * half, 512)
                for pp in range(4):
                    rw_tile(4 * half + pp, wr_, 128 * pp, Ls, noop, carry, save)
            wl_ = load_w(win_v, C_RW + 3072, 256)
            rw_tile(25, wl_, 128, Ls, noop, carry, save)
        for b in range(len(Ls)):
            chunk_scan(b, L, own=False)

    for sbi in range(GEOM['NOWN_B'] // NB):
        Ls = [128] * NB
        rows = [NT * sbi + 128 * b for b in range(NB)]
        slots = [NSO + r_ for r_ in rows]
        front(xown, rows, Ls, slots, slots, (O_k, O_v, O_ki, slots), own=True)
        own_proj(Ls, slots)
        carry = [(car, car)] + [None] * (NB - 1)
        save = [None] * (NB - 1) + [(car, car)]
        rwkv_prep(Ls, True, carry, save)
        for b in range(NB):
            chunk_scan(b, 128, own=True)
        for b in range(NB):
            attention(b, 128, 0, slots[b] + 128, corner=True)
        post(Ls, O_y, rows)
    P.dma('sp', O_wkv.ap(), ST[:].rearrange("p k i -> p (k i)"), r=[ST], w=[O_wkv])
    P.dma('sp', O_shift.ap(), car[:], r=[car], w=[O_shift])

    if SAMPLE:
        for q in range(2):
            sb0 = NSLOT + SSTRIDE * q
            for i0 in range(0, CACHE_ROWS, 128):
                L = min(128, CACHE_ROWS - i0)
                ct = xin.next()
                P.dma('sp', ct[:L, 0, 0:128], ck.ap()[q, i0:i0 + L, :], r=[ck], w=[ct])
                P.dma('sp', ct[:L, 0, 128:192], cik.ap()[q, i0:i0 + L, :], r=[cik], w=[ct])
                pt_ = pm.next()
                T_(lambda e, ct=ct, pt_=pt_, L=L: e.transpose(out=pt_[:, 0:L], in_=ct[:L, 0, 0:128], identity=identF[:L, :L]), r=[ct, identF], w=[pt_])
                T_(lambda e, ct=ct, pt_=pt_, L=L: e.transpose(out=pt_[0:64, 128:128 + L], in_=ct[:L, 0, 128:192], identity=identF[:L, :L]), r=[ct, identF], w=[pt_])
                kt_t = ktt.next(); kit_t = kitt.next()
                V_(lambda e, kt_t=kt_t, pt_=pt_, L=L: e.tensor_copy(out=kt_t[:, 0:L], in_=pt_[:, 0:L]), r=[pt_], w=[kt_t])
                V_(lambda e, kit_t=kit_t, pt_=pt_, L=L: e.tensor_copy(out=kit_t[:, 0:L], in_=pt_[0:64, 128:128 + L]), r=[pt_], w=[kit_t])
                P.dma('pool', D_kt.ap()[:, sb0 + i0:sb0 + i0 + L], kt_t[:, 0:L], r=[kt_t], w=[(D_kt, sb0 + i0)])
                P.dma('pool', D_kit.ap()[:, sb0 + i0:sb0 + i0 + L], kit_t[:, 0:L], r=[kit_t], w=[(D_kit, sb0 + i0)])
        for q in range(2):
            P.dma('sp', cars[:, q, :], sshift.ap()[q], r=[sshift], w=[(cars, q)])
        for q in range(2):
            sb0 = NSLOT + SSTRIDE * q
            Ls = [64]
            rows = [64 * q]
            rrows = [NSO + NOWN + 64 * q]
            slots = [sb0 + CACHE_ROWS]
            front(xsm, rows, Ls, rrows, slots, (O_ks, O_vs, O_kis, rows), own=True)
            own_proj(Ls, rrows)
            rwkv_prep(Ls, True, [(cars[:, q, :], (cars, q))], [(cars[:, q, :], (cars, q))])
            P.dma('sp', ST[:].rearrange("p k i -> p (k i)"), swkv.ap()[q], r=[swkv], w=[ST])
            A_(lambda e: e.activation(out=STb[:], in_=ST[:], func=AF.Copy), r=[ST], w=[STb])
            chunk_scan(0, 64, own=True)
            P.dma('sp', O_wkvs.ap()[q], ST[:].rearrange("p k i -> p (k i)"), r=[ST], w=[(O_wkvs, q)])
            P.dma('sp', O_shifts.ap()[q], cars[:, q, :], r=[(cars, q)], w=[(O_shifts, q)])
            attention(0, 64, sb0, sb0 + CACHE_ROWS + 64, corner=False)
            post(Ls, O_ys, rows)
    nc = P.build()
    return nc, P


def make_consts():
    c = {}
    c["identf"] = np.eye(128, dtype=np.float32)
    b = np.zeros((128, 128), np.float32); b[:64, :64] = 1; b[64:, 64:] = 1
    c["blk1"] = b
    us = np.triu(np.ones((128, 128), np.float32), 1)
    ui = np.triu(np.ones((128, 128), np.float32), 0)
    c["cmask"] = np.concatenate([us, us.T, us, ui, ui], axis=1).astype(np.float32)
    return c


def rope_table(pos):
    pos = np.asarray(pos, np.float32)
    out = np.zeros((len(pos), 48), np.float32)
    for (rot, o) in ((32, 0), (16, 32)):
        inv = (np.float32(500000.0) ** (-np.arange(0, rot, 2, dtype=np.float32) / np.float32(rot))).astype(np.float32)
        ang = (pos[:, None] * inv[None]).astype(np.float32)
        h = rot // 2
        out[:, o:o + h] = np.cos(ang); out[:, o + h:o + 2 * h] = np.sin(ang)
    return out


def colpack(v, n):
    return np.ascontiguousarray(np.asarray(v, np.float32).reshape(n, 128).T)


def st_layout(s):
    s = np.asarray(s, np.float32).reshape(8, 2, 64, 64)
    return np.ascontiguousarray(s.transpose(1, 3, 0, 2).reshape(128, 512))


def st_unlayout(a):
    a = np.asarray(a, np.float32).reshape(2, 64, 8, 64)
    return np.ascontiguousarray(a.transpose(2, 0, 3, 1).reshape(16, 64, 64))


def prep_inputs(inp):
    NSO, NOWN, NSLOT = geom()
    f32 = lambda a: np.ascontiguousarray(np.asarray(a, np.float32))
    consts = make_consts()
    maps = []
    colsf = np.concatenate([
        colpack(inp["ln0_g"], 8), colpack(inp["ln0_b"], 8), colpack(inp["rw_mu"][0], 26), colpack(inp["rw_w0"][0], 8),
        colpack(inp["rw_a0"][0], 8), colpack(inp["rw_k_k"][0], 8), colpack(inp["rw_k_a"][0], 8), colpack(np.asarray(inp["rw_r_k"][0]).reshape(-1), 8)], axis=1)
    shared = dict(consts)
    shared.update(colsf=colsf, w_in=f32(inp["w_in"][0]), rw_w2=f32(inp["rw_w2"][0]), rw_a2=f32(inp["rw_a2"][0]), rw_g2=f32(inp["rw_g2"][0]),
                  ikg=f32(inp["idx_k_ln_g"][0]), ikb=f32(inp["idx_k_ln_b"][0]),
                  ln0_g=f32(inp["ln0_g"]), ln0_b=f32(inp["ln0_b"]), ln1_g=f32(inp["ln1_g"][0]), ln1_b=f32(inp["ln1_b"][0]),
                  ln2_g=f32(inp["ln2_g"][0]), ln2_b=f32(inp["ln2_b"][0]), gn_g=f32(inp["rw_gn_g"][0]), gn_b=f32(inp["rw_gn_b"][0]),
                  w_oa=f32(inp["w_o_attn"][0]), w_or=f32(inp["w_o_rwkv"][0]), w_out=f32(inp["w_out"][0]),
                  w_fg=f32(inp["ffn_w_gate"][0]), w_fu=f32(inp["ffn_w_up"][0]), w_fd=f32(inp["ffn_w_down"][0]))
    meta = f32(inp["meta_tokens"])
    past = int(np.asarray(inp["cache_k"]).shape[2]) - 16
    for c in range(8):
        b, hf = c // 2, c % 2
        xp = f32(inp["x_prompt"][b])
        nfr = NSO - 16
        if hf == 1:
            xso = np.concatenate([meta, xp[:nfr]], 0)
            pos_so = np.arange(NSO)
            xown = xp[nfr:nfr + NOWN]; pos_own = NSO + np.arange(NOWN)
        else:
            xso = np.concatenate([xp[nfr:2 * nfr], meta], 0)
            pos_so = np.concatenate([np.zeros(nfr), np.arange(16)])
            xown = xp[:NOWN]; pos_own = 16 + np.arange(NOWN)
        keyb = np.zeros((1, NSLOT + 2 * SSTRIDE), np.float32)
        if hf == 0:
            keyb[0, :nfr] = -1e30
        xs = f32(inp["x_sample"][2 * c:2 * c + 2]).reshape(128, D)
        pos_sm = np.concatenate([16 + past + np.arange(64)] * 2)
        m = dict(shared)
        m.update(xso=np.ascontiguousarray(xso), xown=np.ascontiguousarray(xown), xsm=xs,
                 rope=rope_table(np.concatenate([pos_so, pos_own, pos_sm])),
                 flag=np.full((128, 1), float(hf), np.float32), keyb=keyb.astype(ml_dtypes.bfloat16),
                 ck=f32(inp["cache_k"][0, 2 * c:2 * c + 2]), cv=f32(inp["cache_v"][0, 2 * c:2 * c + 2]), cik=f32(inp["cache_idx_k"][0, 2 * c:2 * c + 2]),
                 swkv=np.stack([st_layout(inp["state_wkv"][0, 2 * c + q]) for q in range(2)]),
                 sshift=np.stack([colpack(inp["state_shift"][0, 2 * c + q], 26) for q in range(2)]))
        maps.append(m)
    return maps


_CACHE = {}


def run_device(inp):
    key = (GEOM['NSO_B'], GEOM['NOWN_B'], GEOM['SAMPLE'], MAXOPS)
    if key not in _CACHE:
        _CACHE[key] = build_program()
    nc, P = _CACHE[key]
    maps = prep_inputs(inp)
    used = set(P.names)
    maps = [{k: v for k, v in m.items() if k in used} for m in maps]
    res = run_bass_kernel_spmd(nc, maps, core_ids=list(range(8)))
    return res.results


def uncol(a, n):
    return np.ascontiguousarray(np.asarray(a, np.float32).T.reshape(-1))


def kernel(**inputs):
    NSO, NOWN, NSLOT = geom()
    res = run_device(inputs)
    B = 4
    y_p = np.zeros((B, 2 * NOWN, D), np.float32)
    k_p = np.zeros((1, B, NSLOT, 128), np.float32); v_p = np.zeros((1, B, NSLOT, 128), np.float32); ki_p = np.zeros((1, B, NSLOT, 64), np.float32)
    wkv_p = np.zeros((1, B, 16, 64, 64), np.float32); sh_p = np.zeros((1, B, 3328), np.float32)
    y_s = np.zeros((16, 64, D), np.float32)
    k_s = np.zeros((1, 16, 64, 128), np.float32); v_s = np.zeros((1, 16, 64, 128), np.float32); ki_s = np.zeros((1, 16, 64, 64), np.float32)
    wkv_s = np.zeros((1, 16, 16, 64, 64), np.float32); sh_s = np.zeros((1, 16, 3328), np.float32)
    for c in range(8):
        b, hf = c // 2, c % 2
        r = res[c]
        y_p[b, hf * NOWN:(hf + 1) * NOWN] = r["O_y"]
        if hf == 1:
            k_p[0, b] = r["O_k"]; v_p[0, b] = r["O_v"]; ki_p[0, b] = r["O_ki"]
            wkv_p[0, b] = st_unlayout(r["O_wkv"]); sh_p[0, b] = uncol(r["O_shift"], 26)
        y_s[2 * c:2 * c + 2] = r["O_ys"].reshape(2, 64, D)
        k_s[0, 2 * c:2 * c + 2] = r["O_ks"].reshape(2, 64, 128); v_s[0, 2 * c:2 * c + 2] = r["O_vs"].reshape(2, 64, 128)
        ki_s[0, 2 * c:2 * c + 2] = r["O_kis"].reshape(2, 64, 64)
        for q in range(2):
            wkv_s[0, 2 * c + q] = st_unlayout(r["O_wkvs"][q]); sh_s[0, 2 * c + q] = uncol(r["O_shifts"][q], 26)
    return (y_p, y_s, k_p, v_p, ki_p, wkv_p, sh_p, k_s, v_s, ki_s, wkv_s, sh_s)
```

```python
import bisect
import numpy as np
import ml_dtypes
from contextlib import ExitStack
import concourse.bass as bass
import concourse.mybir as mybir
from concourse.bass_utils import run_bass_kernel_spmd

F32 = mybir.dt.float32
BF16 = mybir.dt.bfloat16
U8 = mybir.dt.uint8
ALU = mybir.AluOpType
AF = mybir.ActivationFunctionType

SAME_ENGINE_SYNC = True
MAXOPS = None
ENGS = ('pe', 'act', 'dve', 'pool', 'sp')


class Prog:
    def __init__(self):
        self.nc = bass.Bass("TRN2", target_bir_lowering=False)
        self.es = ExitStack()
        self.ops = []
        self.state = {}
        self.names = set()
        self.psum_names = set()

    def _nm(self, name):
        assert name not in self.names, name
        self.names.add(name)
        return name

    def sb(self, name, shape, dt):
        return self.es.enter_context(self.nc.sbuf_tensor(self._nm(name), list(shape), dt))

    def ps(self, name, shape, dt=F32):
        self.psum_names.add(name)
        return self.es.enter_context(self.nc.psum_tensor(self._nm(name), list(shape), dt))

    def dram(self, name, shape, dt, kind):
        return self.nc.dram_tensor(self._nm(name), list(shape), dt, kind=kind)

    @staticmethod
    def _reg(x):
        if isinstance(x, tuple):
            base, sub = x[0], (x[1],)
        else:
            base, sub = x, ()
        if isinstance(base, Alias):
            return base.name, (base.key,) + sub
        return base.name, sub

    @staticmethod
    def _rel(a, b):
        n = min(len(a), len(b))
        return a[:n] == b[:n]

    def _deps(self, idx, reads, writes, eng=None):
        deps = set()
        for x in reads:
            n, k = self._reg(x)
            st = self.state.setdefault(n, {})
            for kk, ent in st.items():
                if not self._rel(kk, k):
                    continue
                if ent[0] is not None:
                    deps.add(ent[0])
                if n in self.psum_names:
                    for rr in ent[1]:
                        if self.ops[rr]['eng'] != eng:
                            deps.add(rr)
        for x in writes:
            n, k = self._reg(x)
            st = self.state.setdefault(n, {})
            for kk, ent in st.items():
                if not self._rel(kk, k):
                    continue
                if ent[0] is not None:
                    deps.add(ent[0])
                deps.update(ent[1])
        for x in reads:
            n, k = self._reg(x)
            self.state[n].setdefault(k, [None, []])[1].append(idx)
        for x in writes:
            n, k = self._reg(x)
            st = self.state[n]
            for kk in [kk for kk in st if len(kk) >= len(k) and kk[:len(k)] == k]:
                del st[kk]
            st[k] = [idx, []]
        deps.discard(idx)
        return sorted(deps)

    def op(self, eng, fn, r=(), w=()):
        if MAXOPS is not None and len(self.ops) >= MAXOPS:
            return None
        idx = len(self.ops)
        self.ops.append(dict(eng=eng, fn=fn, deps=self._deps(idx, r, w, eng), dma=False, semkey=None))
        return idx

    def dma(self, q, out, in_, r=(), w=(), semkey=None, **kw):
        if MAXOPS is not None and len(self.ops) >= MAXOPS:
            return None
        idx = len(self.ops)
        deps = self._deps(idx, r, w)
        if semkey is None:
            wn = self._reg(w[0])[0]
            semkey = wn if not (wn.startswith('D_') or wn.startswith('O_')) else self._reg(r[0])[0]
        fn = (lambda e, out=out, in_=in_, kw=kw: e.dma_start(out=out, in_=in_, **kw))
        self.ops.append(dict(eng=q, fn=fn, deps=deps, dma=True, semkey=semkey))
        return idx

    def build(self):
        nc, ops = self.nc, self.ops
        n = len(ops)
        need_sig = [False] * n
        for i, o in enumerate(ops):
            for j in o['deps']:
                pj = ops[j]
                if pj['dma']:
                    continue
                if pj['eng'] != o['eng'] or o['dma'] or (SAME_ENGINE_SYNC and o['eng'] != 'pe'):
                    need_sig[j] = True
        cnt = {e: 0 for e in ENGS}
        sigval = [0] * n
        dcnt, semkeys = {}, []
        for i, o in enumerate(ops):
            if o['dma']:
                k = o['semkey']
                if k not in dcnt:
                    dcnt[k] = 0
                    semkeys.append(k)
                dcnt[k] += 16
                sigval[i] = dcnt[k]
            elif need_sig[i]:
                cnt[o['eng']] += 1
                sigval[i] = cnt[o['eng']]
        esem = {e: self.es.enter_context(nc.semaphore("s_" + e)) for e in ENGS}
        dsem = {k: self.es.enter_context(nc.semaphore("d_" + k)) for k in semkeys}
        self.n_sems = len(esem) + len(dsem)
        dma_idx = {}
        for i, o in enumerate(ops):
            if o['dma']:
                dma_idx.setdefault(o['semkey'], []).append(i)
        waited = {e: {} for e in ENGS}
        plan = {e: [] for e in ENGS}
        for i, o in enumerate(ops):
            E = o['eng']
            waits = {}
            for j in o['deps']:
                pj = ops[j]
                if pj['dma']:
                    key = ('d', pj['semkey'])
                    lst = dma_idx[pj['semkey']]
                    val_d = 16 * bisect.bisect_left(lst, i)
                else:
                    if pj['eng'] == E and not o['dma'] and (E == 'pe' or not SAME_ENGINE_SYNC):
                        continue
                    key = ('e', pj['eng'])
                val = val_d if pj['dma'] else sigval[j]
                if waited[E].get(key, 0) >= val:
                    continue
                waits[key] = max(waits.get(key, 0), val)
            for key, val in waits.items():
                waited[E][key] = val
            plan[E].append((i, waits))
        blk = self.es.enter_context(nc.Block())
        engobj = {'pe': 'tensor', 'act': 'scalar', 'dve': 'vector', 'pool': 'gpsimd', 'sp': 'sync'}

        def emit_for(E):
            def body(eng):
                for i, waits in plan[E]:
                    o = ops[i]
                    for (kind, k), val in waits.items():
                        eng.wait_ge(dsem[k] if kind == 'd' else esem[k], val)
                    ins = o['fn'](eng)
                    if o['dma']:
                        ins.then_inc(dsem[o['semkey']], 16)
                    elif need_sig[i]:
                        ins.then_inc(esem[E], 1)
                if E == 'sp':
                    for k, v in dcnt.items():
                        eng.wait_ge(dsem[k], v)
                    for e2 in ENGS:
                        if e2 != 'sp' and cnt[e2] > 0:
                            eng.wait_ge(esem[e2], cnt[e2])
            return body

        for E in ENGS:
            getattr(blk, engobj[E])(emit_for(E))
        self.es.close()
        return nc


class Alias:
    def __init__(self, base, dtype, byte_off, shape, key):
        es = 2 if dtype == BF16 else 4
        self.h = base.bitcast(dtype)
        self.off = byte_off // es
        self.shape = list(shape)
        self.name = base.name
        self.key = key
        n = int(np.prod(shape[1:]))
        v = self.h[0:shape[0], self.off:self.off + n]
        if len(shape) == 3:
            v = v.rearrange("p (a b) -> p a b", a=shape[1])
        self.v = v

    def __getitem__(self, idx):
        return self.v[idx]


class Ring:
    def __init__(self, tiles):
        self.t, self.i = tiles, 0

    def next(self):
        t = self.t[self.i % len(self.t)]
        self.i += 1
        return t


D = 1024
DFF = 2816
GEOM = dict(NSO_B=32, NOWN_B=32, SAMPLE=True)
NB = 1
NT = 128 * NB
WIN_COLS = 7240
C_Q, C_K, C_V, C_QI, C_KI, C_WI, C_G, C_RW = 0, 1024, 1152, 1280, 1792, 1856, 1864, 3912
LN_EPS = 1e-5
GN_EPS = 64e-5
DEC = 0.6065306597126334
ALPHA = 2.0 ** 0.25
CACHE_ROWS = 2064
SSTRIDE = 2176
NBIS = 19
TOPK = 256
O_G0, O_B0, O_MU, O_W0, O_A0, O_KK, O_KA, O_RK, NCOLS = 0, 8, 16, 42, 50, 58, 66, 74, 82


def geom():
    nso = 128 * GEOM['NSO_B'] + 16
    nown = 128 * GEOM['NOWN_B']
    return nso, nown, nso + nown


def build_program():
    NSO, NOWN, NSLOT = geom()
    SAMPLE = GEOM['SAMPLE']
    NSLOT_ALL = NSLOT + 2 * SSTRIDE
    P = Prog()
    nc = P.nc
    V_ = lambda fn, r=(), w=(): P.op('dve', fn, r, w)
    A_ = lambda fn, r=(), w=(): P.op('act', fn, r, w)
    G_ = lambda fn, r=(), w=(): P.op('pool', fn, r, w)
    T_ = lambda fn, r=(), w=(): P.op('pe', fn, r, w)

    din = lambda n, s, dt=F32: P.dram(n, s, dt, "ExternalInput")
    dout = lambda n, s, dt=F32: P.dram(n, s, dt, "ExternalOutput")
    dint = lambda n, s, dt=BF16: P.dram(n, s, dt, "Internal")
    xso = din("xso", [NSO, D]); xown = din("xown", [NOWN, D]); xsm = din("xsm", [128, D])
    rope = din("rope", [NSO + NOWN + 128, 48])
    flag = din("flag", [128, 1])
    colsf = din("colsf", [128, NCOLS])
    cmask = din("cmask", [128, 640])
    identf = din("identf", [128, 128])
    blk1 = din("blk1", [128, 128])
    keyb = din("keyb", [1, NSLOT_ALL], BF16)
    w_in = din("w_in", [D, WIN_COLS])
    rw_w2 = din("rw_w2", [64, D]); rw_a2 = din("rw_a2", [64, D]); rw_g2 = din("rw_g2", [128, D])
    ikg = din("ikg", [64]); ikb = din("ikb", [64])
    vecs = {n: din(n, [D]) for n in ("ln0_g", "ln0_b", "ln1_g", "ln1_b", "ln2_g", "ln2_b", "gn_g", "gn_b")}
    w_oa = din("w_oa", [D, D]); w_or = din("w_or", [D, D]); w_out = din("w_out", [D, D])
    w_fg = din("w_fg", [D, DFF]); w_fu = din("w_fu", [D, DFF]); w_fd = din("w_fd", [DFF, D])
    ck = din("ck", [2, CACHE_ROWS, 128]); cv = din("cv", [2, CACHE_ROWS, 128]); cik = din("cik", [2, CACHE_ROWS, 64])
    swkv = din("swkv", [2, 128, 512]); sshift = din("sshift", [2, 128, 26])

    O_y = dout("O_y", [NOWN, D]); O_ys = dout("O_ys", [128, D])
    O_k = dout("O_k", [NSLOT, 128]); O_v = dout("O_v", [NSLOT, 128]); O_ki = dout("O_ki", [NSLOT, 64])
    O_wkv = dout("O_wkv", [128, 512]); O_shift = dout("O_shift", [128, 26])
    O_ks = dout("O_ks", [128, 128]); O_vs = dout("O_vs", [128, 128]); O_kis = dout("O_kis", [128, 64])
    O_wkvs = dout("O_wkvs", [2, 128, 512]); O_shifts = dout("O_shifts", [2, 128, 26])

    D_win = dint("D_win", [D, WIN_COLS])
    D_w2 = dint("D_w2", [64, D]); D_a2 = dint("D_a2", [64, D]); D_g2 = dint("D_g2", [128, D])
    D_woa = dint("D_woa", [D, D]); D_wor = dint("D_wor", [D, D]); D_wout = dint("D_wout", [D, D])
    D_wfg = dint("D_wfg", [D, DFF]); D_wfu = dint("D_wfu", [D, DFF]); D_wfd = dint("D_wfd", [DFF, D])
    D_kt = dint("D_kt", [128, NSLOT_ALL]); D_kit = dint("D_kit", [64, NSLOT_ALL]); D_v = dint("D_v", [NSLOT_ALL, 128])

    def cast_rows(dst, src, nrows, step):
        for i in range(0, nrows, step):
            n = min(step, nrows - i)
            P.dma('pool', dst.ap()[i:i + n, :], src.ap()[i:i + n, :], r=[src], w=[(dst, i)])
    cast_rows(D_win, w_in, D, 128)
    P.dma('pool', D_w2.ap(), rw_w2.ap(), r=[rw_w2], w=[D_w2])
    P.dma('pool', D_a2.ap(), rw_a2.ap(), r=[rw_a2], w=[D_a2])
    P.dma('pool', D_g2.ap(), rw_g2.ap(), r=[rw_g2], w=[D_g2])
    for (dd, ss, nr) in ((D_woa, w_oa, D), (D_wor, w_or, D), (D_wout, w_out, D), (D_wfg, w_fg, D), (D_wfu, w_fu, D), (D_wfd, w_fd, DFF)):
        cast_rows(dd, ss, nr, 256)
    if SAMPLE:
        for q in range(2):
            sb0 = NSLOT + SSTRIDE * q
            P.dma('pool', D_v.ap()[sb0:sb0 + CACHE_ROWS, :], cv.ap()[q], r=[cv], w=[(D_v, 'c%d' % q)])
    kview = lambda dt_: dt_.ap().rearrange("(k p) n -> p k n", p=128)
    win_v = kview(D_win)

    cols = P.sb("cols", [128, NCOLS], F32)
    colsd = P.sb("colsd", [128, 34], F32)
    identF = P.sb("identF", [128, 128], F32)
    identB = P.sb("identB", [128, 128], BF16)
    I4 = P.sb("I4", [128, 4, 128], BF16)
    blkones = P.sb("blkones", [128, 128], F32)
    blkonesB = P.sb("blkonesB", [128, 128], BF16)
    mask = P.sb("mask", [128, 640], F32)
    ones128 = P.sb("ones128", [128, 128], F32)
    onesrow = P.sb("onesrow", [1, 128], BF16)
    ikg_b = P.sb("ikg_b", [128, 64], F32); ikb_b = P.sb("ikb_b", [128, 64], F32)
    w2b = P.sb("w2b", [128, D], BF16); a2b = P.sb("a2b", [128, D], BF16); g2b = P.sb("g2b", [128, D], BF16)
    flg = P.sb("flg", [128, 1], F32)
    vbr = Ring([P.sb("vbr%d" % i, [128, D], F32) for i in range(4)])

    def getvec(n):
        t = vbr.next()
        P.dma('sp', t[:], vecs[n].ap().partition_broadcast(128), r=[vecs[n]], w=[t])
        return t
    P.dma('sp', cols[:], colsf.ap(), r=[colsf], w=[cols])
    P.dma('sp', identF[:], identf.ap(), r=[identf], w=[identF])
    P.dma('sp', blkones[:], blk1.ap(), r=[blk1], w=[blkones])
    P.dma('sp', mask[:], cmask.ap(), r=[cmask], w=[mask])
    P.dma('sp', flg[:], flag.ap(), r=[flag], w=[flg])
    P.dma('sp', ikg_b[:], ikg.ap().partition_broadcast(128), r=[ikg], w=[ikg_b])
    P.dma('sp', ikb_b[:], ikb.ap().partition_broadcast(128), r=[ikb], w=[ikb_b])
    P.dma('sp', w2b[0:64, :], D_w2.ap(), r=[D_w2], w=[w2b])
    P.dma('sp', a2b[64:128, :], D_a2.ap(), r=[D_a2], w=[a2b])
    P.dma('sp', g2b[:], D_g2.ap(), r=[D_g2], w=[g2b])
    V_(lambda e: e.tensor_copy(out=identB[:], in_=identF[:]), r=[identF], w=[identB])
    for i in range(4):
        V_(lambda e, i=i: e.tensor_copy(out=I4[:, i, :], in_=identF[:]), r=[identF], w=[I4])
    V_(lambda e: e.tensor_copy(out=blkonesB[:], in_=blkones[:]), r=[blkones], w=[blkonesB])
    V_(lambda e: e.memset(ones128[:], 1.0), w=[ones128])
    V_(lambda e: e.memset(onesrow[:], 1.0), w=[onesrow])
    V_(lambda e: e.tensor_scalar(out=colsd[:, 0:26], in0=cols[:, O_MU:O_MU + 26], scalar1=-1.0, scalar2=1.0, op0=ALU.mult, op1=ALU.add),
       r=[cols], w=[colsd])
    V_(lambda e: e.tensor_scalar(out=colsd[:, 26:34], in0=cols[:, O_KA:O_KA + 8], scalar1=-1.0, scalar2=1.0, op0=ALU.mult, op1=ALU.add),
       r=[cols], w=[colsd])

    pm = Ring([P.ps("pm0", [128, 512]), P.ps("pm1", [128, 512])])
    ptr = P.ps("ptr", [128, 8, 128], BF16)
    pLT = P.ps("pLT", [128, 1024])
    pO = P.ps("pO", [128, 1536])
    cbanks = [(pLT, 0), (pLT, 1), (pO, 0), (pO, 1)]
    cb_ap = lambda i: (cbanks[i][0])[:, 512 * cbanks[i][1]:512 * (cbanks[i][1] + 1)]
    xbanks = [(pLT, 0), (pLT, 1), (pO, 2)]
    xq = [0]

    xin = Ring([P.sb("xin%d" % i, [128, NB, D], F32) for i in range(2)])
    hn = P.sb("hn", [128, NB, D], BF16)
    hres = P.sb("hres", [128, NB, D], F32)
    hT = P.sb("hT", [128, 8, NT], BF16)
    small = Ring([P.sb("small%d" % i, [128, 24], F32) for i in range(4)])
    wt = Ring([P.sb("wt%d" % i, [128, 8, 512], BF16) for i in range(3)])
    kvs = [P.sb("kvs%d" % i, [128, 328], F32) for i in range(NB)]
    rtmp = Ring([P.sb("rtmp%d" % i, [128, 512], F32) for i in range(2)])
    rp = [P.sb("rp%d" % i, [128, 48], F32) for i in range(NB)]
    vbt = Ring([P.sb("vbt%d" % i, [128, 128], BF16) for i in range(2)])
    ktt = Ring([P.sb("ktt%d" % i, [128, NT], BF16) for i in range(2)])
    kitt = Ring([P.sb("kitt%d" % i, [64, NT], BF16) for i in range(2)])
    ki2 = Ring([P.sb("ki2_%d" % i, [128, 64], F32) for i in range(2)])
    qT = P.sb("qT", [128, NB, 8, 128], BF16)
    qiT = P.sb("qiT", [128, NB, 4, 128], BF16)
    gsig = P.sb("gsig", [128, NB, 2 * D], BF16)
    car = P.sb("car", [128, 26], F32)
    cars = P.sb("cars", [128, 2, 26], F32)
    TL = P.sb("TL", [128, NT], BF16)
    SLG = P.sb("SLG", [128, NT], BF16)
    f32t = Ring([P.sb("f32t%d" % i, [128, NT], F32) for i in range(8)])
    AT = P.sb("AT", [128, 8, NT], BF16); KTt = P.sb("KTt", [128, 8, NT], BF16); BTt = P.sb("BTt", [128, 8, NT], BF16)
    VTf = P.sb("VTf", [128, 8, NT], BF16); RTt = P.sb("RTt", [128, 8, NT], BF16); KMR = P.sb("KMR", [128, 8, NT], BF16)
    PRD = Ring([P.sb("PRD%d" % i, [128, NT], BF16) for i in range(2)])
    CS = P.sb("CS", [128, 8, NB, 129], F32)
    GC = P.sb("GC", [128, NB, 8], F32)
    Ktok = P.sb("Ktok", [128, NB, D], BF16); Btok = P.sb("Btok", [128, NB, D], BF16); Vtok = P.sb("Vtok", [128, NB, D], BF16)
    gtok = P.sb("gtok", [128, NB, D], BF16)
    bcoef = P.sb("bcoef", [128, NB, 16], F32)
    ST = P.sb("ST", [128, 8, 64], F32); STb = P.sb("STb", [128, 8, 64], BF16)
    F1 = P.sb("F1", [128, D], F32); F2 = P.sb("F2", [128, D], F32)
    Yt, Y2 = F1, F2
    gns = P.sb("gns", [128, 80], F32)
    orwT = P.sb("orwT", [128, NB, 8, 128], BF16)
    orw = P.sb("orw", [128, D], BF16)
    q32 = F1
    SMAX = max(NSLOT, 8208)
    SC = P.sb("SC", [128, SMAX], F32)
    aoff = [0]

    def alias(key, shape, dt_):
        nbytes = int(np.prod(shape[1:])) * (2 if dt_ == BF16 else 4)
        a = Alias(SC, dt_, aoff[0], shape, key)
        aoff[0] += nbytes
        assert aoff[0] <= 4 * SMAX, aoff[0]
        return a
    amat = Ring([alias("amat%d" % i, [128, 4, 128], BF16) for i in range(4)])
    pq = [Ring([alias("pq%d_%d" % (h, i), [128, 2, 128], BF16) for i in range(2)]) for h in range(4)]
    Mt = [Ring([alias("Mt%d_%d" % (h, i), [128, 128], BF16) for i in range(2)]) for h in range(4)]
    Mfin = alias("Mfin", [128, 16, 128], BF16)
    AKs = alias("AKs", [128, 16, 128], BF16)
    ARB = alias("ARB", [128, 16, 128], BF16); ARK = alias("ARK", [128, 16, 128], BF16)
    Wb = alias("Wb", [128, 16, 64], BF16); Ub = alias("Ub", [128, 16, 64], BF16)
    sttmp = alias("sttmp", [128, 8, 64], F32)
    junk = P.sb("junk", [128, 2], BF16)
    tau = P.sb("tau", [128, 8], F32)
    dg = P.sb("dg", [128, 8, 128], BF16)
    kitl = Ring([P.sb("kitl%d" % i, [128, 512], BF16) for i in range(2)])
    kbt = Ring([P.sb("kbt%d" % i, [1, 512], BF16) for i in range(2)])
    ktl = Ring([P.sb("ktl%d" % i, [128, 512], BF16) for i in range(2)])
    vtl = Ring([P.sb("vtl%d" % i, [128, 4, 130], BF16) for i in range(2)])
    mbt = Ring([P.sb("mbt%d" % i, [128, 512], BF16) for i in range(2)])
    Rt = Ring([P.sb("Rt%d" % i, [128, 512], BF16) for i in range(2)])
    PTt = Ring([P.sb("PTt%d" % i, [128, 1024], BF16) for i in range(2)])
    rden = P.sb("rden", [128, 8], F32)
    attn = P.sb("attn", [128, D], BF16)
    attnT = P.sb("attnT", [128, NB, 8, 128], BF16)
    mixs = [attn]
    mixT = P.sb("mixT", [128, 8, NT], BF16)
    x1 = F1.reshape([128, 1, D])
    x1b = hn.reshape([128, D])
    x1T = P.sb("x1T", [128, 8, NT], BF16)
    actg = Ring([P.sb("actg%d" % i, [128, NT], BF16) for i in range(2)])
    actT = Ring([P.sb("actT%d" % i, [128, NT], BF16) for i in range(3)])
    yo = Ring([F2])

    V_(lambda e: e.memset(ST[:], 0.0), w=[ST])
    V_(lambda e: e.memset(STb[:], 0.0), w=[STb])
    V_(lambda e: e.memset(car[:], 0.0), w=[car])
    V_(lambda e: e.memset(CS[:], 0.0), w=[CS])
    for t in vtl.t:
        V_(lambda e, t=t: e.memset(t[:], 1.0), w=[t])

    def load_w(view, c0, ncol):
        t = wt.next()
        P.dma('sp', t[:, :, 0:ncol], view[:, :, c0:c0 + ncol], r=[view.tensor], w=[t])
        return t

    def ln_rows(x_ap, L, n, out_ap, eps, g_b=None):
        xreg, oreg = x_ap.tensor, out_ap.tensor
        sm = small.next()
        nch = (n + 511) // 512
        for c in range(nch):
            V_(lambda e, c=c: e.bn_stats(out=sm[:L, 6 * c:6 * c + 6], in_=x_ap[:, 512 * c:min(n, 512 * (c + 1))]), r=[xreg], w=[sm])
        V_(lambda e: e.bn_aggr(out=sm[:L, 12:14], in_=sm[:L, 0:6 * nch]), r=[sm], w=[sm])
        A_(lambda e: e.activation(out=sm[:L, 14:15], in_=sm[:L, 13:14], func=AF.Sqrt, bias=float(eps), scale=1.0), r=[sm], w=[sm])
        V_(lambda e: e.reciprocal(out=sm[:L, 15:16], in_=sm[:L, 14:15]), r=[sm], w=[sm])
        V_(lambda e: e.tensor_scalar(out=sm[:L, 16:17], in0=sm[:L, 12:13], scalar1=sm[:L, 15:16], scalar2=-1.0, op0=ALU.mult, op1=ALU.mult),
           r=[sm], w=[sm])
        A_(lambda e: e.activation(out=out_ap, in_=x_ap, func=AF.Identity, scale=sm[:L, 15:16], bias=sm[:L, 16:17]),
           r=[sm, xreg], w=[oreg])
        if g_b is not None:
            g, b = g_b
            G_(lambda e: e.tensor_tensor(out=out_ap, in0=out_ap, in1=g[:L, 0:n], op=ALU.mult), r=[oreg, g], w=[oreg])
            G_(lambda e: e.tensor_tensor(out=out_ap, in0=out_ap, in1=b[:L, 0:n], op=ALU.add), r=[oreg, b], w=[oreg])

    def rope_rows(t, L, c0, half, cos_ap, sin_ap, nh=1, stride=0):
        tab = cos_ap.tensor
        tm = rtmp.next()
        if nh == 1:
            x1_ = t[:L, c0:c0 + half]; x2_ = t[:L, c0 + half:c0 + 2 * half]
            a = tm[:L, 0:half]; b = tm[:L, half:2 * half]; c = tm[:L, 2 * half:3 * half]; d = tm[:L, 3 * half:4 * half]
            cs, sn = cos_ap, sin_ap
        else:
            v = t[:L, c0:c0 + nh * stride].rearrange("p (h d) -> p h d", h=nh)
            x1_ = v[:, :, 0:half]; x2_ = v[:, :, half:2 * half]
            tv = tm[:L, 0:4 * nh * half].rearrange("p (q h d) -> p q h d", q=4, h=nh)
            a, b, c, d = tv[:, 0], tv[:, 1], tv[:, 2], tv[:, 3]
            cs = cos_ap.unsqueeze(1).to_broadcast([L, nh, half]); sn = sin_ap.unsqueeze(1).to_broadcast([L, nh, half])
        rr = [t, tm, tab]
        V_(lambda e: e.tensor_tensor(out=a, in0=cs, in1=x1_, op=ALU.mult), r=rr, w=[tm])
        V_(lambda e: e.tensor_tensor(out=b, in0=sn, in1=x2_, op=ALU.mult), r=rr, w=[tm])
        V_(lambda e: e.tensor_tensor(out=c, in0=cs, in1=x2_, op=ALU.mult), r=rr, w=[tm])
        V_(lambda e: e.tensor_tensor(out=d, in0=sn, in1=x1_, op=ALU.mult), r=rr, w=[tm])
        V_(lambda e: e.tensor_tensor(out=x1_, in0=a, in1=b, op=ALU.subtract), r=[tm], w=[t])
        V_(lambda e: e.tensor_tensor(out=x2_, in0=c, in1=d, op=ALU.add), r=[tm], w=[t])

    evq = [0]

    def evac_copy(out_ap, in_ap, r, w):
        evq[0] += 1
        if evq[0] % 2:
            A_(lambda e: e.activation(out=out_ap, in_=in_ap, func=AF.Copy), r=r, w=w)
        else:
            V_(lambda e: e.tensor_copy(out=out_ap, in_=in_ap), r=r, w=w)

    def transpose_to(dst_fn, src_fn, n, L, r, w, ident=None):
        for k in range(n):
            T_(lambda e, k=k: e.transpose(out=ptr[:, k, 0:L], in_=src_fn(k), identity=identB[:L, :L]), r=r + [identB], w=[ptr])

    def front(x_dram, rows, Ls, rope_rows0, slots, outs, own):
        O_k_, O_v_, O_ki_, orow = outs
        xt = xin.next()
        L = Ls[0]
        ntk = sum(Ls)
        for b in range(len(Ls)):
            P.dma('sp', xt[:L, b, :], x_dram.ap()[rows[b]:rows[b] + L, :], r=[x_dram], w=[(xt, b)])
            P.dma('sp', rp[b][:L, :], rope.ap()[rope_rows0[b]:rope_rows0[b] + L, :], r=[rope], w=[rp[b]])
        for b in range(len(Ls)):
            if own:
                ln_rows(xt[:L, b, :], L, D, hres[:L, b, :], LN_EPS)
                G_(lambda e, b=b: e.tensor_copy(out=hn[:L, b, :], in_=hres[:L, b, :]), r=[(hres, b)], w=[(hn, b)])
                vg, vbb = getvec("ln0_g"), getvec("ln0_b")
                G_(lambda e, b=b, vg=vg: e.tensor_tensor(out=hres[:L, b, :], in0=hres[:L, b, :], in1=vg[:L, :], op=ALU.mult),
                   r=[(hres, b), vg], w=[(hres, b)])
                G_(lambda e, b=b, vbb=vbb: e.tensor_tensor(out=hres[:L, b, :], in0=hres[:L, b, :], in1=vbb[:L, :], op=ALU.add),
                   r=[(hres, b), vbb], w=[(hres, b)])
            else:
                ln_rows(xt[:L, b, :], L, D, hn[:L, b, :], LN_EPS)
            for k in range(8):
                T_(lambda e, b=b, k=k: e.transpose(out=ptr[:, k, 0:L], in_=hn[:L, b, 128 * k:128 * (k + 1)], identity=identB[:L, :L]),
                   r=[hn, identB], w=[ptr])
            for k in range(8):
                if k % 2:
                    A_(lambda e, b=b, k=k: e.activation(out=hT[:, k, L * b:L * b + L], in_=ptr[:, k, 0:L], func=AF.Identity,
                                                        scale=cols[:, O_G0 + k:O_G0 + k + 1], bias=cols[:, O_B0 + k:O_B0 + k + 1]),
                       r=[ptr, cols], w=[(hT, (b, k))])
                else:
                    V_(lambda e, b=b, k=k: e.tensor_scalar(out=hT[:, k, L * b:L * b + L], in0=ptr[:, k, 0:L],
                                                           scalar1=cols[:, O_G0 + k:O_G0 + k + 1], scalar2=cols[:, O_B0 + k:O_B0 + k + 1],
                                                           op0=ALU.mult, op1=ALU.add), r=[ptr, cols], w=[(hT, (b, k))])
        wk = wt.next()
        P.dma('sp', wk[:, :, 0:256], win_v[:, :, C_K:C_K + 256], r=[D_win], w=[wk])
        P.dma('sp', wk[:, :, 256:328], win_v[:, :, C_KI:C_KI + 72], r=[D_win], w=[wk])
        kt_t = ktt.next(); kit_t = kitt.next()
        for b in range(len(Ls)):
            pk = pm.next()
            for k in range(8):
                T_(lambda e, b=b, k=k, pk=pk: e.matmul(pk[:L, 0:328], lhsT=hT[:, k, L * b:L * b + L], rhs=wk[:, k, 0:328],
                                                    start=(k == 0), stop=(k == 7)), r=[hT, wk], w=[pk])
            kv = kvs[b]
            A_(lambda e, pk=pk, kv=kv: e.activation(out=kv[:L, :], in_=pk[:L, 0:328], func=AF.Copy), r=[pk], w=[kv])
            rt = rp[b]
            rope_rows(kv, L, 0, 16, rt[:L, 0:16], rt[:L, 16:32])
            k2 = ki2.next()
            ln_rows(kv[:L, 256:320], L, 64, k2[:L, :], LN_EPS, g_b=(ikg_b, ikb_b))
            rope_rows(k2, L, 0, 8, rt[:L, 32:40], rt[:L, 40:48])
            s0 = slots[b]; o0 = orow[b]
            P.dma('pool', O_k_.ap()[o0:o0 + L, :], kv[:L, 0:128], r=[kv], w=[(O_k_, o0)])
            P.dma('pool', O_v_.ap()[o0:o0 + L, :], kv[:L, 128:256], r=[kv], w=[(O_v_, o0)])
            P.dma('pool', O_ki_.ap()[o0:o0 + L, :], k2[:L, :], r=[k2], w=[(O_ki_, o0)])
            vb = vbt.next()
            G_(lambda e, kv=kv, vb=vb: e.tensor_copy(out=vb[:L, :], in_=kv[:L, 128:256]), r=[kv], w=[vb])
            P.dma('pool', D_v.ap()[s0:s0 + L, :], vb[:L, :], r=[vb], w=[(D_v, s0)])
            pt_ = pm.next()
            T_(lambda e, kv=kv, pt_=pt_: e.transpose(out=pt_[:, 0:L], in_=kv[:L, 0:128], identity=identF[:L, :L]), r=[kv, identF], w=[pt_])
            T_(lambda e, k2=k2, pt_=pt_: e.transpose(out=pt_[0:64, 128:128 + L], in_=k2[:L, 0:64], identity=identF[:L, :L]), r=[k2, identF], w=[pt_])
            V_(lambda e, b=b, kt_t=kt_t, pt_=pt_: e.tensor_copy(out=kt_t[:, L * b:L * b + L], in_=pt_[:, 0:L]), r=[pt_], w=[kt_t])
            V_(lambda e, b=b, kit_t=kit_t, pt_=pt_: e.tensor_copy(out=kit_t[:, L * b:L * b + L], in_=pt_[0:64, 128:128 + L]), r=[pt_], w=[kit_t])
            P.dma('pool', D_kt.ap()[:, s0:s0 + L], kt_t[:, L * b:L * b + L], r=[kt_t], w=[(D_kt, s0)])
            P.dma('pool', D_kit.ap()[:, s0:s0 + L], kit_t[:, L * b:L * b + L], r=[kit_t], w=[(D_kit, s0)])

    def own_proj(Ls, rope_rows0):
        L = Ls[0]
        nbk = len(Ls)
        wq = [load_w(win_v, C_Q + 512 * g, 512) for g in range(2)]
        for b in range(nbk):
            for g in range(2):
                pq_ = pm.next()
                for k in range(8):
                    T_(lambda e, b=b, g=g, k=k, pq_=pq_: e.matmul(pq_[:L, :], lhsT=hT[:, k, L * b:L * b + L], rhs=wq[g][:, k, :], start=(k == 0), stop=(k == 7)),
                       r=[hT, wq[g]], w=[pq_])
                evac_copy(q32[:L, 512 * g:512 * (g + 1)], pq_[:L, :], r=[pq_], w=[(q32, g)])
            rope_rows(q32, L, 0, 16, rp[b][:L, 0:16], rp[b][:L, 16:32], nh=8, stride=128)
            for h in range(8):
                T_(lambda e, h=h: e.transpose(out=pLT[:, 128 * h:128 * h + L], in_=q32[:L, 128 * h:128 * (h + 1)], identity=identF[:L, :L]),
                   r=[q32, identF], w=[(pLT, h // 4)])
            evac_copy(qT[:, b, 0:4, 0:L], pLT[:, 0:512].rearrange("p (h t) -> p h t", h=4)[:, :, 0:L], r=[(pLT, 0)], w=[(qT, b)])
            evac_copy(qT[:, b, 4:8, 0:L], pLT[:, 512:1024].rearrange("p (h t) -> p h t", h=4)[:, :, 0:L], r=[(pLT, 1)], w=[(qT, b)])
        wqi = load_w(win_v, C_QI, 512)
        for b in range(nbk):
            pq_ = pm.next()
            for k in range(8):
                T_(lambda e, b=b, k=k, pq_=pq_: e.matmul(pq_[:L, :], lhsT=hT[:, k, L * b:L * b + L], rhs=wqi[:, k, :], start=(k == 0), stop=(k == 7)),
                   r=[hT, wqi], w=[pq_])
            evac_copy(q32[:L, 0:512], pq_[:L, :], r=[pq_], w=[(q32, 0)])
            rope_rows(q32, L, 0, 8, rp[b][:L, 32:40], rp[b][:L, 40:48], nh=8, stride=64)
            pt_ = pm.next()
            for pp in range(4):
                T_(lambda e, pp=pp, pt_=pt_: e.transpose(out=pt_[:, 128 * pp:128 * pp + L], in_=q32[:L, 128 * pp:128 * (pp + 1)], identity=identF[:L, :L]),
                   r=[q32, identF], w=[pt_])
            evac_copy(qiT[:, b, :, 0:L], pt_[:, :].rearrange("p (h t) -> p h t", h=4)[:, :, 0:L], r=[pt_], w=[(qiT, b)])
        for g in range(4):
            wg_ = load_w(win_v, C_G + 512 * g, 512)
            for b in range(nbk):
                pq_ = pm.next()
                for k in range(8):
                    T_(lambda e, b=b, k=k, pq_=pq_, wg_=wg_: e.matmul(pq_[:L, :], lhsT=hT[:, k, L * b:L * b + L], rhs=wg_[:, k, :], start=(k == 0), stop=(k == 7)),
                       r=[hT, wg_], w=[pq_])
                A_(lambda e, b=b, g=g, pq_=pq_: e.activation(out=gsig[:L, b, 512 * g:512 * (g + 1)], in_=pq_[:L, :], func=AF.Sigmoid), r=[pq_], w=[(gsig, (b, g))])

    def rw_tile(c, wtile, wcol, Ls, out_fn, carry_aps, save_last):
        L = Ls[0]
        ntk = sum(Ls)
        ps_ = pm.next()
        for k in range(8):
            T_(lambda e, k=k: e.matmul(ps_[:, 0:ntk], lhsT=wtile[:, k, wcol:wcol + 128], rhs=hT[:, k, 0:ntk], start=(k == 0), stop=(k == 7)),
               r=[hT, wtile], w=[ps_])
        tmp = f32t.next(); xs = f32t.next()
        A_(lambda e: e.activation(out=tmp[:, 0:ntk], in_=ps_[:, 0:ntk], func=AF.Identity, scale=colsd[:, c:c + 1]), r=[ps_, colsd], w=[tmp])
        if ntk > 1:
            V_(lambda e: e.scalar_tensor_tensor(out=xs[:, 1:ntk], in0=ps_[:, 0:ntk - 1], scalar=cols[:, O_MU + c:O_MU + c + 1],
                                                in1=tmp[:, 1:ntk], op0=ALU.mult, op1=ALU.add), r=[ps_, cols, tmp], w=[xs])
        for b in range(len(Ls)):
            ca = carry_aps[b]
            if ca is None:
                continue
            cap, creg = ca
            V_(lambda e, b=b, cap=cap: e.scalar_tensor_tensor(out=xs[:, L * b:L * b + 1], in0=cap[:, c:c + 1], scalar=cols[:, O_MU + c:O_MU + c + 1],
                                                             in1=tmp[:, L * b:L * b + 1], op0=ALU.mult, op1=ALU.add), r=[creg, cols, tmp], w=[xs])
        for b in range(len(Ls)):
            sl = save_last[b]
            if sl is None:
                continue
            sap, sreg = sl
            V_(lambda e, b=b, sap=sap: e.tensor_copy(out=sap[:, c:c + 1], in_=ps_[:, L * b + L - 1:L * b + L]), r=[ps_], w=[sreg])
        out_fn(xs)

    def rwkv_prep(Ls, own, carry_aps, save_last):
        ntk = sum(Ls)
        nb = len(Ls)
        L = Ls[0]
        rwt = lambda c, wtile, wcol, fn: rw_tile(c, wtile, wcol, Ls, fn, carry_aps, save_last)
        wl = load_w(win_v, C_RW + 3072, 256)

        def f24(xs):
            A_(lambda e: e.activation(out=TL[0:64, 0:ntk], in_=xs[0:64, 0:ntk], func=AF.Tanh), r=[xs], w=[TL])
            V_(lambda e: e.tensor_copy(out=TL[64:128, 0:ntk], in_=xs[64:128, 0:ntk]), r=[xs], w=[TL])
        rwt(24, wl, 0, f24)
        if own:
            def f25(xs):
                A_(lambda e: e.activation(out=SLG[:, 0:ntk], in_=xs[:, 0:ntk], func=AF.Sigmoid), r=[xs], w=[SLG])
            rwt(25, wl, 128, f25)
            for b in range(nb):
                for g in range(2):
                    pg = pm.next()
                    T_(lambda e, b=b, g=g, pg=pg: e.matmul(pg[:L, :], lhsT=SLG[:, L * b:L * b + L], rhs=g2b[:, 512 * g:512 * (g + 1)], start=True, stop=True),
                       r=[SLG, g2b], w=[pg])
                    evac_copy(gtok[:L, b, 512 * g:512 * (g + 1)], pg[:L, :], r=[pg], w=[(gtok, (b, g))])

        def blkview(t):
            return t[:, 0:ntk].rearrange("p (b l) -> p b l", b=nb)

        for half in range(2):
            wk_ = load_w(win_v, C_RW + 1024 + 512 * half, 512)
            for pp in range(4):
                p = 4 * half + pp

                def fk(kraw, p=p):
                    ps1 = pm.next()
                    T_(lambda e: e.matmul(ps1[:, 0:ntk], lhsT=w2b[0:64, 128 * p:128 * (p + 1)], rhs=TL[0:64, 0:ntk], start=True, stop=True),
                       r=[w2b, TL], w=[ps1])
                    sg = f32t.next()
                    A_(lambda e: e.activation(out=sg[:, 0:ntk], in_=ps1[:, 0:ntk], func=AF.Sigmoid, bias=cols[:, O_W0 + p:O_W0 + p + 1], scale=1.0),
                       r=[ps1, cols], w=[sg])
                    for b in range(nb):
                        V_(lambda e, b=b: e.tensor_tensor_scan(out=CS[:, p, b, 1:1 + L], data0=ones128[:, 0:L], data1=sg[:, L * b:L * (b + 1)],
                                                               initial=0.0, op0=ALU.mult, op1=ALU.add), r=[ones128, sg], w=[(CS, p)])
                    ps2 = pm.next()
                    T_(lambda e: e.matmul(ps2[:, 0:ntk], lhsT=a2b[64:128, 128 * p:128 * (p + 1)], rhs=TL[64:128, 0:ntk], start=True, stop=True),
                       r=[a2b, TL], w=[ps2])
                    asig = f32t.next()
                    A_(lambda e: e.activation(out=asig[:, 0:ntk], in_=ps2[:, 0:ntk], func=AF.Sigmoid, bias=cols[:, O_A0 + p:O_A0 + p + 1], scale=1.0),
                       r=[ps2, cols], w=[asig])
                    eex = f32t.next(); em = f32t.next()
                    A_(lambda e: e.activation(out=blkview(eex), in_=CS[:, p, 0:nb, 0:L], func=AF.Exp, scale=-DEC), r=[(CS, p)], w=[eex])
                    A_(lambda e: e.activation(out=blkview(em), in_=CS[:, p, 0:nb, 1:1 + L], func=AF.Exp, scale=DEC), r=[(CS, p)], w=[em])
                    A_(lambda e: e.activation(out=GC[:, 0:nb, p], in_=CS[:, p, 0:nb, L], func=AF.Exp, scale=-DEC), r=[(CS, p)], w=[(GC, p)])
                    kkr = f32t.next(); sq = f32t.next()
                    V_(lambda e: e.tensor_scalar(out=kkr[:, 0:ntk], in0=kraw[:, 0:ntk], scalar1=cols[:, O_KK + p:O_KK + p + 1], scalar2=None, op0=ALU.mult),
                       r=[kraw, cols], w=[kkr])
                    G_(lambda e: e.tensor_tensor(out=sq[:, 0:ntk], in0=kkr[:, 0:ntk], in1=kkr[:, 0:ntk], op=ALU.mult), r=[kkr], w=[sq])
                    ps3 = pm.next()
                    T_(lambda e: e.matmul(ps3[:, 0:ntk], lhsT=blkones[:], rhs=sq[:, 0:ntk], start=True, stop=True), r=[blkones, sq], w=[ps3])
                    A_(lambda e: e.activation(out=sq[:, 0:ntk], in_=ps3[:, 0:ntk], func=AF.Sqrt), r=[ps3], w=[sq])
                    V_(lambda e: e.tensor_scalar(out=sq[:, 0:ntk], in0=sq[:, 0:ntk], scalar1=1e-12, scalar2=None, op0=ALU.max), r=[sq], w=[sq])
                    V_(lambda e: e.reciprocal(out=sq[:, 0:ntk], in_=sq[:, 0:ntk]), r=[sq], w=[sq])
                    V_(lambda e: e.tensor_tensor(out=kkr[:, 0:ntk], in0=kkr[:, 0:ntk], in1=sq[:, 0:ntk], op=ALU.mult), r=[kkr, sq], w=[kkr])
                    V_(lambda e: e.tensor_scalar(out=sq[:, 0:ntk], in0=asig[:, 0:ntk], scalar1=cols[:, O_KA + p:O_KA + p + 1],
                                                 scalar2=colsd[:, 26 + p:27 + p], op0=ALU.mult, op1=ALU.add), r=[asig, cols, colsd], w=[sq])
                    G_(lambda e: e.tensor_tensor(out=sq[:, 0:ntk], in0=sq[:, 0:ntk], in1=kraw[:, 0:ntk], op=ALU.mult), r=[sq, kraw], w=[sq])
                    G_(lambda e: e.tensor_tensor(out=asig[:, 0:ntk], in0=asig[:, 0:ntk], in1=kkr[:, 0:ntk], op=ALU.mult), r=[asig, kkr], w=[asig])
                    V_(lambda e: e.scalar_tensor_tensor(out=AT[:, p, 0:ntk], in0=kkr[:, 0:ntk], scalar=-1.0, in1=eex[:, 0:ntk], op0=ALU.mult, op1=ALU.mult),
                       r=[kkr, eex], w=[(AT, p)])
                    V_(lambda e: e.tensor_tensor(out=KTt[:, p, 0:ntk], in0=sq[:, 0:ntk], in1=em[:, 0:ntk], op=ALU.mult), r=[sq, em], w=[(KTt, p)])
                    G_(lambda e: e.tensor_tensor(out=BTt[:, p, 0:ntk], in0=asig[:, 0:ntk], in1=em[:, 0:ntk], op=ALU.mult), r=[asig, em], w=[(BTt, p)])
                    if own:
                        G_(lambda e: e.tensor_scalar(out=KMR[:, p, 0:ntk], in0=sq[:, 0:ntk], scalar1=cols[:, O_RK + p:O_RK + p + 1], scalar2=None, op0=ALU.mult),
                           r=[sq, cols], w=[(KMR, p)])
                rwt(8 + p, wk_, 128 * pp, fk)
        for half in range(2):
            wv_ = load_w(win_v, C_RW + 2048 + 512 * half, 512)
            for pp in range(4):
                p = 4 * half + pp

                def fv(xs, p=p):
                    A_(lambda e: e.activation(out=VTf[:, p, 0:ntk], in_=xs[:, 0:ntk], func=AF.Copy), r=[xs], w=[(VTf, p)])
                rwt(16 + p, wv_, 128 * pp, fv)
        if own:
            pbc = pO[:, 1024:1536]
            pbreg = (pO, 2)
            first = [True]
            for half in range(2):
                wr_ = load_w(win_v, C_RW + 512 * half, 512)
                for pp in range(4):
                    p = 4 * half + pp

                    def fr(xs, p=p):
                        ep = f32t.next()
                        A_(lambda e: e.activation(out=blkview(ep), in_=CS[:, p, 0:nb, 1:1 + L], func=AF.Exp, scale=-DEC), r=[(CS, p)], w=[ep])
                        V_(lambda e: e.tensor_tensor(out=RTt[:, p, 0:ntk], in0=xs[:, 0:ntk], in1=ep[:, 0:ntk], op=ALU.mult), r=[xs, ep], w=[(RTt, p)])
                        prd = PRD.next()
                        G_(lambda e: e.tensor_tensor(out=prd[:, 0:ntk], in0=xs[:, 0:ntk], in1=KMR[:, p, 0:ntk], op=ALU.mult), r=[xs, (KMR, p)], w=[prd])
                        for b in range(nb):
                            T_(lambda e, b=b: e.matmul(pbc[:L, 16 * b + 2 * p:16 * b + 2 * p + 2], lhsT=prd[:, L * b:L * b + L], rhs=blkonesB[:, 0:128:64],
                                                       start=first[0] and b == 0, stop=True, skip_group_check=True), r=[prd, blkonesB], w=[pbreg])
                        first[0] = False
                    rwt(p, wr_, 128 * pp, fr)
            first[0] = True
            V_(lambda e: e.tensor_copy(out=bcoef[:L, 0:nb, :], in_=pbc[:L, 0:16 * nb].rearrange("p (b h) -> p b h", b=nb)), r=[pbreg], w=[bcoef])
        for b in range(nb):
            for (src, dst) in ((KTt, Ktok), (BTt, Btok), (VTf, Vtok)):
                for p in range(8):
                    T_(lambda e, p=p, b=b, src=src: e.transpose(out=ptr[0:L, p, :], in_=src[:, p, L * b:L * (b + 1)], identity=identB[:]),
                       r=[(src, p), identB], w=[ptr])
                evac_copy(dst[0:L, b, :].rearrange("p (k j) -> p k j", k=8), ptr[0:L, :, :], r=[ptr], w=[(dst, b)])

    def chunk_scan(b, L, own):
        nl = max(int(np.ceil(np.log2(L))) - 1, 0)
        c0 = L * b
        Lk = 128 if L == 64 else L
        if L == 64:
            for t_ in (AKs, ARB, ARK):
                G_(lambda e, t_=t_: e.memset(t_[64:128, :, :], 0.0), w=[t_])
            G_(lambda e: e.memset(Vtok[64:128, b, :], 0.0), w=[(Vtok, b)])
            G_(lambda e: e.memset(Ub[64:128, :, :], 0.0), w=[Ub])
        for hg in range(4):
            heads = [4 * hg + i for i in range(4)]
            cur = {}
            for i, h in enumerate(heads):
                p, bp = h // 2, 64 * (h % 2)
                bank = cb_ap(i); breg = cbanks[i]
                at = AT[bp:bp + 64, p, c0:c0 + L]; bt = BTt[bp:bp + 64, p, c0:c0 + L]; kt = KTt[bp:bp + 64, p, c0:c0 + L]
                T_(lambda e, bank=bank, bt=bt, at=at: e.matmul(bank[0:L, 0:L], lhsT=bt, rhs=at, start=True, stop=True), r=[(AT, p), (BTt, p)], w=[breg])
                T_(lambda e, bank=bank, bt=bt, at=at: e.matmul(bank[0:L, 128:128 + L], lhsT=at, rhs=bt, start=False, stop=True, skip_group_check=True),
                   r=[(AT, p), (BTt, p)], w=[breg])
                T_(lambda e, bank=bank, kt=kt, at=at: e.matmul(bank[0:L, 256:256 + L], lhsT=kt, rhs=at, start=False, stop=True, skip_group_check=True),
                   r=[(AT, p), (KTt, p)], w=[breg])
                nm_ = 3
                if own:
                    rt_ = RTt[bp:bp + 64, p, c0:c0 + L]
                    T_(lambda e, bank=bank, bt=bt, rt_=rt_: e.matmul(bank[0:L, 384:384 + L], lhsT=bt, rhs=rt_, start=False, stop=True, skip_group_check=True),
                       r=[(RTt, p), (BTt, p)], w=[breg])
                    nm_ = 4
                am = amat.next()
                V_(lambda e, bank=bank, am=am, nm_=nm_: e.tensor_tensor(out=am[0:L, 0:nm_, 0:L], in0=bank[0:L, 0:128 * nm_].rearrange("p (m t) -> p m t", m=nm_)[:, :, 0:L],
                                                                       in1=mask[0:L, 0:128 * nm_].rearrange("p (m t) -> p m t", m=nm_)[:, :, 0:L], op=ALU.mult),
                   r=[breg, mask], w=[am])
                G_(lambda e, am=am, h=h: e.tensor_copy(out=AKs[0:L, h, 0:L], in_=am[0:L, 2, 0:L]), r=[am], w=[(AKs, h)])
                if own:
                    G_(lambda e, am=am, h=h: e.tensor_copy(out=ARB[0:L, h, 0:L], in_=am[0:L, 3, 0:L]), r=[am], w=[(ARB, h)])
                m0 = Mt[i].next()
                G_(lambda e, am=am, m0=m0: e.tensor_tensor(out=m0[0:L, 0:L], in0=am[0:L, 0, 0:L], in1=identB[0:L, 0:L], op=ALU.add), r=[am, identB], w=[m0])
                cur[h] = dict(P=am[0:L, 0, 0:L], Q=am[0:L, 1, 0:L], Pt=am, Qt=am, M=m0)
            for lev in range(nl):
                last = (lev == nl - 1)
                for i, h in enumerate(heads):
                    bank = cb_ap(i); breg = cbanks[i]
                    c = cur[h]
                    if not last:
                        T_(lambda e, bank=bank, cq=c['Q'], cp=c['P']: e.matmul(bank[0:L, 0:L], lhsT=cq, rhs=cp, start=True, stop=True), r=[c['Pt'], c['Qt']], w=[breg])
                        T_(lambda e, bank=bank, cq=c['Q'], cp=c['P']: e.matmul(bank[0:L, 128:128 + L], lhsT=cp, rhs=cq, start=False, stop=True, skip_group_check=True),
                           r=[c['Pt'], c['Qt']], w=[breg])
                    else:
                        T_(lambda e, bank=bank, cq=c['Q'], cp=c['P']: e.matmul(bank[0:L, 128:128 + L], lhsT=cp, rhs=cq, start=True, stop=True),
                           r=[c['Pt'], c['Qt']], w=[breg])
                    nq = pq[i].next()
                    lo = 1 if last else 0
                    A_(lambda e, bank=bank, nq=nq, lo=lo: e.activation(out=nq[0:L, lo:2, 0:L],
                                                                      in_=bank[0:L, 128 * lo:256].rearrange("p (m t) -> p m t", m=2 - lo)[:, :, 0:L], func=AF.Copy),
                       r=[breg], w=[nq])
                    c['P'] = nq[0:L, 0, 0:L]; c['Q'] = nq[0:L, 1, 0:L]; c['Pt'] = nq; c['Qt'] = nq
                for i, h in enumerate(heads):
                    bank = cb_ap(i); breg = cbanks[i]
                    c = cur[h]
                    pmb = pm.next()
                    T_(lambda e, pmb=pmb, cq=c['Q'], cm=c['M']: e.matmul(pmb[0:L, 0:L], lhsT=cq, rhs=cm[0:L, 0:L], start=True, stop=True),
                       r=[c['Qt'], c['M']], w=[pmb])
                    mn = Mt[i].next()
                    V_(lambda e, pmb=pmb, cm=c['M'], mn=mn: e.tensor_tensor(out=mn[0:L, 0:L], in0=pmb[0:L, 0:L], in1=cm[0:L, 0:L], op=ALU.add),
                       r=[pmb, c['M']], w=[mn])
                    c['M'] = mn
            for i, h in enumerate(heads):
                G_(lambda e, h=h, m=cur[h]['M']: e.tensor_copy(out=Mfin[0:L, h, 0:L], in_=m[0:L, 0:L]), r=[cur[h]['M']], w=[(Mfin, h)])
        for g in range(2):
            bank = cb_ap(g); breg = cbanks[g]
            for hh in range(8):
                h = 8 * g + hh
                p, bp = h // 2, 64 * (h % 2)
                T_(lambda e, bank=bank, hh=hh, p=p, bp=bp: e.matmul(bank[0:L, 64 * hh:64 * hh + 64], lhsT=AT[bp:bp + 64, p, c0:c0 + L], rhs=STb[bp:bp + 64, p, :],
                                                                   start=(hh == 0), stop=False, skip_group_check=True), r=[(AT, p), STb], w=[breg])
                T_(lambda e, bank=bank, hh=hh, h=h: e.matmul(bank[0:L, 64 * hh:64 * hh + 64], lhsT=AKs[0:Lk, h, 0:L], rhs=Vtok[0:Lk, b, 64 * h:64 * h + 64],
                                                            start=False, stop=True, skip_group_check=True), r=[(AKs, h), (Vtok, b)], w=[breg])
            evac_copy(Wb[0:L, 8 * g:8 * g + 8, :], bank[0:L, 0:512].rearrange("p (h i) -> p h i", h=8), r=[breg], w=[(Wb, g)])
        for g in range(2):
            bank = cb_ap(2 + g); breg = cbanks[2 + g]
            for hh in range(8):
                h = 8 * g + hh
                T_(lambda e, bank=bank, hh=hh, h=h: e.matmul(bank[0:L, 64 * hh:64 * hh + 64], lhsT=Mfin[0:L, h, 0:L], rhs=Wb[0:L, h, :],
                                                            start=(hh == 0), stop=True, skip_group_check=True), r=[(Mfin, h), (Wb, g)], w=[breg])
            evac_copy(Ub[0:L, 8 * g:8 * g + 8, :], bank[0:L, 0:512].rearrange("p (h i) -> p h i", h=8), r=[breg], w=[(Ub, g)])
        if own:
            for rnd in range(2):
                for par in range(2):
                    bank = cb_ap(par); breg = cbanks[par]
                    for j in range(4):
                        h = 8 * rnd + 2 * j + par
                        p, bp = h // 2, 64 * par
                        T_(lambda e, bank=bank, j=j, p=p, bp=bp: e.matmul(bank[0:L, 128 * j:128 * j + L], lhsT=KTt[bp:bp + 64, p, c0:c0 + L], rhs=RTt[bp:bp + 64, p, c0:c0 + L],
                                                                         start=(j == 0), stop=True, skip_group_check=True), r=[(KTt, p), (RTt, p)], w=[breg])
                for par in range(2):
                    bank = cb_ap(par); breg = cbanks[par]
                    for j in range(4):
                        h = 8 * rnd + 2 * j + par
                        V_(lambda e, bank=bank, j=j, h=h: e.tensor_tensor(out=ARK[0:L, h, 0:L], in0=bank[0:L, 128 * j:128 * j + L], in1=mask[0:L, 384:384 + L], op=ALU.mult),
                           r=[breg, mask], w=[(ARK, h)])
            for g in range(2):
                bank = cb_ap(2 + g); breg = cbanks[2 + g]
                for hh in range(8):
                    h = 8 * g + hh
                    p, bp = h // 2, 64 * (h % 2)
                    oc = bank[0:L, 64 * hh:64 * hh + 64]
                    T_(lambda e, oc=oc, hh=hh, p=p, bp=bp: e.matmul(oc, lhsT=RTt[bp:bp + 64, p, c0:c0 + L], rhs=STb[bp:bp + 64, p, :],
                                                                   start=(hh == 0), stop=False, skip_group_check=True), r=[(RTt, p), STb], w=[breg])
                    T_(lambda e, oc=oc, h=h: e.matmul(oc, lhsT=ARK[0:Lk, h, 0:L], rhs=Vtok[0:Lk, b, 64 * h:64 * h + 64], start=False, stop=False, skip_group_check=True),
                       r=[(ARK, h), (Vtok, b)], w=[breg])
                    T_(lambda e, oc=oc, h=h: e.matmul(oc, lhsT=ARB[0:Lk, h, 0:L], rhs=Ub[0:Lk, h, :], start=False, stop=True, skip_group_check=True),
                       r=[(ARB, h), (Ub, h // 8)], w=[breg])
                evac_copy(Yt[0:L, 512 * g:512 * (g + 1)], bank[0:L, 0:512], r=[breg], w=[(Yt, g)])
        bank = cb_ap(0); breg = cbanks[0]
        for h in range(16):
            p, bp = h // 2, 64 * (h % 2)
            T_(lambda e, h=h, p=p, bp=bp: e.matmul(bank[bp:bp + 64, 64 * p:64 * p + 64], lhsT=Ktok[0:L, b, 64 * h:64 * h + 64], rhs=Vtok[0:L, b, 64 * h:64 * h + 64],
                                                  start=(h < 2), stop=False, skip_group_check=True), r=[(Ktok, b), (Vtok, b)], w=[breg])
            T_(lambda e, h=h, p=p, bp=bp: e.matmul(bank[bp:bp + 64, 64 * p:64 * p + 64], lhsT=Btok[0:L, b, 64 * h:64 * h + 64], rhs=Ub[0:L, h, :],
                                                  start=False, stop=True, skip_group_check=True), r=[(Btok, b), (Ub, h // 8)], w=[breg])
        V_(lambda e: e.tensor_tensor(out=sttmp[:].rearrange("p k i -> p (k i)"), in0=bank[:, 0:512], in1=ST[:].rearrange("p k i -> p (k i)"), op=ALU.add),
           r=[breg, ST], w=[sttmp])
        V_(lambda e: e.tensor_tensor(out=ST[:], in0=GC[:, b, :].unsqueeze(2).to_broadcast([128, 8, 64]), in1=sttmp[:], op=ALU.mult),
           r=[sttmp, GC], w=[ST])
        A_(lambda e: e.activation(out=STb[:], in_=ST[:], func=AF.Copy), r=[ST], w=[STb])
        if own:
            rwkv_out(b, L)

    def rwkv_out(b, L):
        y3 = Yt[0:L, :].rearrange("p (h i) -> p h i", h=16)
        y23 = Y2[0:L, :].rearrange("p (h i) -> p h i", h=16)
        V_(lambda e: e.tensor_reduce(out=gns[0:L, 0:16], in_=y3, axis=mybir.AxisListType.X, op=ALU.add), r=[Yt], w=[gns])
        A_(lambda e: e.activation(out=Y2[0:L, :], in_=Yt[0:L, :], func=AF.Square), r=[Yt], w=[Y2])
        V_(lambda e: e.tensor_reduce(out=gns[0:L, 16:32], in_=y23, axis=mybir.AxisListType.X, op=ALU.add), r=[Y2], w=[gns])
        V_(lambda e: e.tensor_scalar(out=gns[0:L, 0:16], in0=gns[0:L, 0:16], scalar1=1.0 / 64, scalar2=None, op0=ALU.mult), r=[gns], w=[gns])
        V_(lambda e: e.tensor_tensor(out=gns[0:L, 32:48], in0=gns[0:L, 0:16], in1=gns[0:L, 0:16], op=ALU.mult), r=[gns], w=[gns])
        V_(lambda e: e.scalar_tensor_tensor(out=gns[0:L, 48:64], in0=gns[0:L, 16:32], scalar=1.0 / 64, in1=gns[0:L, 32:48], op0=ALU.mult, op1=ALU.subtract),
           r=[gns], w=[gns])
        A_(lambda e: e.activation(out=gns[0:L, 48:64], in_=gns[0:L, 48:64], func=AF.Sqrt, bias=float(GN_EPS), scale=1.0), r=[gns], w=[gns])
        V_(lambda e: e.reciprocal(out=gns[0:L, 48:64], in_=gns[0:L, 48:64]), r=[gns], w=[gns])
        V_(lambda e: e.tensor_scalar(out=gns[0:L, 64:80], in0=gns[0:L, 48:64], scalar1=-1.0, scalar2=None, op0=ALU.mult), r=[gns], w=[gns])
        V_(lambda e: e.tensor_tensor(out=y23, in0=gns[0:L, 0:16].unsqueeze(2).to_broadcast([L, 16, 64]), in1=y3, op=ALU.subtract), r=[gns, Yt], w=[Y2])
        V_(lambda e: e.tensor_tensor(out=y23, in0=gns[0:L, 64:80].unsqueeze(2).to_broadcast([L, 16, 64]), in1=y23, op=ALU.mult), r=[gns, Y2], w=[Y2])
        vg, vbb = getvec("gn_g"), getvec("gn_b")
        G_(lambda e: e.tensor_tensor(out=Y2[0:L, :], in0=Y2[0:L, :], in1=vg[0:L, :], op=ALU.mult), r=[Y2, vg], w=[Y2])
        G_(lambda e: e.tensor_tensor(out=Y2[0:L, :], in0=Y2[0:L, :], in1=vbb[0:L, :], op=ALU.add), r=[Y2, vbb], w=[Y2])
        V_(lambda e: e.tensor_tensor(out=y3, in0=bcoef[0:L, b, :].unsqueeze(2).to_broadcast([L, 16, 64]), in1=Vtok[0:L, b, :].rearrange("p (h i) -> p h i", h=16), op=ALU.mult),
           r=[bcoef, (Vtok, b)], w=[Yt])
        G_(lambda e: e.tensor_tensor(out=Y2[0:L, :], in0=Y2[0:L, :], in1=Yt[0:L, :], op=ALU.add), r=[Y2, Yt], w=[Y2])
        G_(lambda e: e.tensor_tensor(out=orw[0:L, :], in0=Y2[0:L, :], in1=gtok[0:L, b, :], op=ALU.mult), r=[Y2, gtok], w=[orw])
        for k in range(8):
            T_(lambda e, k=k: e.transpose(out=ptr[:, k, 0:L], in_=orw[0:L, 128 * k:128 * (k + 1)], identity=identB[:L, :L]), r=[orw, identB], w=[ptr])
        evac_copy(orwT[:, b, :, 0:L], ptr[:, :, 0:L], r=[ptr], w=[(orwT, b)])

    def attention(b, Lq, slot_lo, slot_hi, corner):
        S = slot_hi - slot_lo
        wi = kvs[b]
        for h in range(8):
            V_(lambda e, h=h: e.tensor_scalar(out=dg[0:Lq, h, 0:Lq], in0=identB[0:Lq, 0:Lq], scalar1=wi[0:Lq, 320 + h:321 + h], scalar2=None, op0=ALU.mult),
               r=[identB, wi], w=[dg])
        cidx = 0.125 * (8.0 ** -0.5)
        ntile = (S + 511) // 512
        for ti in range(ntile):
            s0 = slot_lo + 512 * ti
            n = min(512, slot_hi - s0)
            kit = kitl.next(); kb_ = kbt.next()
            P.dma('sp', kit[0:64, 0:n], D_kit.ap()[:, s0:s0 + n], r=[D_kit], w=[kit])
            P.dma('sp', kit[64:128, 0:n], D_kit.ap()[:, s0:s0 + n], r=[D_kit], w=[kit])
            P.dma('sp', kb_[:, 0:n], keyb.ap()[:, s0:s0 + n], r=[keyb], w=[kb_])
            psc = pm.next()
            T_(lambda e, psc=psc, kb_=kb_, n=n: e.matmul(psc[0:Lq, 0:n], lhsT=onesrow[0:1, 0:Lq], rhs=kb_[0:1, 0:n], start=True, stop=False), r=[onesrow, kb_], w=[psc])
            def emit_x(h, kit=kit, n=n):
                pp, bp = h // 2, 64 * (h % 2)
                xb_ = xbanks[xq[0] % 3]; xq[0] += 1
                px = xb_[0][:, 512 * xb_[1]:512 * (xb_[1] + 1)]
                T_(lambda e, px=px, pp=pp, bp=bp: e.matmul(px[0:Lq, 0:n], lhsT=qiT[bp:bp + 64, b, pp, 0:Lq], rhs=kit[bp:bp + 64, 0:n], start=True, stop=True),
                   r=[(qiT, b), kit], w=[xb_])
                return px, xb_
            nxt = emit_x(0)
            for h in range(8):
                px, xb_ = nxt
                if h < 7:
                    nxt = emit_x(h + 1)
                r_ = Rt.next()
                if h % 2 == 0:
                    A_(lambda e, px=px, r_=r_, n=n: e.activation(out=r_[0:Lq, 0:n], in_=px[0:Lq, 0:n], func=AF.Relu, scale=cidx), r=[xb_], w=[r_])
                else:
                    V_(lambda e, px=px, r_=r_, n=n: e.tensor_scalar(out=r_[0:Lq, 0:n], in0=px[0:Lq, 0:n], scalar1=cidx, scalar2=0.0, op0=ALU.mult, op1=ALU.max),
                       r=[xb_], w=[r_])
                T_(lambda e, psc=psc, h=h, r_=r_, n=n: e.matmul(psc[0:Lq, 0:n], lhsT=dg[0:Lq, h, 0:Lq], rhs=r_[0:Lq, 0:n], start=False, stop=(h == 7)), r=[dg, r_], w=[psc])
            V_(lambda e, psc=psc, ti=ti, n=n: e.tensor_copy(out=SC[0:Lq, 512 * ti:512 * ti + n], in_=psc[0:Lq, 0:n]), r=[psc], w=[(SC, ti)])
        if corner:
            V_(lambda e: e.memset(SC[0:64, S - 64:S], -1e30), r=[], w=[SC])
        V_(lambda e: e.memset(tau[0:Lq, 0:1], 0.0), w=[tau])
        V_(lambda e: e.memset(tau[0:Lq, 4:5], 0.0), w=[tau])
        Sa = (int(S * 0.42) // 16) * 16
        nbk_ = S - Sa
        for it in range(NBIS):
            s_ = 16.0 * (0.5 ** (it + 1))
            V_(lambda e: e.tensor_scalar(out=junk[0:Lq, 0:1].to_broadcast([Lq, Sa]), in0=SC[0:Lq, 0:Sa], scalar1=tau[0:Lq, 0:1], scalar2=None,
                                         op0=ALU.is_ge, op1=ALU.add, accum_out=tau[0:Lq, 1:2]), r=[SC, (tau, 't')], w=[(junk, 0), (tau, 'ca')])
            A_(lambda e: e.activation(out=junk[0:Lq, 1:2].to_broadcast([Lq, nbk_]), in_=SC[0:Lq, Sa:S], func=AF.Sign, bias=tau[0:Lq, 4:5], scale=1.0,
                                      accum_out=tau[0:Lq, 5:6]), r=[SC, (tau, 'nt')], w=[(junk, 1), (tau, 'cb')])
            V_(lambda e: e.scalar_tensor_tensor(out=tau[0:Lq, 2:3], in0=tau[0:Lq, 1:2], scalar=2.0, in1=tau[0:Lq, 5:6], op0=ALU.mult, op1=ALU.add),
               r=[(tau, 'ca'), (tau, 'cb')], w=[(tau, 'd')])
            V_(lambda e, s_=s_: e.tensor_scalar(out=tau[0:Lq, 2:3], in0=tau[0:Lq, 2:3], scalar1=float(2 * TOPK - 1 - nbk_), scalar2=2.0 * s_, op0=ALU.is_ge, op1=ALU.mult),
               r=[(tau, 'd')], w=[(tau, 'd')])
            V_(lambda e, s_=s_: e.scalar_tensor_tensor(out=tau[0:Lq, 0:1], in0=tau[0:Lq, 2:3], scalar=-s_, in1=tau[0:Lq, 0:1], op0=ALU.add, op1=ALU.add),
               r=[(tau, 'd'), (tau, 't')], w=[(tau, 't')])
            V_(lambda e: e.tensor_scalar(out=tau[0:Lq, 4:5], in0=tau[0:Lq, 0:1], scalar1=-1.0, scalar2=None, op0=ALU.mult), r=[(tau, 't')], w=[(tau, 'nt')])
        s_last = 16.0 * (0.5 ** NBIS)
        V_(lambda e: e.tensor_scalar(out=tau[0:Lq, 3:4], in0=tau[0:Lq, 0:1], scalar1=-s_last, scalar2=None, op0=ALU.add), r=[(tau, 't')], w=[(tau, 'u')])
        blocks = []
        for ti in range(ntile):
            s0 = slot_lo + 512 * ti
            n = min(512, slot_hi - s0)
            for a in range((n + 127) // 128):
                blocks.append((ti, s0, n, a, min(128, n - 128 * a)))
        tiles = {}

        def tile_res(ti, s0, n):
            if ti in tiles:
                return tiles[ti]
            kt_ = ktl.next(); vt_ = vtl.next(); mb = mbt.next()
            P.dma('sp', kt_[:, 0:n], D_kt.ap()[:, s0:s0 + n], r=[D_kt], w=[kt_])
            nfull, rem = n // 128, n % 128
            if nfull:
                P.dma('sp', vt_[:, 0:nfull, 0:128], D_v.ap()[s0:s0 + 128 * nfull, :].rearrange("(a p) d -> p a d", p=128), r=[D_v], w=[vt_])
            if rem:
                P.dma('sp', vt_[0:rem, nfull, 0:128], D_v.ap()[s0 + 128 * nfull:s0 + n, :], r=[D_v], w=[vt_])
            V_(lambda e: e.tensor_scalar(out=mb[0:Lq, 0:n], in0=SC[0:Lq, 512 * ti:512 * ti + n], scalar1=tau[0:Lq, 3:4], scalar2=-30000.0,
                                         op0=ALU.is_lt, op1=ALU.mult), r=[(SC, ti), (tau, 'u')], w=[mb])
            tiles[ti] = (kt_, vt_, mb)
            return tiles[ti]

        def buf(j, hh):
            if j % 2 == 0:
                return pLT[:, 512 * hh:512 * hh + 512], (pLT, hh)
            return pm.t[hh][:, :], pm.t[hh]

        def emit_LT(j):
            ti, s0, n, a, ns = blocks[j]
            kt_, vt_, mb = tile_res(ti, s0, n)
            for hh in range(2):
                ap_, reg = buf(j, hh)
                T_(lambda e, ap_=ap_, hh=hh: e.matmul(ap_[0:ns, 0:4 * Lq], lhsT=kt_[:, 128 * a:128 * a + ns], rhs=qT[:, b, 4 * hh:4 * hh + 4, 0:Lq], start=True, stop=False),
                   r=[kt_, (qT, b)], w=[reg])
                T_(lambda e, ap_=ap_: e.matmul(ap_[0:ns, 0:4 * Lq], lhsT=mb[0:Lq, 128 * a:128 * a + ns], rhs=I4[0:Lq, :, 0:Lq], start=False, stop=True),
                   r=[mb, I4], w=[reg])
        emit_LT(0)
        for j in range(len(blocks)):
            if j + 1 < len(blocks):
                emit_LT(j + 1)
            ti, s0, n, a, ns = blocks[j]
            kt_, vt_, mb = tiles[ti]
            pt = PTt.next()
            for hh in range(2):
                ap_, reg = buf(j, hh)
                A_(lambda e, ap_=ap_, hh=hh, pt=pt, ns=ns: e.activation(out=pt[0:ns, 4 * Lq * hh:4 * Lq * (hh + 1)], in_=ap_[0:ns, 0:4 * Lq], func=AF.Exp, scale=128.0 ** -0.5),
                   r=[reg], w=[(pt, hh)])
            last = (j == len(blocks) - 1)
            for h in range(8):
                off = 512 * (h // 3) + 129 * (h % 3)
                T_(lambda e, h=h, off=off, ns=ns, a=a, pt=pt, vt_=vt_, st_=(j == 0 and h % 3 == 0), last=last: e.matmul(
                    pO[0:Lq, off:off + 129], lhsT=pt[0:ns, Lq * h:Lq * h + Lq], rhs=vt_[0:ns, a, 0:129], start=st_, stop=last, skip_group_check=True),
                   r=[pt, vt_], w=[(pO, h // 3)])
        for h in range(8):
            off = 512 * (h // 3) + 129 * (h % 3)
            V_(lambda e, h=h, off=off: e.reciprocal(out=rden[0:Lq, h:h + 1], in_=pO[0:Lq, off + 128:off + 129]), r=[(pO, h // 3)], w=[rden])
            V_(lambda e, h=h, off=off: e.tensor_scalar(out=attn[0:Lq, 128 * h:128 * h + 128], in0=pO[0:Lq, off:off + 128], scalar1=rden[0:Lq, h:h + 1], scalar2=None, op0=ALU.mult),
               r=[(pO, h // 3), rden], w=[attn])
        for k in range(8):
            T_(lambda e, k=k: e.transpose(out=ptr[:, k, 0:Lq], in_=attn[0:Lq, 128 * k:128 * (k + 1)], identity=identB[:Lq, :Lq]), r=[attn, identB], w=[ptr])
        evac_copy(attnT[:, b, :, 0:Lq], ptr[:, :, 0:Lq], r=[ptr], w=[(attnT, b)])


    def post(Ls, y_dram, yrows):
        L = Ls[0]
        nbk = len(Ls)
        ntk = sum(Ls)
        woa = [load_w(kview(D_woa), 512 * g, 512) for g in range(2)]
        for b in range(nbk):
            for g in range(2):
                po_ = pm.next()
                for k in range(8):
                    T_(lambda e, b=b, g=g, k=k, po_=po_: e.matmul(po_[:L, :], lhsT=attnT[:, b, k, 0:L], rhs=woa[g][:, k, :], start=(k == 0), stop=(k == 7)),
                       r=[(attnT, b), woa[g]], w=[po_])
                mx = mixs[b]
                V_(lambda e, b=b, g=g, po_=po_, mx=mx: e.tensor_tensor(out=mx[:L, 512 * g:512 * (g + 1)], in0=po_[:L, :],
                                                                    in1=gsig[:L, b, 512 * g:512 * (g + 1)], op=ALU.mult), r=[po_, gsig], w=[(mx, g)])
        wor = [load_w(kview(D_wor), 512 * g, 512) for g in range(2)]
        for b in range(nbk):
            mx = mixs[b]
            for g in range(2):
                po_ = pm.next()
                for k in range(8):
                    T_(lambda e, b=b, g=g, k=k, po_=po_: e.matmul(po_[:L, :], lhsT=orwT[:, b, k, 0:L], rhs=wor[g][:, k, :], start=(k == 0), stop=(k == 7)),
                       r=[(orwT, b), wor[g]], w=[po_])
                tq = rtmp.next()
                V_(lambda e, b=b, g=g, po_=po_, tq=tq: e.tensor_tensor(out=tq[:L, :], in0=po_[:L, :], in1=gsig[:L, b, D + 512 * g:D + 512 * (g + 1)], op=ALU.mult),
                   r=[po_, gsig], w=[tq])
                G_(lambda e, g=g, mx=mx, tq=tq: e.tensor_tensor(out=mx[:L, 512 * g:512 * (g + 1)], in0=mx[:L, 512 * g:512 * (g + 1)], in1=tq[:L, :], op=ALU.add),
                   r=[tq, (mx, g)], w=[(mx, g)])
            for k in range(8):
                T_(lambda e, k=k, mx=mx: e.transpose(out=ptr[:, k, 0:L], in_=mx[:L, 128 * k:128 * (k + 1)], identity=identB[:L, :L]), r=[mx, identB], w=[ptr])
            evac_copy(mixT[:, :, L * b:L * b + L], ptr[:, :, 0:L], r=[ptr], w=[(mixT, b)])
        wo = [load_w(kview(D_wout), 512 * g, 512) for g in range(2)]
        for b in range(nbk):
            for g in range(2):
                po_ = pm.next()
                for k in range(8):
                    T_(lambda e, b=b, g=g, k=k, po_=po_: e.matmul(po_[:L, :], lhsT=mixT[:, k, L * b:L * b + L], rhs=wo[g][:, k, :], start=(k == 0), stop=(k == 7)),
                       r=[(mixT, b), wo[g]], w=[po_])
                V_(lambda e, b=b, g=g, po_=po_: e.scalar_tensor_tensor(out=x1[:L, b, 512 * g:512 * (g + 1)], in0=hres[:L, b, 512 * g:512 * (g + 1)], scalar=float(ALPHA),
                                                                    in1=po_[:L, :], op0=ALU.mult, op1=ALU.add), r=[po_, (hres, b)], w=[(x1, b)])
            ln_rows(x1[:L, b, :], L, D, x1[:L, b, :], LN_EPS, g_b=(getvec("ln1_g"), getvec("ln1_b")))
            G_(lambda e, b=b: e.tensor_copy(out=x1b[:L, :], in_=x1[:L, b, :]), r=[x1], w=[x1b])
            for k in range(8):
                T_(lambda e, k=k: e.transpose(out=ptr[:, k, 0:L], in_=x1b[:L, 128 * k:128 * (k + 1)], identity=identB[:L, :L]), r=[x1b, identB], w=[ptr])
            evac_copy(x1T[:, :, L * b:L * b + L], ptr[:, :, 0:L], r=[ptr], w=[(x1T, b)])
        accs = [[(pLT, 0), (pLT, 1)], [(pO, 0), (pO, 1)]]
        acc_ap = lambda b, g: (accs[b][g][0])[:, 512 * accs[b][g][1]:512 * (accs[b][g][1] + 1)]
        nfc = DFF // 128
        fbanks = [pm.t[0], pm.t[1], (pO, 2)]
        fbq = [0]
        for fq in range(0, nfc, 4):
            nq = min(4, nfc - fq)
            wg_ = load_w(kview(D_wfg), 128 * fq, 128 * nq)
            wu_ = load_w(kview(D_wfu), 128 * fq, 128 * nq)
            wd_ = wt.next()
            wdv = wd_[:].rearrange("p (a c) n -> p a (c n)", c=2)
            P.dma('sp', wdv[:, 0:nq, :], D_wfd.ap()[128 * fq:128 * (fq + nq), :].rearrange("(a p) n -> p a n", p=128), r=[D_wfd], w=[wd_])
            def emit_gu(j, wg_=wg_, wu_=wu_):
                bg = fbanks[fbq[0] % 3]; bu = fbanks[(fbq[0] + 1) % 3]; fbq[0] += 2
                pg_ = bg[0][:, 512 * bg[1]:512 * bg[1] + 512] if isinstance(bg, tuple) else bg[:, :]
                pu_ = bu[0][:, 512 * bu[1]:512 * bu[1] + 512] if isinstance(bu, tuple) else bu[:, :]
                for k in range(8):
                    T_(lambda e, k=k: e.matmul(pg_[:, 0:ntk], lhsT=wg_[:, k, 128 * j:128 * (j + 1)], rhs=x1T[:, k, 0:ntk], start=(k == 0), stop=(k == 7)),
                       r=[x1T, wg_], w=[bg])
                ag = actg.next()
                A_(lambda e: e.activation(out=ag[:, 0:ntk], in_=pg_[:, 0:ntk], func=AF.Silu), r=[bg], w=[ag])
                for k in range(8):
                    T_(lambda e, k=k: e.matmul(pu_[:, 0:ntk], lhsT=wu_[:, k, 128 * j:128 * (j + 1)], rhs=x1T[:, k, 0:ntk], start=(k == 0), stop=(k == 7)),
                       r=[x1T, wu_], w=[bu])
                at_ = actT.next()
                V_(lambda e: e.tensor_tensor(out=at_[:, 0:ntk], in0=pu_[:, 0:ntk], in1=ag[:, 0:ntk], op=ALU.mult), r=[bu, ag], w=[at_])
                return at_

            def emit_down(j, at_, wd_=wd_, wdv=wdv, fq=fq):
                fc = fq + j
                for b in range(nbk):
                    for g in range(2):
                        T_(lambda e, b=b, g=g: e.matmul(acc_ap(b, g)[:L, :], lhsT=at_[:, L * b:L * b + L], rhs=wdv[:, j, 512 * g:512 * (g + 1)],
                                                        start=(fc == 0), stop=(fc == nfc - 1)), r=[at_, wd_], w=[accs[b][g]])
            pend = emit_gu(0)
            for j in range(nq):
                nxt_ = emit_gu(j + 1) if j + 1 < nq else None
                emit_down(j, pend)
                pend = nxt_
        for b in range(nbk):
            yt = yo.next()
            for g in range(2):
                V_(lambda e, b=b, g=g, yt=yt: e.scalar_tensor_tensor(out=yt[:L, 512 * g:512 * (g + 1)], in0=x1[:L, b, 512 * g:512 * (g + 1)], scalar=float(ALPHA),
                                                                  in1=acc_ap(b, g)[:L, :], op0=ALU.mult, op1=ALU.add), r=[accs[b][g], x1], w=[(yt, g)])
            ln_rows(yt[:L, :], L, D, yt[:L, :], LN_EPS, g_b=(getvec("ln2_g"), getvec("ln2_b")))
            P.dma('pool', y_dram.ap()[yrows[b]:yrows[b] + L, :], yt[:L, :], r=[yt], w=[(y_dram, yrows[b])])

    nso_sb = GEOM['NSO_B'] // NB
    so_blocks = [(NT * i, [128] * NB) for i in range(nso_sb)] + [(128 * GEOM['NSO_B'], [16])]
    for (row0, Ls) in so_blocks:
        L = Ls[0]
        if L == 16:
            V_(lambda e: e.tensor_scalar(out=ST[:].rearrange("p k i -> p (k i)"), in0=ST[:].rearrange("p k i -> p (k i)"), scalar1=flg[:, 0:1], scalar2=None, op0=ALU.mult),
               r=[ST, flg], w=[ST])
            A_(lambda e: e.activation(out=STb[:], in_=ST[:], func=AF.Copy), r=[ST], w=[STb])
            V_(lambda e: e.tensor_scalar(out=car[:], in0=car[:], scalar1=flg[:, 0:1], scalar2=None, op0=ALU.mult), r=[car, flg], w=[car])
        rows = [row0 + L * b for b in range(len(Ls))]
        front(xso, rows, Ls, rows, rows, (O_k, O_v, O_ki, rows), own=False)
        carry = [(car, car)] + [None] * (len(Ls) - 1)
        save = [None] * (len(Ls) - 1) + [(car, car)]
        rwkv_prep(Ls, False, carry, save)
        if L == 16:
            noop = lambda xs: None
            for half in range(2):
                wr_ = load_w(win_v, C_RW + 512 * half, 512)
                for pp in range(4):
                    rw_tile(4 * half + pp, wr_, 128 * pp, Ls, noop, carry, save)
            wl_ = load_w(win_v, C_RW + 3072, 256)
            rw_tile(25, wl_, 128, Ls, noop, carry, save)
        for b in range(len(Ls)):
            chunk_scan(b, L, own=False)

    for sbi in range(GEOM['NOWN_B'] // NB):
        Ls = [128] * NB
        rows = [NT * sbi + 128 * b for b in range(NB)]
        slots = [NSO + r_ for r_ in rows]
        front(xown, rows, Ls, slots, slots, (O_k, O_v, O_ki, slots), own=True)
        own_proj(Ls, slots)
        carry = [(car, car)] + [None] * (NB - 1)
        save = [None] * (NB - 1) + [(car, car)]
        rwkv_prep(Ls, True, carry, save)
        for b in range(NB):
            chunk_scan(b, 128, own=True)
        for b in range(NB):
            attention(b, 128, 0, slots[b] + 128, corner=True)
        post(Ls, O_y, rows)
    P.dma('sp', O_wkv.ap(), ST[:].rearrange("p k i -> p (k i)"), r=[ST], w=[O_wkv])
    P.dma('sp', O_shift.ap(), car[:], r=[car], w=[O_shift])

    if SAMPLE:
        for q in range(2):
            sb0 = NSLOT + SSTRIDE * q
            for i0 in range(0, CACHE_ROWS, 128):
                L = min(128, CACHE_ROWS - i0)
                ct = xin.next()
                P.dma('sp', ct[:L, 0, 0:128], ck.ap()[q, i0:i0 + L, :], r=[ck], w=[ct])
                P.dma('sp', ct[:L, 0, 128:192], cik.ap()[q, i0:i0 + L, :], r=[cik], w=[ct])
                pt_ = pm.next()
                T_(lambda e, ct=ct, pt_=pt_, L=L: e.transpose(out=pt_[:, 0:L], in_=ct[:L, 0, 0:128], identity=identF[:L, :L]), r=[ct, identF], w=[pt_])
                T_(lambda e, ct=ct, pt_=pt_, L=L: e.transpose(out=pt_[0:64, 128:128 + L], in_=ct[:L, 0, 128:192], identity=identF[:L, :L]), r=[ct, identF], w=[pt_])
                kt_t = ktt.next(); kit_t = kitt.next()
                V_(lambda e, kt_t=kt_t, pt_=pt_, L=L: e.tensor_copy(out=kt_t[:, 0:L], in_=pt_[:, 0:L]), r=[pt_], w=[kt_t])
                V_(lambda e, kit_t=kit_t, pt_=pt_, L=L: e.tensor_copy(out=kit_t[:, 0:L], in_=pt_[0:64, 128:128 + L]), r=[pt_], w=[kit_t])
                P.dma('pool', D_kt.ap()[:, sb0 + i0:sb0 + i0 + L], kt_t[:, 0:L], r=[kt_t], w=[(D_kt, sb0 + i0)])
                P.dma('pool', D_kit.ap()[:, sb0 + i0:sb0 + i0 + L], kit_t[:, 0:L], r=[kit_t], w=[(D_kit, sb0 + i0)])
        for q in range(2):
            P.dma('sp', cars[:, q, :], sshift.ap()[q], r=[sshift], w=[(cars, q)])
        for q in range(2):
            sb0 = NSLOT + SSTRIDE * q
            Ls = [64]
            rows = [64 * q]
            rrows = [NSO + NOWN + 64 * q]
            slots = [sb0 + CACHE_ROWS]
            front(xsm, rows, Ls, rrows, slots, (O_ks, O_vs, O_kis, rows), own=True)
            own_proj(Ls, rrows)
            rwkv_prep(Ls, True, [(cars[:, q, :], (cars, q))], [(cars[:, q, :], (cars, q))])
            P.dma('sp', ST[:].rearrange("p k i -> p (k i)"), swkv.ap()[q], r=[swkv], w=[ST])
            A_(lambda e: e.activation(out=STb[:], in_=ST[:], func=AF.Copy), r=[ST], w=[STb])
            chunk_scan(0, 64, own=True)
            P.dma('sp', O_wkvs.ap()[q], ST[:].rearrange("p k i -> p (k i)"), r=[ST], w=[(O_wkvs, q)])
            P.dma('sp', O_shifts.ap()[q], cars[:, q, :], r=[(cars, q)], w=[(O_shifts, q)])
            attention(0, 64, sb0, sb0 + CACHE_ROWS + 64, corner=False)
            post(Ls, O_ys, rows)
    nc = P.build()
    return nc, P


def make_consts():
    c = {}
    c["identf"] = np.eye(128, dtype=np.float32)
    b = np.zeros((128, 128), np.float32); b[:64, :64] = 1; b[64:, 64:] = 1
    c["blk1"] = b
    us = np.triu(np.ones((128, 128), np.float32), 1)
    ui = np.triu(np.ones((128, 128), np.float32), 0)
    c["cmask"] = np.concatenate([us, us.T, us, ui, ui], axis=1).astype(np.float32)
    return c


def rope_table(pos):
    pos = np.asarray(pos, np.float32)
    out = np.zeros((len(pos), 48), np.float32)
    for (rot, o) in ((32, 0), (16, 32)):
        inv = (np.float32(500000.0) ** (-np.arange(0, rot, 2, dtype=np.float32) / np.float32(rot))).astype(np.float32)
        ang = (pos[:, None] * inv[None]).astype(np.float32)
        h = rot // 2
        out[:, o:o + h] = np.cos(ang); out[:, o + h:o + 2 * h] = np.sin(ang)
    return out


def colpack(v, n):
    return np.ascontiguousarray(np.asarray(v, np.float32).reshape(n, 128).T)


def st_layout(s):
    s = np.asarray(s, np.float32).reshape(8, 2, 64, 64)
    return np.ascontiguousarray(s.transpose(1, 3, 0, 2).reshape(128, 512))


def st_unlayout(a):
    a = np.asarray(a, np.float32).reshape(2, 64, 8, 64)
    return np.ascontiguousarray(a.transpose(2, 0, 3, 1).reshape(16, 64, 64))


def prep_inputs(inp):
    NSO, NOWN, NSLOT = geom()
    f32 = lambda a: np.ascontiguousarray(np.asarray(a, np.float32))
    consts = make_consts()
    maps = []
    colsf = np.concatenate([
        colpack(inp["ln0_g"], 8), colpack(inp["ln0_b"], 8), colpack(inp["rw_mu"][0], 26), colpack(inp["rw_w0"][0], 8),
        colpack(inp["rw_a0"][0], 8), colpack(inp["rw_k_k"][0], 8), colpack(inp["rw_k_a"][0], 8), colpack(np.asarray(inp["rw_r_k"][0]).reshape(-1), 8)], axis=1)
    shared = dict(consts)
    shared.update(colsf=colsf, w_in=f32(inp["w_in"][0]), rw_w2=f32(inp["rw_w2"][0]), rw_a2=f32(inp["rw_a2"][0]), rw_g2=f32(inp["rw_g2"][0]),
                  ikg=f32(inp["idx_k_ln_g"][0]), ikb=f32(inp["idx_k_ln_b"][0]),
                  ln0_g=f32(inp["ln0_g"]), ln0_b=f32(inp["ln0_b"]), ln1_g=f32(inp["ln1_g"][0]), ln1_b=f32(inp["ln1_b"][0]),
                  ln2_g=f32(inp["ln2_g"][0]), ln2_b=f32(inp["ln2_b"][0]), gn_g=f32(inp["rw_gn_g"][0]), gn_b=f32(inp["rw_gn_b"][0]),
                  w_oa=f32(inp["w_o_attn"][0]), w_or=f32(inp["w_o_rwkv"][0]), w_out=f32(inp["w_out"][0]),
                  w_fg=f32(inp["ffn_w_gate"][0]), w_fu=f32(inp["ffn_w_up"][0]), w_fd=f32(inp["ffn_w_down"][0]))
    meta = f32(inp["meta_tokens"])
    past = int(np.asarray(inp["cache_k"]).shape[2]) - 16
    for c in range(8):
        b, hf = c // 2, c % 2
        xp = f32(inp["x_prompt"][b])
        nfr = NSO - 16
        if hf == 1:
            xso = np.concatenate([meta, xp[:nfr]], 0)
            pos_so = np.arange(NSO)
            xown = xp[nfr:nfr + NOWN]; pos_own = NSO + np.arange(NOWN)
        else:
            xso = np.concatenate([xp[nfr:2 * nfr], meta], 0)
            pos_so = np.concatenate([np.zeros(nfr), np.arange(16)])
            xown = xp[:NOWN]; pos_own = 16 + np.arange(NOWN)
        keyb = np.zeros((1, NSLOT + 2 * SSTRIDE), np.float32)
        if hf == 0:
            keyb[0, :nfr] = -1e30
        xs = f32(inp["x_sample"][2 * c:2 * c + 2]).reshape(128, D)
        pos_sm = np.concatenate([16 + past + np.arange(64)] * 2)
        m = dict(shared)
        m.update(xso=np.ascontiguousarray(xso), xown=np.ascontiguousarray(xown), xsm=xs,
                 rope=rope_table(np.concatenate([pos_so, pos_own, pos_sm])),
                 flag=np.full((128, 1), float(hf), np.float32), keyb=keyb.astype(ml_dtypes.bfloat16),
                 ck=f32(inp["cache_k"][0, 2 * c:2 * c + 2]), cv=f32(inp["cache_v"][0, 2 * c:2 * c + 2]), cik=f32(inp["cache_idx_k"][0, 2 * c:2 * c + 2]),
                 swkv=np.stack([st_layout(inp["state_wkv"][0, 2 * c + q]) for q in range(2)]),
                 sshift=np.stack([colpack(inp["state_shift"][0, 2 * c + q], 26) for q in range(2)]))
        maps.append(m)
    return maps


_CACHE = {}


def run_device(inp):
    key = (GEOM['NSO_B'], GEOM['NOWN_B'], GEOM['SAMPLE'], MAXOPS)
    if key not in _CACHE:
        _CACHE[key] = build_program()
    nc, P = _CACHE[key]
    maps = prep_inputs(inp)
    used = set(P.names)
    maps = [{k: v for k, v in m.items() if k in used} for m in maps]
    res = run_bass_kernel_spmd(nc, maps, core_ids=list(range(8)))
    return res.results


def uncol(a, n):
    return np.ascontiguousarray(np.asarray(a, np.float32).T.reshape(-1))


def kernel(**inputs):
    NSO, NOWN, NSLOT = geom()
    res = run_device(inputs)
    B = 4
    y_p = np.zeros((B, 2 * NOWN, D), np.float32)
    k_p = np.zeros((1, B, NSLOT, 128), np.float32); v_p = np.zeros((1, B, NSLOT, 128), np.float32); ki_p = np.zeros((1, B, NSLOT, 64), np.float32)
    wkv_p = np.zeros((1, B, 16, 64, 64), np.float32); sh_p = np.zeros((1, B, 3328), np.float32)
    y_s = np.zeros((16, 64, D), np.float32)
    k_s = np.zeros((1, 16, 64, 128), np.float32); v_s = np.zeros((1, 16, 64, 128), np.float32); ki_s = np.zeros((1, 16, 64, 64), np.float32)
    wkv_s = np.zeros((1, 16, 16, 64, 64), np.float32); sh_s = np.zeros((1, 16, 3328), np.float32)
    for c in range(8):
        b, hf = c // 2, c % 2
        r = res[c]
        y_p[b, hf * NOWN:(hf + 1) * NOWN] = r["O_y"]
        if hf == 1:
            k_p[0, b] = r["O_k"]; v_p[0, b] = r["O_v"]; ki_p[0, b] = r["O_ki"]
            wkv_p[0, b] = st_unlayout(r["O_wkv"]); sh_p[0, b] = uncol(r["O_shift"], 26)
        y_s[2 * c:2 * c + 2] = r["O_ys"].reshape(2, 64, D)
        k_s[0, 2 * c:2 * c + 2] = r["O_ks"].reshape(2, 64, 128); v_s[0, 2 * c:2 * c + 2] = r["O_vs"].reshape(2, 64, 128)
        ki_s[0, 2 * c:2 * c + 2] = r["O_kis"].reshape(2, 64, 64)
        for q in range(2):
            wkv_s[0, 2 * c + q] = st_unlayout(r["O_wkvs"][q]); sh_s[0, 2 * c + q] = uncol(r["O_shifts"][q], 26)
    return (y_p, y_s, k_p, v_p, ki_p, wkv_p, sh_p, k_s, v_s, ki_s, wkv_s, sh_s)
```

```python
import bisect
import numpy as np
import ml_dtypes
from contextlib import ExitStack
import concourse.bass as bass
import concourse.mybir as mybir
from concourse.bass_utils import run_bass_kernel_spmd

F32 = mybir.dt.float32
BF16 = mybir.dt.bfloat16
U8 = mybir.dt.uint8
ALU = mybir.AluOpType
AF = mybir.ActivationFunctionType

SAME_ENGINE_SYNC = True
MAXOPS = None
ENGS = ('pe', 'act', 'dve', 'pool', 'sp')


class Prog:
    def __init__(self):
        self.nc = bass.Bass("TRN2", target_bir_lowering=False)
        self.es = ExitStack()
        self.ops = []
        self.state = {}
        self.names = set()
        self.psum_names = set()

    def _nm(self, name):
        assert name not in self.names, name
        self.names.add(name)
        return name

    def sb(self, name, shape, dt):
        return self.es.enter_context(self.nc.sbuf_tensor(self._nm(name), list(shape), dt))

    def ps(self, name, shape, dt=F32):
        self.psum_names.add(name)
        return self.es.enter_context(self.nc.psum_tensor(self._nm(name), list(shape), dt))

    def dram(self, name, shape, dt, kind):
        return self.nc.dram_tensor(self._nm(name), list(shape), dt, kind=kind)

    @staticmethod
    def _reg(x):
        if isinstance(x, tuple):
            base, sub = x[0], (x[1],)
        else:
            base, sub = x, ()
        if isinstance(base, Alias):
            return base.name, (base.key,) + sub
        return base.name, sub

    @staticmethod
    def _rel(a, b):
        n = min(len(a), len(b))
        return a[:n] == b[:n]

    def _deps(self, idx, reads, writes, eng=None):
        deps = set()
        for x in reads:
            n, k = self._reg(x)
            st = self.state.setdefault(n, {})
            for kk, ent in st.items():
                if not self._rel(kk, k):
                    continue
                if ent[0] is not None:
                    deps.add(ent[0])
                if n in self.psum_names:
                    for rr in ent[1]:
                        if self.ops[rr]['eng'] != eng:
                            deps.add(rr)
        for x in writes:
            n, k = self._reg(x)
            st = self.state.setdefault(n, {})
            for kk, ent in st.items():
                if not self._rel(kk, k):
                    continue
                if ent[0] is not None:
                    deps.add(ent[0])
                deps.update(ent[1])
        for x in reads:
            n, k = self._reg(x)
            self.state[n].setdefault(k, [None, []])[1].append(idx)
        for x in writes:
            n, k = self._reg(x)
            st = self.state[n]
            for kk in [kk for kk in st if len(kk) >= len(k) and kk[:len(k)] == k]:
                del st[kk]
            st[k] = [idx, []]
        deps.discard(idx)
        return sorted(deps)

    def op(self, eng, fn, r=(), w=()):
        if MAXOPS is not None and len(self.ops) >= MAXOPS:
            return None
        idx = len(self.ops)
        self.ops.append(dict(eng=eng, fn=fn, deps=self._deps(idx, r, w, eng), dma=False, semkey=None))
        return idx

    def dma(self, q, out, in_, r=(), w=(), semkey=None, **kw):
        if MAXOPS is not None and len(self.ops) >= MAXOPS:
            return None
        idx = len(self.ops)
        deps = self._deps(idx, r, w)
        if semkey is None:
            wn = self._reg(w[0])[0]
            semkey = wn if not (wn.startswith('D_') or wn.startswith('O_')) else self._reg(r[0])[0]
        fn = (lambda e, out=out, in_=in_, kw=kw: e.dma_start(out=out, in_=in_, **kw))
        self.ops.append(dict(eng=q, fn=fn, deps=deps, dma=True, semkey=semkey))
        return idx

    def build(self):
        nc, ops = self.nc, self.ops
        n = len(ops)
        need_sig = [False] * n
        for i, o in enumerate(ops):
            for j in o['deps']:
                pj = ops[j]
                if pj['dma']:
                    continue
                if pj['eng'] != o['eng'] or o['dma'] or (SAME_ENGINE_SYNC and o['eng'] != 'pe'):
                    need_sig[j] = True
        cnt = {e: 0 for e in ENGS}
        sigval = [0] * n
        dcnt, semkeys = {}, []
        for i, o in enumerate(ops):
            if o['dma']:
                k = o['semkey']
                if k not in dcnt:
                    dcnt[k] = 0
                    semkeys.append(k)
                dcnt[k] += 16
                sigval[i] = dcnt[k]
            elif need_sig[i]:
                cnt[o['eng']] += 1
                sigval[i] = cnt[o['eng']]
        esem = {e: self.es.enter_context(nc.semaphore("s_" + e)) for e in ENGS}
        dsem = {k: self.es.enter_context(nc.semaphore("d_" + k)) for k in semkeys}
        self.n_sems = len(esem) + len(dsem)
        dma_idx = {}
        for i, o in enumerate(ops):
            if o['dma']:
                dma_idx.setdefault(o['semkey'], []).append(i)
        waited = {e: {} for e in ENGS}
        plan = {e: [] for e in ENGS}
        for i, o in enumerate(ops):
            E = o['eng']
            waits = {}
            for j in o['deps']:
                pj = ops[j]
                if pj['dma']:
                    key = ('d', pj['semkey'])
                    lst = dma_idx[pj['semkey']]
                    val_d = 16 * bisect.bisect_left(lst, i)
                else:
                    if pj['eng'] == E and not o['dma'] and (E == 'pe' or not SAME_ENGINE_SYNC):
                        continue
                    key = ('e', pj['eng'])
                val = val_d if pj['dma'] else sigval[j]
                if waited[E].get(key, 0) >= val:
                    continue
                waits[key] = max(waits.get(key, 0), val)
            for key, val in waits.items():
                waited[E][key] = val
            plan[E].append((i, waits))
        blk = self.es.enter_context(nc.Block())
        engobj = {'pe': 'tensor', 'act': 'scalar', 'dve': 'vector', 'pool': 'gpsimd', 'sp': 'sync'}

        def emit_for(E):
            def body(eng):
                for i, waits in plan[E]:
                    o = ops[i]
                    for (kind, k), val in waits.items():
                        eng.wait_ge(dsem[k] if kind == 'd' else esem[k], val)
                    ins = o['fn'](eng)
                    if o['dma']:
                        ins.then_inc(dsem[o['semkey']], 16)
                    elif need_sig[i]:
                        ins.then_inc(esem[E], 1)
                if E == 'sp':
                    for k, v in dcnt.items():
                        eng.wait_ge(dsem[k], v)
                    for e2 in ENGS:
                        if e2 != 'sp' and cnt[e2] > 0:
                            eng.wait_ge(esem[e2], cnt[e2])
            return body

        for E in ENGS:
            getattr(blk, engobj[E])(emit_for(E))
        self.es.close()
        return nc


class Alias:
    def __init__(self, base, dtype, byte_off, shape, key):
        es = 2 if dtype == BF16 else 4
        self.h = base.bitcast(dtype)
        self.off = byte_off // es
        self.shape = list(shape)
        self.name = base.name
        self.key = key
        n = int(np.prod(shape[1:]))
        v = self.h[0:shape[0], self.off:self.off + n]
        if len(shape) == 3:
            v = v.rearrange("p (a b) -> p a b", a=shape[1])
        self.v = v

    def __getitem__(self, idx):
        return self.v[idx]


class Ring:
    def __init__(self, tiles):
        self.t, self.i = tiles, 0

    def next(self):
        t = self.t[self.i % len(self.t)]
        self.i += 1
        return t


D = 1024
DFF = 2816
GEOM = dict(NSO_B=32, NOWN_B=32, SAMPLE=True)
NB = 1
NT = 128 * NB
WIN_COLS = 7240
C_Q, C_K, C_V, C_QI, C_KI, C_WI, C_G, C_RW = 0, 1024, 1152, 1280, 1792, 1856, 1864, 3912
LN_EPS = 1e-5
GN_EPS = 64e-5
DEC = 0.6065306597126334
ALPHA = 2.0 ** 0.25
CACHE_ROWS = 2064
SSTRIDE = 2176
NBIS = 19
TOPK = 256
O_G0, O_B0, O_MU, O_W0, O_A0, O_KK, O_KA, O_RK, NCOLS = 0, 8, 16, 42, 50, 58, 66, 74, 82


def geom():
    nso = 128 * GEOM['NSO_B'] + 16
    nown = 128 * GEOM['NOWN_B']
    return nso, nown, nso + nown


def build_program():
    NSO, NOWN, NSLOT = geom()
    SAMPLE = GEOM['SAMPLE']
    NSLOT_ALL = NSLOT + 2 * SSTRIDE
    P = Prog()
    nc = P.nc
    V_ = lambda fn, r=(), w=(): P.op('dve', fn, r, w)
    A_ = lambda fn, r=(), w=(): P.op('act', fn, r, w)
    G_ = lambda fn, r=(), w=(): P.op('pool', fn, r, w)
    T_ = lambda fn, r=(), w=(): P.op('pe', fn, r, w)

    din = lambda n, s, dt=F32: P.dram(n, s, dt, "ExternalInput")
    dout = lambda n, s, dt=F32: P.dram(n, s, dt, "ExternalOutput")
    dint = lambda n, s, dt=BF16: P.dram(n, s, dt, "Internal")
    xso = din("xso", [NSO, D]); xown = din("xown", [NOWN, D]); xsm = din("xsm", [128, D])
    rope = din("rope", [NSO + NOWN + 128, 48])
    flag = din("flag", [128, 1])
    colsf = din("colsf", [128, NCOLS])
    cmask = din("cmask", [128, 640])
    identf = din("identf", [128, 128])
    blk1 = din("blk1", [128, 128])
    keyb = din("keyb", [1, NSLOT_ALL], BF16)
    w_in = din("w_in", [D, WIN_COLS])
    rw_w2 = din("rw_w2", [64, D]); rw_a2 = din("rw_a2", [64, D]); rw_g2 = din("rw_g2", [128, D])
    ikg = din("ikg", [64]); ikb = din("ikb", [64])
    vecs = {n: din(n, [D]) for n in ("ln0_g", "ln0_b", "ln1_g", "ln1_b", "ln2_g", "ln2_b", "gn_g", "gn_b")}
    w_oa = din("w_oa", [D, D]); w_or = din("w_or", [D, D]); w_out = din("w_out", [D, D])
    w_fg = din("w_fg", [D, DFF]); w_fu = din("w_fu", [D, DFF]); w_fd = din("w_fd", [DFF, D])
    ck = din("ck", [2, CACHE_ROWS, 128]); cv = din("cv", [2, CACHE_ROWS, 128]); cik = din("cik", [2, CACHE_ROWS, 64])
    swkv = din("swkv", [2, 128, 512]); sshift = din("sshift", [2, 128, 26])

    O_y = dout("O_y", [NOWN, D]); O_ys = dout("O_ys", [128, D])
    O_k = dout("O_k", [NSLOT, 128]); O_v = dout("O_v", [NSLOT, 128]); O_ki = dout("O_ki", [NSLOT, 64])
    O_wkv = dout("O_wkv", [128, 512]); O_shift = dout("O_shift", [128, 26])
    O_ks = dout("O_ks", [128, 128]); O_vs = dout("O_vs", [128, 128]); O_kis = dout("O_kis", [128, 64])
    O_wkvs = dout("O_wkvs", [2, 128, 512]); O_shifts = dout("O_shifts", [2, 128, 26])

    D_win = dint("D_win", [D, WIN_COLS])
    D_w2 = dint("D_w2", [64, D]); D_a2 = dint("D_a2", [64, D]); D_g2 = dint("D_g2", [128, D])
    D_woa = dint("D_woa", [D, D]); D_wor = dint("D_wor", [D, D]); D_wout = dint("D_wout", [D, D])
    D_wfg = dint("D_wfg", [D, DFF]); D_wfu = dint("D_wfu", [D, DFF]); D_wfd = dint("D_wfd", [DFF, D])
    D_kt = dint("D_kt", [128, NSLOT_ALL]); D_kit = dint("D_kit", [64, NSLOT_ALL]); D_v = dint("D_v", [NSLOT_ALL, 128])

    def cast_rows(dst, src, nrows, step):
        for i in range(0, nrows, step):
            n = min(step, nrows - i)
            P.dma('pool', dst.ap()[i:i + n, :], src.ap()[i:i + n, :], r=[src], w=[(dst, i)])
    cast_rows(D_win, w_in, D, 128)
    P.dma('pool', D_w2.ap(), rw_w2.ap(), r=[rw_w2], w=[D_w2])
    P.dma('pool', D_a2.ap(), rw_a2.ap(), r=[rw_a2], w=[D_a2])
    P.dma('pool', D_g2.ap(), rw_g2.ap(), r=[rw_g2], w=[D_g2])
    for (dd, ss, nr) in ((D_woa, w_oa, D), (D_wor, w_or, D), (D_wout, w_out, D), (D_wfg, w_fg, D), (D_wfu, w_fu, D), (D_wfd, w_fd, DFF)):
        cast_rows(dd, ss, nr, 256)
    if SAMPLE:
        for q in range(2):
            sb0 = NSLOT + SSTRIDE * q
            P.dma('pool', D_v.ap()[sb0:sb0 + CACHE_ROWS, :], cv.ap()[q], r=[cv], w=[(D_v, 'c%d' % q)])
    kview = lambda dt_: dt_.ap().rearrange("(k p) n -> p k n", p=128)
    win_v = kview(D_win)

    cols = P.sb("cols", [128, NCOLS], F32)
    colsd = P.sb("colsd", [128, 34], F32)
    identF = P.sb("identF", [128, 128], F32)
    identB = P.sb("identB", [128, 128], BF16)
    I4 = P.sb("I4", [128, 4, 128], BF16)
    blkones = P.sb("blkones", [128, 128], F32)
    blkonesB = P.sb("blkonesB", [128, 128], BF16)
    mask = P.sb("mask", [128, 640], F32)
    ones128 = P.sb("ones128", [128, 128], F32)
    onesrow = P.sb("onesrow", [1, 128], BF16)
    ikg_b = P.sb("ikg_b", [128, 64], F32); ikb_b = P.sb("ikb_b", [128, 64], F32)
    w2b = P.sb("w2b", [128, D], BF16); a2b = P.sb("a2b", [128, D], BF16); g2b = P.sb("g2b", [128, D], BF16)
    flg = P.sb("flg", [128, 1], F32)
    vbr = Ring([P.sb("vbr%d" % i, [128, D], F32) for i in range(4)])

    def getvec(n):
        t = vbr.next()
        P.dma('sp', t[:], vecs[n].ap().partition_broadcast(128), r=[vecs[n]], w=[t])
        return t
    P.dma('sp', cols[:], colsf.ap(), r=[colsf], w=[cols])
    P.dma('sp', identF[:], identf.ap(), r=[identf], w=[identF])
    P.dma('sp', blkones[:], blk1.ap(), r=[blk1], w=[blkones])
    P.dma('sp', mask[:], cmask.ap(), r=[cmask], w=[mask])
    P.dma('sp', flg[:], flag.ap(), r=[flag], w=[flg])
    P.dma('sp', ikg_b[:], ikg.ap().partition_broadcast(128), r=[ikg], w=[ikg_b])
    P.dma('sp', ikb_b[:], ikb.ap().partition_broadcast(128), r=[ikb], w=[ikb_b])
    P.dma('sp', w2b[0:64, :], D_w2.ap(), r=[D_w2], w=[w2b])
    P.dma('sp', a2b[64:128, :], D_a2.ap(), r=[D_a2], w=[a2b])
    P.dma('sp', g2b[:], D_g2.ap(), r=[D_g2], w=[g2b])
    V_(lambda e: e.tensor_copy(out=identB[:], in_=identF[:]), r=[identF], w=[identB])
    for i in range(4):
        V_(lambda e, i=i: e.tensor_copy(out=I4[:, i, :], in_=identF[:]), r=[identF], w=[I4])
    V_(lambda e: e.tensor_copy(out=blkonesB[:], in_=blkones[:]), r=[blkones], w=[blkonesB])
    V_(lambda e: e.memset(ones128[:], 1.0), w=[ones128])
    V_(lambda e: e.memset(onesrow[:], 1.0), w=[onesrow])
    V_(lambda e: e.tensor_scalar(out=colsd[:, 0:26], in0=cols[:, O_MU:O_MU + 26], scalar1=-1.0, scalar2=1.0, op0=ALU.mult, op1=ALU.add),
       r=[cols], w=[colsd])
    V_(lambda e: e.tensor_scalar(out=colsd[:, 26:34], in0=cols[:, O_KA:O_KA + 8], scalar1=-1.0, scalar2=1.0, op0=ALU.mult, op1=ALU.add),
       r=[cols], w=[colsd])

    pm = Ring([P.ps("pm0", [128, 512]), P.ps("pm1", [128, 512])])
    ptr = P.ps("ptr", [128, 8, 128], BF16)
    pLT = P.ps("pLT", [128, 1024])
    pO = P.ps("pO", [128, 1536])
    cbanks = [(pLT, 0), (pLT, 1), (pO, 0), (pO, 1)]
    cb_ap = lambda i: (cbanks[i][0])[:, 512 * cbanks[i][1]:512 * (cbanks[i][1] + 1)]
    xbanks = [(pLT, 0), (pLT, 1), (pO, 2)]
    xq = [0]

    xin = Ring([P.sb("xin%d" % i, [128, NB, D], F32) for i in range(2)])
    hn = P.sb("hn", [128, NB, D], BF16)
    hres = P.sb("hres", [128, NB, D], F32)
    hT = P.sb("hT", [128, 8, NT], BF16)
    small = Ring([P.sb("small%d" % i, [128, 24], F32) for i in range(4)])
    wt = Ring([P.sb("wt%d" % i, [128, 8, 512], BF16) for i in range(3)])
    kvs = [P.sb("kvs%d" % i, [128, 328], F32) for i in range(NB)]
    rtmp = Ring([P.sb("rtmp%d" % i, [128, 512], F32) for i in range(2)])
    rp = [P.sb("rp%d" % i, [128, 48], F32) for i in range(NB)]
    vbt = Ring([P.sb("vbt%d" % i, [128, 128], BF16) for i in range(2)])
    ktt = Ring([P.sb("ktt%d" % i, [128, NT], BF16) for i in range(2)])
    kitt = Ring([P.sb("kitt%d" % i, [64, NT], BF16) for i in range(2)])
    ki2 = Ring([P.sb("ki2_%d" % i, [128, 64], F32) for i in range(2)])
    qT = P.sb("qT", [128, NB, 8, 128], BF16)
    qiT = P.sb("qiT", [128, NB, 4, 128], BF16)
    gsig = P.sb("gsig", [128, NB, 2 * D], BF16)
    car = P.sb("car", [128, 26], F32)
    cars = P.sb("cars", [128, 2, 26], F32)
    TL = P.sb("TL", [128, NT], BF16)
    SLG = P.sb("SLG", [128, NT], BF16)
    f32t = Ring([P.sb("f32t%d" % i, [128, NT], F32) for i in range(8)])
    AT = P.sb("AT", [128, 8, NT], BF16); KTt = P.sb("KTt", [128, 8, NT], BF16); BTt = P.sb("BTt", [128, 8, NT], BF16)
    VTf = P.sb("VTf", [128, 8, NT], BF16); RTt = P.sb("RTt", [128, 8, NT], BF16); KMR = P.sb("KMR", [128, 8, NT], BF16)
    PRD = Ring([P.sb("PRD%d" % i, [128, NT], BF16) for i in range(2)])
    CS = P.sb("CS", [128, 8, NB, 129], F32)
    GC = P.sb("GC", [128, NB, 8], F32)
    Ktok = P.sb("Ktok", [128, NB, D], BF16); Btok = P.sb("Btok", [128, NB, D], BF16); Vtok = P.sb("Vtok", [128, NB, D], BF16)
    gtok = P.sb("gtok", [128, NB, D], BF16)
    bcoef = P.sb("bcoef", [128, NB, 16], F32)
    ST = P.sb("ST", [128, 8, 64], F32); STb = P.sb("STb", [128, 8, 64], BF16)
    F1 = P.sb("F1", [128, D], F32); F2 = P.sb("F2", [128, D], F32)
    Yt, Y2 = F1, F2
    gns = P.sb("gns", [128, 80], F32)
    orwT = P.sb("orwT", [128, NB, 8, 128], BF16)
    orw = P.sb("orw", [128, D], BF16)
    q32 = F1
    SMAX = max(NSLOT, 8208)
    SC = P.sb("SC", [128, SMAX], F32)
    aoff = [0]

    def alias(key, shape, dt_):
        nbytes = int(np.prod(shape[1:])) * (2 if dt_ == BF16 else 4)
        a = Alias(SC, dt_, aoff[0], shape, key)
        aoff[0] += nbytes
        assert aoff[0] <= 4 * SMAX, aoff[0]
        return a
    amat = Ring([alias("amat%d" % i, [128, 4, 128], BF16) for i in range(4)])
    pq = [Ring([alias("pq%d_%d" % (h, i), [128, 2, 128], BF16) for i in range(2)]) for h in range(4)]
    Mt = [Ring([alias("Mt%d_%d" % (h, i), [128, 128], BF16) for i in range(2)]) for h in range(4)]
    Mfin = alias("Mfin", [128, 16, 128], BF16)
    AKs = alias("AKs", [128, 16, 128], BF16)
    ARB = alias("ARB", [128, 16, 128], BF16); ARK = alias("ARK", [128, 16, 128], BF16)
    Wb = alias("Wb", [128, 16, 64], BF16); Ub = alias("Ub", [128, 16, 64], BF16)
    sttmp = alias("sttmp", [128, 8, 64], F32)
    junk = P.sb("junk", [128, 2], BF16)
    tau = P.sb("tau", [128, 8], F32)
    dg = P.sb("dg", [128, 8, 128], BF16)
    kitl = Ring([P.sb("kitl%d" % i, [128, 512], BF16) for i in range(2)])
    kbt = Ring([P.sb("kbt%d" % i, [1, 512], BF16) for i in range(2)])
    ktl = Ring([P.sb("ktl%d" % i, [128, 512], BF16) for i in range(2)])
    vtl = Ring([P.sb("vtl%d" % i, [128, 4, 130], BF16) for i in range(2)])
    mbt = Ring([P.sb("mbt%d" % i, [128, 512], BF16) for i in range(2)])
    Rt = Ring([P.sb("Rt%d" % i, [128, 512], BF16) for i in range(2)])
    PTt = Ring([P.sb("PTt%d" % i, [128, 1024], BF16) for i in range(2)])
    rden = P.sb("rden", [128, 8], F32)
    attn = P.sb("attn", [128, D], BF16)
    attnT = P.sb("attnT", [128, NB, 8, 128], BF16)
    mixs = [attn]
    mixT = P.sb("mixT", [128, 8, NT], BF16)
    x1 = F1.reshape([128, 1, D])
    x1b = hn.reshape([128, D])
    x1T = P.sb("x1T", [128, 8, NT], BF16)
    actg = Ring([P.sb("actg%d" % i, [128, NT], BF16) for i in range(2)])
    actT = Ring([P.sb("actT%d" % i, [128, NT], BF16) for i in range(3)])
    yo = Ring([F2])

    V_(lambda e: e.memset(ST[:], 0.0), w=[ST])
    V_(lambda e: e.memset(STb[:], 0.0), w=[STb])
    V_(lambda e: e.memset(car[:], 0.0), w=[car])
    V_(lambda e: e.memset(CS[:], 0.0), w=[CS])
    for t in vtl.t:
        V_(lambda e, t=t: e.memset(t[:], 1.0), w=[t])

    def load_w(view, c0, ncol):
        t = wt.next()
        P.dma('sp', t[:, :, 0:ncol], view[:, :, c0:c0 + ncol], r=[view.tensor], w=[t])
        return t

    def ln_rows(x_ap, L, n, out_ap, eps, g_b=None):
        xreg, oreg = x_ap.tensor, out_ap.tensor
        sm = small.next()
        nch = (n + 511) // 512
        for c in range(nch):
            V_(lambda e, c=c: e.bn_stats(out=sm[:L, 6 * c:6 * c + 6], in_=x_ap[:, 512 * c:min(n, 512 * (c + 1))]), r=[xreg], w=[sm])
        V_(lambda e: e.bn_aggr(out=sm[:L, 12:14], in_=sm[:L, 0:6 * nch]), r=[sm], w=[sm])
        A_(lambda e: e.activation(out=sm[:L, 14:15], in_=sm[:L, 13:14], func=AF.Sqrt, bias=float(eps), scale=1.0), r=[sm], w=[sm])
        V_(lambda e: e.reciprocal(out=sm[:L, 15:16], in_=sm[:L, 14:15]), r=[sm], w=[sm])
        V_(lambda e: e.tensor_scalar(out=sm[:L, 16:17], in0=sm[:L, 12:13], scalar1=sm[:L, 15:16], scalar2=-1.0, op0=ALU.mult, op1=ALU.mult),
           r=[sm], w=[sm])
        A_(lambda e: e.activation(out=out_ap, in_=x_ap, func=AF.Identity, scale=sm[:L, 15:16], bias=sm[:L, 16:17]),
           r=[sm, xreg], w=[oreg])
        if g_b is not None:
            g, b = g_b
            G_(lambda e: e.tensor_tensor(out=out_ap, in0=out_ap, in1=g[:L, 0:n], op=ALU.mult), r=[oreg, g], w=[oreg])
            G_(lambda e: e.tensor_tensor(out=out_ap, in0=out_ap, in1=b[:L, 0:n], op=ALU.add), r=[oreg, b], w=[oreg])

    def rope_rows(t, L, c0, half, cos_ap, sin_ap, nh=1, stride=0):
        tab = cos_ap.tensor
        tm = rtmp.next()
        if nh == 1:
            x1_ = t[:L, c0:c0 + half]; x2_ = t[:L, c0 + half:c0 + 2 * half]
            a = tm[:L, 0:half]; b = tm[:L, half:2 * half]; c = tm[:L, 2 * half:3 * half]; d = tm[:L, 3 * half:4 * half]
            cs, sn = cos_ap, sin_ap
        else:
            v = t[:L, c0:c0 + nh * stride].rearrange("p (h d) -> p h d", h=nh)
            x1_ = v[:, :, 0:half]; x2_ = v[:, :, half:2 * half]
            tv = tm[:L, 0:4 * nh * half].rearrange("p (q h d) -> p q h d", q=4, h=nh)
            a, b, c, d = tv[:, 0], tv[:, 1], tv[:, 2], tv[:, 3]
            cs = cos_ap.unsqueeze(1).to_broadcast([L, nh, half]); sn = sin_ap.unsqueeze(1).to_broadcast([L, nh, half])
        rr = [t, tm, tab]
        V_(lambda e: e.tensor_tensor(out=a, in0=cs, in1=x1_, op=ALU.mult), r=rr, w=[tm])
        V_(lambda e: e.tensor_tensor(out=b, in0=sn, in1=x2_, op=ALU.mult), r=rr, w=[tm])
        V_(lambda e: e.tensor_tensor(out=c, in0=cs, in1=x2_, op=ALU.mult), r=rr, w=[tm])
        V_(lambda e: e.tensor_tensor(out=d, in0=sn, in1=x1_, op=ALU.mult), r=rr, w=[tm])
        V_(lambda e: e.tensor_tensor(out=x1_, in0=a, in1=b, op=ALU.subtract), r=[tm], w=[t])
        V_(lambda e: e.tensor_tensor(out=x2_, in0=c, in1=d, op=ALU.add), r=[tm], w=[t])

    evq = [0]

    def evac_copy(out_ap, in_ap, r, w):
        evq[0] += 1
        if evq[0] % 2:
            A_(lambda e: e.activation(out=out_ap, in_=in_ap, func=AF.Copy), r=r, w=w)
        else:
            V_(lambda e: e.tensor_copy(out=out_ap, in_=in_ap), r=r, w=w)

    def transpose_to(dst_fn, src_fn, n, L, r, w, ident=None):
        for k in range(n):
            T_(lambda e, k=k: e.transpose(out=ptr[:, k, 0:L], in_=src_fn(k), identity=identB[:L, :L]), r=r + [identB], w=[ptr])

    def front(x_dram, rows, Ls, rope_rows0, slots, outs, own):
        O_k_, O_v_, O_ki_, orow = outs
        xt = xin.next()
        L = Ls[0]
        ntk = sum(Ls)
        for b in range(len(Ls)):
            P.dma('sp', xt[:L, b, :], x_dram.ap()[rows[b]:rows[b] + L, :], r=[x_dram], w=[(xt, b)])
            P.dma('sp', rp[b][:L, :], rope.ap()[rope_rows0[b]:rope_rows0[b] + L, :], r=[rope], w=[rp[b]])
        for b in range(len(Ls)):
            if own:
                ln_rows(xt[:L, b, :], L, D, hres[:L, b, :], LN_EPS)
                G_(lambda e, b=b: e.tensor_copy(out=hn[:L, b, :], in_=hres[:L, b, :]), r=[(hres, b)], w=[(hn, b)])
                vg, vbb = getvec("ln0_g"), getvec("ln0_b")
                G_(lambda e, b=b, vg=vg: e.tensor_tensor(out=hres[:L, b, :], in0=hres[:L, b, :], in1=vg[:L, :], op=ALU.mult),
                   r=[(hres, b), vg], w=[(hres, b)])
                G_(lambda e, b=b, vbb=vbb: e.tensor_tensor(out=hres[:L, b, :], in0=hres[:L, b, :], in1=vbb[:L, :], op=ALU.add),
                   r=[(hres, b), vbb], w=[(hres, b)])
            else:
                ln_rows(xt[:L, b, :], L, D, hn[:L, b, :], LN_EPS)
            for k in range(8):
                T_(lambda e, b=b, k=k: e.transpose(out=ptr[:, k, 0:L], in_=hn[:L, b, 128 * k:128 * (k + 1)], identity=identB[:L, :L]),
                   r=[hn, identB], w=[ptr])
            for k in range(8):
                if k % 2:
                    A_(lambda e, b=b, k=k: e.activation(out=hT[:, k, L * b:L * b + L], in_=ptr[:, k, 0:L], func=AF.Identity,
                                                        scale=cols[:, O_G0 + k:O_G0 + k + 1], bias=cols[:, O_B0 + k:O_B0 + k + 1]),
                       r=[ptr, cols], w=[(hT, (b, k))])
                else:
                    V_(lambda e, b=b, k=k: e.tensor_scalar(out=hT[:, k, L * b:L * b + L], in0=ptr[:, k, 0:L],
                                                           scalar1=cols[:, O_G0 + k:O_G0 + k + 1], scalar2=cols[:, O_B0 + k:O_B0 + k + 1],
                                                           op0=ALU.mult, op1=ALU.add), r=[ptr, cols], w=[(hT, (b, k))])
        wk = wt.next()
        P.dma('sp', wk[:, :, 0:256], win_v[:, :, C_K:C_K + 256], r=[D_win], w=[wk])
        P.dma('sp', wk[:, :, 256:328], win_v[:, :, C_KI:C_KI + 72], r=[D_win], w=[wk])
        kt_t = ktt.next(); kit_t = kitt.next()
        for b in range(len(Ls)):
            pk = pm.next()
            for k in range(8):
                T_(lambda e, b=b, k=k, pk=pk: e.matmul(pk[:L, 0:328], lhsT=hT[:, k, L * b:L * b + L], rhs=wk[:, k, 0:328],
                                                    start=(k == 0), stop=(k == 7)), r=[hT, wk], w=[pk])
            kv = kvs[b]
            A_(lambda e, pk=pk, kv=kv: e.activation(out=kv[:L, :], in_=pk[:L, 0:328], func=AF.Copy), r=[pk], w=[kv])
            rt = rp[b]
            rope_rows(kv, L, 0, 16, rt[:L, 0:16], rt[:L, 16:32])
            k2 = ki2.next()
            ln_rows(kv[:L, 256:320], L, 64, k2[:L, :], LN_EPS, g_b=(ikg_b, ikb_b))
            rope_rows(k2, L, 0, 8, rt[:L, 32:40], rt[:L, 40:48])
            s0 = slots[b]; o0 = orow[b]
            P.dma('pool', O_k_.ap()[o0:o0 + L, :], kv[:L, 0:128], r=[kv], w=[(O_k_, o0)])
            P.dma('pool', O_v_.ap()[o0:o0 + L, :], kv[:L, 128:256], r=[kv], w=[(O_v_, o0)])
            P.dma('pool', O_ki_.ap()[o0:o0 + L, :], k2[:L, :], r=[k2], w=[(O_ki_, o0)])
            vb = vbt.next()
            G_(lambda e, kv=kv, vb=vb: e.tensor_copy(out=vb[:L, :], in_=kv[:L, 128:256]), r=[kv], w=[vb])
            P.dma('pool', D_v.ap()[s0:s0 + L, :], vb[:L, :], r=[vb], w=[(D_v, s0)])
            pt_ = pm.next()
            T_(lambda e, kv=kv, pt_=pt_: e.transpose(out=pt_[:, 0:L], in_=kv[:L, 0:128], identity=identF[:L, :L]), r=[kv, identF], w=[pt_])
            T_(lambda e, k2=k2, pt_=pt_: e.transpose(out=pt_[0:64, 128:128 + L], in_=k2[:L, 0:64], identity=identF[:L, :L]), r=[k2, identF], w=[pt_])
            V_(lambda e, b=b, kt_t=kt_t, pt_=pt_: e.tensor_copy(out=kt_t[:, L * b:L * b + L], in_=pt_[:, 0:L]), r=[pt_], w=[kt_t])
            V_(lambda e, b=b, kit_t=kit_t, pt_=pt_: e.tensor_copy(out=kit_t[:, L * b:L * b + L], in_=pt_[0:64, 128:128 + L]), r=[pt_], w=[kit_t])
            P.dma('pool', D_kt.ap()[:, s0:s0 + L], kt_t[:, L * b:L * b + L], r=[kt_t], w=[(D_kt, s0)])
            P.dma('pool', D_kit.ap()[:, s0:s0 + L], kit_t[:, L * b:L * b + L], r=[kit_t], w=[(D_kit, s0)])

    def own_proj(Ls, rope_rows0):
        L = Ls[0]
        nbk = len(Ls)
        wq = [load_w(win_v, C_Q + 512 * g, 512) for g in range(2)]
        for b in range(nbk):
            for g in range(2):
                pq_ = pm.next()
                for k in range(8):
                    T_(lambda e, b=b, g=g, k=k, pq_=pq_: e.matmul(pq_[:L, :], lhsT=hT[:, k, L * b:L * b + L], rhs=wq[g][:, k, :], start=(k == 0), stop=(k == 7)),
                       r=[hT, wq[g]], w=[pq_])
                evac_copy(q32[:L, 512 * g:512 * (g + 1)], pq_[:L, :], r=[pq_], w=[(q32, g)])
            rope_rows(q32, L, 0, 16, rp[b][:L, 0:16], rp[b][:L, 16:32], nh=8, stride=128)
            for h in range(8):
                T_(lambda e, h=h: e.transpose(out=pLT[:, 128 * h:128 * h + L], in_=q32[:L, 128 * h:128 * (h + 1)], identity=identF[:L, :L]),
                   r=[q32, identF], w=[(pLT, h // 4)])
            evac_copy(qT[:, b, 0:4, 0:L], pLT[:, 0:512].rearrange("p (h t) -> p h t", h=4)[:, :, 0:L], r=[(pLT, 0)], w=[(qT, b)])
            evac_copy(qT[:, b, 4:8, 0:L], pLT[:, 512:1024].rearrange("p (h t) -> p h t", h=4)[:, :, 0:L], r=[(pLT, 1)], w=[(qT, b)])
        wqi = load_w(win_v, C_QI, 512)
        for b in range(nbk):
            pq_ = pm.next()
            for k in range(8):
                T_(lambda e, b=b, k=k, pq_=pq_: e.matmul(pq_[:L, :], lhsT=hT[:, k, L * b:L * b + L], rhs=wqi[:, k, :], start=(k == 0), stop=(k == 7)),
                   r=[hT, wqi], w=[pq_])
            evac_copy(q32[:L, 0:512], pq_[:L, :], r=[pq_], w=[(q32, 0)])
            rope_rows(q32, L, 0, 8, rp[b][:L, 32:40], rp[b][:L, 40:48], nh=8, stride=64)
            pt_ = pm.next()
            for pp in range(4):
                T_(lambda e, pp=pp, pt_=pt_: e.transpose(out=pt_[:, 128 * pp:128 * pp + L], in_=q32[:L, 128 * pp:128 * (pp + 1)], identity=identF[:L, :L]),
                   r=[q32, identF], w=[pt_])
            evac_copy(qiT[:, b, :, 0:L], pt_[:, :].rearrange("p (h t) -> p h t", h=4)[:, :, 0:L], r=[pt_], w=[(qiT, b)])
        for g in range(4):
            wg_ = load_w(win_v, C_G + 512 * g, 512)
            for b in range(nbk):
                pq_ = pm.next()
                for k in range(8):
                    T_(lambda e, b=b, k=k, pq_=pq_, wg_=wg_: e.matmul(pq_[:L, :], lhsT=hT[:, k, L * b:L * b + L], rhs=wg_[:, k, :], start=(k == 0), stop=(k == 7)),
                       r=[hT, wg_], w=[pq_])
                A_(lambda e, b=b, g=g, pq_=pq_: e.activation(out=gsig[:L, b, 512 * g:512 * (g + 1)], in_=pq_[:L, :], func=AF.Sigmoid), r=[pq_], w=[(gsig, (b, g))])

    def rw_tile(c, wtile, wcol, Ls, out_fn, carry_aps, save_last):
        L = Ls[0]
        ntk = sum(Ls)
        ps_ = pm.next()
        for k in range(8):
            T_(lambda e, k=k: e.matmul(ps_[:, 0:ntk], lhsT=wtile[:, k, wcol:wcol + 128], rhs=hT[:, k, 0:ntk], start=(k == 0), stop=(k == 7)),
               r=[hT, wtile], w=[ps_])
        tmp = f32t.next(); xs = f32t.next()
        A_(lambda e: e.activation(out=tmp[:, 0:ntk], in_=ps_[:, 0:ntk], func=AF.Identity, scale=colsd[:, c:c + 1]), r=[ps_, colsd], w=[tmp])
        if ntk > 1:
            V_(lambda e: e.scalar_tensor_tensor(out=xs[:, 1:ntk], in0=ps_[:, 0:ntk - 1], scalar=cols[:, O_MU + c:O_MU + c + 1],
                                                in1=tmp[:, 1:ntk], op0=ALU.mult, op1=ALU.add), r=[ps_, cols, tmp], w=[xs])
        for b in range(len(Ls)):
            ca = carry_aps[b]
            if ca is None:
                continue
            cap, creg = ca
            V_(lambda e, b=b, cap=cap: e.scalar_tensor_tensor(out=xs[:, L * b:L * b + 1], in0=cap[:, c:c + 1], scalar=cols[:, O_MU + c:O_MU + c + 1],
                                                             in1=tmp[:, L * b:L * b + 1], op0=ALU.mult, op1=ALU.add), r=[creg, cols, tmp], w=[xs])
        for b in range(len(Ls)):
            sl = save_last[b]
            if sl is None:
                continue
            sap, sreg = sl
            V_(lambda e, b=b, sap=sap: e.tensor_copy(out=sap[:, c:c + 1], in_=ps_[:, L * b + L - 1:L * b + L]), r=[ps_], w=[sreg])
        out_fn(xs)

    def rwkv_prep(Ls, own, carry_aps, save_last):
        ntk = sum(Ls)
        nb = len(Ls)
        L = Ls[0]
        rwt = lambda c, wtile, wcol, fn: rw_tile(c, wtile, wcol, Ls, fn, carry_aps, save_last)
        wl = load_w(win_v, C_RW + 3072, 256)

        def f24(xs):
            A_(lambda e: e.activation(out=TL[0:64, 0:ntk], in_=xs[0:64, 0:ntk], func=AF.Tanh), r=[xs], w=[TL])
            V_(lambda e: e.tensor_copy(out=TL[64:128, 0:ntk], in_=xs[64:128, 0:ntk]), r=[xs], w=[TL])
        rwt(24, wl, 0, f24)
        if own:
            def f25(xs):
                A_(lambda e: e.activation(out=SLG[:, 0:ntk], in_=xs[:, 0:ntk], func=AF.Sigmoid), r=[xs], w=[SLG])
            rwt(25, wl, 128, f25)
            for b in range(nb):
                for g in range(2):
                    pg = pm.next()
                    T_(lambda e, b=b, g=g, pg=pg: e.matmul(pg[:L, :], lhsT=SLG[:, L * b:L * b + L], rhs=g2b[:, 512 * g:512 * (g + 1)], start=True, stop=True),
                       r=[SLG, g2b], w=[pg])
                    evac_copy(gtok[:L, b, 512 * g:512 * (g + 1)], pg[:L, :], r=[pg], w=[(gtok, (b, g))])

        def blkview(t):
            return t[:, 0:ntk].rearrange("p (b l) -> p b l", b=nb)

        for half in range(2):
            wk_ = load_w(win_v, C_RW + 1024 + 512 * half, 512)
            for pp in range(4):
                p = 4 * half + pp

                def fk(kraw, p=p):
                    ps1 = pm.next()
                    T_(lambda e: e.matmul(ps1[:, 0:ntk], lhsT=w2b[0:64, 128 * p:128 * (p + 1)], rhs=TL[0:64, 0:ntk], start=True, stop=True),
                       r=[w2b, TL], w=[ps1])
                    sg = f32t.next()
                    A_(lambda e: e.activation(out=sg[:, 0:ntk], in_=ps1[:, 0:ntk], func=AF.Sigmoid, bias=cols[:, O_W0 + p:O_W0 + p + 1], scale=1.0),
                       r=[ps1, cols], w=[sg])
                    for b in range(nb):
                        V_(lambda e, b=b: e.tensor_tensor_scan(out=CS[:, p, b, 1:1 + L], data0=ones128[:, 0:L], data1=sg[:, L * b:L * (b + 1)],
                                                               initial=0.0, op0=ALU.mult, op1=ALU.add), r=[ones128, sg], w=[(CS, p)])
                    ps2 = pm.next()
                    T_(lambda e: e.matmul(ps2[:, 0:ntk], lhsT=a2b[64:128, 128 * p:128 * (p + 1)], rhs=TL[64:128, 0:ntk], start=True, stop=True),
                       r=[a2b, TL], w=[ps2])
                    asig = f32t.next()
                    A_(lambda e: e.activation(out=asig[:, 0:ntk], in_=ps2[:, 0:ntk], func=AF.Sigmoid, bias=cols[:, O_A0 + p:O_A0 + p + 1], scale=1.0),
                       r=[ps2, cols], w=[asig])
                    eex = f32t.next(); em = f32t.next()
                    A_(lambda e: e.activation(out=blkview(eex), in_=CS[:, p, 0:nb, 0:L], func=AF.Exp, scale=-DEC), r=[(CS, p)], w=[eex])
                    A_(lambda e: e.activation(out=blkview(em), in_=CS[:, p, 0:nb, 1:1 + L], func=AF.Exp, scale=DEC), r=[(CS, p)], w=[em])
                    A_(lambda e: e.activation(out=GC[:, 0:nb, p], in_=CS[:, p, 0:nb, L], func=AF.Exp, scale=-DEC), r=[(CS, p)], w=[(GC, p)])
                    kkr = f32t.next(); sq = f32t.next()
                    V_(lambda e: e.tensor_scalar(out=kkr[:, 0:ntk], in0=kraw[:, 0:ntk], scalar1=cols[:, O_KK + p:O_KK + p + 1], scalar2=None, op0=ALU.mult),
                       r=[kraw, cols], w=[kkr])
                    G_(lambda e: e.tensor_tensor(out=sq[:, 0:ntk], in0=kkr[:, 0:ntk], in1=kkr[:, 0:ntk], op=ALU.mult), r=[kkr], w=[sq])
                    ps3 = pm.next()
                    T_(lambda e: e.matmul(ps3[:, 0:ntk], lhsT=blkones[:], rhs=sq[:, 0:ntk], start=True, stop=True), r=[blkones, sq], w=[ps3])
                    A_(lambda e: e.activation(out=sq[:, 0:ntk], in_=ps3[:, 0:ntk], func=AF.Sqrt), r=[ps3], w=[sq])
                    V_(lambda e: e.tensor_scalar(out=sq[:, 0:ntk], in0=sq[:, 0:ntk], scalar1=1e-12, scalar2=None, op0=ALU.max), r=[sq], w=[sq])
                    V_(lambda e: e.reciprocal(out=sq[:, 0:ntk], in_=sq[:, 0:ntk]), r=[sq], w=[sq])
                    V_(lambda e: e.tensor_tensor(out=kkr[:, 0:ntk], in0=kkr[:, 0:ntk], in1=sq[:, 0:ntk], op=ALU.mult), r=[kkr, sq], w=[kkr])
                    V_(lambda e: e.tensor_scalar(out=sq[:, 0:ntk], in0=asig[:, 0:ntk], scalar1=cols[:, O_KA + p:O_KA + p + 1],
                                                 scalar2=colsd[:, 26 + p:27 + p], op0=ALU.mult, op1=ALU.add), r=[asig, cols, colsd], w=[sq])
                    G_(lambda e: e.tensor_tensor(out=sq[:, 0:ntk], in0=sq[:, 0:ntk], in1=kraw[:, 0:ntk], op=ALU.mult), r=[sq, kraw], w=[sq])
                    G_(lambda e: e.tensor_tensor(out=asig[:, 0:ntk], in0=asig[:, 0:ntk], in1=kkr[:, 0:ntk], op=ALU.mult), r=[asig, kkr], w=[asig])
                    V_(lambda e: e.scalar_tensor_tensor(out=AT[:, p, 0:ntk], in0=kkr[:, 0:ntk], scalar=-1.0, in1=eex[:, 0:ntk], op0=ALU.mult, op1=ALU.mult),
                       r=[kkr, eex], w=[(AT, p)])
                    V_(lambda e: e.tensor_tensor(out=KTt[:, p, 0:ntk], in0=sq[:, 0:ntk], in1=em[:, 0:ntk], op=ALU.mult), r=[sq, em], w=[(KTt, p)])
                    G_(lambda e: e.tensor_tensor(out=BTt[:, p, 0:ntk], in0=asig[:, 0:ntk], in1=em[:, 0:ntk], op=ALU.mult), r=[asig, em], w=[(BTt, p)])
                    if own:
                        G_(lambda e: e.tensor_scalar(out=KMR[:, p, 0:ntk], in0=sq[:, 0:ntk], scalar1=cols[:, O_RK + p:O_RK + p + 1], scalar2=None, op0=ALU.mult),
                           r=[sq, cols], w=[(KMR, p)])
                rwt(8 + p, wk_, 128 * pp, fk)
        for half in range(2):
            wv_ = load_w(win_v, C_RW + 2048 + 512 * half, 512)
            for pp in range(4):
                p = 4 * half + pp

                def fv(xs, p=p):
                    A_(lambda e: e.activation(out=VTf[:, p, 0:ntk], in_=xs[:, 0:ntk], func=AF.Copy), r=[xs], w=[(VTf, p)])
                rwt(16 + p, wv_, 128 * pp, fv)
        if own:
            pbc = pO[:, 1024:1536]
            pbreg = (pO, 2)
            first = [True]
            for half in range(2):
                wr_ = load_w(win_v, C_RW + 512 * half, 512)
                for pp in range(4):
                    p = 4 * half + pp

                    def fr(xs, p=p):
                        ep = f32t.next()
                        A_(lambda e: e.activation(out=blkview(ep), in_=CS[:, p, 0:nb, 1:1 + L], func=AF.Exp, scale=-DEC), r=[(CS, p)], w=[ep])
                        V_(lambda e: e.tensor_tensor(out=RTt[:, p, 0:ntk], in0=xs[:, 0:ntk], in1=ep[:, 0:ntk], op=ALU.mult), r=[xs, ep], w=[(RTt, p)])
                        prd = PRD.next()
                        G_(lambda e: e.tensor_tensor(out=prd[:, 0:ntk], in0=xs[:, 0:ntk], in1=KMR[:, p, 0:ntk], op=ALU.mult), r=[xs, (KMR, p)], w=[prd])
                        for b in range(nb):
                            T_(lambda e, b=b: e.matmul(pbc[:L, 16 * b + 2 * p:16 * b + 2 * p + 2], lhsT=prd[:, L * b:L * b + L], rhs=blkonesB[:, 0:128:64],
                                                       start=first[0] and b == 0, stop=True, skip_group_check=True), r=[prd, blkonesB], w=[pbreg])
                        first[0] = False
                    rwt(p, wr_, 128 * pp, fr)
            first[0] = True
            V_(lambda e: e.tensor_copy(out=bcoef[:L, 0:nb, :], in_=pbc[:L, 0:16 * nb].rearrange("p (b h) -> p b h", b=nb)), r=[pbreg], w=[bcoef])
        for b in range(nb):
            for (src, dst) in ((KTt, Ktok), (BTt, Btok), (VTf, Vtok)):
                for p in range(8):
                    T_(lambda e, p=p, b=b, src=src: e.transpose(out=ptr[0:L, p, :], in_=src[:, p, L * b:L * (b + 1)], identity=identB[:]),
                       r=[(src, p), identB], w=[ptr])
                evac_copy(dst[0:L, b, :].rearrange("p (k j) -> p k j", k=8), ptr[0:L, :, :], r=[ptr], w=[(dst, b)])

    def chunk_scan(b, L, own):
        nl = max(int(np.ceil(np.log2(L))) - 1, 0)
        c0 = L * b
        Lk = 128 if L == 64 else L
        if L == 64:
            for t_ in (AKs, ARB, ARK):
                G_(lambda e, t_=t_: e.memset(t_[64:128, :, :], 0.0), w=[t_])
            G_(lambda e: e.memset(Vtok[64:128, b, :], 0.0), w=[(Vtok, b)])
            G_(lambda e: e.memset(Ub[64:128, :, :], 0.0), w=[Ub])
        for hg in range(4):
            heads = [4 * hg + i for i in range(4)]
            cur = {}
            for i, h in enumerate(heads):
                p, bp = h // 2, 64 * (h % 2)
                bank = cb_ap(i); breg = cbanks[i]
                at = AT[bp:bp + 64, p, c0:c0 + L]; bt = BTt[bp:bp + 64, p, c0:c0 + L]; kt = KTt[bp:bp + 64, p, c0:c0 + L]
                T_(lambda e, bank=bank, bt=bt, at=at: e.matmul(bank[0:L, 0:L], lhsT=bt, rhs=at, start=True, stop=True), r=[(AT, p), (BTt, p)], w=[breg])
                T_(lambda e, bank=bank, bt=bt, at=at: e.matmul(bank[0:L, 128:128 + L], lhsT=at, rhs=bt, start=False, stop=True, skip_group_check=True),
                   r=[(AT, p), (BTt, p)], w=[breg])
                T_(lambda e, bank=bank, kt=kt, at=at: e.matmul(bank[0:L, 256:256 + L], lhsT=kt, rhs=at, start=False, stop=True, skip_group_check=True),
                   r=[(AT, p), (KTt, p)], w=[breg])
                nm_ = 3
                if own:
                    rt_ = RTt[bp:bp + 64, p, c0:c0 + L]
                    T_(lambda e, bank=bank, bt=bt, rt_=rt_: e.matmul(bank[0:L, 384:384 + L], lhsT=bt, rhs=rt_, start=False, stop=True, skip_group_check=True),
                       r=[(RTt, p), (BTt, p)], w=[breg])
                    nm_ = 4
                am = amat.next()
                V_(lambda e, bank=bank, am=am, nm_=nm_: e.tensor_tensor(out=am[0:L, 0:nm_, 0:L], in0=bank[0:L, 0:128 * nm_].rearrange("p (m t) -> p m t", m=nm_)[:, :, 0:L],
                                                                       in1=mask[0:L, 0:128 * nm_].rearrange("p (m t) -> p m t", m=nm_)[:, :, 0:L], op=ALU.mult),
                   r=[breg, mask], w=[am])
                G_(lambda e, am=am, h=h: e.tensor_copy(out=AKs[0:L, h, 0:L], in_=am[0:L, 2, 0:L]), r=[am], w=[(AKs, h)])
                if own:
                    G_(lambda e, am=am, h=h: e.tensor_copy(out=ARB[0:L, h, 0:L], in_=am[0:L, 3, 0:L]), r=[am], w=[(ARB, h)])
                m0 = Mt[i].next()
                G_(lambda e, am=am, m0=m0: e.tensor_tensor(out=m0[0:L, 0:L], in0=am[0:L, 0, 0:L], in1=identB[0:L, 0:L], op=ALU.add), r=[am, identB], w=[m0])
                cur[h] = dict(P=am[0:L, 0, 0:L], Q=am[0:L, 1, 0:L], Pt=am, Qt=am, M=m0)
            for lev in range(nl):
                last = (lev == nl - 1)
                for i, h in enumerate(heads):
                    bank = cb_ap(i); breg = cbanks[i]
                    c = cur[h]
                    if not last:
                        T_(lambda e, bank=bank, cq=c['Q'], cp=c['P']: e.matmul(bank[0:L, 0:L], lhsT=cq, rhs=cp, start=True, stop=True), r=[c['Pt'], c['Qt']], w=[breg])
                        T_(lambda e, bank=bank, cq=c['Q'], cp=c['P']: e.matmul(bank[0:L, 128:128 + L], lhsT=cp, rhs=cq, start=False, stop=True, skip_group_check=True),
                           r=[c['Pt'], c['Qt']], w=[breg])
                    else:
                        T_(lambda e, bank=bank, cq=c['Q'], cp=c['P']: e.matmul(bank[0:L, 128:128 + L], lhsT=cp, rhs=cq, start=True, stop=True),
                           r=[c['Pt'], c['Qt']], w=[breg])
                    nq = pq[i].next()
                    lo = 1 if last else 0
                    A_(lambda e, bank=bank, nq=nq, lo=lo: e.activation(out=nq[0:L, lo:2, 0:L],
                                                                      in_=bank[0:L, 128 * lo:256].rearrange("p (m t) -> p m t", m=2 - lo)[:, :, 0:L], func=AF.Copy),
                       r=[breg], w=[nq])
                    c['P'] = nq[0:L, 0, 0:L]; c['Q'] = nq[0:L, 1, 0:L]; c['Pt'] = nq; c['Qt'] = nq
                for i, h in enumerate(heads):
                    bank = cb_ap(i); breg = cbanks[i]
                    c = cur[h]
                    pmb = pm.next()
                    T_(lambda e, pmb=pmb, cq=c['Q'], cm=c['M']: e.matmul(pmb[0:L, 0:L], lhsT=cq, rhs=cm[0:L, 0:L], start=True, stop=True),
                       r=[c['Qt'], c['M']], w=[pmb])
                    mn = Mt[i].next()
                    V_(lambda e, pmb=pmb, cm=c['M'], mn=mn: e.tensor_tensor(out=mn[0:L, 0:L], in0=pmb[0:L, 0:L], in1=cm[0:L, 0:L], op=ALU.add),
                       r=[pmb, c['M']], w=[mn])
                    c['M'] = mn
            for i, h in enumerate(heads):
                G_(lambda e, h=h, m=cur[h]['M']: e.tensor_copy(out=Mfin[0:L, h, 0:L], in_=m[0:L, 0:L]), r=[cur[h]['M']], w=[(Mfin, h)])
        for g in range(2):
            bank = cb_ap(g); breg = cbanks[g]
            for hh in range(8):
                h = 8 * g + hh
                p, bp = h // 2, 64 * (h % 2)
                T_(lambda e, bank=bank, hh=hh, p=p, bp=bp: e.matmul(bank[0:L, 64 * hh:64 * hh + 64], lhsT=AT[bp:bp + 64, p, c0:c0 + L], rhs=STb[bp:bp + 64, p, :],
                                                                   start=(hh == 0), stop=False, skip_group_check=True), r=[(AT, p), STb], w=[breg])
                T_(lambda e, bank=bank, hh=hh, h=h: e.matmul(bank[0:L, 64 * hh:64 * hh + 64], lhsT=AKs[0:Lk, h, 0:L], rhs=Vtok[0:Lk, b, 64 * h:64 * h + 64],
                                                            start=False, stop=True, skip_group_check=True), r=[(AKs, h), (Vtok, b)], w=[breg])
            evac_copy(Wb[0:L, 8 * g:8 * g + 8, :], bank[0:L, 0:512].rearrange("p (h i) -> p h i", h=8), r=[breg], w=[(Wb, g)])
        for g in range(2):
            bank = cb_ap(2 + g); breg = cbanks[2 + g]
            for hh in range(8):
                h = 8 * g + hh
                T_(lambda e, bank=bank, hh=hh, h=h: e.matmul(bank[0:L, 64 * hh:64 * hh + 64], lhsT=Mfin[0:L, h, 0:L], rhs=Wb[0:L, h, :],
                                                            start=(hh == 0), stop=True, skip_group_check=True), r=[(Mfin, h), (Wb, g)], w=[breg])
            evac_copy(Ub[0:L, 8 * g:8 * g + 8, :], bank[0:L, 0:512].rearrange("p (h i) -> p h i", h=8), r=[breg], w=[(Ub, g)])
        if own:
            for rnd in range(2):
                for par in range(2):
                    bank = cb_ap(par); breg = cbanks[par]
                    for j in range(4):
                        h = 8 * rnd + 2 * j + par
                        p, bp = h // 2, 64 * par
                        T_(lambda e, bank=bank, j=j, p=p, bp=bp: e.matmul(bank[0:L, 128 * j:128 * j + L], lhsT=KTt[bp:bp + 64, p, c0:c0 + L], rhs=RTt[bp:bp + 64, p, c0:c0 + L],
                                                                         start=(j == 0), stop=True, skip_group_check=True), r=[(KTt, p), (RTt, p)], w=[breg])
                for par in range(2):
                    bank = cb_ap(par); breg = cbanks[par]
                    for j in range(4):
                        h = 8 * rnd + 2 * j + par
                        V_(lambda e, bank=bank, j=j, h=h: e.tensor_tensor(out=ARK[0:L, h, 0:L], in0=bank[0:L, 128 * j:128 * j + L], in1=mask[0:L, 384:384 + L], op=ALU.mult),
                           r=[breg, mask], w=[(ARK, h)])
            for g in range(2):
                bank = cb_ap(2 + g); breg = cbanks[2 + g]
                for hh in range(8):
                    h = 8 * g + hh
                    p, bp = h // 2, 64 * (h % 2)
                    oc = bank[0:L, 64 * hh:64 * hh + 64]
                    T_(lambda e, oc=oc, hh=hh, p=p, bp=bp: e.matmul(oc, lhsT=RTt[bp:bp + 64, p, c0:c0 + L], rhs=STb[bp:bp + 64, p, :],
                                                                   start=(hh == 0), stop=False, skip_group_check=True), r=[(RTt, p), STb], w=[breg])
                    T_(lambda e, oc=oc, h=h: e.matmul(oc, lhsT=ARK[0:Lk, h, 0:L], rhs=Vtok[0:Lk, b, 64 * h:64 * h + 64], start=False, stop=False, skip_group_check=True),
                       r=[(ARK, h), (Vtok, b)], w=[breg])
                    T_(lambda e, oc=oc, h=h: e.matmul(oc, lhsT=ARB[0:Lk, h, 0:L], rhs=Ub[0:Lk, h, :], start=False, stop=True, skip_group_check=True),
                       r=[(ARB, h), (Ub, h // 8)], w=[breg])
                evac_copy(Yt[0:L, 512 * g:512 * (g + 1)], bank[0:L, 0:512], r=[breg], w=[(Yt, g)])
        bank = cb_ap(0); breg = cbanks[0]
        for h in range(16):
            p, bp = h // 2, 64 * (h % 2)
            T_(lambda e, h=h, p=p, bp=bp: e.matmul(bank[bp:bp + 64, 64 * p:64 * p + 64], lhsT=Ktok[0:L, b, 64 * h:64 * h + 64], rhs=Vtok[0:L, b, 64 * h:64 * h + 64],
                                                  start=(h < 2), stop=False, skip_group_check=True), r=[(Ktok, b), (Vtok, b)], w=[breg])
            T_(lambda e, h=h, p=p, bp=bp: e.matmul(bank[bp:bp + 64, 64 * p:64 * p + 64], lhsT=Btok[0:L, b, 64 * h:64 * h + 64], rhs=Ub[0:L, h, :],
                                                  start=False, stop=True, skip_group_check=True), r=[(Btok, b), (Ub, h // 8)], w=[breg])
        V_(lambda e: e.tensor_tensor(out=sttmp[:].rearrange("p k i -> p (k i)"), in0=bank[:, 0:512], in1=ST[:].rearrange("p k i -> p (k i)"), op=ALU.add),
           r=[breg, ST], w=[sttmp])
        V_(lambda e: e.tensor_tensor(out=ST[:], in0=GC[:, b, :].unsqueeze(2).to_broadcast([128, 8, 64]), in1=sttmp[:], op=ALU.mult),
           r=[sttmp, GC], w=[ST])
        A_(lambda e: e.activation(out=STb[:], in_=ST[:], func=AF.Copy), r=[ST], w=[STb])
        if own:
            rwkv_out(b, L)

    def rwkv_out(b, L):
        y3 = Yt[0:L, :].rearrange("p (h i) -> p h i", h=16)
        y23 = Y2[0:L, :].rearrange("p (h i) -> p h i", h=16)
        V_(lambda e: e.tensor_reduce(out=gns[0:L, 0:16], in_=y3, axis=mybir.AxisListType.X, op=ALU.add), r=[Yt], w=[gns])
        A_(lambda e: e.activation(out=Y2[0:L, :], in_=Yt[0:L, :], func=AF.Square), r=[Yt], w=[Y2])
        V_(lambda e: e.tensor_reduce(out=gns[0:L, 16:32], in_=y23, axis=mybir.AxisListType.X, op=ALU.add), r=[Y2], w=[gns])
        V_(lambda e: e.tensor_scalar(out=gns[0:L, 0:16], in0=gns[0:L, 0:16], scalar1=1.0 / 64, scalar2=None, op0=ALU.mult), r=[gns], w=[gns])
        V_(lambda e: e.tensor_tensor(out=gns[0:L, 32:48], in0=gns[0:L, 0:16], in1=gns[0:L, 0:16], op=ALU.mult), r=[gns], w=[gns])
        V_(lambda e: e.scalar_tensor_tensor(out=gns[0:L, 48:64], in0=gns[0:L, 16:32], scalar=1.0 / 64, in1=gns[0:L, 32:48], op0=ALU.mult, op1=ALU.subtract),
           r=[gns], w=[gns])
        A_(lambda e: e.activation(out=gns[0:L, 48:64], in_=gns[0:L, 48:64], func=AF.Sqrt, bias=float(GN_EPS), scale=1.0), r=[gns], w=[gns])
        V_(lambda e: e.reciprocal(out=gns[0:L, 48:64], in_=gns[0:L, 48:64]), r=[gns], w=[gns])
        V_(lambda e: e.tensor_scalar(out=gns[0:L, 64:80], in0=gns[0:L, 48:64], scalar1=-1.0, scalar2=None, op0=ALU.mult), r=[gns], w=[gns])
        V_(lambda e: e.tensor_tensor(out=y23, in0=gns[0:L, 0:16].unsqueeze(2).to_broadcast([L, 16, 64]), in1=y3, op=ALU.subtract), r=[gns, Yt], w=[Y2])
        V_(lambda e: e.tensor_tensor(out=y23, in0=gns[0:L, 64:80].unsqueeze(2).to_broadcast([L, 16, 64]), in1=y23, op=ALU.mult), r=[gns, Y2], w=[Y2])
        vg, vbb = getvec("gn_g"), getvec("gn_b")
        G_(lambda e: e.tensor_tensor(out=Y2[0:L, :], in0=Y2[0:L, :], in1=vg[0:L, :], op=ALU.mult), r=[Y2, vg], w=[Y2])
        G_(lambda e: e.tensor_tensor(out=Y2[0:L, :], in0=Y2[0:L, :], in1=vbb[0:L, :], op=ALU.add), r=[Y2, vbb], w=[Y2])
        V_(lambda e: e.tensor_tensor(out=y3, in0=bcoef[0:L, b, :].unsqueeze(2).to_broadcast([L, 16, 64]), in1=Vtok[0:L, b, :].rearrange("p (h i) -> p h i", h=16), op=ALU.mult),
           r=[bcoef, (Vtok, b)], w=[Yt])
        G_(lambda e: e.tensor_tensor(out=Y2[0:L, :], in0=Y2[0:L, :], in1=Yt[0:L, :], op=ALU.add), r=[Y2, Yt], w=[Y2])
        G_(lambda e: e.tensor_tensor(out=orw[0:L, :], in0=Y2[0:L, :], in1=gtok[0:L, b, :], op=ALU.mult), r=[Y2, gtok], w=[orw])
        for k in range(8):
            T_(lambda e, k=k: e.transpose(out=ptr[:, k, 0:L], in_=orw[0:L, 128 * k:128 * (k + 1)], identity=identB[:L, :L]), r=[orw, identB], w=[ptr])
        evac_copy(orwT[:, b, :, 0:L], ptr[:, :, 0:L], r=[ptr], w=[(orwT, b)])

    def attention(b, Lq, slot_lo, slot_hi, corner):
        S = slot_hi - slot_lo
        wi = kvs[b]
        for h in range(8):
            V_(lambda e, h=h: e.tensor_scalar(out=dg[0:Lq, h, 0:Lq], in0=identB[0:Lq, 0:Lq], scalar1=wi[0:Lq, 320 + h:321 + h], scalar2=None, op0=ALU.mult),
               r=[identB, wi], w=[dg])
        cidx = 0.125 * (8.0 ** -0.5)
        ntile = (S + 511) // 512
        for ti in range(ntile):
            s0 = slot_lo + 512 * ti
            n = min(512, slot_hi - s0)
            kit = kitl.next(); kb_ = kbt.next()
            P.dma('sp', kit[0:64, 0:n], D_kit.ap()[:, s0:s0 + n], r=[D_kit], w=[kit])
            P.dma('sp', kit[64:128, 0:n], D_kit.ap()[:, s0:s0 + n], r=[D_kit], w=[kit])
            P.dma('sp', kb_[:, 0:n], keyb.ap()[:, s0:s0 + n], r=[keyb], w=[kb_])
            psc = pm.next()
            T_(lambda e, psc=psc, kb_=kb_, n=n: e.matmul(psc[0:Lq, 0:n], lhsT=onesrow[0:1, 0:Lq], rhs=kb_[0:1, 0:n], start=True, stop=False), r=[onesrow, kb_], w=[psc])
            def emit_x(h, kit=kit, n=n):
                pp, bp = h // 2, 64 * (h % 2)
                xb_ = xbanks[xq[0] % 3]; xq[0] += 1
                px = xb_[0][:, 512 * xb_[1]:512 * (xb_[1] + 1)]
                T_(lambda e, px=px, pp=pp, bp=bp: e.matmul(px[0:Lq, 0:n], lhsT=qiT[bp:bp + 64, b, pp, 0:Lq], rhs=kit[bp:bp + 64, 0:n], start=True, stop=True),
                   r=[(qiT, b), kit], w=[xb_])
                return px, xb_
            nxt = emit_x(0)
            for h in range(8):
                px, xb_ = nxt
                if h < 7:
                    nxt = emit_x(h + 1)
                r_ = Rt.next()
                A_(lambda e, px=px, r_=r_, n=n: e.activation(out=r_[0:Lq, 0:n], in_=px[0:Lq, 0:n], func=AF.Relu, scale=cidx), r=[xb_], w=[r_])
                T_(lambda e, psc=psc, h=h, r_=r_, n=n: e.matmul(psc[0:Lq, 0:n], lhsT=dg[0:Lq, h, 0:Lq], rhs=r_[0:Lq, 0:n], start=False, stop=(h == 7)), r=[dg, r_], w=[psc])
            V_(lambda e, psc=psc, ti=ti, n=n: e.tensor_copy(out=SC[0:Lq, 512 * ti:512 * ti + n], in_=psc[0:Lq, 0:n]), r=[psc], w=[(SC, ti)])
        if corner:
            V_(lambda e: e.memset(SC[0:64, S - 64:S], -1e30), r=[], w=[SC])
        V_(lambda e: e.memset(tau[0:Lq, 0:1], 0.0), w=[tau])
        V_(lambda e: e.memset(tau[0:Lq, 4:5], 0.0), w=[tau])
        Sa = (int(S * 0.42) // 16) * 16
        nbk_ = S - Sa
        for it in range(NBIS):
            s_ = 16.0 * (0.5 ** (it + 1))
            V_(lambda e: e.tensor_scalar(out=junk[0:Lq, 0:1].to_broadcast([Lq, Sa]), in0=SC[0:Lq, 0:Sa], scalar1=tau[0:Lq, 0:1], scalar2=None,
                                         op0=ALU.is_ge, op1=ALU.add, accum_out=tau[0:Lq, 1:2]), r=[SC, (tau, 't')], w=[(junk, 0), (tau, 'ca')])
            A_(lambda e: e.activation(out=junk[0:Lq, 1:2].to_broadcast([Lq, nbk_]), in_=SC[0:Lq, Sa:S], func=AF.Sign, bias=tau[0:Lq, 0:1], scale=-1.0,
                                      accum_out=tau[0:Lq, 5:6]), r=[SC, (tau, 't')], w=[(junk, 1), (tau, 'cb')])
            V_(lambda e: e.scalar_tensor_tensor(out=tau[0:Lq, 2:3], in0=tau[0:Lq, 1:2], scalar=2.0, in1=tau[0:Lq, 5:6], op0=ALU.mult, op1=ALU.subtract),
               r=[(tau, 'ca'), (tau, 'cb')], w=[(tau, 'd')])
            V_(lambda e, s_=s_: e.tensor_scalar(out=tau[0:Lq, 2:3], in0=tau[0:Lq, 2:3], scalar1=float(2 * TOPK - 1 - nbk_), scalar2=2.0 * s_, op0=ALU.is_ge, op1=ALU.mult),
               r=[(tau, 'd')], w=[(tau, 'd')])
            V_(lambda e, s_=s_: e.scalar_tensor_tensor(out=tau[0:Lq, 0:1], in0=tau[0:Lq, 2:3], scalar=-s_, in1=tau[0:Lq, 0:1], op0=ALU.add, op1=ALU.add),
               r=[(tau, 'd'), (tau, 't')], w=[(tau, 't')])
        s_last = 16.0 * (0.5 ** NBIS)
        V_(lambda e: e.tensor_scalar(out=tau[0:Lq, 3:4], in0=tau[0:Lq, 0:1], scalar1=-s_last, scalar2=None, op0=ALU.add), r=[(tau, 't')], w=[(tau, 'u')])
        blocks = []
        for ti in range(ntile):
            s0 = slot_lo + 512 * ti
            n = min(512, slot_hi - s0)
            for a in range((n + 127) // 128):
                blocks.append((ti, s0, n, a, min(128, n - 128 * a)))
        tiles = {}

        def tile_res(ti, s0, n):
            if ti in tiles:
                return tiles[ti]
            kt_ = ktl.next(); vt_ = vtl.next(); mb = mbt.next()
            P.dma('sp', kt_[:, 0:n], D_kt.ap()[:, s0:s0 + n], r=[D_kt], w=[kt_])
            nfull, rem = n // 128, n % 128
            if nfull:
                P.dma('sp', vt_[:, 0:nfull, 0:128], D_v.ap()[s0:s0 + 128 * nfull, :].rearrange("(a p) d -> p a d", p=128), r=[D_v], w=[vt_])
            if rem:
                P.dma('sp', vt_[0:rem, nfull, 0:128], D_v.ap()[s0 + 128 * nfull:s0 + n, :], r=[D_v], w=[vt_])
            V_(lambda e: e.tensor_scalar(out=mb[0:Lq, 0:n], in0=SC[0:Lq, 512 * ti:512 * ti + n], scalar1=tau[0:Lq, 3:4], scalar2=-30000.0,
                                         op0=ALU.is_lt, op1=ALU.mult), r=[(SC, ti), (tau, 'u')], w=[mb])
            tiles[ti] = (kt_, vt_, mb)
            return tiles[ti]

        def buf(j, hh):
            if j % 2 == 0:
                return pLT[:, 512 * hh:512 * hh + 512], (pLT, hh)
            return pm.t[hh][:, :], pm.t[hh]

        def emit_LT(j):
            ti, s0, n, a, ns = blocks[j]
            kt_, vt_, mb = tile_res(ti, s0, n)
            for hh in range(2):
                ap_, reg = buf(j, hh)
                T_(lambda e, ap_=ap_, hh=hh: e.matmul(ap_[0:ns, 0:4 * Lq], lhsT=kt_[:, 128 * a:128 * a + ns], rhs=qT[:, b, 4 * hh:4 * hh + 4, 0:Lq], start=True, stop=False),
                   r=[kt_, (qT, b)], w=[reg])
                T_(lambda e, ap_=ap_: e.matmul(ap_[0:ns, 0:4 * Lq], lhsT=mb[0:Lq, 128 * a:128 * a + ns], rhs=I4[0:Lq, :, 0:Lq], start=False, stop=True),
                   r=[mb, I4], w=[reg])
        emit_LT(0)
        for j in range(len(blocks)):
            if j + 1 < len(blocks):
                emit_LT(j + 1)
            ti, s0, n, a, ns = blocks[j]
            kt_, vt_, mb = tiles[ti]
            pt = PTt.next()
            for hh in range(2):
                ap_, reg = buf(j, hh)
                A_(lambda e, ap_=ap_, hh=hh, pt=pt, ns=ns: e.activation(out=pt[0:ns, 4 * Lq * hh:4 * Lq * (hh + 1)], in_=ap_[0:ns, 0:4 * Lq], func=AF.Exp, scale=128.0 ** -0.5),
                   r=[reg], w=[(pt, hh)])
            last = (j == len(blocks) - 1)
            for h in range(8):
                off = 512 * (h // 3) + 129 * (h % 3)
                T_(lambda e, h=h, off=off, ns=ns, a=a, pt=pt, vt_=vt_, st_=(j == 0 and h % 3 == 0), last=last: e.matmul(
                    pO[0:Lq, off:off + 129], lhsT=pt[0:ns, Lq * h:Lq * h + Lq], rhs=vt_[0:ns, a, 0:129], start=st_, stop=last, skip_group_check=True),
                   r=[pt, vt_], w=[(pO, h // 3)])
        for h in range(8):
            off = 512 * (h // 3) + 129 * (h % 3)
            V_(lambda e, h=h, off=off: e.reciprocal(out=rden[0:Lq, h:h + 1], in_=pO[0:Lq, off + 128:off + 129]), r=[(pO, h // 3)], w=[rden])
            V_(lambda e, h=h, off=off: e.tensor_scalar(out=attn[0:Lq, 128 * h:128 * h + 128], in0=pO[0:Lq, off:off + 128], scalar1=rden[0:Lq, h:h + 1], scalar2=None, op0=ALU.mult),
               r=[(pO, h // 3), rden], w=[attn])
        for k in range(8):
            T_(lambda e, k=k: e.transpose(out=ptr[:, k, 0:Lq], in_=attn[0:Lq, 128 * k:128 * (k + 1)], identity=identB[:Lq, :Lq]), r=[attn, identB], w=[ptr])
        evac_copy(attnT[:, b, :, 0:Lq], ptr[:, :, 0:Lq], r=[ptr], w=[(attnT, b)])


    def post(Ls, y_dram, yrows):
        L = Ls[0]
        nbk = len(Ls)
        ntk = sum(Ls)
        woa = [load_w(kview(D_woa), 512 * g, 512) for g in range(2)]
        for b in range(nbk):
            for g in range(2):
                po_ = pm.next()
                for k in range(8):
                    T_(lambda e, b=b, g=g, k=k, po_=po_: e.matmul(po_[:L, :], lhsT=attnT[:, b, k, 0:L], rhs=woa[g][:, k, :], start=(k == 0), stop=(k == 7)),
                       r=[(attnT, b), woa[g]], w=[po_])
                mx = mixs[b]
                V_(lambda e, b=b, g=g, po_=po_, mx=mx: e.tensor_tensor(out=mx[:L, 512 * g:512 * (g + 1)], in0=po_[:L, :],
                                                                    in1=gsig[:L, b, 512 * g:512 * (g + 1)], op=ALU.mult), r=[po_, gsig], w=[(mx, g)])
        wor = [load_w(kview(D_wor), 512 * g, 512) for g in range(2)]
        for b in range(nbk):
            mx = mixs[b]
            for g in range(2):
                po_ = pm.next()
                for k in range(8):
                    T_(lambda e, b=b, g=g, k=k, po_=po_: e.matmul(po_[:L, :], lhsT=orwT[:, b, k, 0:L], rhs=wor[g][:, k, :], start=(k == 0), stop=(k == 7)),
                       r=[(orwT, b), wor[g]], w=[po_])
                tq = rtmp.next()
                V_(lambda e, b=b, g=g, po_=po_, tq=tq: e.tensor_tensor(out=tq[:L, :], in0=po_[:L, :], in1=gsig[:L, b, D + 512 * g:D + 512 * (g + 1)], op=ALU.mult),
                   r=[po_, gsig], w=[tq])
                G_(lambda e, g=g, mx=mx, tq=tq: e.tensor_tensor(out=mx[:L, 512 * g:512 * (g + 1)], in0=mx[:L, 512 * g:512 * (g + 1)], in1=tq[:L, :], op=ALU.add),
                   r=[tq, (mx, g)], w=[(mx, g)])
            for k in range(8):
                T_(lambda e, k=k, mx=mx: e.transpose(out=ptr[:, k, 0:L], in_=mx[:L, 128 * k:128 * (k + 1)], identity=identB[:L, :L]), r=[mx, identB], w=[ptr])
            evac_copy(mixT[:, :, L * b:L * b + L], ptr[:, :, 0:L], r=[ptr], w=[(mixT, b)])
        wo = [load_w(kview(D_wout), 512 * g, 512) for g in range(2)]
        for b in range(nbk):
            for g in range(2):
                po_ = pm.next()
                for k in range(8):
                    T_(lambda e, b=b, g=g, k=k, po_=po_: e.matmul(po_[:L, :], lhsT=mixT[:, k, L * b:L * b + L], rhs=wo[g][:, k, :], start=(k == 0), stop=(k == 7)),
                       r=[(mixT, b), wo[g]], w=[po_])
                V_(lambda e, b=b, g=g, po_=po_: e.scalar_tensor_tensor(out=x1[:L, b, 512 * g:512 * (g + 1)], in0=hres[:L, b, 512 * g:512 * (g + 1)], scalar=float(ALPHA),
                                                                    in1=po_[:L, :], op0=ALU.mult, op1=ALU.add), r=[po_, (hres, b)], w=[(x1, b)])
            ln_rows(x1[:L, b, :], L, D, x1[:L, b, :], LN_EPS, g_b=(getvec("ln1_g"), getvec("ln1_b")))
            G_(lambda e, b=b: e.tensor_copy(out=x1b[:L, :], in_=x1[:L, b, :]), r=[x1], w=[x1b])
            for k in range(8):
                T_(lambda e, k=k: e.transpose(out=ptr[:, k, 0:L], in_=x1b[:L, 128 * k:128 * (k + 1)], identity=identB[:L, :L]), r=[x1b, identB], w=[ptr])
            evac_copy(x1T[:, :, L * b:L * b + L], ptr[:, :, 0:L], r=[ptr], w=[(x1T, b)])
        accs = [[(pLT, 0), (pLT, 1)], [(pO, 0), (pO, 1)]]
        acc_ap = lambda b, g: (accs[b][g][0])[:, 512 * accs[b][g][1]:512 * (accs[b][g][1] + 1)]
        nfc = DFF // 128
        fbanks = [pm.t[0], pm.t[1], (pO, 2)]
        fbq = [0]
        for fq in range(0, nfc, 4):
            nq = min(4, nfc - fq)
            wg_ = load_w(kview(D_wfg), 128 * fq, 128 * nq)
            wu_ = load_w(kview(D_wfu), 128 * fq, 128 * nq)
            wd_ = wt.next()
            wdv = wd_[:].rearrange("p (a c) n -> p a (c n)", c=2)
            P.dma('sp', wdv[:, 0:nq, :], D_wfd.ap()[128 * fq:128 * (fq + nq), :].rearrange("(a p) n -> p a n", p=128), r=[D_wfd], w=[wd_])
            def emit_gu(j, wg_=wg_, wu_=wu_):
                bg = fbanks[fbq[0] % 3]; bu = fbanks[(fbq[0] + 1) % 3]; fbq[0] += 2
                pg_ = bg[0][:, 512 * bg[1]:512 * bg[1] + 512] if isinstance(bg, tuple) else bg[:, :]
                pu_ = bu[0][:, 512 * bu[1]:512 * bu[1] + 512] if isinstance(bu, tuple) else bu[:, :]
                for k in range(8):
                    T_(lambda e, k=k: e.matmul(pg_[:, 0:ntk], lhsT=wg_[:, k, 128 * j:128 * (j + 1)], rhs=x1T[:, k, 0:ntk], start=(k == 0), stop=(k == 7)),
                       r=[x1T, wg_], w=[bg])
                ag = actg.next()
                A_(lambda e: e.activation(out=ag[:, 0:ntk], in_=pg_[:, 0:ntk], func=AF.Silu), r=[bg], w=[ag])
                for k in range(8):
                    T_(lambda e, k=k: e.matmul(pu_[:, 0:ntk], lhsT=wu_[:, k, 128 * j:128 * (j + 1)], rhs=x1T[:, k, 0:ntk], start=(k == 0), stop=(k == 7)),
                       r=[x1T, wu_], w=[bu])
                at_ = actT.next()
                V_(lambda e: e.tensor_tensor(out=at_[:, 0:ntk], in0=pu_[:, 0:ntk], in1=ag[:, 0:ntk], op=ALU.mult), r=[bu, ag], w=[at_])
                return at_

            def emit_down(j, at_, wd_=wd_, wdv=wdv, fq=fq):
                fc = fq + j
                for b in range(nbk):
                    for g in range(2):
                        T_(lambda e, b=b, g=g: e.matmul(acc_ap(b, g)[:L, :], lhsT=at_[:, L * b:L * b + L], rhs=wdv[:, j, 512 * g:512 * (g + 1)],
                                                        start=(fc == 0), stop=(fc == nfc - 1)), r=[at_, wd_], w=[accs[b][g]])
            pend = emit_gu(0)
            for j in range(nq):
                nxt_ = emit_gu(j + 1) if j + 1 < nq else None
                emit_down(j, pend)
                pend = nxt_
        for b in range(nbk):
            yt = yo.next()
            for g in range(2):
                V_(lambda e, b=b, g=g, yt=yt: e.scalar_tensor_tensor(out=yt[:L, 512 * g:512 * (g + 1)], in0=x1[:L, b, 512 * g:512 * (g + 1)], scalar=float(ALPHA),
                                                                  in1=acc_ap(b, g)[:L, :], op0=ALU.mult, op1=ALU.add), r=[accs[b][g], x1], w=[(yt, g)])
            ln_rows(yt[:L, :], L, D, yt[:L, :], LN_EPS, g_b=(getvec("ln2_g"), getvec("ln2_b")))
            P.dma('pool', y_dram.ap()[yrows[b]:yrows[b] + L, :], yt[:L, :], r=[yt], w=[(y_dram, yrows[b])])

    nso_sb = GEOM['NSO_B'] // NB
    so_blocks = [(NT * i, [128] * NB) for i in range(nso_sb)] + [(128 * GEOM['NSO_B'], [16])]
    for (row0, Ls) in so_blocks:
        L = Ls[0]
        if L == 16:
            V_(lambda e: e.tensor_scalar(out=ST[:].rearrange("p k i -> p (k i)"), in0=ST[:].rearrange("p k i -> p (k i)"), scalar1=flg[:, 0:1], scalar2=None, op0=ALU.mult),
               r=[ST, flg], w=[ST])
            A_(lambda e: e.activation(out=STb[:], in_=ST[:], func=AF.Copy), r=[ST], w=[STb])
            V_(lambda e: e.tensor_scalar(out=car[:], in0=car[:], scalar1=flg[:, 0:1], scalar2=None, op0=ALU.mult), r=[car, flg], w=[car])
        rows = [row0 + L * b for b in range(len(Ls))]
        front(xso, rows, Ls, rows, rows, (O_k, O_v, O_ki, rows), own=False)
        carry = [(car, car)] + [None] * (len(Ls) - 1)
        save = [None] * (len(Ls) - 1) + [(car, car)]
        rwkv_prep(Ls, False, carry, save)
        if L == 16:
            noop = lambda xs: None
            for half in range(2):
                wr_ = load_w(win_v, C_RW + 512 * half, 512)
                for pp in range(4):
                    rw_tile(4 * half + pp, wr_, 128 * pp, Ls, noop, carry, save)
            wl_ = load_w(win_v, C_RW + 3072, 256)
            rw_tile(25, wl_, 128, Ls, noop, carry, save)
        for b in range(len(Ls)):
            chunk_scan(b, L, own=False)

    for sbi in range(GEOM['NOWN_B'] // NB):
        Ls = [128] * NB
        rows = [NT * sbi + 128 * b for b in range(NB)]
        slots = [NSO + r_ for r_ in rows]
        front(xown, rows, Ls, slots, slots, (O_k, O_v, O_ki, slots), own=True)
        own_proj(Ls, slots)
        carry = [(car, car)] + [None] * (NB - 1)
        save = [None] * (NB - 1) + [(car, car)]
        rwkv_prep(Ls, True, carry, save)
        for b in range(NB):
            chunk_scan(b, 128, own=True)
        for b in range(NB):
            attention(b, 128, 0, slots[b] + 128, corner=True)
        post(Ls, O_y, rows)
    P.dma('sp', O_wkv.ap(), ST[:].rearrange("p k i -> p (k i)"), r=[ST], w=[O_wkv])
    P.dma('sp', O_shift.ap(), car[:], r=[car], w=[O_shift])

    if SAMPLE:
        for q in range(2):
            sb0 = NSLOT + SSTRIDE * q
            for i0 in range(0, CACHE_ROWS, 128):
                L = min(128, CACHE_ROWS - i0)
                ct = xin.next()
                P.dma('sp', ct[:L, 0, 0:128], ck.ap()[q, i0:i0 + L, :], r=[ck], w=[ct])
                P.dma('sp', ct[:L, 0, 128:192], cik.ap()[q, i0:i0 + L, :], r=[cik], w=[ct])
                pt_ = pm.next()
                T_(lambda e, ct=ct, pt_=pt_, L=L: e.transpose(out=pt_[:, 0:L], in_=ct[:L, 0, 0:128], identity=identF[:L, :L]), r=[ct, identF], w=[pt_])
                T_(lambda e, ct=ct, pt_=pt_, L=L: e.transpose(out=pt_[0:64, 128:128 + L], in_=ct[:L, 0, 128:192], identity=identF[:L, :L]), r=[ct, identF], w=[pt_])
                kt_t = ktt.next(); kit_t = kitt.next()
                V_(lambda e, kt_t=kt_t, pt_=pt_, L=L: e.tensor_copy(out=kt_t[:, 0:L], in_=pt_[:, 0:L]), r=[pt_], w=[kt_t])
                V_(lambda e, kit_t=kit_t, pt_=pt_, L=L: e.tensor_copy(out=kit_t[:, 0:L], in_=pt_[0:64, 128:128 + L]), r=[pt_], w=[kit_t])
                P.dma('pool', D_kt.ap()[:, sb0 + i0:sb0 + i0 + L], kt_t[:, 0:L], r=[kt_t], w=[(D_kt, sb0 + i0)])
                P.dma('pool', D_kit.ap()[:, sb0 + i0:sb0 + i0 + L], kit_t[:, 0:L], r=[kit_t], w=[(D_kit, sb0 + i0)])
        for q in range(2):
            P.dma('sp', cars[:, q, :], sshift.ap()[q], r=[sshift], w=[(cars, q)])
        for q in range(2):
            sb0 = NSLOT + SSTRIDE * q
            Ls = [64]
            rows = [64 * q]
            rrows = [NSO + NOWN + 64 * q]
            slots = [sb0 + CACHE_ROWS]
            front(xsm, rows, Ls, rrows, slots, (O_ks, O_vs, O_kis, rows), own=True)
            own_proj(Ls, rrows)
            rwkv_prep(Ls, True, [(cars[:, q, :], (cars, q))], [(cars[:, q, :], (cars, q))])
            P.dma('sp', ST[:].rearrange("p k i -> p (k i)"), swkv.ap()[q], r=[swkv], w=[ST])
            A_(lambda e: e.activation(out=STb[:], in_=ST[:], func=AF.Copy), r=[ST], w=[STb])
            chunk_scan(0, 64, own=True)
            P.dma('sp', O_wkvs.ap()[q], ST[:].rearrange("p k i -> p (k i)"), r=[ST], w=[(O_wkvs, q)])
            P.dma('sp', O_shifts.ap()[q], cars[:, q, :], r=[(cars, q)], w=[(O_shifts, q)])
            attention(0, 64, sb0, sb0 + CACHE_ROWS + 64, corner=False)
            post(Ls, O_ys, rows)
    nc = P.build()
    return nc, P


def make_consts():
    c = {}
    c["identf"] = np.eye(128, dtype=np.float32)
    b = np.zeros((128, 128), np.float32); b[:64, :64] = 1; b[64:, 64:] = 1
    c["blk1"] = b
    us = np.triu(np.ones((128, 128), np.float32), 1)
    ui = np.triu(np.ones((128, 128), np.float32), 0)
    c["cmask"] = np.concatenate([us, us.T, us, ui, ui], axis=1).astype(np.float32)
    return c


def rope_table(pos):
    pos = np.asarray(pos, np.float32)
    out = np.zeros((len(pos), 48), np.float32)
    for (rot, o) in ((32, 0), (16, 32)):
        inv = (np.float32(500000.0) ** (-np.arange(0, rot, 2, dtype=np.float32) / np.float32(rot))).astype(np.float32)
        ang = (pos[:, None] * inv[None]).astype(np.float32)
        h = rot // 2
        out[:, o:o + h] = np.cos(ang); out[:, o + h:o + 2 * h] = np.sin(ang)
    return out


def colpack(v, n):
    return np.ascontiguousarray(np.asarray(v, np.float32).reshape(n, 128).T)


def st_layout(s):
    s = np.asarray(s, np.float32).reshape(8, 2, 64, 64)
    return np.ascontiguousarray(s.transpose(1, 3, 0, 2).reshape(128, 512))


def st_unlayout(a):
    a = np.asarray(a, np.float32).reshape(2, 64, 8, 64)
    return np.ascontiguousarray(a.transpose(2, 0, 3, 1).reshape(16, 64, 64))


def prep_inputs(inp):
    NSO, NOWN, NSLOT = geom()
    f32 = lambda a: np.ascontiguousarray(np.asarray(a, np.float32))
    consts = make_consts()
    maps = []
    colsf = np.concatenate([
        colpack(inp["ln0_g"], 8), colpack(inp["ln0_b"], 8), colpack(inp["rw_mu"][0], 26), colpack(inp["rw_w0"][0], 8),
        colpack(inp["rw_a0"][0], 8), colpack(inp["rw_k_k"][0], 8), colpack(inp["rw_k_a"][0], 8), colpack(np.asarray(inp["rw_r_k"][0]).reshape(-1), 8)], axis=1)
    shared = dict(consts)
    shared.update(colsf=colsf, w_in=f32(inp["w_in"][0]), rw_w2=f32(inp["rw_w2"][0]), rw_a2=f32(inp["rw_a2"][0]), rw_g2=f32(inp["rw_g2"][0]),
                  ikg=f32(inp["idx_k_ln_g"][0]), ikb=f32(inp["idx_k_ln_b"][0]),
                  ln0_g=f32(inp["ln0_g"]), ln0_b=f32(inp["ln0_b"]), ln1_g=f32(inp["ln1_g"][0]), ln1_b=f32(inp["ln1_b"][0]),
                  ln2_g=f32(inp["ln2_g"][0]), ln2_b=f32(inp["ln2_b"][0]), gn_g=f32(inp["rw_gn_g"][0]), gn_b=f32(inp["rw_gn_b"][0]),
                  w_oa=f32(inp["w_o_attn"][0]), w_or=f32(inp["w_o_rwkv"][0]), w_out=f32(inp["w_out"][0]),
                  w_fg=f32(inp["ffn_w_gate"][0]), w_fu=f32(inp["ffn_w_up"][0]), w_fd=f32(inp["ffn_w_down"][0]))
    meta = f32(inp["meta_tokens"])
    past = int(np.asarray(inp["cache_k"]).shape[2]) - 16
    for c in range(8):
        b, hf = c // 2, c % 2
        xp = f32(inp["x_prompt"][b])
        nfr = NSO - 16
        if hf == 1:
            xso = np.concatenate([meta, xp[:nfr]], 0)
            pos_so = np.arange(NSO)
            xown = xp[nfr:nfr + NOWN]; pos_own = NSO + np.arange(NOWN)
        else:
            xso = np.concatenate([xp[nfr:2 * nfr], meta], 0)
            pos_so = np.concatenate([np.zeros(nfr), np.arange(16)])
            xown = xp[:NOWN]; pos_own = 16 + np.arange(NOWN)
        keyb = np.zeros((1, NSLOT + 2 * SSTRIDE), np.float32)
        if hf == 0:
            keyb[0, :nfr] = -1e30
        xs = f32(inp["x_sample"][2 * c:2 * c + 2]).reshape(128, D)
        pos_sm = np.concatenate([16 + past + np.arange(64)] * 2)
        m = dict(shared)
        m.update(xso=np.ascontiguousarray(xso), xown=np.ascontiguousarray(xown), xsm=xs,
                 rope=rope_table(np.concatenate([pos_so, pos_own, pos_sm])),
                 flag=np.full((128, 1), float(hf), np.float32), keyb=keyb.astype(ml_dtypes.bfloat16),
                 ck=f32(inp["cache_k"][0, 2 * c:2 * c + 2]), cv=f32(inp["cache_v"][0, 2 * c:2 * c + 2]), cik=f32(inp["cache_idx_k"][0, 2 * c:2 * c + 2]),
                 swkv=np.stack([st_layout(inp["state_wkv"][0, 2 * c + q]) for q in range(2)]),
                 sshift=np.stack([colpack(inp["state_shift"][0, 2 * c + q], 26) for q in range(2)]))
        maps.append(m)
    return maps


_CACHE = {}


def run_device(inp):
    key = (GEOM['NSO_B'], GEOM['NOWN_B'], GEOM['SAMPLE'], MAXOPS)
    if key not in _CACHE:
        _CACHE[key] = build_program()
    nc, P = _CACHE[key]
    maps = prep_inputs(inp)
    used = set(P.names)
    maps = [{k: v for k, v in m.items() if k in used} for m in maps]
    res = run_bass_kernel_spmd(nc, maps, core_ids=list(range(8)))
    return res.results


def uncol(a, n):
    return np.ascontiguousarray(np.asarray(a, np.float32).T.reshape(-1))


def kernel(**inputs):
    NSO, NOWN, NSLOT = geom()
    res = run_device(inputs)
    B = 4
    y_p = np.zeros((B, 2 * NOWN, D), np.float32)
    k_p = np.zeros((1, B, NSLOT, 128), np.float32); v_p = np.zeros((1, B, NSLOT, 128), np.float32); ki_p = np.zeros((1, B, NSLOT, 64), np.float32)
    wkv_p = np.zeros((1, B, 16, 64, 64), np.float32); sh_p = np.zeros((1, B, 3328), np.float32)
    y_s = np.zeros((16, 64, D), np.float32)
    k_s = np.zeros((1, 16, 64, 128), np.float32); v_s = np.zeros((1, 16, 64, 128), np.float32); ki_s = np.zeros((1, 16, 64, 64), np.float32)
    wkv_s = np.zeros((1, 16, 16, 64, 64), np.float32); sh_s = np.zeros((1, 16, 3328), np.float32)
    for c in range(8):
        b, hf = c // 2, c % 2
        r = res[c]
        y_p[b, hf * NOWN:(hf + 1) * NOWN] = r["O_y"]
        if hf == 1:
            k_p[0, b] = r["O_k"]; v_p[0, b] = r["O_v"]; ki_p[0, b] = r["O_ki"]
            wkv_p[0, b] = st_unlayout(r["O_wkv"]); sh_p[0, b] = uncol(r["O_shift"], 26)
        y_s[2 * c:2 * c + 2] = r["O_ys"].reshape(2, 64, D)
        k_s[0, 2 * c:2 * c + 2] = r["O_ks"].reshape(2, 64, 128); v_s[0, 2 * c:2 * c + 2] = r["O_vs"].reshape(2, 64, 128)
        ki_s[0, 2 * c:2 * c + 2] = r["O_kis"].reshape(2, 64, 64)
        for q in range(2):
            wkv_s[0, 2 * c + q] = st_unlayout(r["O_wkvs"][q]); sh_s[0, 2 * c + q] = uncol(r["O_shifts"][q], 26)
    return (y_p, y_s, k_p, v_p, ki_p, wkv_p, sh_p, k_s, v_s, ki_s, wkv_s, sh_s)
```
